# Optimizing a Trainium2 kernel written in Bass

```python
import math
import jax, jax.numpy as jnp
from jax import lax
import numpy as np

D_MODEL = 1024
BATCH = 4
SEQ = 8192
DEPTH = 2

GRID_W = 64
CTX_LEN = 256
EPS = 1e-6

SSD_INNER = 1024
SSD_HEAD_DIM = 64
SSD_HEADS = SSD_INNER // SSD_HEAD_DIM
SSD_GROUPS = 4
SSD_HPG = SSD_HEADS // SSD_GROUPS
SSD_STATE = 128
SSD_CONV = 5
SSD_CHUNK = 128
SSD_CONV_DIM = SSD_INNER + 2 * SSD_GROUPS * SSD_STATE
ROPE_FREQS = SSD_STATE // 4
ROPE_BASE = 10000.0

NA_HEAD_DIM = 64
NA_WIDTH = 512
NA_HEADS = NA_WIDTH // NA_HEAD_DIM
NA_ROWS = 8
NA_COLS = 16

GM_WIDTH = 512
GM_GROUPS = 8
GM_GROUP_DIM = GM_WIDTH // GM_GROUPS
GM_CHUNK = 128

N_BRANCH = 3
FFN_HIDDEN = ((8 * D_MODEL + 3 * 256 - 1) // (3 * 256)) * 256

IN_SIZES = (SSD_INNER, SSD_CONV_DIM, 2 * SSD_HEADS, 3 * NA_WIDTH, 2 * GM_WIDTH, N_BRANCH * D_MODEL)
IN_WIDTH = sum(IN_SIZES)

kernel_name = 'hybrid_ssd_natten_gmlp_dit_block'


def split_cols(t, sizes):
    out, start = [], 0
    for s in sizes:
        out.append(t[..., start:start + s])
        start += s
    return out


def rmsnorm(t, w):
    tf = t.astype(jnp.float32)
    tf = tf * lax.rsqrt(jnp.mean(tf * tf, axis=-1, keepdims=True) + EPS)
    return tf.astype(t.dtype) * w


def modulate(t, shift, scale):
    return t * (1 + scale) + shift


def dwconv_centred(t, w, b):
    ch = t.shape[-1]
    y = lax.conv_general_dilated(t, w[:, None, :], window_strides=(1,),
                                 padding=[(SSD_CONV // 2, SSD_CONV // 2)],
                                 dimension_numbers=('NWC', 'WIO', 'NWC'),
                                 feature_group_count=ch)
    return y + b


def rope_2d(t, ang_row, ang_col):
    def rot(u, ang):
        cos = jnp.cos(ang)[:, None, :].astype(u.dtype)
        sin = jnp.sin(ang)[:, None, :].astype(u.dtype)
        u1, u2 = u[..., :ROPE_FREQS], u[..., ROPE_FREQS:]
        return jnp.concatenate([u1 * cos - u2 * sin, u1 * sin + u2 * cos], axis=-1)
    half = SSD_STATE // 2
    return jnp.concatenate([rot(t[..., :half], ang_row), rot(t[..., half:], ang_col)], axis=-1)


def ssd_scan(x, dt, A, Bm, Cm, h0, return_y):
    Bt, L, G, R, P = x.shape
    N = Bm.shape[-1]
    Q = SSD_CHUNK
    nc = L // Q
    f32 = jnp.float32
    xdt = (x.astype(f32) * dt[..., None]).reshape(Bt, nc, Q, G, R, P)
    a_cum = jnp.cumsum((dt * A).reshape(Bt, nc, Q, G, R), axis=2)
    Bc = Bm.astype(f32).reshape(Bt, nc, Q, G, N)
    Cc = Cm.astype(f32).reshape(Bt, nc, Q, G, N)
    states = jnp.einsum('bcsgn,bcsgr,bcsgrp->bcgrpn', Bc, jnp.exp(a_cum[:, :, -1:] - a_cum), xdt)
    chunk_decay = jnp.exp(a_cum[:, :, -1])

    def step(h, inp):
        s_c, d_c = inp
        return d_c[..., None, None] * h + s_c, h

    h_final, h_prev = lax.scan(step, h0, (jnp.moveaxis(states, 1, 0), jnp.moveaxis(chunk_decay, 1, 0)))
    if not return_y:
        return None, h_final
    h_prev = jnp.moveaxis(h_prev, 0, 1)
    tri = jnp.tril(jnp.ones((Q, Q), bool))[None, None, :, :, None, None]
    seg = a_cum[:, :, :, None] - a_cum[:, :, None, :]
    decay_ls = jnp.exp(jnp.where(tri, seg, -jnp.inf))
    cb = jnp.einsum('bclgn,bcsgn->bclsg', Cc, Bc)
    y = (jnp.einsum('bclsg,bclsgr,bcsgrp->bclgrp', cb, decay_ls, xdt)
         + jnp.einsum('bclgn,bcgrpn,bclgr->bclgrp', Cc, h_prev, jnp.exp(a_cum)))
    return y.reshape(Bt, L, G, R, P).astype(x.dtype), h_final


def ssd_stream(z, xbc, dt_raw, angles, h0_f, h0_b, conv_w, conv_b, a_log, dt_bias, d_skip, norm_w, return_y):
    Bt, L, _ = xbc.shape
    xbc = jax.nn.silu(dwconv_centred(xbc, conv_w, conv_b))
    xs, Bm, Cm = split_cols(xbc, (SSD_INNER, SSD_GROUPS * SSD_STATE, SSD_GROUPS * SSD_STATE))
    Bm = Bm.reshape(Bt, L, SSD_GROUPS, SSD_STATE)
    Cm = Cm.reshape(Bt, L, SSD_GROUPS, SSD_STATE)
    if angles is not None:
        Bm = rope_2d(Bm, *angles)
        Cm = rope_2d(Cm, *angles)
    xs = xs.reshape(Bt, L, SSD_GROUPS, SSD_HPG, SSD_HEAD_DIM)
    dt = jax.nn.softplus((dt_raw + dt_bias.reshape(-1)).astype(jnp.float32))
    dt = dt.reshape(Bt, L, 2, SSD_GROUPS, SSD_HPG)
    A = -jnp.exp(a_log.astype(jnp.float32)).reshape(2, SSD_GROUPS, SSD_HPG)
    flip = lambda t: jnp.flip(t, axis=1)
    y_f, s_f = ssd_scan(xs, dt[:, :, 0], A[0], Bm, Cm, h0_f, return_y)
    y_b, s_b = ssd_scan(flip(xs), flip(dt[:, :, 1]), A[1], flip(Bm), flip(Cm), h0_b, return_y)
    if not return_y:
        return None, s_f, s_b
    y = y_f + flip(y_b) + d_skip.reshape(SSD_GROUPS, SSD_HPG)[:, :, None] * xs
    y = rmsnorm(y.reshape(Bt, L, SSD_INNER) * jax.nn.silu(z), norm_w)
    return y, s_f, s_b


def qkv_heads(t, q_norm, k_norm):
    Bt, L, _ = t.shape
    q, k, v = [u.reshape(Bt, L, NA_HEADS, NA_HEAD_DIM) for u in split_cols(t, (NA_WIDTH,) * 3)]
    return rmsnorm(q, q_norm), rmsnorm(k, k_norm), v


def na_latent(q, k, v, kc, vc, rpb):
    Bt, L, H, Dh = q.shape
    rows_n = L // GRID_W
    wr = min(NA_ROWS, rows_n)
    scale = Dh ** -0.5
    qg = q.reshape(Bt, rows_n, GRID_W, H, Dh)
    kg = k.reshape(Bt, rows_n, GRID_W, H, Dh)
    vg = v.reshape(Bt, rows_n, GRID_W, H, Dh)
    r = jnp.arange(rows_n)
    rs = jnp.clip(r - wr // 2, 0, rows_n - wr)
    row_idx = rs[:, None] + jnp.arange(wr)
    k_win = kg[:, row_idx]
    v_win = vg[:, row_idx]
    cidx = jnp.arange(GRID_W)
    cs = jnp.clip(cidx - NA_COLS // 2, 0, GRID_W - NA_COLS)
    col_ok = (cidx[None, :] >= cs[:, None]) & (cidx[None, :] < cs[:, None] + NA_COLS)
    ri = row_idx - r[:, None] + (NA_ROWS - 1)
    ci = jnp.clip(cidx[None, :] - cidx[:, None] + (NA_COLS - 1), 0, 2 * NA_COLS - 2)
    bias = jnp.transpose(rpb, (1, 2, 0))[ri[:, None, :, None], ci[None, :, None, :]]
    s_lat = jnp.einsum('brqhd,brjkhd->brqjkh', qg, k_win) * scale + bias
    s_lat = jnp.where(col_ok[None, None, :, None, :, None], s_lat, -jnp.inf)
    s_lat = s_lat.reshape(Bt, rows_n, GRID_W, wr * GRID_W, H)
    s_ctx = jnp.einsum('brqhd,bchd->brqch', qg, kc) * scale
    p = jax.nn.softmax(jnp.concatenate([s_lat, s_ctx], axis=3).astype(jnp.float32), axis=3).astype(v.dtype)
    p_lat = p[:, :, :, :wr * GRID_W].reshape(Bt, rows_n, GRID_W, wr, GRID_W, H)
    p_ctx = p[:, :, :, wr * GRID_W:]
    out = (jnp.einsum('brqjkh,brjkhd->brqhd', p_lat, v_win)
           + jnp.einsum('brqch,bchd->brqhd', p_ctx, vc))
    return out.reshape(Bt, L, H * Dh)


def attn_ctx(q, k, v):
    Bt, Lc, H, Dh = q.shape
    s = jnp.einsum('bqhd,bkhd->bhqk', q, k) * (Dh ** -0.5)
    p = jax.nn.softmax(s.astype(jnp.float32), axis=-1).astype(v.dtype)
    return jnp.einsum('bhqk,bkhd->bqhd', p, v).reshape(Bt, Lc, H * Dh)


def gmlp_mix(uv, norm_w, w_s, b_s):
    u, v = jnp.split(jax.nn.gelu(uv), 2, axis=-1)
    v = rmsnorm(v, norm_w)
    Bt, L, _ = v.shape
    vg = v.reshape(Bt, L // GM_CHUNK, GM_CHUNK, GM_GROUPS, GM_GROUP_DIM)
    mixed = jnp.einsum('gts,bcsgd->bctgd', w_s, vg) + b_s.T[:, :, None]
    return u * mixed.reshape(Bt, L, GM_WIDTH)


def merge_branches(gate_raw, b_gate, y_ssd, y_na, y_gm, w_a, w_b, w_c):
    g_a, g_b, g_c = jnp.split(jax.nn.sigmoid(gate_raw + b_gate), N_BRANCH, axis=-1)
    return g_a * (y_ssd @ w_a) + g_b * (y_na @ w_b) + g_c * (y_gm @ w_c)


def swiglu(h, w_in, w_out):
    a, b = jnp.split(h @ w_in, 2, axis=-1)
    return (jax.nn.silu(a) * b) @ w_out


def hybrid_layer(x, xc, mod, mod_c, angles, norm1, w_in, b_gate, conv_w, conv_b, a_log, dt_bias,
                 d_skip, ssd_norm, q_norm, k_norm, rpb, gm_norm, w_spatial, b_spatial,
                 w_branch_ssd, w_branch_na, w_branch_gm, w_out, norm2, w_ffn_in, w_ffn_out,
                 with_ctx_out):
    sh1, sc1, g1, sh2, sc2, g2 = [m[:, None, :] for m in jnp.split(mod, 6, axis=-1)]
    csh1, csc1, cg1, csh2, csc2, cg2 = jnp.split(mod_c, 6, axis=-1)
    h = modulate(rmsnorm(x, norm1), sh1, sc1)
    hc = modulate(rmsnorm(xc, norm1), csh1, csc1)
    z, xbc, dt_raw, qkv, uv, gate_raw = split_cols(h @ w_in, IN_SIZES)
    zc, xbcc, dtc, qkvc, uvc, gate_rawc = split_cols(hc @ w_in, IN_SIZES)

    ssd_p = (conv_w, conv_b, a_log, dt_bias, d_skip, ssd_norm)
    h0 = jnp.zeros((x.shape[0], SSD_GROUPS, SSD_HPG, SSD_HEAD_DIM, SSD_STATE), jnp.float32)
    y_ssd_c, s_f, s_b = ssd_stream(zc, xbcc, dtc, None, h0, h0, *ssd_p, return_y=with_ctx_out)
    y_ssd, _, _ = ssd_stream(z, xbc, dt_raw, angles, s_f, s_b, *ssd_p, return_y=True)

    q, k, v = qkv_heads(qkv, q_norm, k_norm)
    qc, kc, vc = qkv_heads(qkvc, q_norm, k_norm)
    y_na = na_latent(q, k, v, kc, vc, rpb)

    y_gm = gmlp_mix(uv, gm_norm, w_spatial, b_spatial)

    mixed = merge_branches(gate_raw, b_gate, y_ssd, y_na, y_gm, w_branch_ssd, w_branch_na, w_branch_gm)
    x = x + g1 * (mixed @ w_out)
    x = x + g2 * swiglu(modulate(rmsnorm(x, norm2), sh2, sc2), w_ffn_in, w_ffn_out)

    if with_ctx_out:
        y_na_c = attn_ctx(qc, kc, vc)
        y_gm_c = gmlp_mix(uvc, gm_norm, w_spatial, b_spatial)
        mixed_c = merge_branches(gate_rawc, b_gate, y_ssd_c, y_na_c, y_gm_c,
                                 w_branch_ssd, w_branch_na, w_branch_gm)
        xc = xc + cg1 * (mixed_c @ w_out)
        xc = xc + cg2 * swiglu(modulate(rmsnorm(xc, norm2), csh2, csc2), w_ffn_in, w_ffn_out)
    return x, xc


def setup_inputs(seed: int = 0) -> dict:
    key = jax.random.key(seed)
    ks = jax.random.split(key, 28)
    f32 = jnp.float32
    nrm = lambda k, shape, s: jax.random.normal(k, shape, f32) * s
    D = D_MODEL
    dt0 = jnp.exp(jax.random.uniform(ks[12], (DEPTH, 2, SSD_HEADS), f32,
                                     minval=math.log(1e-3), maxval=math.log(1e-1)))
    return {
        'x': nrm(ks[0], (BATCH, SEQ, D), 1.0),
        'c': nrm(ks[1], (BATCH, D), 1.0),
        'ctx': nrm(ks[2], (BATCH, CTX_LEN, D), 1.0),
        'c_ctx': nrm(ks[3], (D,), 1.0),
        'w_mod': nrm(ks[4], (DEPTH, D, 6 * D), D ** -0.5),
        'b_mod': nrm(ks[5], (DEPTH, 6 * D), 0.02),
        'norm1': 1.0 + nrm(ks[6], (DEPTH, D), 0.02),
        'w_in': nrm(ks[7], (DEPTH, D, IN_WIDTH), D ** -0.5),
        'b_gate': nrm(ks[8], (DEPTH, N_BRANCH * D), 0.02),
        'conv_w': nrm(ks[9], (DEPTH, SSD_CONV, SSD_CONV_DIM), SSD_CONV ** -0.5),
        'conv_b': nrm(ks[10], (DEPTH, SSD_CONV_DIM), 0.02),
        'a_log': jnp.log(jax.random.uniform(ks[11], (DEPTH, 2, SSD_HEADS), f32, minval=1.0, maxval=16.0)),
        'dt_bias': dt0 + jnp.log(-jnp.expm1(-dt0)),
        'd_skip': 1.0 + nrm(ks[13], (DEPTH, SSD_HEADS), 0.02),
        'ssd_norm': 1.0 + nrm(ks[14], (DEPTH, SSD_INNER), 0.02),
        'q_norm': 1.0 + nrm(ks[15], (DEPTH, NA_HEAD_DIM), 0.02),
        'k_norm': 1.0 + nrm(ks[16], (DEPTH, NA_HEAD_DIM), 0.02),
        'rpb': nrm(ks[17], (DEPTH, NA_HEADS, 2 * NA_ROWS - 1, 2 * NA_COLS - 1), 0.1),
        'gm_norm': 1.0 + nrm(ks[18], (DEPTH, GM_WIDTH), 0.02),
        'w_spatial': nrm(ks[19], (DEPTH, GM_GROUPS, GM_CHUNK, GM_CHUNK), GM_CHUNK ** -0.5),
        'b_spatial': 1.0 + nrm(ks[20], (DEPTH, GM_GROUPS, GM_CHUNK), 0.02),
        'w_branch_ssd': nrm(ks[21], (DEPTH, SSD_INNER, D), SSD_INNER ** -0.5),
        'w_branch_na': nrm(ks[22], (DEPTH, NA_WIDTH, D), NA_WIDTH ** -0.5),
        'w_branch_gm': nrm(ks[23], (DEPTH, GM_WIDTH, D), GM_WIDTH ** -0.5),
        'w_out': nrm(ks[24], (DEPTH, D, D), D ** -0.5),
        'norm2': 1.0 + nrm(ks[25], (DEPTH, D), 0.02),
        'w_ffn_in': nrm(ks[26], (DEPTH, D, 2 * FFN_HIDDEN), D ** -0.5),
        'w_ffn_out': nrm(ks[27], (DEPTH, FFN_HIDDEN, D), FFN_HIDDEN ** -0.5),
    }


def reference(x, c, ctx, c_ctx, w_mod, b_mod, norm1, w_in, b_gate, conv_w, conv_b, a_log, dt_bias,
              d_skip, ssd_norm, q_norm, k_norm, rpb, gm_norm, w_spatial, b_spatial,
              w_branch_ssd, w_branch_na, w_branch_gm, w_out, norm2, w_ffn_in, w_ffn_out):
    L = x.shape[1]
    pos = jnp.arange(L)
    freqs = ROPE_BASE ** (-jnp.arange(ROPE_FREQS, dtype=jnp.float32) / ROPE_FREQS)
    ang_row = (pos // GRID_W).astype(jnp.float32)[:, None] * freqs
    ang_col = (pos % GRID_W).astype(jnp.float32)[:, None] * freqs
    angles = (ang_row, ang_col)
    sc = jax.nn.silu(c)
    scc = jax.nn.silu(c_ctx)
    xc = ctx
    for l in range(DEPTH):
        mod = sc @ w_mod[l] + b_mod[l]
        mod_c = scc @ w_mod[l] + b_mod[l]
        x, xc = hybrid_layer(x, xc, mod, mod_c, angles, norm1[l], w_in[l], b_gate[l], conv_w[l], conv_b[l],
                             a_log[l], dt_bias[l], d_skip[l], ssd_norm[l], q_norm[l], k_norm[l], rpb[l],
                             gm_norm[l], w_spatial[l], b_spatial[l], w_branch_ssd[l], w_branch_na[l],
                             w_branch_gm[l], w_out[l], norm2[l], w_ffn_in[l], w_ffn_out[l],
                             with_ctx_out=(l < DEPTH - 1))
    return x
```

```python
import math
import numpy as np
import concourse.bass as bass
import concourse.mybir as mybir
from concourse.bass_utils import run_bass_kernel_spmd
from contextlib import ExitStack

F32 = mybir.dt.float32
BF16 = mybir.dt.bfloat16
U8 = mybir.dt.uint8
AF = mybir.ActivationFunctionType
ALU = mybir.AluOpType
AX = mybir.AxisListType

D = 1024
CTX = 256
EPS = 1e-6
NBLK = 43
NPV = 202
NBR = 592
NCONST = 768 + 2048
NEG = -30000.0

ENGS = ['pe', 'act', 'dve', 'pool', 'sp']
N_DSEM = 48
DSEM_RANGE = {'sp': (0, 22), 'act': (22, 32), 'pool': (32, 48)}


class Sched:
    def __init__(self, nc):
        self.nc = nc
        self.ops = {e: [] for e in ENGS}
        self.last_write = {}
        self.readers = {}
        self.known = {e: {} for e in ENGS}
        self.dsem_uses = [0] * N_DSEM
        self.dsem_next = {'sp': 0, 'pool': 0, 'act': 0}
        self.signal = {e: set() for e in ENGS}
        self.rec = None

    def record(self):
        self.rec = []

    def stop(self):
        r = self.rec
        self.rec = None
        return r

    def replay(self, lists):
        idx = [0] * len(lists)
        tot = [max(1, len(x)) for x in lists]
        while True:
            live = [k for k in range(len(lists)) if idx[k] < len(lists[k])]
            if not live:
                break
            k = min(live, key=lambda k: idx[k] / tot[k])
            self.op(*lists[k][idx[k]])
            idx[k] += 1

    def _need(self, eng, tok, waits):
        kind, key, val = tok
        if kind == 'eng' and key == eng and eng in ('pe', 'sp'):
            return
        k = (kind, key)
        if self.known[eng].get(k, 0) >= val:
            return
        self.known[eng][k] = val
        waits[k] = max(waits.get(k, 0), val)
        if kind == 'eng':
            self.signal[key].add(val)

    def op(self, eng, fn, reads=(), writes=(), dma=False):
        if self.rec is not None:
            self.rec.append((eng, fn, tuple(reads), tuple(writes), dma))
            return None
        waits = {}
        for b in reads:
            w = self.last_write.get(b)
            if w is not None:
                self._need(eng, w, waits)
        for b in writes:
            w = self.last_write.get(b)
            if w is not None:
                self._need(eng, w, waits)
            for r in self.readers.get(b, ()):
                self._need(eng, r, waits)
        idx = len(self.ops[eng]) + 1
        if dma:
            lo, hi = DSEM_RANGE[eng]
            j = lo + self.dsem_next[eng]
            self.dsem_next[eng] = (self.dsem_next[eng] + 1) % (hi - lo)
            n = self.dsem_uses[j]
            if n > 0:
                self._need(eng, ('dma', j, 16 * n), waits)
            self.dsem_uses[j] = n + 1
            tok = ('dma', j, 16 * (n + 1))
        else:
            tok = ('eng', eng, idx)
        for b in reads:
            self.readers.setdefault(b, []).append(tok)
        for b in writes:
            self.last_write[b] = tok
            self.readers[b] = []
        self.ops[eng].append((fn, waits, tok))
        return tok

    def barrier(self):
        toks = []
        for e in ENGS:
            for i in range(len(self.ops[e]), 0, -1):
                t = self.ops[e][i - 1][2]
                if t is not None and t[0] == 'eng':
                    toks.append(t)
                    break
        for j in range(N_DSEM):
            if self.dsem_uses[j] > 0:
                toks.append(('dma', j, 16 * self.dsem_uses[j]))
        for e in ENGS:
            waits = {}
            for t in toks:
                self._need(e, t, waits)
            self.ops[e].append((None, waits, None))
        self.last_write.clear()
        self.readers.clear()

    def emit(self):
        nc = self.nc
        with ExitStack() as st:
            esem = {e: st.enter_context(nc.semaphore('es_' + e)) for e in ENGS}
            dsem = [st.enter_context(nc.semaphore('ds_%d' % j)) for j in range(N_DSEM)]
            block = st.enter_context(nc.Block())
            sigmap = {}
            for e in ENGS:
                cnt = 0
                m = {}
                ss = self.signal[e]
                for i in range(1, len(self.ops[e]) + 1):
                    if i in ss:
                        cnt += 1
                        m[i] = cnt
                sigmap[e] = m

            def run(e, engobj):
                for i, (fn, waits, tok) in enumerate(self.ops[e], start=1):
                    for (kind, key), val in waits.items():
                        if kind == 'eng':
                            engobj.wait_ge(esem[key], sigmap[key][val])
                        else:
                            engobj.wait_ge(dsem[key], val)
                    if fn is None:
                        continue
                    ins = fn(engobj)
                    if tok[0] == 'dma':
                        ins.then_inc(dsem[tok[1]], 16)
                    elif i in sigmap[e]:
                        ins.then_inc(esem[e], 1)

            @block.tensor
            def _(eng):
                run('pe', eng)

            @block.scalar
            def _(eng):
                run('act', eng)

            @block.vector
            def _(eng):
                run('dve', eng)

            @block.gpsimd
            def _(eng):
                run('pool', eng)

            @block.sync
            def _(eng):
                run('sp', eng)


class Arena:
    def __init__(self, ap, size):
        self.ap = ap
        self.size = size
        self.off = 0

    def alloc(self, free_shape, dt):
        esz = 4 if dt == F32 else 2
        n = 1
        for s in free_shape:
            n *= s
        nb = (n * esz + 63) // 64 * 64
        assert self.off + nb <= self.size, "SBUF arena overflow %d + %d" % (self.off, nb)
        v = self.ap[:, self.off:self.off + n * esz].bitcast(dt)
        self.off += nb
        if len(free_shape) == 2:
            v = v.rearrange("p (a b) -> p a b", b=free_shape[1])
        elif len(free_shape) == 3:
            v = v.rearrange("p (a b c) -> p a b c", b=free_shape[1], c=free_shape[2])
        return v


def na_classes(L):
    R = L // 64
    NQ = R // 2

    def rs(r):
        return min(max(r - 4, 0), R - 8)
    info = []
    for qt in range(NQ):
        r0, r1 = 2 * qt, 2 * qt + 1
        lo = min(rs(r0), rs(r1)) // 2
        hi = (max(rs(r0), rs(r1)) + 7) // 2
        info.append(tuple(k - qt for k in range(lo, hi + 1)))
    cls = []
    for qt in range(NQ):
        if qt == 0:
            cls.append(0)
        elif qt == 1:
            cls.append(1)
        elif qt == NQ - 2:
            cls.append(3)
        elif qt == NQ - 1:
            cls.append(4)
        else:
            cls.append(2)
    rel = {}
    rep = {}
    for qt in range(NQ):
        c = cls[qt]
        if c in rel:
            assert rel[c] == info[qt], (c, rel[c], info[qt])
        else:
            rel[c] = info[qt]
            rep[c] = qt
    return cls, rel, rep


def build(L, depth=2, dbg=()):
    T = CTX + L
    NCH = T // 128
    TP = T + 8
    tiles = [(0, 256)] + [(CTX + 512 * i, 512) for i in range(L // 512)]
    nc = bass.Bass("TRN2", target_bir_lowering=False)

    def din(name, shape, dt=F32):
        return nc.dram_tensor(name, shape, dt, kind="ExternalInput").ap()

    def dscr(name, shape, dt):
        kind = "ExternalOutput" if name in dbg else "Internal"
        return nc.dram_tensor(name, shape, dt, kind=kind).ap()

    xT_in = din("xT", [D, T])
    cT_in = din("cT", [128, 16])
    wmod_in = din("wmod", [depth * 12, 128, 4096])
    wblk_in = din("wblk", [depth * NBLK, 128, 4096])
    pvec_in = din("pvec", [depth, 128, NPV])
    brow_in = din("brow", [depth, NBR])
    wsT_in = din("wsT", [depth, 128, 1024])
    ropeC_in = din("ropeC", [128, T])
    ropeS_in = din("ropeS", [128, T])
    natab_in = din("natab", [depth * 200, 128, 128])
    consts_in = din("consts", [128, NCONST])
    outT = nc.dram_tensor("outT", [D, L], F32, kind="ExternalOutput").ap()

    wblk16 = dscr("wblk16", [depth * NBLK, 128, 4096], BF16)
    natab16 = dscr("natab16", [depth * 200, 128, 128], BF16)
    x1T = dscr("x1T", [D, T], F32)
    zs_d = dscr("zs", [T, 1024], BF16)
    xbcT_d = dscr("xbcT", [2048, TP], BF16)
    dts_d = dscr("dts", [T, 32], F32)
    qT_d = dscr("qT", [512, T], BF16)
    kT_d = dscr("kT", [512, T], BF16)
    vv_d = dscr("vv", [T, 512], BF16)
    ygmT_d = dscr("ygmT", [512, T], BF16)
    gateT_d = dscr("gateT", [3072, T], BF16)
    xs_d = dscr("xs", [T, 1024], BF16)
    BT_d = dscr("BTs", [512, T], BF16)
    CT_d = dscr("CTs", [512, T], BF16)
    Bt_d = dscr("Btok", [T, 512], BF16)
    yp_d = dscr("ypart", [T, 1024], F32)
    yssdT_d = dscr("yssdT", [1024, T], BF16)
    ynaT_d = dscr("ynaT", [512, T], BF16)

    st = ExitStack()
    with st:
        ARENA = 186 * 1024
        arena_t = st.enter_context(nc.sbuf_tensor("arena", [128, ARENA], U8))
        ps = [st.enter_context(nc.psum_tensor("ps%d" % i, [128, 512], F32)) for i in range(8)]
        S = Sched(nc)
        AR = Arena(arena_t, ARENA)
        pb_ctr = [0]

        def PSB(i):
            return ps[i][:], 'P%d' % i

        def nextbank(lo=0, hi=8):
            i = lo + pb_ctr[0] % (hi - lo)
            pb_ctr[0] += 1
            return i

        def dma(out, in_, reads, writes, q='sp'):
            return S.op(q, lambda e: e.dma_start(out=out, in_=in_), reads, writes, dma=True)

        def act(out, in_, func, reads, writes, bias=None, scale=None, accum_out=None):
            kw = {}
            if bias is not None:
                kw['bias'] = bias
            if scale is not None:
                kw['scale'] = scale
            if accum_out is not None:
                kw['accum_out'] = accum_out
            return S.op('act', lambda e: e.activation(out=out, in_=in_, func=func, **kw), reads, writes)

        def tt(out, in0, in1, op, reads, writes, eng='dve'):
            return S.op(eng, lambda e: e.tensor_tensor(out=out, in0=in0, in1=in1, op=op), reads, writes)

        def ts(out, in0, s1, s2, op0, op1, reads, writes, eng='dve'):
            if s2 is None:
                return S.op(eng, lambda e: e.tensor_scalar(out=out, in0=in0, scalar1=s1, scalar2=None, op0=op0), reads, writes)
            return S.op(eng, lambda e: e.tensor_scalar(out=out, in0=in0, scalar1=s1, scalar2=s2, op0=op0, op1=op1), reads, writes)

        def stt(out, in0, scalar, in1, op0, op1, reads, writes):
            return S.op('dve', lambda e: e.scalar_tensor_tensor(out=out, in0=in0, scalar=scalar, in1=in1, op0=op0, op1=op1), reads, writes)

        def recip(out, in_, reads, writes):
            return S.op('dve', lambda e: e.reciprocal(out=out, in_=in_), reads, writes)

        def cp(out, in_, reads, writes, eng='dve'):
            if eng == 'act':
                return S.op('act', lambda e: e.copy(out=out, in_=in_), reads, writes)
            return S.op(eng, lambda e: e.tensor_copy(out=out, in_=in_), reads, writes)

        def mms(lst, reads, writes):
            def fn(e):
                ins = None
                for (o, l, r, s0, s1) in lst:
                    ins = e.matmul(o, lhsT=l, rhs=r, start=s0, stop=s1)
                return ins
            return S.op('pe', fn, reads, writes)

        def trs(lst, reads, writes):
            def fn(e):
                ins = None
                for (o, i_, idn) in lst:
                    ins = e.transpose(o, i_, idn)
                return ins
            return S.op('pe', fn, reads, writes)

        c32 = AR.alloc([NCONST], F32)
        c16 = AR.alloc([512], BF16)
        epsb = AR.alloc([1], F32)
        ident32 = c32[:, 0:128]
        Uf32 = c32[:, 128:256]
        Ub32 = c32[:, 256:384]
        sel32 = c32[0:16, 768:768 + 2048].rearrange("p (h s) -> p h s", s=128)
        ident16 = c16[:, 0:128]
        ones16 = c16[:, 128:256]
        bo16 = c16[:, 256:384]
        pm16 = c16[:, 384:512]
        pv = [AR.alloc([NPV], F32) for _ in range(depth)]
        br = [AR.alloc([NBR], F32) for _ in range(depth)]
        modT = [AR.alloc([48, 2], F32) for _ in range(depth)]
        A1 = [AR.alloc([8, 2], F32) for _ in range(depth)]
        A2 = [AR.alloc([8, 2], F32) for _ in range(depth)]
        scs = AR.alloc([8, 2], F32)
        mneg16 = [AR.alloc([128], BF16) for _ in range(2)]
        PERSIST = AR.off

        dma(c32, consts_in, [], ['c32'])
        S.op('dve', lambda e: e.memset(epsb, EPS), [], ['epsb'])
        cp(c16[:, 0:128], c32[:, 0:128], ['c32'], ['c16'])
        cp(c16[:, 128:512], c32[:, 384:768], ['c32'], ['c16'])
        for d_ in range(2):
            ts(mneg16[d_], (Uf32 if d_ == 0 else Ub32), -1.0, -NEG, ALU.add, ALU.mult, ['c32'], ['mneg16'])
        for l in range(depth):
            dma(pv[l], pvec_in[l], [], ['pv%d' % l])
            dma(br[l], brow_in[l:l + 1, :].partition_broadcast(128), [], ['br%d' % l])
        wsrc = wblk_in.rearrange("b p (h e) -> b p h e", e=2048)
        wdst = wblk16.rearrange("b p (h e) -> b p h e", e=2048)
        CB = 4

        def cast_weights(l_):
            for b0 in range(l_ * NBLK, (l_ + 1) * NBLK, CB):
                b1 = min((l_ + 1) * NBLK, b0 + CB)
                dma(wdst[b0:b1], wsrc[b0:b1], [], ['w16_%d' % b for b in range(b0, b1)], q='pool')

        def cast_natab(l_):
            for k in range(0, 200, 50):
                dma(natab16[l_ * 200 + k:l_ * 200 + k + 50], natab_in[l_ * 200 + k:l_ * 200 + k + 50], [], ['natab16_%d' % l_], q='pool')
        cast_weights(0)

        sc_raw = AR.alloc([16], F32)
        wm = [AR.alloc([8, 512], F32) for _ in range(2)]
        dma(sc_raw, cT_in, [], ['sc_raw'])
        act(scs.rearrange("p a b -> p (a b)"), sc_raw, AF.Silu, ['sc_raw'], ['scs'])
        for l in range(depth):
            psM, psMn = PSB(0)
            for blk in range(12):
                slot = blk % 2
                dma(wm[slot].rearrange("p a b -> p (a b)"), wmod_in[l * 12 + blk], [], ['wm%d' % slot])
                lst = []
                for oc4 in range(4):
                    oc = blk * 4 + oc4
                    for kc in range(8):
                        lst.append((psM[:, oc * 2:oc * 2 + 2], wm[slot][:, kc, oc4 * 128:(oc4 + 1) * 128], scs[:, kc, :], kc == 0, kc == 7))
                mms(lst, ['wm%d' % slot, 'scs'], [psMn])
            tt(modT[l], psM[:, 0:96].rearrange("p (a b) -> p a b", b=2), pv[l][:, 0:48].unsqueeze(2).to_broadcast([128, 48, 2]),
               ALU.add, [psMn, 'pv%d' % l], ['modT%d' % l])
            for (Ax, sci, nrm) in ((A1[l], 8, 48), (A2[l], 32, 56)):
                ts(Ax, modT[l][:, sci:sci + 8, :], 1.0, None, ALU.add, None, ['modT%d' % l], ['A%d' % l])
                tt(Ax, Ax, pv[l][:, nrm:nrm + 8].unsqueeze(2).to_broadcast([128, 8, 2]), ALU.mult, ['A%d' % l, 'pv%d' % l], ['A%d' % l])
        S.barrier()

        def norm_mod(xt, n, Acol, shcol, hT, xtn, hTn):
            sq = ph['sq']
            act(sq[:, :, :n], xt[:, :, :n], AF.Square, [xtn], ['sq'])
            bi = nextbank()
            pss, pn = PSB(bi)
            mms([(pss[:, :n], ones16, sq[:, c, :n], c == 0, c == 7) for c in range(8)], ['sq', 'c16'], [pn])
            rs = ph['rs']
            act(rs[:, :n], pss[:, :n], AF.Sqrt, [pn, 'epsb'], ['rs'], bias=epsb[:, 0:1], scale=1.0 / D)
            recip(rs[:, :n], rs[:, :n], ['rs'], ['rs'])
            for c in range(8):
                tmp = ph['ntmp'][c % 2]
                tt(tmp[:, :n], xt[:, c, :n], rs[:, :n], ALU.mult, [xtn, 'rs'], ['ntmp%d' % (c % 2)])
                act(hT[:, c, :n], tmp[:, :n], AF.Identity, ['ntmp%d' % (c % 2)], [hTn], bias=shcol[:, c:c + 1], scale=Acol[:, c:c + 1])

        def col_of(t):
            return 2 + t if t < CTX else 6 + t

        for l in range(depth):
            last = (l == depth - 1)
            xsrc = xT_in if l == 0 else x1T
            xsrc3 = xsrc.rearrange("(c p) t -> p c t", p=128)
            wb0 = l * NBLK
            pvl = pv[l]
            brl = br[l]
            AR.off = PERSIST
            ph = {}
            ph['sq'] = AR.alloc([8, 512], BF16)
            ph['rs'] = AR.alloc([512], F32)
            ph['ntmp'] = [AR.alloc([512], F32) for _ in range(2)]
            xt = AR.alloc([8, 512], F32)
            hT2 = [AR.alloc([8, 512], BF16) for _ in range(2)]
            wbuf = [AR.alloc([4096], BF16) for _ in range(3)]
            wsT32 = AR.alloc([1024], F32)
            wsT16 = AR.alloc([8, 128], BF16)
            zst = AR.alloc([4, 1024], BF16)
            xbcst = AR.alloc([16, 512], BF16)
            dst_ = AR.alloc([4, 32], F32)
            dtmp = AR.alloc([32], F32)
            qkst = AR.alloc([8, 512], BF16)
            sq16 = [AR.alloc([512], BF16) for _ in range(2)]
            rs2 = [AR.alloc([512], F32) for _ in range(2)]
            qtmp = [AR.alloc([512], F32) for _ in range(2)]
            qraw = [AR.alloc([512], F32) for _ in range(2)]
            vst = AR.alloc([4, 512], BF16)
            u16 = AR.alloc([4, 512], BF16)
            vg = AR.alloc([512], F32)
            vjunk = AR.alloc([512], BF16)
            ssv4 = AR.alloc([4, 4], F32)
            vn16 = [AR.alloc([512], BF16) for _ in range(4)]
            gtmp = AR.alloc([512], F32)
            ygm16 = [AR.alloc([512], BF16) for _ in range(4)]
            ygst = AR.alloc([4, 512], BF16)
            gst2 = [AR.alloc([4, 512], BF16) for _ in range(2)]
            dma(wsT32, wsT_in[l], [], ['wsT32'])
            cp(wsT16.rearrange("p a b -> p (a b)"), wsT32, ['wsT32'], ['wsT16'])
            wctr = [0]

            def load_w(blk):
                slot = wctr[0] % 3
                wctr[0] += 1
                dma(wbuf[slot], wblk16[wb0 + blk], ['w16_%d' % (wb0 + blk)], ['wb%d' % slot])
                return wbuf[slot], 'wb%d' % slot

            def a_norm(ti):
                (t0_, n_) = tiles[ti]
                j_ = 1 if t0_ < CTX else 0
                dma(xt[:, :, :n_], xsrc3[:, :, t0_:t0_ + n_], [], ['Axt'])
                norm_mod(xt, n_, A1[l][:, :, j_], modT[l][:, 0:8, j_], hT2[ti % 2], 'Axt', 'AhT%d' % (ti % 2))
            a_norm(0)
            for ti, (t0, n) in enumerate(tiles):
                isctx = t0 < CTX
                j = 1 if isctx else 0
                ntc = n // 128
                hT = hT2[ti % 2]
                hTn = 'AhT%d' % (ti % 2)

                def fm_group(w3, oc4, tagw):
                    bi = nextbank()
                    p_, pn = PSB(bi)
                    mms([(p_[:, :n], w3[:, kc, oc4 * 128:(oc4 + 1) * 128], hT[:, kc, :n], kc == 0, kc == 7) for kc in range(8)],
                        [tagw, hTn], [pn])
                    return p_, pn

                def tm_group(w3, tc, ncols, tagw):
                    bi = nextbank()
                    p_, pn = PSB(bi)
                    mms([(p_[:, :ncols], hT[:, kc, tc * 128:(tc + 1) * 128], w3[:, kc, :ncols], kc == 0, kc == 7) for kc in range(8)],
                        [tagw, hTn], [pn])
                    return p_, pn

                for blk in range(2):
                    w, wn = load_w(blk)
                    w3 = w.rearrange("p (k c) -> p k c", c=512)
                    for tc in range(ntc):
                        p_, pn = tm_group(w3, tc, 512, wn)
                        act(zst[:, tc, blk * 512:(blk + 1) * 512], p_, AF.Silu, [pn], ['zst'])
                dma(zs_d[t0:t0 + n, :].rearrange("(c p) f -> p c f", p=128), zst[:, :ntc, :], ['zst'], [], q='act')
                for blk in range(2, 6):
                    w, wn = load_w(blk)
                    w3 = w.rearrange("p (k c) -> p k c", c=512)
                    for oc4 in range(4):
                        p_, pn = fm_group(w3, oc4, wn)
                        ch = (blk - 2) * 4 + oc4
                        if ch % 2 == 0:
                            cp(xbcst[:, ch, :n], p_[:, :n], [pn], ['xbcst'], eng='act')
                        else:
                            cp(xbcst[:, ch, :n], p_[:, :n], [pn], ['xbcst'])
                c0 = col_of(t0)
                dma(xbcT_d.rearrange("(c p) t -> p c t", p=128)[:, :, c0:c0 + n], xbcst[:, :, :n], ['xbcst'], [], q='act')
                if ti + 1 < len(tiles):
                    a_norm(ti + 1)
                gi = 0
                for blk in range(6, 8):
                    w, wn = load_w(blk)
                    w3 = w.rearrange("p (k c) -> p k c", c=512)
                    for oc4 in range(4):
                        par = gi % 2
                        gi += 1
                        sfx = '%d' % par
                        p_, pn = fm_group(w3, oc4, wn)
                        act(sq16[par][:, :n], p_[:, :n], AF.Square, [pn], ['sq16' + sfx])
                        cp(qraw[par][:, :n], p_[:, :n], [pn], ['qraw' + sfx], eng='act')
                        b2 = nextbank()
                        p2, p2n = PSB(b2)
                        mms([(p2[:, :n], bo16, sq16[par][:, :n], True, True)], ['sq16' + sfx, 'c16'], [p2n])
                        act(rs2[par][:, :n], p2[:, :n], AF.Sqrt, [p2n, 'epsb'], ['rs2' + sfx], bias=epsb[:, 0:1], scale=1.0 / 64)
                        recip(rs2[par][:, :n], rs2[par][:, :n], ['rs2' + sfx], ['rs2' + sfx])
                        stt(qtmp[par][:, :n], qraw[par][:, :n], 0.125 if blk == 6 else 1.0, rs2[par][:, :n], ALU.mult, ALU.mult,
                            ['qraw' + sfx, 'rs2' + sfx], ['qtmp' + sfx])
                        wcol = 192 if blk == 6 else 193
                        act(qkst[:, (blk - 6) * 4 + oc4, :n], qtmp[par][:, :n], AF.Copy, ['qtmp' + sfx, 'pv%d' % l], ['qkst'],
                            scale=pvl[:, wcol:wcol + 1])
                dma(qT_d.rearrange("(c p) t -> p c t", p=128)[:, :, t0:t0 + n], qkst[:, 0:4, :n], ['qkst'], [], q='act')
                dma(kT_d.rearrange("(c p) t -> p c t", p=128)[:, :, t0:t0 + n], qkst[:, 4:8, :n], ['qkst'], [], q='act')
                w, wn = load_w(8)
                w3 = w.rearrange("p (k c) -> p k c", c=512)
                for tc in range(ntc):
                    p_, pn = tm_group(w3, tc, 512, wn)
                    cp(vst[:, tc, :], p_, [pn], ['vst'], eng='act')
                dma(vv_d[t0:t0 + n, :].rearrange("(c p) f -> p c f", p=128), vst[:, :ntc, :], ['vst'], [], q='act')
                w, wn = load_w(9)
                w3 = w.rearrange("p (k c) -> p k c", c=512)
                for tc in range(ntc):
                    p_, pn = tm_group(w3, tc, 512, wn)
                    act(u16[:, tc, :], p_, AF.Gelu_apprx_tanh, [pn], ['u16'])
                w, wn = load_w(10)
                w3 = w.rearrange("p (k c) -> p k c", c=512)
                for tc in range(ntc):
                    p_, pn = tm_group(w3, tc, 512, wn)
                    act(vg, p_, AF.Gelu_apprx_tanh, [pn], ['vg'])
                    act(vjunk, vg, AF.Square, ['vg'], ['vjunk', 'ssv%d' % tc], accum_out=ssv4[:, tc, 0:1])
                    act(ssv4[:, tc, 1:2], ssv4[:, tc, 0:1], AF.Sqrt, ['ssv%d' % tc, 'epsb'], ['ssv%d' % tc], bias=epsb[:, 0:1], scale=1.0 / 512)
                    recip(ssv4[:, tc, 2:3], ssv4[:, tc, 1:2], ['ssv%d' % tc], ['ssv%d' % tc])
                    stt(vn16[tc], vg, ssv4[:, tc, 2:3], brl[:, 80:592], ALU.mult, ALU.mult, ['vg', 'ssv%d' % tc, 'br%d' % l], ['vn16_%d' % tc])

                def gate_blocks(blks):
                    for blk in blks:
                        w, wn = load_w(blk)
                        w3 = w.rearrange("p (k c) -> p k c", c=512)
                        for oc4 in range(4):
                            p_, pn = fm_group(w3, oc4, wn)
                            gc = (blk - 11) * 4 + oc4
                            gsl = gst2[blk % 2]
                            act(gsl[:, oc4, :n], p_[:, :n], AF.Sigmoid, [pn, 'pv%d' % l], ['gst%d' % (blk % 2)], bias=pvl[:, 64 + gc:65 + gc])
                        r0 = (blk - 11) * 512
                        dma(gateT_d[r0:r0 + 512, :].rearrange("(c p) t -> p c t", p=128)[:, :, t0:t0 + n], gst2[blk % 2][:, :, :n],
                            ['gst%d' % (blk % 2)], [], q='act')
                gate_blocks([11, 12, 13])
                for tc in range(ntc):
                    bm = nextbank()
                    pm_, pmn = PSB(bm)
                    mms([(pm_[:, g * 64:(g + 1) * 64], wsT16[:, g, :], vn16[tc][:, g * 64:(g + 1) * 64], True, True) for g in range(8)],
                        ['wsT16', 'vn16_%d' % tc], [pmn])
                    tt(gtmp.rearrange("p (g d) -> p g d", d=64), pm_.rearrange("p (g d) -> p g d", d=64),
                       pvl[:, 194:202].unsqueeze(2).to_broadcast([128, 8, 64]), ALU.add, [pmn, 'pv%d' % l], ['gtmp'])
                    tt(ygm16[tc], gtmp, u16[:, tc, :], ALU.mult, ['gtmp', 'u16'], ['ygm16_%d' % tc])
                gate_blocks([14, 15])
                for tc in range(ntc):
                    bt_ = nextbank()
                    pt_, ptn = PSB(bt_)
                    pt16 = pt_.bitcast(BF16)
                    trs([(pt16[:, c * 128:(c + 1) * 128], ygm16[tc][:, c * 128:(c + 1) * 128], ident16) for c in range(4)],
                        ['ygm16_%d' % tc, 'c16'], [ptn])
                    cp(ygst[:, :, tc * 128:(tc + 1) * 128], pt16[:, 0:512].rearrange("p (c t) -> p c t", t=128), [ptn], ['ygst'], eng='act')
                dma(ygmT_d.rearrange("(c p) t -> p c t", p=128)[:, :, t0:t0 + n], ygst[:, :, :n], ['ygst'], [], q='act')
                gate_blocks([16])
                w, wn = load_w(17)
                w3 = w[:, 0:256].rearrange("p (k c) -> p k c", c=32)
                for tc in range(ntc):
                    p_, pn = tm_group(w3, tc, 32, wn)
                    tt(dtmp, p_[:, 0:32], brl[:, 0:32], ALU.add, [pn, 'br%d' % l], ['dtmp'])
                    act(dtmp, dtmp, AF.Exp, ['dtmp'], ['dtmp'])
                    act(dst_[:, tc, :], dtmp, AF.Ln, ['dtmp'], ['dst'], bias=1.0)
                dma(dts_d[t0:t0 + n, :].rearrange("(c p) f -> p c f", p=128), dst_[:, :ntc, :], ['dst'], [], q='act')
            S.barrier()
            if 'stopA' in dbg and l == 0:
                break
            AR.off = PERSIST
            cast_natab(l)
            if l + 1 < depth:
                cast_weights(l + 1)
            cdiag = AR.alloc([80, 128], BF16)
            for j in range(80):
                ts(cdiag[:, j, :], ident32, pvl[:, 104 + j:105 + j], None, ALU.mult, None, ['c32', 'pv%d' % l], ['cdiag'])
            zt = AR.alloc([16, 4], BF16)
            S.op('dve', lambda e: e.memset(zt.rearrange("p a b -> p (a b)"), 0.0), [], ['zt'])
            xbcT3 = xbcT_d.rearrange("(c p) t -> p c t", p=128)
            dma(xbcT3[:, :, 0:2], zt[:, :, 0:2], ['zt'], ['xbcpad'], q='act')
            dma(xbcT3[:, :, 258:262], zt[:, :, 0:4], ['zt'], ['xbcpad'], q='act')
            dma(xbcT3[:, :, 262 + L:264 + L], zt[:, :, 0:2], ['zt'], ['xbcpad'], q='act')
            xpre = AR.alloc([16, 516], BF16)
            xact = AR.alloc([16, 512], BF16)
            cosT = AR.alloc([512], F32)
            sinT = AR.alloc([512], F32)
            t1 = AR.alloc([512], F32)
            t2 = AR.alloc([512], F32)
            bcrot = AR.alloc([8, 512], BF16)
            xsst = AR.alloc([4, 1024], BF16)
            btst = AR.alloc([4, 512], BF16)
            BT_d3 = BT_d.rearrange("(c p) t -> p c t", p=128)
            CT_d3 = CT_d.rearrange("(c p) t -> p c t", p=128)
            for (t0, n) in tiles:
                ntc = n // 128
                c0 = col_of(t0)
                dma(xpre[:, :, :n + 4], xbcT3[:, :, c0 - 2:c0 + n + 2], ['xbcpad'], ['xpre'])
                dma(cosT[:, :n], ropeC_in[:, t0:t0 + n], [], ['cosT'])
                dma(sinT[:, :n], ropeS_in[:, t0:t0 + n], [], ['sinT'])
                for ch in range(16):
                    bi = nextbank()
                    p_, pn = PSB(bi)
                    mms([(p_[:, :n], cdiag[:, k * 16 + ch, :], xpre[:, ch, k:k + n], k == 0, k == 4) for k in range(5)],
                        ['cdiag', 'xpre'], [pn])
                    act(xact[:, ch, :n], p_[:, :n], AF.Silu, [pn, 'pv%d' % l], ['xact'], bias=pvl[:, 88 + ch:89 + ch])
                for tc in range(ntc):
                    bi = nextbank()
                    p_, pn = PSB(bi)
                    pt16 = p_.bitcast(BF16)
                    trs([(pt16[:, ch * 128:(ch + 1) * 128], xact[:, ch, tc * 128:(tc + 1) * 128], ident16) for ch in range(8)],
                        ['xact', 'c16'], [pn])
                    cp(xsst[:, tc, :], pt16, [pn], ['xsst'], eng='act')
                dma(xs_d[t0:t0 + n, :].rearrange("(c p) f -> p c f", p=128), xsst[:, :ntc, :], ['xsst'], [], q='act')
                for i in range(8):
                    ch = 8 + i
                    bi = nextbank()
                    p_, pn = PSB(bi)
                    mms([(p_[:, :n], pm16, xact[:, ch, :n], True, True)], ['c16', 'xact'], [pn])
                    tt(t1[:, :n], xact[:, ch, :n], cosT[:, :n], ALU.mult, ['xact', 'cosT'], ['t1'])
                    tt(t2[:, :n], p_[:, :n], sinT[:, :n], ALU.mult, [pn, 'sinT'], ['t2'])
                    tt(bcrot[:, i, :n], t1[:, :n], t2[:, :n], ALU.add, ['t1', 't2'], ['bcrot'])
                dma(BT_d3[:, :, t0:t0 + n], bcrot[:, 0:4, :n], ['bcrot'], [], q='act')
                dma(CT_d3[:, :, t0:t0 + n], bcrot[:, 4:8, :n], ['bcrot'], [], q='act')
                for tc in range(ntc):
                    bi = nextbank()
                    p_, pn = PSB(bi)
                    pt16 = p_.bitcast(BF16)
                    trs([(pt16[:, g * 128:(g + 1) * 128], bcrot[:, g, tc * 128:(tc + 1) * 128], ident16) for g in range(4)],
                        ['bcrot', 'c16'], [pn])
                    cp(btst[:, tc, :], pt16[:, 0:512], [pn], ['btst'], eng='act')
                dma(Bt_d[t0:t0 + n, :].rearrange("(c p) f -> p c f", p=128), btst[:, :ntc, :], ['btst'], [], q='act')
            S.barrier()
            AR.off = PERSIST
            h32 = [AR.alloc([1024], F32) for _ in range(2)]
            h16 = [AR.alloc([1024], BF16) for _ in range(2)]
            Abc = AR.alloc([32], F32)
            for d in range(2):
                S.op('dve', lambda e, d=d: e.memset(h32[d], 0.0), [], ['h32_%d' % d])
                S.op('dve', lambda e, d=d: e.memset(h16[d], 0.0), [], ['h16_%d' % d])
            act(Abc, brl[:, 32:64], AF.Exp, ['br%d' % l], ['Abc'])
            ts(Abc, Abc, -1.0, None, ALU.mult, None, ['Abc'], ['Abc'])

            class _B:
                pass
            p1bufs = [[None, None], [None, None]]
            p2bufs = [None, None]
            for d in range(2):
                for par in range(2):
                    B_ = _B()
                    B_.xs = AR.alloc([1024], BF16)
                    B_.BT = AR.alloc([4, 128], BF16)
                    B_.CT = AR.alloc([4, 128], BF16)
                    B_.Bt = AR.alloc([512], BF16)
                    B_.dt = AR.alloc([32], F32)
                    B_.dtA = AR.alloc([16], F32)
                    B_.xdt = AR.alloc([1024], BF16)
                    if d == 0:
                        B_.xsD = AR.alloc([1024], BF16)
                    B_.nega = AR.alloc([16], F32)
                    B_.expa = AR.alloc([16], F32)
                    B_.acT = AR.alloc([128], F32)
                    B_.dec = AR.alloc([16, 128], BF16)
                    B_.cdB = AR.alloc([16], F32)
                    B_.cbm = AR.alloc([4, 128], BF16)
                    B_.MT = AR.alloc([16, 128], BF16)
                    B_.xdd = AR.alloc([1024], BF16)
                    p1bufs[d][par] = B_
                C_ = _B()
                C_.yi = AR.alloc([1024], F32)
                C_.ydir = AR.alloc([1024], F32)
                C_.yp = AR.alloc([1024], F32)
                C_.zs = AR.alloc([1024], BF16)
                C_.gg = AR.alloc([1024], F32)
                C_.gjunk = AR.alloc([1024], BF16)
                C_.ss = AR.alloc([4], F32)
                C_.gn = AR.alloc([1024], BF16)
                C_.yst = AR.alloc([8, 128], BF16)
                p2bufs[d] = C_
            yssdT3 = yssdT_d.rearrange("(c p) t -> p c t", p=128)

            def r3(v):
                return v.rearrange("p (h d) -> p h d", d=64)

            def scan_p1(c, d, want_y, par):
                sf = '_%d_%d' % (d, par)
                U32 = Uf32 if d == 0 else Ub32
                lend = 127 if d == 0 else 0
                tk = c * 128
                B_ = p1bufs[d][par]
                dma(B_.xs, xs_d[tk:tk + 128, :], [], ['xs' + sf])
                dma(B_.BT, BT_d3[:, :, tk:tk + 128], [], ['BT' + sf])
                dma(B_.CT, CT_d3[:, :, tk:tk + 128], [], ['CT' + sf])
                dma(B_.Bt, Bt_d[tk:tk + 128, :], [], ['Bt' + sf])
                dma(B_.dt, dts_d[tk:tk + 128, :], [], ['dt' + sf])
                dsl = B_.dt[:, d * 16:(d + 1) * 16]
                tt(B_.dtA, dsl, Abc[:, d * 16:(d + 1) * 16], ALU.mult, ['dt' + sf, 'Abc'], ['dtA' + sf])
                tt(r3(B_.xdt), r3(B_.xs), dsl.unsqueeze(2).to_broadcast([128, 16, 64]), ALU.mult, ['xs' + sf, 'dt' + sf], ['xdt' + sf], eng='pool')
                P0, P0n = PSB(0)
                mms([(P0[:, 0:16], U32, B_.dtA, True, True)], ['c32', 'dtA' + sf], [P0n])
                mms([(P0[0:16, 64:192], B_.dtA, U32, True, True)], ['c32', 'dtA' + sf], [P0n])
                act(B_.nega, P0[:, 0:16], AF.Copy, [P0n], ['nega' + sf], scale=-1.0)
                act(B_.expa, P0[:, 0:16], AF.Exp, [P0n], ['expa' + sf])
                act(B_.acT[0:16, :], P0[0:16, 64:192], AF.Copy, [P0n], ['acT' + sf])
                for q4 in range(4):
                    pb, pbn = PSB(1 if q4 % 2 == 0 else 0)
                    lst_ = []
                    for hh in range(4):
                        lst_.append((pb[:, hh * 128:(hh + 1) * 128], sel32[:, q4 * 4 + hh, :], B_.acT[0:16, :], True, False))
                        lst_.append((pb[:, hh * 128:(hh + 1) * 128], ident16, mneg16[d], False, True))
                    mms(lst_, ['c32', 'c16', 'mneg16', 'acT' + sf], [pbn])
                    for hh in range(4):
                        h = q4 * 4 + hh
                        act(B_.dec[:, h, :], pb[:, hh * 128:(hh + 1) * 128], AF.Exp, [pbn, 'nega' + sf], ['dec' + sf], bias=B_.nega[:, h:h + 1])
                    act(B_.cdB[:, q4 * 4:q4 * 4 + 4], pb.rearrange("p (h l) -> p h l", l=128)[:, :, lend], AF.Exp, [pbn], ['cdB' + sf])
                if want_y:
                    P3, P3n = PSB(3)
                    mms([(P3[:, g * 128:(g + 1) * 128], B_.BT[:, g, :], B_.CT[:, g, :], True, True) for g in range(4)],
                        ['BT' + sf, 'CT' + sf], [P3n])
                    P33 = P3.rearrange("p (g l) -> p g l", l=128)
                    for g in range(4):
                        tt(B_.MT[:, 4 * g:4 * g + 4, :], P33[:, g:g + 1, :].to_broadcast([128, 4, 128]), B_.dec[:, 4 * g:4 * g + 4, :], ALU.mult,
                           [P3n, 'dec' + sf], ['MT' + sf])
                    if d == 0:
                        tt(r3(B_.xsD), r3(B_.xs), brl[:, 64:80].unsqueeze(2).to_broadcast([128, 16, 64]), ALU.mult,
                           ['xs' + sf, 'br%d' % l], ['xsD' + sf], eng='pool')
                tt(r3(B_.xdd), r3(B_.xdt), B_.dec[:, :, lend:lend + 1].to_broadcast([128, 16, 64]), ALU.mult,
                   ['xdt' + sf, 'dec' + sf], ['xdd' + sf], eng='pool')

            def scan_p2(c, d, second, want_y, par):
                sf = '_%d_%d' % (d, par)
                sg = '_%d' % d
                tk = c * 128
                B_ = p1bufs[d][par]
                C_ = p2bufs[d]
                if want_y:
                    for half in range(2):
                        py, pyn = PSB(4 + half)
                        lst = []
                        if d == 0:
                            lst.append((py, ident16, B_.xsD[:, half * 512:(half + 1) * 512], True, False))
                        for hh in range(8):
                            h = half * 8 + hh
                            lst.append((py[:, hh * 64:(hh + 1) * 64], B_.MT[:, h, :], B_.xdt[:, h * 64:(h + 1) * 64],
                                        d != 0, (d != 0) or hh == 7))
                        mms(lst, ['c16', 'xsD' + sf, 'MT' + sf, 'xdt' + sf], [pyn])
                for half in range(2):
                    pS, pSn = PSB(6 + half)
                    mms([(pS[:, gg * 256:(gg + 1) * 256], B_.Bt[:, (half * 2 + gg) * 128:(half * 2 + gg + 1) * 128],
                          B_.xdd[:, (half * 2 + gg) * 256:(half * 2 + gg + 1) * 256], True, True) for gg in range(2)],
                        ['Bt' + sf, 'xdd' + sf], [pSn])
                if want_y:
                    for half in range(2):
                        pyi, pyin = PSB(2)
                        mms([(pyi[:, gg * 256:(gg + 1) * 256], B_.CT[:, half * 2 + gg, :],
                              h16[d][:, (half * 2 + gg) * 256:(half * 2 + gg + 1) * 256], True, True) for gg in range(2)],
                            ['CT' + sf, 'h16' + sg], [pyin])
                        cp(C_.yi[:, half * 512:(half + 1) * 512], pyi, [pyin], ['yi' + sg], eng='act')
                    tt(r3(C_.yi), r3(C_.yi), B_.expa.unsqueeze(2).to_broadcast([128, 16, 64]), ALU.mult, ['yi' + sg, 'expa' + sf], ['yi' + sg])
                    for half in range(2):
                        py, pyn = PSB(4 + half)
                        tt(C_.ydir[:, half * 512:(half + 1) * 512], C_.yi[:, half * 512:(half + 1) * 512], py, ALU.add,
                           ['yi' + sg, pyn], ['ydir' + sg])
                tt(r3(h32[d]), r3(h32[d]), B_.cdB.unsqueeze(2).to_broadcast([128, 16, 64]), ALU.mult, ['h32' + sg, 'cdB' + sf], ['h32' + sg], eng='pool')
                for half in range(2):
                    pS, pSn = PSB(6 + half)
                    tt(h32[d][:, half * 512:(half + 1) * 512], h32[d][:, half * 512:(half + 1) * 512], pS, ALU.add,
                       ['h32' + sg, pSn], ['h32' + sg])
                cp(h16[d], h32[d], ['h32' + sg], ['h16' + sg], eng='act')
                if not want_y:
                    return
                if not second:
                    dma(yp_d[tk:tk + 128, :], C_.ydir, ['ydir' + sg], ['ypd%d' % c], q='act')
                    return
                dma(C_.yp, yp_d[tk:tk + 128, :], ['ypd%d' % c], ['yp' + sg])
                dma(C_.zs, zs_d[tk:tk + 128, :], [], ['zs' + sg])
                tt(C_.ydir, C_.ydir, C_.yp, ALU.add, ['ydir' + sg, 'yp' + sg], ['ydir' + sg])
                tt(C_.gg, C_.ydir, C_.zs, ALU.mult, ['ydir' + sg, 'zs' + sg], ['gg' + sg])
                act(C_.gjunk, C_.gg, AF.Square, ['gg' + sg], ['gjunk' + sg, 'ss' + sg], accum_out=C_.ss[:, 0:1])
                act(C_.ss[:, 1:2], C_.ss[:, 0:1], AF.Sqrt, ['ss' + sg, 'epsb'], ['ss' + sg], bias=epsb[:, 0:1], scale=1.0 / 1024)
                recip(C_.ss[:, 2:3], C_.ss[:, 1:2], ['ss' + sg], ['ss' + sg])
                act(C_.gn, C_.gg, AF.Copy, ['gg' + sg, 'ss' + sg], ['gn' + sg], scale=C_.ss[:, 2:3])
                P2, P2n = PSB(2)
                pt16 = P2.bitcast(BF16)
                trs([(pt16[:, ch * 128:(ch + 1) * 128], C_.gn[:, ch * 128:(ch + 1) * 128], ident16) for ch in range(8)],
                    ['gn' + sg, 'c16'], [P2n])
                for ch in range(8):
                    act(C_.yst[:, ch, :], pt16[:, ch * 128:(ch + 1) * 128], AF.Copy, [P2n, 'pv%d' % l], ['yst' + sg],
                        scale=pvl[:, 184 + ch:185 + ch])
                dma(yssdT3[:, :, tk:tk + 128], C_.yst, ['yst' + sg], [], q='act')

            chain_f = list(range(NCH))
            chain_b = [1, 0] + list(range(NCH - 1, 1, -1))
            seen = set()

            def wy(c):
                return not (last and c < 2)

            def do_p2(c, d, par):
                scan_p2(c, d, c in seen, wy(c), par)
                seen.add(c)
            scan_p1(chain_f[0], 0, wy(chain_f[0]), 0)
            scan_p1(chain_b[0], 1, wy(chain_b[0]), 0)
            for i in range(NCH):
                la = []
                if i + 1 < NCH:
                    S.record()
                    scan_p1(chain_f[i + 1], 0, wy(chain_f[i + 1]), (i + 1) % 2)
                    la = S.stop()
                S.record()
                do_p2(chain_b[i], 1, i % 2)
                lb = S.stop()
                S.replay([la, lb])
                la = []
                if i + 1 < NCH:
                    S.record()
                    scan_p1(chain_b[i + 1], 1, wy(chain_b[i + 1]), (i + 1) % 2)
                    la = S.stop()
                S.record()
                do_p2(chain_f[i], 0, i % 2)
                lb = S.stop()
                S.replay([la, lb])
            S.barrier()
            if 'stopB' in dbg and l == 0:
                break
            AR.off = PERSIST
            cls, rel, rep = na_classes(L)
            NQ = L // 128
            tab = AR.alloc([40, 128], BF16)
            kctx = AR.alloc([2, 4, 128], BF16)
            vctx = [AR.alloc([8, 65], BF16) for _ in range(2)]
            NRING = 8
            kt_buf = [AR.alloc([4, 128], BF16) for _ in range(NRING)]
            v_buf = [AR.alloc([8, 65], BF16) for _ in range(NRING)]
            ring_has = {}
            qt_buf = [AR.alloc([4, 128], BF16) for _ in range(2)]
            pT = [AR.alloc([7 * 128], BF16) for _ in range(2)]
            rec = AR.alloc([8], F32)
            yna = AR.alloc([512], BF16)
            ynast = AR.alloc([4, 128], BF16)
            kT_d3 = kT_d.rearrange("(c p) t -> p c t", p=128)
            qT_d3 = qT_d.rearrange("(c p) t -> p c t", p=128)
            ynaT3 = ynaT_d.rearrange("(c p) t -> p c t", p=128)
            for i in range(2):
                S.op('dve', lambda e, i=i: e.memset(vctx[i].rearrange("p a b -> p (a b)"), 1.0), [], ['vctx%d' % i])
                dma(kctx[:, i, :, :], kT_d3[:, :, i * 128:(i + 1) * 128], [], ['kctx'])
                dma(vctx[i][:, :, 0:64], vv_d[i * 128:(i + 1) * 128, :].rearrange("t (h d) -> t h d", d=64), [], ['vctx%d' % i])
            for i in range(NRING):
                S.op('dve', lambda e, i=i: e.memset(v_buf[i].rearrange("p a b -> p (a b)"), 1.0), [], ['vbuf%d' % i])

            def ring_load(kt):
                sl = kt % NRING
                if ring_has.get(sl) == kt:
                    return
                ring_has[sl] = kt
                ktok = CTX + kt * 128
                dma(kt_buf[sl], kT_d3[:, :, ktok:ktok + 128], [], ['ktb%d' % sl])
                dma(v_buf[sl][:, :, 0:64], vv_d[ktok:ktok + 128, :].rearrange("t (h d) -> t h d", d=64), [], ['vbuf%d' % sl])

            def na_tile(qtok, kts, use_tab, nxt=()):
                qb = qt_buf[(qtok // 128) % 2]
                qbn = 'qt_buf%d' % ((qtok // 128) % 2)
                dma(qb, qT_d3[:, :, qtok:qtok + 128], [], [qbn])
                nk = len(kts)
                for kt in list(kts) + list(nxt):
                    ring_load(kt)
                slots = [kt % NRING for kt in kts]
                nb = nk + 2
                for h in range(8):
                    cc, e = h // 2, h % 2
                    pA, pAn = PSB(2 * e)
                    pB, pBn = PSB(2 * e + 1)
                    lstA, lstB = [], []
                    rdA, rdB = set([qbn, 'c16', 'tab', 'kctx']), set([qbn, 'c16', 'tab', 'kctx'])
                    for j in range(nb):
                        bank, lst, rd = (pA, lstA, rdA) if j < 4 else (pB, lstB, rdB)
                        col = (j % 4) * 128
                        qop = qb[e * 64:(e + 1) * 64, cc, :]
                        if j < nk:
                            rd.add('ktb%d' % slots[j])
                            lst.append((bank[:, col:col + 128], kt_buf[slots[j]][e * 64:(e + 1) * 64, cc, :], qop, True, not use_tab))
                            if use_tab:
                                lst.append((bank[:, col:col + 128], ident16, tab[:, j * 8 + h, :], False, True))
                        else:
                            lst.append((bank[:, col:col + 128], kctx[e * 64:(e + 1) * 64, j - nk, cc, :], qop, True, True))
                    mms(lstA, sorted(rdA), [pAn])
                    if lstB:
                        mms(lstB, sorted(rdB), [pBn])
                    na_ = min(nb, 4)
                    act(pT[e][:, 0:na_ * 128], pA[:, 0:na_ * 128], AF.Exp, [pAn], ['pT%d' % e])
                    if nb > 4:
                        act(pT[e][:, 512:512 + (nb - 4) * 128], pB[:, 0:(nb - 4) * 128], AF.Exp, [pBn], ['pT%d' % e])
                    po, pon = PSB(4 + h // 4)
                    col = (h % 4) * 65
                    lst = []
                    rd = ['pT%d' % e, 'vctx0', 'vctx1']
                    for j in range(nb):
                        if j < nk:
                            vb = v_buf[slots[j]]
                            rd.append('vbuf%d' % slots[j])
                        else:
                            vb = vctx[j - nk]
                        lst.append((po[:, col:col + 65], pT[e][:, j * 128:(j + 1) * 128], vb[:, h, :], j == 0, j == nb - 1))
                    mms(lst, rd, [pon])
                for b4 in range(2):
                    po, pon = PSB(4 + b4)
                    po3 = po[:, 0:260].rearrange("p (h e) -> p h e", e=65)
                    recip(rec[:, b4 * 4:(b4 + 1) * 4], po3[:, :, 64], [pon], ['rec'])
                    tt(yna.rearrange("p (h d) -> p h d", d=64)[:, b4 * 4:(b4 + 1) * 4, :], po3[:, :, 0:64],
                       rec[:, b4 * 4:(b4 + 1) * 4].unsqueeze(2).to_broadcast([128, 4, 64]), ALU.mult, [pon, 'rec'], ['yna'])
                p6, p6n = PSB(6)
                pt16 = p6.bitcast(BF16)
                trs([(pt16[:, c * 128:(c + 1) * 128], yna[:, c * 128:(c + 1) * 128], ident16) for c in range(4)], ['yna', 'c16'], [p6n])
                cp(ynast, pt16[:, 0:512].rearrange("p (c t) -> p c t", t=128), [p6n], ['ynast'], eng='act')
                dma(ynaT3[:, :, qtok:qtok + 128], ynast, ['ynast'], [], q='act')

            if not last:
                for i in range(2):
                    na_tile(i * 128, [], False)
            cur = -1
            for qt in range(NQ):
                c = cls[qt]
                if c != cur:
                    cur = c
                    b0 = l * 200 + c * 40
                    dma(tab, natab16[b0:b0 + 40].rearrange("s k q -> k s q"), ['natab16_%d' % l], ['tab'])
                nxt = [qt + 1 + dk for dk in rel[cls[qt + 1]]] if qt + 1 < NQ else []
                na_tile(CTX + qt * 128, [qt + dk for dk in rel[c]], True, nxt)
            S.barrier()
            if 'stopC' in dbg and l == 0:
                break
            AR.off = PERSIST
            ph = {}
            ph['sq'] = AR.alloc([8, 512], BF16)
            ph['rs'] = AR.alloc([512], F32)
            ph['ntmp'] = [AR.alloc([512], F32) for _ in range(2)]
            xt2 = [AR.alloc([8, 512], F32) for _ in range(2)]
            ys = AR.alloc([8, 512], BF16)
            yn_ = AR.alloc([4, 512], BF16)
            yg = AR.alloc([4, 512], BF16)
            gt = AR.alloc([24, 512], BF16)
            mg = AR.alloc([8, 512], BF16)
            h2 = AR.alloc([8, 512], BF16)
            actb = AR.alloc([22, 512], BF16)
            m1 = AR.alloc([512], F32)
            m2 = AR.alloc([512], F32)
            sa = AR.alloc([512], BF16)
            wbuf = [AR.alloc([4096], BF16) for _ in range(3)]
            wctr[0] = 0
            ygmT3 = ygmT_d.rearrange("(c p) t -> p c t", p=128)
            gateT3 = gateT_d.rearrange("(c p) t -> p c t", p=128)
            x1T3 = x1T.rearrange("(c p) t -> p c t", p=128)
            outT3 = outT.rearrange("(c p) t -> p c t", p=128)

            def load_w(blk):
                slot = wctr[0] % 3
                wctr[0] += 1
                dma(wbuf[slot], wblk16[wb0 + blk], ['w16_%d' % (wb0 + blk)], ['wb%d' % slot])
                return wbuf[slot], 'wb%d' % slot

            dtiles = (tiles[1:] if last else tiles)

            def d_loads(ti):
                (t0_, n_) = dtiles[ti]
                dma(xt2[ti % 2][:, :, :n_], xsrc3[:, :, t0_:t0_ + n_], [], ['Dxt%d' % (ti % 2)])
                dma(ys[:, :, :n_], yssdT3[:, :, t0_:t0_ + n_], [], ['ys'])
                dma(yn_[:, :, :n_], ynaT3[:, :, t0_:t0_ + n_], [], ['yn'])
                dma(yg[:, :, :n_], ygmT3[:, :, t0_:t0_ + n_], [], ['yg'])
                dma(gt[:, :, :n_], gateT3[:, :, t0_:t0_ + n_], [], ['gt'])
            d_loads(0)
            for ti, (t0, n) in enumerate(dtiles):
                isctx = t0 < CTX
                j = 1 if isctx else 0
                xt = xt2[ti % 2]
                xtn = 'Dxt%d' % (ti % 2)
                for jb in range(4):
                    w, wn = load_w(18 + jb)
                    for oo in range(2):
                        oc = 2 * jb + oo
                        base = oo * 2048
                        pa, pan = PSB(nextbank())
                        mms([(pa[:, :n], w[:, base + kc * 128:base + (kc + 1) * 128], ys[:, kc, :n], kc == 0, kc == 7) for kc in range(8)],
                            [wn, 'ys'], [pan])
                        pb_, pbn = PSB(nextbank())
                        mms([(pb_[:, :n], w[:, base + 1024 + kc * 128:base + 1024 + (kc + 1) * 128], yn_[:, kc, :n], kc == 0, kc == 3) for kc in range(4)],
                            [wn, 'yn'], [pbn])
                        pc_, pcn = PSB(nextbank())
                        mms([(pc_[:, :n], w[:, base + 1536 + kc * 128:base + 1536 + (kc + 1) * 128], yg[:, kc, :n], kc == 0, kc == 3) for kc in range(4)],
                            [wn, 'yg'], [pcn])
                        tt(m1[:, :n], pa[:, :n], gt[:, oc, :n], ALU.mult, [pan, 'gt'], ['m1'])
                        tt(m2[:, :n], pb_[:, :n], gt[:, 8 + oc, :n], ALU.mult, [pbn, 'gt'], ['m2'])
                        tt(m1[:, :n], m1[:, :n], m2[:, :n], ALU.add, ['m1', 'm2'], ['m1'])
                        tt(m2[:, :n], pc_[:, :n], gt[:, 16 + oc, :n], ALU.mult, [pcn, 'gt'], ['m2'])
                        tt(mg[:, oc, :n], m1[:, :n], m2[:, :n], ALU.add, ['m1', 'm2'], ['mg'])
                if ti + 1 < len(dtiles):
                    d_loads(ti + 1)
                for jb in range(2):
                    w, wn = load_w(22 + jb)
                    w3 = w.rearrange("p (k c) -> p k c", c=512)
                    for oc4 in range(4):
                        oc = jb * 4 + oc4
                        p_, pn = PSB(nextbank())
                        mms([(p_[:, :n], w3[:, kc, oc4 * 128:(oc4 + 1) * 128], mg[:, kc, :n], kc == 0, kc == 7) for kc in range(8)],
                            [wn, 'mg'], [pn])
                        stt(xt[:, oc, :n], p_[:, :n], modT[l][:, 16 + oc, j:j + 1], xt[:, oc, :n], ALU.mult, ALU.add,
                            [pn, 'modT%d' % l, xtn], [xtn])
                norm_mod(xt, n, A2[l][:, :, j], modT[l][:, 24:32, j], h2, xtn, 'DhT')
                for jb in range(11):
                    w, wn = load_w(24 + jb)
                    w3 = w.rearrange("p (k c) -> p k c", c=512)
                    for hcl in range(2):
                        hc = 2 * jb + hcl
                        pa, pan = PSB(nextbank())
                        mms([(pa[:, :n], w3[:, kc, (hcl * 2) * 128:(hcl * 2 + 1) * 128], h2[:, kc, :n], kc == 0, kc == 7) for kc in range(8)],
                            [wn, 'DhT'], [pan])
                        pb_, pbn = PSB(nextbank())
                        mms([(pb_[:, :n], w3[:, kc, (hcl * 2 + 1) * 128:(hcl * 2 + 2) * 128], h2[:, kc, :n], kc == 0, kc == 7) for kc in range(8)],
                            [wn, 'DhT'], [pbn])
                        act(sa[:, :n], pa[:, :n], AF.Silu, [pan], ['sa'])
                        tt(actb[:, hc, :n], sa[:, :n], pb_[:, :n], ALU.mult, ['sa', pbn], ['actb'])
                for oc in range(8):
                    w, wn = load_w(35 + oc)
                    wv = w[:, 0:2816].rearrange("p (k c) -> p k c", c=128)
                    p_, pn = PSB(nextbank())
                    mms([(p_[:, :n], wv[:, kc, :], actb[:, kc, :n], kc == 0, kc == 21) for kc in range(22)], [wn, 'actb'], [pn])
                    stt(xt[:, oc, :n], p_[:, :n], modT[l][:, 40 + oc, j:j + 1], xt[:, oc, :n], ALU.mult, ALU.add,
                        [pn, 'modT%d' % l, xtn], [xtn])
                if last:
                    dma(outT3[:, :, t0 - CTX:t0 - CTX + n], xt[:, :, :n], [xtn], [], q='act')
                else:
                    dma(x1T3[:, :, t0:t0 + n], xt[:, :, :n], [xtn], [], q='act')
            S.barrier()
        S.emit()
    return nc


def _slab(W):
    k = W.shape[0] // 128
    c = W.shape[1]
    out = np.zeros((128, 4096), np.float32)
    out[:, :k * c] = W.reshape(k, 128, c).transpose(1, 0, 2).reshape(128, k * c)
    return out


def pack_weights(inp, l):
    w_in = inp['w_in'][l]
    blks = []
    segs = [(0, 512), (512, 1024),
            (1024, 1536), (1536, 2048), (2048, 2560), (2560, 3072),
            (3104, 3616), (3616, 4128), (4128, 4640),
            (4640, 5152), (5152, 5664)]
    segs += [(5664 + 512 * i, 5664 + 512 * (i + 1)) for i in range(6)]
    for (a, b) in segs:
        blks.append(_slab(w_in[:, a:b]))
    blks.append(_slab(w_in[:, 3072:3104]))
    wa, wb_, wc = inp['w_branch_ssd'][l], inp['w_branch_na'][l], inp['w_branch_gm'][l]
    for j in range(4):
        blk = np.zeros((128, 4096), np.float32)
        for oo in range(2):
            oc = 2 * j + oo
            base = oo * 2048
            blk[:, base:base + 1024] = wa[:, oc * 128:(oc + 1) * 128].reshape(8, 128, 128).transpose(1, 0, 2).reshape(128, 1024)
            blk[:, base + 1024:base + 1536] = wb_[:, oc * 128:(oc + 1) * 128].reshape(4, 128, 128).transpose(1, 0, 2).reshape(128, 512)
            blk[:, base + 1536:base + 2048] = wc[:, oc * 128:(oc + 1) * 128].reshape(4, 128, 128).transpose(1, 0, 2).reshape(128, 512)
        blks.append(blk)
    wo = inp['w_out'][l]
    for j in range(2):
        blks.append(_slab(wo[:, j * 512:(j + 1) * 512]))
    wf = inp['w_ffn_in'][l]
    for j in range(11):
        slab = np.zeros((1024, 512), np.float32)
        for hcl in range(2):
            for ab in range(2):
                src = ab * 2816 + (2 * j + hcl) * 128
                slab[:, (hcl * 2 + ab) * 128:(hcl * 2 + ab + 1) * 128] = wf[:, src:src + 128]
        blks.append(_slab(slab))
    wfo = inp['w_ffn_out'][l]
    for oc in range(8):
        blks.append(_slab(wfo[:, oc * 128:(oc + 1) * 128]))
    assert len(blks) == NBLK
    return np.stack(blks)


def pcol(v):
    return np.ascontiguousarray(v.reshape(-1, 128).T)


def pack_shared(inp, L, depth):
    T = CTX + L
    R = L // 64
    f32 = np.float32
    wmod = np.stack([inp['w_mod'][l].reshape(8, 128, 12, 512).transpose(2, 1, 0, 3).reshape(12, 128, 4096)
                     for l in range(depth)]).reshape(depth * 12, 128, 4096)
    wblk = np.concatenate([pack_weights(inp, l) for l in range(depth)], axis=0)
    pvec = np.zeros((depth, 128, NPV), f32)
    brow = np.zeros((depth, NBR), f32)
    wsT = np.zeros((depth, 128, 1024), f32)
    for l in range(depth):
        pvec[l, :, 0:48] = pcol(inp['b_mod'][l])
        pvec[l, :, 48:56] = pcol(inp['norm1'][l])
        pvec[l, :, 56:64] = pcol(inp['norm2'][l])
        pvec[l, :, 64:88] = pcol(inp['b_gate'][l])
        pvec[l, :, 88:104] = pcol(inp['conv_b'][l])
        for k in range(5):
            pvec[l, :, 104 + k * 16:104 + (k + 1) * 16] = pcol(inp['conv_w'][l][k])
        pvec[l, :, 184:192] = pcol(inp['ssd_norm'][l])
        pvec[l, :, 192] = np.tile(inp['q_norm'][l], 2)
        pvec[l, :, 193] = np.tile(inp['k_norm'][l], 2)
        pvec[l, :, 194:202] = inp['b_spatial'][l].T
        brow[l, 0:32] = inp['dt_bias'][l].reshape(-1)
        brow[l, 32:64] = inp['a_log'][l].reshape(-1)
        brow[l, 64:80] = inp['d_skip'][l]
        brow[l, 80:592] = inp['gm_norm'][l]
        wsT[l] = inp['w_spatial'][l].transpose(2, 0, 1).reshape(128, 1024)
    freqs = (np.float32(10000.0) ** (-np.arange(32, dtype=f32) / np.float32(32))).astype(f32)
    pos = np.arange(L)
    ang_row = (pos // 64).astype(f32)[:, None] * freqs
    ang_col = (pos % 64).astype(f32)[:, None] * freqs
    ropeC = np.ones((128, T), f32)
    ropeS = np.zeros((128, T), f32)
    for half, ang in ((0, ang_row), (1, ang_col)):
        c = np.cos(ang).T.astype(f32)
        s = np.sin(ang).T.astype(f32)
        ropeC[half * 64:half * 64 + 32, CTX:] = c
        ropeC[half * 64 + 32:half * 64 + 64, CTX:] = c
        ropeS[half * 64:half * 64 + 32, CTX:] = s
        ropeS[half * 64 + 32:half * 64 + 64, CTX:] = s
    consts = np.zeros((128, NCONST), f32)
    i = np.arange(128)
    consts[:, 0:128] = np.eye(128, dtype=f32)
    consts[:, 128:256] = (i[:, None] <= i[None, :]).astype(f32)
    consts[:, 256:384] = (i[:, None] >= i[None, :]).astype(f32)
    consts[:, 384:512] = 1.0
    consts[:, 512:640] = ((i[:, None] // 64) == (i[None, :] // 64)).astype(f32)
    pm = np.zeros((128, 128), f32)
    for base in (0, 64):
        for n in range(32):
            pm[base + n + 32, base + n] = -1.0
            pm[base + n, base + n + 32] = 1.0
    consts[:, 640:768] = pm
    sel = np.zeros((16, 16, 128), f32)
    for h in range(16):
        sel[h, h, :] = 1.0
    consts[0:16, 768:768 + 2048] = sel.reshape(16, 2048)
    cls, rel, rep = na_classes(L)

    def rs_(r):
        return min(max(r - 4, 0), R - 8)
    natab = np.full((depth, 5, 5, 8, 128, 128), NEG, f32)
    kc = np.arange(64)
    qc = np.arange(64)
    cs = np.clip(qc - 8, 0, 48)
    colok = (kc[:, None] >= cs[None, :]) & (kc[:, None] < cs[None, :] + 16)
    ci = np.clip(kc[:, None] - qc[None, :] + 15, 0, 30)
    for c, qt in rep.items():
        for slot, dk in enumerate(rel[c]):
            kt = qt + dk
            for a in range(2):
                r = 2 * qt + a
                for b in range(2):
                    krow = 2 * kt + b
                    if not (rs_(r) <= krow < rs_(r) + 8):
                        continue
                    ri = krow - r + 7
                    for l in range(depth):
                        vals = inp['rpb'][l][:, ri, :][:, ci]
                        blk = np.where(colok[None], vals, np.float32(NEG))
                        natab[l, c, slot, :, b * 64:(b + 1) * 64, a * 64:(a + 1) * 64] = blk
    natab = natab.reshape(depth * 200, 128, 128)
    return dict(wmod=wmod, wblk=wblk, pvec=pvec, brow=brow, wsT=wsT, ropeC=ropeC, ropeS=ropeS,
                natab=natab, consts=consts)


def pack_core(inp, b, shared):
    xT = np.ascontiguousarray(np.concatenate([inp['ctx'][b].T, inp['x'][b].T], axis=1))
    cT = np.zeros((128, 8, 2), np.float32)
    cT[:, :, 0] = pcol(inp['c'][b])
    cT[:, :, 1] = pcol(inp['c_ctx'])
    m = dict(shared)
    m['xT'] = xT
    m['cT'] = cT.reshape(128, 16)
    return m


_NC_CACHE = {}


def kernel(**inputs):
    inp = {k: np.asarray(v, dtype=np.float32) for k, v in inputs.items()}
    B, L, _ = inp['x'].shape
    depth = inp['w_in'].shape[0]
    key = (L, depth)
    if key not in _NC_CACHE:
        _NC_CACHE[key] = build(L, depth)
    nc = _NC_CACHE[key]
    shared = pack_shared(inp, L, depth)
    in_maps = [pack_core(inp, b, shared) for b in range(B)]
    res = run_bass_kernel_spmd(nc, in_maps, core_ids=list(range(B)))
    out = np.stack([np.ascontiguousarray(res.results[b]['outT'].T) for b in range(B)])
    return out.astype(np.float32)
```

```python
import math
import numpy as np
import concourse.bass as bass
import concourse.mybir as mybir
from concourse.bass_utils import run_bass_kernel_spmd
from contextlib import ExitStack

F32 = mybir.dt.float32
BF16 = mybir.dt.bfloat16
U8 = mybir.dt.uint8
AF = mybir.ActivationFunctionType
ALU = mybir.AluOpType
AX = mybir.AxisListType

D = 1024
CTX = 256
EPS = 1e-6
NBLK = 43
NPV = 202
NBR = 592
NCONST = 768 + 2048
NEG = -30000.0

ENGS = ['pe', 'act', 'dve', 'pool', 'sp']
N_DSEM = 48
DSEM_RANGE = {'sp': (0, 22), 'act': (22, 32), 'pool': (32, 48)}


class Sched:
    def __init__(self, nc):
        self.nc = nc
        self.ops = {e: [] for e in ENGS}
        self.last_write = {}
        self.readers = {}
        self.known = {e: {} for e in ENGS}
        self.dsem_uses = [0] * N_DSEM
        self.dsem_next = {'sp': 0, 'pool': 0, 'act': 0}
        self.signal = {e: set() for e in ENGS}
        self.rec = None

    def record(self):
        self.rec = []

    def stop(self):
        r = self.rec
        self.rec = None
        return r

    def replay(self, lists):
        idx = [0] * len(lists)
        tot = [max(1, len(x)) for x in lists]
        while True:
            live = [k for k in range(len(lists)) if idx[k] < len(lists[k])]
            if not live:
                break
            k = min(live, key=lambda k: idx[k] / tot[k])
            self.op(*lists[k][idx[k]])
            idx[k] += 1

    def _need(self, eng, tok, waits):
        kind, key, val = tok
        if kind == 'eng' and key == eng and eng in ('pe', 'sp'):
            return
        k = (kind, key)
        if self.known[eng].get(k, 0) >= val:
            return
        self.known[eng][k] = val
        waits[k] = max(waits.get(k, 0), val)
        if kind == 'eng':
            self.signal[key].add(val)

    def op(self, eng, fn, reads=(), writes=(), dma=False):
        if self.rec is not None:
            self.rec.append((eng, fn, tuple(reads), tuple(writes), dma))
            return None
        waits = {}
        for b in reads:
            w = self.last_write.get(b)
            if w is not None:
                self._need(eng, w, waits)
        for b in writes:
            w = self.last_write.get(b)
            if w is not None:
                self._need(eng, w, waits)
            for r in self.readers.get(b, ()):
                self._need(eng, r, waits)
        idx = len(self.ops[eng]) + 1
        if dma:
            lo, hi = DSEM_RANGE[eng]
            j = lo + self.dsem_next[eng]
            self.dsem_next[eng] = (self.dsem_next[eng] + 1) % (hi - lo)
            n = self.dsem_uses[j]
            if n > 0:
                self._need(eng, ('dma', j, 16 * n), waits)
            self.dsem_uses[j] = n + 1
            tok = ('dma', j, 16 * (n + 1))
        else:
            tok = ('eng', eng, idx)
        for b in reads:
            self.readers.setdefault(b, []).append(tok)
        for b in writes:
            self.last_write[b] = tok
            self.readers[b] = []
        self.ops[eng].append((fn, waits, tok))
        return tok

    def barrier(self):
        toks = []
        for e in ENGS:
            for i in range(len(self.ops[e]), 0, -1):
                t = self.ops[e][i - 1][2]
                if t is not None and t[0] == 'eng':
                    toks.append(t)
                    break
        for j in range(N_DSEM):
            if self.dsem_uses[j] > 0:
                toks.append(('dma', j, 16 * self.dsem_uses[j]))
        for e in ENGS:
            waits = {}
            for t in toks:
                self._need(e, t, waits)
            self.ops[e].append((None, waits, None))
        self.last_write.clear()
        self.readers.clear()

    def emit(self):
        nc = self.nc
        with ExitStack() as st:
            esem = {e: st.enter_context(nc.semaphore('es_' + e)) for e in ENGS}
            dsem = [st.enter_context(nc.semaphore('ds_%d' % j)) for j in range(N_DSEM)]
            block = st.enter_context(nc.Block())
            sigmap = {}
            for e in ENGS:
                cnt = 0
                m = {}
                ss = self.signal[e]
                for i in range(1, len(self.ops[e]) + 1):
                    if i in ss:
                        cnt += 1
                        m[i] = cnt
                sigmap[e] = m

            def run(e, engobj):
                for i, (fn, waits, tok) in enumerate(self.ops[e], start=1):
                    for (kind, key), val in waits.items():
                        if kind == 'eng':
                            engobj.wait_ge(esem[key], sigmap[key][val])
                        else:
                            engobj.wait_ge(dsem[key], val)
                    if fn is None:
                        continue
                    ins = fn(engobj)
                    if tok[0] == 'dma':
                        ins.then_inc(dsem[tok[1]], 16)
                    elif i in sigmap[e]:
                        ins.then_inc(esem[e], 1)

            @block.tensor
            def _(eng):
                run('pe', eng)

            @block.scalar
            def _(eng):
                run('act', eng)

            @block.vector
            def _(eng):
                run('dve', eng)

            @block.gpsimd
            def _(eng):
                run('pool', eng)

            @block.sync
            def _(eng):
                run('sp', eng)


class Arena:
    def __init__(self, ap, size):
        self.ap = ap
        self.size = size
        self.off = 0

    def alloc(self, free_shape, dt):
        esz = 4 if dt == F32 else 2
        n = 1
        for s in free_shape:
            n *= s
        nb = (n * esz + 63) // 64 * 64
        assert self.off + nb <= self.size, "SBUF arena overflow %d + %d" % (self.off, nb)
        v = self.ap[:, self.off:self.off + n * esz].bitcast(dt)
        self.off += nb
        if len(free_shape) == 2:
            v = v.rearrange("p (a b) -> p a b", b=free_shape[1])
        elif len(free_shape) == 3:
            v = v.rearrange("p (a b c) -> p a b c", b=free_shape[1], c=free_shape[2])
        return v


def na_classes(L):
    R = L // 64
    NQ = R // 2

    def rs(r):
        return min(max(r - 4, 0), R - 8)
    info = []
    for qt in range(NQ):
        r0, r1 = 2 * qt, 2 * qt + 1
        lo = min(rs(r0), rs(r1)) // 2
        hi = (max(rs(r0), rs(r1)) + 7) // 2
        info.append(tuple(k - qt for k in range(lo, hi + 1)))
    cls = []
    for qt in range(NQ):
        if qt == 0:
            cls.append(0)
        elif qt == 1:
            cls.append(1)
        elif qt == NQ - 2:
            cls.append(3)
        elif qt == NQ - 1:
            cls.append(4)
        else:
            cls.append(2)
    rel = {}
    rep = {}
    for qt in range(NQ):
        c = cls[qt]
        if c in rel:
            assert rel[c] == info[qt], (c, rel[c], info[qt])
        else:
            rel[c] = info[qt]
            rep[c] = qt
    return cls, rel, rep


def build(L, depth=2, dbg=()):
    T = CTX + L
    NCH = T // 128
    TP = T + 8
    tiles = [(0, 256)] + [(CTX + 512 * i, 512) for i in range(L // 512)]
    nc = bass.Bass("TRN2", target_bir_lowering=False)

    def din(name, shape, dt=F32):
        return nc.dram_tensor(name, shape, dt, kind="ExternalInput").ap()

    def dscr(name, shape, dt):
        kind = "ExternalOutput" if name in dbg else "Internal"
        return nc.dram_tensor(name, shape, dt, kind=kind).ap()

    xT_in = din("xT", [D, T])
    cT_in = din("cT", [128, 16])
    wmod_in = din("wmod", [depth * 12, 128, 4096])
    wblk_in = din("wblk", [depth * NBLK, 128, 4096])
    pvec_in = din("pvec", [depth, 128, NPV])
    brow_in = din("brow", [depth, NBR])
    wsT_in = din("wsT", [depth, 128, 1024])
    ropeC_in = din("ropeC", [128, T])
    ropeS_in = din("ropeS", [128, T])
    natab_in = din("natab", [depth * 200, 128, 128])
    consts_in = din("consts", [128, NCONST])
    outT = nc.dram_tensor("outT", [D, L], F32, kind="ExternalOutput").ap()

    wblk16 = dscr("wblk16", [depth * NBLK, 128, 4096], BF16)
    natab16 = dscr("natab16", [depth * 200, 128, 128], BF16)
    x1T = dscr("x1T", [D, T], F32)
    zs_d = dscr("zs", [T, 1024], BF16)
    xbcT_d = dscr("xbcT", [2048, TP], BF16)
    dts_d = dscr("dts", [T, 32], F32)
    qT_d = dscr("qT", [512, T], BF16)
    kT_d = dscr("kT", [512, T], BF16)
    vv_d = dscr("vv", [T, 512], BF16)
    ygmT_d = dscr("ygmT", [512, T], BF16)
    gateT_d = dscr("gateT", [3072, T], BF16)
    xs_d = dscr("xs", [T, 1024], BF16)
    BT_d = dscr("BTs", [512, T], BF16)
    CT_d = dscr("CTs", [512, T], BF16)
    Bt_d = dscr("Btok", [T, 512], BF16)
    yp_d = dscr("ypart", [T, 1024], F32)
    yssdT_d = dscr("yssdT", [1024, T], BF16)
    ynaT_d = dscr("ynaT", [512, T], BF16)

    st = ExitStack()
    with st:
        ARENA = 186 * 1024
        arena_t = st.enter_context(nc.sbuf_tensor("arena", [128, ARENA], U8))
        ps = [st.enter_context(nc.psum_tensor("ps%d" % i, [128, 512], F32)) for i in range(8)]
        S = Sched(nc)
        AR = Arena(arena_t, ARENA)
        pb_ctr = [0]

        def PSB(i):
            return ps[i][:], 'P%d' % i

        def nextbank(lo=0, hi=8):
            i = lo + pb_ctr[0] % (hi - lo)
            pb_ctr[0] += 1
            return i

        def dma(out, in_, reads, writes, q='sp'):
            return S.op(q, lambda e: e.dma_start(out=out, in_=in_), reads, writes, dma=True)

        def act(out, in_, func, reads, writes, bias=None, scale=None, accum_out=None):
            kw = {}
            if bias is not None:
                kw['bias'] = bias
            if scale is not None:
                kw['scale'] = scale
            if accum_out is not None:
                kw['accum_out'] = accum_out
            return S.op('act', lambda e: e.activation(out=out, in_=in_, func=func, **kw), reads, writes)

        def tt(out, in0, in1, op, reads, writes, eng='dve'):
            return S.op(eng, lambda e: e.tensor_tensor(out=out, in0=in0, in1=in1, op=op), reads, writes)

        def ts(out, in0, s1, s2, op0, op1, reads, writes, eng='dve'):
            if s2 is None:
                return S.op(eng, lambda e: e.tensor_scalar(out=out, in0=in0, scalar1=s1, scalar2=None, op0=op0), reads, writes)
            return S.op(eng, lambda e: e.tensor_scalar(out=out, in0=in0, scalar1=s1, scalar2=s2, op0=op0, op1=op1), reads, writes)

        def stt(out, in0, scalar, in1, op0, op1, reads, writes):
            return S.op('dve', lambda e: e.scalar_tensor_tensor(out=out, in0=in0, scalar=scalar, in1=in1, op0=op0, op1=op1), reads, writes)

        def recip(out, in_, reads, writes):
            return S.op('dve', lambda e: e.reciprocal(out=out, in_=in_), reads, writes)

        def cp(out, in_, reads, writes, eng='dve'):
            if eng == 'act':
                return S.op('act', lambda e: e.copy(out=out, in_=in_), reads, writes)
            return S.op(eng, lambda e: e.tensor_copy(out=out, in_=in_), reads, writes)

        def mms(lst, reads, writes):
            def fn(e):
                ins = None
                for (o, l, r, s0, s1) in lst:
                    ins = e.matmul(o, lhsT=l, rhs=r, start=s0, stop=s1)
                return ins
            return S.op('pe', fn, reads, writes)

        def trs(lst, reads, writes):
            def fn(e):
                ins = None
                for (o, i_, idn) in lst:
                    ins = e.transpose(o, i_, idn)
                return ins
            return S.op('pe', fn, reads, writes)

        c32 = AR.alloc([NCONST], F32)
        c16 = AR.alloc([512], BF16)
        epsb = AR.alloc([1], F32)
        ident32 = c32[:, 0:128]
        Uf32 = c32[:, 128:256]
        Ub32 = c32[:, 256:384]
        sel32 = c32[0:16, 768:768 + 2048].rearrange("p (h s) -> p h s", s=128)
        ident16 = c16[:, 0:128]
        ones16 = c16[:, 128:256]
        bo16 = c16[:, 256:384]
        pm16 = c16[:, 384:512]
        pv = [AR.alloc([NPV], F32) for _ in range(depth)]
        br = [AR.alloc([NBR], F32) for _ in range(depth)]
        modT = [AR.alloc([48, 2], F32) for _ in range(depth)]
        A1 = [AR.alloc([8, 2], F32) for _ in range(depth)]
        A2 = [AR.alloc([8, 2], F32) for _ in range(depth)]
        scs = AR.alloc([8, 2], F32)
        mneg16 = [AR.alloc([128], BF16) for _ in range(2)]
        PERSIST = AR.off

        dma(c32, consts_in, [], ['c32'])
        S.op('dve', lambda e: e.memset(epsb, EPS), [], ['epsb'])
        cp(c16[:, 0:128], c32[:, 0:128], ['c32'], ['c16'])
        cp(c16[:, 128:512], c32[:, 384:768], ['c32'], ['c16'])
        for d_ in range(2):
            ts(mneg16[d_], (Uf32 if d_ == 0 else Ub32), -1.0, -NEG, ALU.add, ALU.mult, ['c32'], ['mneg16'])
        for l in range(depth):
            dma(pv[l], pvec_in[l], [], ['pv%d' % l])
            dma(br[l], brow_in[l:l + 1, :].partition_broadcast(128), [], ['br%d' % l])
        wsrc = wblk_in.rearrange("b p (h e) -> b p h e", e=2048)
        wdst = wblk16.rearrange("b p (h e) -> b p h e", e=2048)
        CB = 4

        def cast_weights(l_):
            for b0 in range(l_ * NBLK, (l_ + 1) * NBLK, CB):
                b1 = min((l_ + 1) * NBLK, b0 + CB)
                dma(wdst[b0:b1], wsrc[b0:b1], [], ['w16_%d' % b for b in range(b0, b1)], q='pool')

        def cast_natab(l_):
            for k in range(0, 200, 50):
                dma(natab16[l_ * 200 + k:l_ * 200 + k + 50], natab_in[l_ * 200 + k:l_ * 200 + k + 50], [], ['natab16_%d' % l_], q='pool')
        cast_weights(0)

        sc_raw = AR.alloc([16], F32)
        wm = [AR.alloc([8, 512], F32) for _ in range(2)]
        dma(sc_raw, cT_in, [], ['sc_raw'])
        act(scs.rearrange("p a b -> p (a b)"), sc_raw, AF.Silu, ['sc_raw'], ['scs'])
        for l in range(depth):
            psM, psMn = PSB(0)
            for blk in range(12):
                slot = blk % 2
                dma(wm[slot].rearrange("p a b -> p (a b)"), wmod_in[l * 12 + blk], [], ['wm%d' % slot])
                lst = []
                for oc4 in range(4):
                    oc = blk * 4 + oc4
                    for kc in range(8):
                        lst.append((psM[:, oc * 2:oc * 2 + 2], wm[slot][:, kc, oc4 * 128:(oc4 + 1) * 128], scs[:, kc, :], kc == 0, kc == 7))
                mms(lst, ['wm%d' % slot, 'scs'], [psMn])
            tt(modT[l], psM[:, 0:96].rearrange("p (a b) -> p a b", b=2), pv[l][:, 0:48].unsqueeze(2).to_broadcast([128, 48, 2]),
               ALU.add, [psMn, 'pv%d' % l], ['modT%d' % l])
            for (Ax, sci, nrm) in ((A1[l], 8, 48), (A2[l], 32, 56)):
                ts(Ax, modT[l][:, sci:sci + 8, :], 1.0, None, ALU.add, None, ['modT%d' % l], ['A%d' % l])
                tt(Ax, Ax, pv[l][:, nrm:nrm + 8].unsqueeze(2).to_broadcast([128, 8, 2]), ALU.mult, ['A%d' % l, 'pv%d' % l], ['A%d' % l])
        S.barrier()

        def norm_mod(xt, n, Acol, shcol, hT, xtn, hTn):
            sq = ph['sq']
            act(sq[:, :, :n], xt[:, :, :n], AF.Square, [xtn], ['sq'])
            bi = nextbank()
            pss, pn = PSB(bi)
            mms([(pss[:, :n], ones16, sq[:, c, :n], c == 0, c == 7) for c in range(8)], ['sq', 'c16'], [pn])
            rs = ph['rs']
            act(rs[:, :n], pss[:, :n], AF.Sqrt, [pn, 'epsb'], ['rs'], bias=epsb[:, 0:1], scale=1.0 / D)
            recip(rs[:, :n], rs[:, :n], ['rs'], ['rs'])
            for c in range(8):
                tmp = ph['ntmp'][c % 2]
                tt(tmp[:, :n], xt[:, c, :n], rs[:, :n], ALU.mult, [xtn, 'rs'], ['ntmp%d' % (c % 2)])
                act(hT[:, c, :n], tmp[:, :n], AF.Identity, ['ntmp%d' % (c % 2)], [hTn], bias=shcol[:, c:c + 1], scale=Acol[:, c:c + 1])

        def col_of(t):
            return 2 + t if t < CTX else 6 + t

        for l in range(depth):
            last = (l == depth - 1)
            xsrc = xT_in if l == 0 else x1T
            xsrc3 = xsrc.rearrange("(c p) t -> p c t", p=128)
            wb0 = l * NBLK
            pvl = pv[l]
            brl = br[l]
            AR.off = PERSIST
            ph = {}
            ph['sq'] = AR.alloc([8, 512], BF16)
            ph['rs'] = AR.alloc([512], F32)
            ph['ntmp'] = [AR.alloc([512], F32) for _ in range(2)]
            xt = AR.alloc([8, 512], F32)
            hT2 = [AR.alloc([8, 512], BF16) for _ in range(2)]
            wbuf = [AR.alloc([4096], BF16) for _ in range(3)]
            wsT32 = AR.alloc([1024], F32)
            wsT16 = AR.alloc([8, 128], BF16)
            zst = AR.alloc([4, 1024], BF16)
            xbcst = AR.alloc([16, 512], BF16)
            dst_ = AR.alloc([4, 32], F32)
            dtmp = AR.alloc([32], F32)
            qkst = AR.alloc([8, 512], BF16)
            sq16 = [AR.alloc([512], BF16) for _ in range(2)]
            rs2 = [AR.alloc([512], F32) for _ in range(2)]
            qtmp = [AR.alloc([512], F32) for _ in range(2)]
            qraw = [AR.alloc([512], F32) for _ in range(2)]
            vst = AR.alloc([4, 512], BF16)
            u16 = AR.alloc([4, 512], BF16)
            vg = AR.alloc([512], F32)
            vjunk = AR.alloc([512], BF16)
            ssv4 = AR.alloc([4, 4], F32)
            vn16 = [AR.alloc([512], BF16) for _ in range(4)]
            gtmp = AR.alloc([512], F32)
            ygm16 = [AR.alloc([512], BF16) for _ in range(4)]
            ygst = AR.alloc([4, 512], BF16)
            gst2 = [AR.alloc([4, 512], BF16) for _ in range(2)]
            dma(wsT32, wsT_in[l], [], ['wsT32'])
            cp(wsT16.rearrange("p a b -> p (a b)"), wsT32, ['wsT32'], ['wsT16'])
            wctr = [0]

            def load_w(blk):
                slot = wctr[0] % 3
                wctr[0] += 1
                dma(wbuf[slot], wblk16[wb0 + blk], ['w16_%d' % (wb0 + blk)], ['wb%d' % slot])
                return wbuf[slot], 'wb%d' % slot

            def a_norm(ti):
                (t0_, n_) = tiles[ti]
                j_ = 1 if t0_ < CTX else 0
                dma(xt[:, :, :n_], xsrc3[:, :, t0_:t0_ + n_], [], ['Axt'])
                norm_mod(xt, n_, A1[l][:, :, j_], modT[l][:, 0:8, j_], hT2[ti % 2], 'Axt', 'AhT%d' % (ti % 2))
            a_norm(0)
            for ti, (t0, n) in enumerate(tiles):
                isctx = t0 < CTX
                j = 1 if isctx else 0
                ntc = n // 128
                hT = hT2[ti % 2]
                hTn = 'AhT%d' % (ti % 2)
                if ti == len(tiles) // 2:
                    cast_natab(l)
                    if l + 1 < depth:
                        cast_weights(l + 1)

                def fm_group(w3, oc4, tagw):
                    bi = nextbank()
                    p_, pn = PSB(bi)
                    mms([(p_[:, :n], w3[:, kc, oc4 * 128:(oc4 + 1) * 128], hT[:, kc, :n], kc == 0, kc == 7) for kc in range(8)],
                        [tagw, hTn], [pn])
                    return p_, pn

                def tm_group(w3, tc, ncols, tagw):
                    bi = nextbank()
                    p_, pn = PSB(bi)
                    mms([(p_[:, :ncols], hT[:, kc, tc * 128:(tc + 1) * 128], w3[:, kc, :ncols], kc == 0, kc == 7) for kc in range(8)],
                        [tagw, hTn], [pn])
                    return p_, pn

                for blk in range(2):
                    w, wn = load_w(blk)
                    w3 = w.rearrange("p (k c) -> p k c", c=512)
                    for tc in range(ntc):
                        p_, pn = tm_group(w3, tc, 512, wn)
                        act(zst[:, tc, blk * 512:(blk + 1) * 512], p_, AF.Silu, [pn], ['zst'])
                dma(zs_d[t0:t0 + n, :].rearrange("(c p) f -> p c f", p=128), zst[:, :ntc, :], ['zst'], [], q='act')
                for blk in range(2, 6):
                    w, wn = load_w(blk)
                    w3 = w.rearrange("p (k c) -> p k c", c=512)
                    for oc4 in range(4):
                        p_, pn = fm_group(w3, oc4, wn)
                        ch = (blk - 2) * 4 + oc4
                        if ch % 2 == 0:
                            cp(xbcst[:, ch, :n], p_[:, :n], [pn], ['xbcst'], eng='act')
                        else:
                            cp(xbcst[:, ch, :n], p_[:, :n], [pn], ['xbcst'])
                c0 = col_of(t0)
                dma(xbcT_d.rearrange("(c p) t -> p c t", p=128)[:, :, c0:c0 + n], xbcst[:, :, :n], ['xbcst'], [], q='act')
                if ti + 1 < len(tiles):
                    a_norm(ti + 1)
                gi = 0
                for blk in range(6, 8):
                    w, wn = load_w(blk)
                    w3 = w.rearrange("p (k c) -> p k c", c=512)
                    for oc4 in range(4):
                        par = gi % 2
                        gi += 1
                        sfx = '%d' % par
                        p_, pn = fm_group(w3, oc4, wn)
                        act(sq16[par][:, :n], p_[:, :n], AF.Square, [pn], ['sq16' + sfx])
                        cp(qraw[par][:, :n], p_[:, :n], [pn], ['qraw' + sfx], eng='act')
                        b2 = nextbank()
                        p2, p2n = PSB(b2)
                        mms([(p2[:, :n], bo16, sq16[par][:, :n], True, True)], ['sq16' + sfx, 'c16'], [p2n])
                        act(rs2[par][:, :n], p2[:, :n], AF.Sqrt, [p2n, 'epsb'], ['rs2' + sfx], bias=epsb[:, 0:1], scale=1.0 / 64)
                        recip(rs2[par][:, :n], rs2[par][:, :n], ['rs2' + sfx], ['rs2' + sfx])
                        stt(qtmp[par][:, :n], qraw[par][:, :n], 0.125 if blk == 6 else 1.0, rs2[par][:, :n], ALU.mult, ALU.mult,
                            ['qraw' + sfx, 'rs2' + sfx], ['qtmp' + sfx])
                        wcol = 192 if blk == 6 else 193
                        act(qkst[:, (blk - 6) * 4 + oc4, :n], qtmp[par][:, :n], AF.Copy, ['qtmp' + sfx, 'pv%d' % l], ['qkst'],
                            scale=pvl[:, wcol:wcol + 1])
                dma(qT_d.rearrange("(c p) t -> p c t", p=128)[:, :, t0:t0 + n], qkst[:, 0:4, :n], ['qkst'], [], q='act')
                dma(kT_d.rearrange("(c p) t -> p c t", p=128)[:, :, t0:t0 + n], qkst[:, 4:8, :n], ['qkst'], [], q='act')
                w, wn = load_w(8)
                w3 = w.rearrange("p (k c) -> p k c", c=512)
                for tc in range(ntc):
                    p_, pn = tm_group(w3, tc, 512, wn)
                    cp(vst[:, tc, :], p_, [pn], ['vst'], eng='act')
                dma(vv_d[t0:t0 + n, :].rearrange("(c p) f -> p c f", p=128), vst[:, :ntc, :], ['vst'], [], q='act')
                w, wn = load_w(9)
                w3 = w.rearrange("p (k c) -> p k c", c=512)
                for tc in range(ntc):
                    p_, pn = tm_group(w3, tc, 512, wn)
                    act(u16[:, tc, :], p_, AF.Gelu_apprx_tanh, [pn], ['u16'])
                w, wn = load_w(10)
                w3 = w.rearrange("p (k c) -> p k c", c=512)
                for tc in range(ntc):
                    p_, pn = tm_group(w3, tc, 512, wn)
                    act(vg, p_, AF.Gelu_apprx_tanh, [pn], ['vg'])
                    act(vjunk, vg, AF.Square, ['vg'], ['vjunk', 'ssv%d' % tc], accum_out=ssv4[:, tc, 0:1])
                    act(ssv4[:, tc, 1:2], ssv4[:, tc, 0:1], AF.Sqrt, ['ssv%d' % tc, 'epsb'], ['ssv%d' % tc], bias=epsb[:, 0:1], scale=1.0 / 512)
                    recip(ssv4[:, tc, 2:3], ssv4[:, tc, 1:2], ['ssv%d' % tc], ['ssv%d' % tc])
                    stt(vn16[tc], vg, ssv4[:, tc, 2:3], brl[:, 80:592], ALU.mult, ALU.mult, ['vg', 'ssv%d' % tc, 'br%d' % l], ['vn16_%d' % tc])

                def gate_blocks(blks):
                    for blk in blks:
                        w, wn = load_w(blk)
                        w3 = w.rearrange("p (k c) -> p k c", c=512)
                        for oc4 in range(4):
                            p_, pn = fm_group(w3, oc4, wn)
                            gc = (blk - 11) * 4 + oc4
                            gsl = gst2[blk % 2]
                            act(gsl[:, oc4, :n], p_[:, :n], AF.Sigmoid, [pn, 'pv%d' % l], ['gst%d' % (blk % 2)], bias=pvl[:, 64 + gc:65 + gc])
                        r0 = (blk - 11) * 512
                        dma(gateT_d[r0:r0 + 512, :].rearrange("(c p) t -> p c t", p=128)[:, :, t0:t0 + n], gst2[blk % 2][:, :, :n],
                            ['gst%d' % (blk % 2)], [], q='act')
                gate_blocks([11, 12, 13])
                for tc in range(ntc):
                    bm = nextbank()
                    pm_, pmn = PSB(bm)
                    mms([(pm_[:, g * 64:(g + 1) * 64], wsT16[:, g, :], vn16[tc][:, g * 64:(g + 1) * 64], True, True) for g in range(8)],
                        ['wsT16', 'vn16_%d' % tc], [pmn])
                    tt(gtmp.rearrange("p (g d) -> p g d", d=64), pm_.rearrange("p (g d) -> p g d", d=64),
                       pvl[:, 194:202].unsqueeze(2).to_broadcast([128, 8, 64]), ALU.add, [pmn, 'pv%d' % l], ['gtmp'])
                    tt(ygm16[tc], gtmp, u16[:, tc, :], ALU.mult, ['gtmp', 'u16'], ['ygm16_%d' % tc])
                gate_blocks([14, 15])
                for tc in range(ntc):
                    bt_ = nextbank()
                    pt_, ptn = PSB(bt_)
                    pt16 = pt_.bitcast(BF16)
                    trs([(pt16[:, c * 128:(c + 1) * 128], ygm16[tc][:, c * 128:(c + 1) * 128], ident16) for c in range(4)],
                        ['ygm16_%d' % tc, 'c16'], [ptn])
                    cp(ygst[:, :, tc * 128:(tc + 1) * 128], pt16[:, 0:512].rearrange("p (c t) -> p c t", t=128), [ptn], ['ygst'], eng='act')
                dma(ygmT_d.rearrange("(c p) t -> p c t", p=128)[:, :, t0:t0 + n], ygst[:, :, :n], ['ygst'], [], q='act')
                gate_blocks([16])
                w, wn = load_w(17)
                w3 = w[:, 0:256].rearrange("p (k c) -> p k c", c=32)
                for tc in range(ntc):
                    p_, pn = tm_group(w3, tc, 32, wn)
                    tt(dtmp, p_[:, 0:32], brl[:, 0:32], ALU.add, [pn, 'br%d' % l], ['dtmp'])
                    act(dtmp, dtmp, AF.Exp, ['dtmp'], ['dtmp'])
                    act(dst_[:, tc, :], dtmp, AF.Ln, ['dtmp'], ['dst'], bias=1.0)
                dma(dts_d[t0:t0 + n, :].rearrange("(c p) f -> p c f", p=128), dst_[:, :ntc, :], ['dst'], [], q='act')
            S.barrier()
            if 'stopA' in dbg and l == 0:
                break
            AR.off = PERSIST
            cdiag = AR.alloc([80, 128], BF16)
            for j in range(80):
                ts(cdiag[:, j, :], ident32, pvl[:, 104 + j:105 + j], None, ALU.mult, None, ['c32', 'pv%d' % l], ['cdiag'])
            zt = AR.alloc([16, 4], BF16)
            S.op('dve', lambda e: e.memset(zt.rearrange("p a b -> p (a b)"), 0.0), [], ['zt'])
            xbcT3 = xbcT_d.rearrange("(c p) t -> p c t", p=128)
            dma(xbcT3[:, :, 0:2], zt[:, :, 0:2], ['zt'], ['xbcpad'], q='act')
            dma(xbcT3[:, :, 258:262], zt[:, :, 0:4], ['zt'], ['xbcpad'], q='act')
            dma(xbcT3[:, :, 262 + L:264 + L], zt[:, :, 0:2], ['zt'], ['xbcpad'], q='act')
            xpre = AR.alloc([16, 516], BF16)
            xact = AR.alloc([16, 512], BF16)
            cosT = AR.alloc([512], F32)
            sinT = AR.alloc([512], F32)
            t1 = AR.alloc([512], F32)
            t2 = AR.alloc([512], F32)
            bcrot = AR.alloc([8, 512], BF16)
            xsst = AR.alloc([4, 1024], BF16)
            btst = AR.alloc([4, 512], BF16)
            BT_d3 = BT_d.rearrange("(c p) t -> p c t", p=128)
            CT_d3 = CT_d.rearrange("(c p) t -> p c t", p=128)
            for (t0, n) in tiles:
                ntc = n // 128
                c0 = col_of(t0)
                dma(xpre[:, :, :n + 4], xbcT3[:, :, c0 - 2:c0 + n + 2], ['xbcpad'], ['xpre'])
                dma(cosT[:, :n], ropeC_in[:, t0:t0 + n], [], ['cosT'])
                dma(sinT[:, :n], ropeS_in[:, t0:t0 + n], [], ['sinT'])
                for ch in range(16):
                    bi = nextbank()
                    p_, pn = PSB(bi)
                    mms([(p_[:, :n], cdiag[:, k * 16 + ch, :], xpre[:, ch, k:k + n], k == 0, k == 4) for k in range(5)],
                        ['cdiag', 'xpre'], [pn])
                    act(xact[:, ch, :n], p_[:, :n], AF.Silu, [pn, 'pv%d' % l], ['xact'], bias=pvl[:, 88 + ch:89 + ch])
                for tc in range(ntc):
                    bi = nextbank()
                    p_, pn = PSB(bi)
                    pt16 = p_.bitcast(BF16)
                    trs([(pt16[:, ch * 128:(ch + 1) * 128], xact[:, ch, tc * 128:(tc + 1) * 128], ident16) for ch in range(8)],
                        ['xact', 'c16'], [pn])
                    cp(xsst[:, tc, :], pt16, [pn], ['xsst'], eng='act')
                dma(xs_d[t0:t0 + n, :].rearrange("(c p) f -> p c f", p=128), xsst[:, :ntc, :], ['xsst'], [], q='act')
                for i in range(8):
                    ch = 8 + i
                    bi = nextbank()
                    p_, pn = PSB(bi)
                    mms([(p_[:, :n], pm16, xact[:, ch, :n], True, True)], ['c16', 'xact'], [pn])
                    tt(t1[:, :n], xact[:, ch, :n], cosT[:, :n], ALU.mult, ['xact', 'cosT'], ['t1'])
                    tt(t2[:, :n], p_[:, :n], sinT[:, :n], ALU.mult, [pn, 'sinT'], ['t2'])
                    tt(bcrot[:, i, :n], t1[:, :n], t2[:, :n], ALU.add, ['t1', 't2'], ['bcrot'])
                dma(BT_d3[:, :, t0:t0 + n], bcrot[:, 0:4, :n], ['bcrot'], [], q='act')
                dma(CT_d3[:, :, t0:t0 + n], bcrot[:, 4:8, :n], ['bcrot'], [], q='act')
                for tc in range(ntc):
                    bi = nextbank()
                    p_, pn = PSB(bi)
                    pt16 = p_.bitcast(BF16)
                    trs([(pt16[:, g * 128:(g + 1) * 128], bcrot[:, g, tc * 128:(tc + 1) * 128], ident16) for g in range(4)],
                        ['bcrot', 'c16'], [pn])
                    cp(btst[:, tc, :], pt16[:, 0:512], [pn], ['btst'], eng='act')
                dma(Bt_d[t0:t0 + n, :].rearrange("(c p) f -> p c f", p=128), btst[:, :ntc, :], ['btst'], [], q='act')
            S.barrier()
            AR.off = PERSIST
            h32 = [AR.alloc([1024], F32) for _ in range(2)]
            h16 = [AR.alloc([1024], BF16) for _ in range(2)]
            Abc = AR.alloc([32], F32)
            for d in range(2):
                S.op('dve', lambda e, d=d: e.memset(h32[d], 0.0), [], ['h32_%d' % d])
                S.op('dve', lambda e, d=d: e.memset(h16[d], 0.0), [], ['h16_%d' % d])
            act(Abc, brl[:, 32:64], AF.Exp, ['br%d' % l], ['Abc'])
            ts(Abc, Abc, -1.0, None, ALU.mult, None, ['Abc'], ['Abc'])

            class _B:
                pass
            p1bufs = [[None, None], [None, None]]
            p2bufs = [None, None]
            for d in range(2):
                for par in range(2):
                    B_ = _B()
                    B_.xs = AR.alloc([1024], BF16)
                    B_.BT = AR.alloc([4, 128], BF16)
                    B_.CT = AR.alloc([4, 128], BF16)
                    B_.Bt = AR.alloc([512], BF16)
                    B_.dt = AR.alloc([32], F32)
                    B_.dtA = AR.alloc([16], F32)
                    B_.xdt = AR.alloc([1024], BF16)
                    if d == 0:
                        B_.xsD = AR.alloc([1024], BF16)
                    B_.nega = AR.alloc([16], F32)
                    B_.expa = AR.alloc([16], F32)
                    B_.acT = AR.alloc([128], F32)
                    B_.dec = AR.alloc([16, 128], BF16)
                    B_.cdB = AR.alloc([16], F32)
                    B_.cbm = AR.alloc([4, 128], BF16)
                    B_.MT = AR.alloc([16, 128], BF16)
                    B_.xdd = AR.alloc([1024], BF16)
                    p1bufs[d][par] = B_
                C_ = _B()
                C_.yi = AR.alloc([1024], F32)
                C_.ydir = AR.alloc([1024], F32)
                C_.yp = AR.alloc([1024], F32)
                C_.zs = AR.alloc([1024], BF16)
                C_.gg = AR.alloc([1024], F32)
                C_.gjunk = AR.alloc([1024], BF16)
                C_.ss = AR.alloc([4], F32)
                C_.gn = AR.alloc([1024], BF16)
                C_.yst = AR.alloc([8, 128], BF16)
                p2bufs[d] = C_
            yssdT3 = yssdT_d.rearrange("(c p) t -> p c t", p=128)

            def r3(v):
                return v.rearrange("p (h d) -> p h d", d=64)

            def scan_p1(c, d, want_y, par):
                sf = '_%d_%d' % (d, par)
                U32 = Uf32 if d == 0 else Ub32
                lend = 127 if d == 0 else 0
                tk = c * 128
                B_ = p1bufs[d][par]
                dma(B_.xs, xs_d[tk:tk + 128, :], [], ['xs' + sf])
                dma(B_.BT, BT_d3[:, :, tk:tk + 128], [], ['BT' + sf])
                dma(B_.CT, CT_d3[:, :, tk:tk + 128], [], ['CT' + sf])
                dma(B_.Bt, Bt_d[tk:tk + 128, :], [], ['Bt' + sf])
                dma(B_.dt, dts_d[tk:tk + 128, :], [], ['dt' + sf])
                dsl = B_.dt[:, d * 16:(d + 1) * 16]
                tt(B_.dtA, dsl, Abc[:, d * 16:(d + 1) * 16], ALU.mult, ['dt' + sf, 'Abc'], ['dtA' + sf])
                tt(r3(B_.xdt), r3(B_.xs), dsl.unsqueeze(2).to_broadcast([128, 16, 64]), ALU.mult, ['xs' + sf, 'dt' + sf], ['xdt' + sf], eng='pool')
                P0, P0n = PSB(0)
                mms([(P0[:, 0:16], U32, B_.dtA, True, True)], ['c32', 'dtA' + sf], [P0n])
                mms([(P0[0:16, 64:192], B_.dtA, U32, True, True)], ['c32', 'dtA' + sf], [P0n])
                act(B_.nega, P0[:, 0:16], AF.Copy, [P0n], ['nega' + sf], scale=-1.0)
                act(B_.expa, P0[:, 0:16], AF.Exp, [P0n], ['expa' + sf])
                act(B_.acT[0:16, :], P0[0:16, 64:192], AF.Copy, [P0n], ['acT' + sf])
                for q4 in range(4):
                    pb, pbn = PSB(1 if q4 % 2 == 0 else 0)
                    lst_ = []
                    for hh in range(4):
                        lst_.append((pb[:, hh * 128:(hh + 1) * 128], sel32[:, q4 * 4 + hh, :], B_.acT[0:16, :], True, False))
                        lst_.append((pb[:, hh * 128:(hh + 1) * 128], ident16, mneg16[d], False, True))
                    mms(lst_, ['c32', 'c16', 'mneg16', 'acT' + sf], [pbn])
                    for hh in range(4):
                        h = q4 * 4 + hh
                        act(B_.dec[:, h, :], pb[:, hh * 128:(hh + 1) * 128], AF.Exp, [pbn, 'nega' + sf], ['dec' + sf], bias=B_.nega[:, h:h + 1])
                    act(B_.cdB[:, q4 * 4:q4 * 4 + 4], pb.rearrange("p (h l) -> p h l", l=128)[:, :, lend], AF.Exp, [pbn], ['cdB' + sf])
                if want_y:
                    P3, P3n = PSB(3)
                    mms([(P3[:, g * 128:(g + 1) * 128], B_.BT[:, g, :], B_.CT[:, g, :], True, True) for g in range(4)],
                        ['BT' + sf, 'CT' + sf], [P3n])
                    P33 = P3.rearrange("p (g l) -> p g l", l=128)
                    for g in range(4):
                        tt(B_.MT[:, 4 * g:4 * g + 4, :], P33[:, g:g + 1, :].to_broadcast([128, 4, 128]), B_.dec[:, 4 * g:4 * g + 4, :], ALU.mult,
                           [P3n, 'dec' + sf], ['MT' + sf])
                    if d == 0:
                        tt(r3(B_.xsD), r3(B_.xs), brl[:, 64:80].unsqueeze(2).to_broadcast([128, 16, 64]), ALU.mult,
                           ['xs' + sf, 'br%d' % l], ['xsD' + sf], eng='pool')
                tt(r3(B_.xdd), r3(B_.xdt), B_.dec[:, :, lend:lend + 1].to_broadcast([128, 16, 64]), ALU.mult,
                   ['xdt' + sf, 'dec' + sf], ['xdd' + sf], eng='pool')

            def scan_p2(c, d, second, want_y, par):
                sf = '_%d_%d' % (d, par)
                sg = '_%d' % d
                tk = c * 128
                B_ = p1bufs[d][par]
                C_ = p2bufs[d]
                if want_y:
                    for half in range(2):
                        py, pyn = PSB(4 + half)
                        lst = []
                        if d == 0:
                            lst.append((py, ident16, B_.xsD[:, half * 512:(half + 1) * 512], True, False))
                        for hh in range(8):
                            h = half * 8 + hh
                            lst.append((py[:, hh * 64:(hh + 1) * 64], B_.MT[:, h, :], B_.xdt[:, h * 64:(h + 1) * 64],
                                        d != 0, (d != 0) or hh == 7))
                        mms(lst, ['c16', 'xsD' + sf, 'MT' + sf, 'xdt' + sf], [pyn])
                for half in range(2):
                    pS, pSn = PSB(6 + half)
                    mms([(pS[:, gg * 256:(gg + 1) * 256], B_.Bt[:, (half * 2 + gg) * 128:(half * 2 + gg + 1) * 128],
                          B_.xdd[:, (half * 2 + gg) * 256:(half * 2 + gg + 1) * 256], True, True) for gg in range(2)],
                        ['Bt' + sf, 'xdd' + sf], [pSn])
                if want_y:
                    for half in range(2):
                        pyi, pyin = PSB(2)
                        mms([(pyi[:, gg * 256:(gg + 1) * 256], B_.CT[:, half * 2 + gg, :],
                              h16[d][:, (half * 2 + gg) * 256:(half * 2 + gg + 1) * 256], True, True) for gg in range(2)],
                            ['CT' + sf, 'h16' + sg], [pyin])
                        cp(C_.yi[:, half * 512:(half + 1) * 512], pyi, [pyin], ['yi' + sg], eng='act')
                    tt(r3(C_.yi), r3(C_.yi), B_.expa.unsqueeze(2).to_broadcast([128, 16, 64]), ALU.mult, ['yi' + sg, 'expa' + sf], ['yi' + sg])
                    for half in range(2):
                        py, pyn = PSB(4 + half)
                        tt(C_.ydir[:, half * 512:(half + 1) * 512], C_.yi[:, half * 512:(half + 1) * 512], py, ALU.add,
                           ['yi' + sg, pyn], ['ydir' + sg])
                tt(r3(h32[d]), r3(h32[d]), B_.cdB.unsqueeze(2).to_broadcast([128, 16, 64]), ALU.mult, ['h32' + sg, 'cdB' + sf], ['h32' + sg], eng='pool')
                for half in range(2):
                    pS, pSn = PSB(6 + half)
                    tt(h32[d][:, half * 512:(half + 1) * 512], h32[d][:, half * 512:(half + 1) * 512], pS, ALU.add,
                       ['h32' + sg, pSn], ['h32' + sg])
                cp(h16[d], h32[d], ['h32' + sg], ['h16' + sg], eng='act')
                if not want_y:
                    return
                if not second:
                    dma(yp_d[tk:tk + 128, :], C_.ydir, ['ydir' + sg], ['ypd%d' % c], q='act')
                    return
                dma(C_.yp, yp_d[tk:tk + 128, :], ['ypd%d' % c], ['yp' + sg])
                dma(C_.zs, zs_d[tk:tk + 128, :], [], ['zs' + sg])
                tt(C_.ydir, C_.ydir, C_.yp, ALU.add, ['ydir' + sg, 'yp' + sg], ['ydir' + sg])
                tt(C_.gg, C_.ydir, C_.zs, ALU.mult, ['ydir' + sg, 'zs' + sg], ['gg' + sg])
                act(C_.gjunk, C_.gg, AF.Square, ['gg' + sg], ['gjunk' + sg, 'ss' + sg], accum_out=C_.ss[:, 0:1])
                act(C_.ss[:, 1:2], C_.ss[:, 0:1], AF.Sqrt, ['ss' + sg, 'epsb'], ['ss' + sg], bias=epsb[:, 0:1], scale=1.0 / 1024)
                recip(C_.ss[:, 2:3], C_.ss[:, 1:2], ['ss' + sg], ['ss' + sg])
                act(C_.gn, C_.gg, AF.Copy, ['gg' + sg, 'ss' + sg], ['gn' + sg], scale=C_.ss[:, 2:3])
                P2, P2n = PSB(2)
                pt16 = P2.bitcast(BF16)
                trs([(pt16[:, ch * 128:(ch + 1) * 128], C_.gn[:, ch * 128:(ch + 1) * 128], ident16) for ch in range(8)],
                    ['gn' + sg, 'c16'], [P2n])
                for ch in range(8):
                    act(C_.yst[:, ch, :], pt16[:, ch * 128:(ch + 1) * 128], AF.Copy, [P2n, 'pv%d' % l], ['yst' + sg],
                        scale=pvl[:, 184 + ch:185 + ch])
                dma(yssdT3[:, :, tk:tk + 128], C_.yst, ['yst' + sg], [], q='act')

            chain_f = list(range(NCH))
            chain_b = [1, 0] + list(range(NCH - 1, 1, -1))
            seen = set()

            def wy(c):
                return not (last and c < 2)

            def do_p2(c, d, par):
                scan_p2(c, d, c in seen, wy(c), par)
                seen.add(c)
            scan_p1(chain_f[0], 0, wy(chain_f[0]), 0)
            scan_p1(chain_b[0], 1, wy(chain_b[0]), 0)
            for i in range(NCH):
                la = []
                if i + 1 < NCH:
                    S.record()
                    scan_p1(chain_f[i + 1], 0, wy(chain_f[i + 1]), (i + 1) % 2)
                    la = S.stop()
                S.record()
                do_p2(chain_b[i], 1, i % 2)
                lb = S.stop()
                S.replay([la, lb])
                la = []
                if i + 1 < NCH:
                    S.record()
                    scan_p1(chain_b[i + 1], 1, wy(chain_b[i + 1]), (i + 1) % 2)
                    la = S.stop()
                S.record()
                do_p2(chain_f[i], 0, i % 2)
                lb = S.stop()
                S.replay([la, lb])
            S.barrier()
            if 'stopB' in dbg and l == 0:
                break
            AR.off = PERSIST
            cls, rel, rep = na_classes(L)
            NQ = L // 128
            tab = AR.alloc([40, 128], BF16)
            tabE = AR.alloc([40, 128], BF16)
            kctx = AR.alloc([2, 4, 128], BF16)
            vctx = [AR.alloc([8, 65], BF16) for _ in range(2)]
            NRING = 8
            kt_buf = [AR.alloc([4, 128], BF16) for _ in range(NRING)]
            v_buf = [AR.alloc([8, 65], BF16) for _ in range(NRING)]
            ring_has = {}
            qt_buf = [AR.alloc([4, 128], BF16) for _ in range(2)]
            pT = [AR.alloc([7 * 128], BF16) for _ in range(2)]
            rec = AR.alloc([8], F32)
            yna = AR.alloc([512], BF16)
            ynast = AR.alloc([4, 128], BF16)
            kT_d3 = kT_d.rearrange("(c p) t -> p c t", p=128)
            qT_d3 = qT_d.rearrange("(c p) t -> p c t", p=128)
            ynaT3 = ynaT_d.rearrange("(c p) t -> p c t", p=128)
            for i in range(2):
                S.op('dve', lambda e, i=i: e.memset(vctx[i].rearrange("p a b -> p (a b)"), 1.0), [], ['vctx%d' % i])
                dma(kctx[:, i, :, :], kT_d3[:, :, i * 128:(i + 1) * 128], [], ['kctx'])
                dma(vctx[i][:, :, 0:64], vv_d[i * 128:(i + 1) * 128, :].rearrange("t (h d) -> t h d", d=64), [], ['vctx%d' % i])
            for i in range(NRING):
                S.op('dve', lambda e, i=i: e.memset(v_buf[i].rearrange("p a b -> p (a b)"), 1.0), [], ['vbuf%d' % i])

            def ring_load(kt):
                sl = kt % NRING
                if ring_has.get(sl) == kt:
                    return
                ring_has[sl] = kt
                ktok = CTX + kt * 128
                dma(kt_buf[sl], kT_d3[:, :, ktok:ktok + 128], [], ['ktb%d' % sl])
                dma(v_buf[sl][:, :, 0:64], vv_d[ktok:ktok + 128, :].rearrange("t (h d) -> t h d", d=64), [], ['vbuf%d' % sl])

            def na_tile(qtok, kts, use_tab, nxt=()):
                qb = qt_buf[(qtok // 128) % 2]
                qbn = 'qt_buf%d' % ((qtok // 128) % 2)
                dma(qb, qT_d3[:, :, qtok:qtok + 128], [], [qbn])
                nk = len(kts)
                for kt in list(kts) + list(nxt):
                    ring_load(kt)
                slots = [kt % NRING for kt in kts]
                nb = nk + 2
                for h in range(8):
                    cc, e = h // 2, h % 2
                    pA, pAn = PSB(2 * e)
                    pB, pBn = PSB(2 * e + 1)
                    lstA, lstB = [], []
                    rdA, rdB = set([qbn, 'kctx']), set([qbn, 'kctx'])
                    for j in range(nb):
                        bank, lst, rd = (pA, lstA, rdA) if j < 4 else (pB, lstB, rdB)
                        col = (j % 4) * 128
                        qop = qb[e * 64:(e + 1) * 64, cc, :]
                        if j < nk:
                            rd.add('ktb%d' % slots[j])
                            lst.append((bank[:, col:col + 128], kt_buf[slots[j]][e * 64:(e + 1) * 64, cc, :], qop, True, True))
                        else:
                            lst.append((bank[:, col:col + 128], kctx[e * 64:(e + 1) * 64, j - nk, cc, :], qop, True, True))
                    mms(lstA, sorted(rdA), [pAn])
                    if lstB:
                        mms(lstB, sorted(rdB), [pBn])
                    na_ = min(nb, 4)
                    act(pT[e][:, 0:na_ * 128], pA[:, 0:na_ * 128], AF.Exp, [pAn], ['pT%d' % e])
                    if nb > 4:
                        act(pT[e][:, 512:512 + (nb - 4) * 128], pB[:, 0:(nb - 4) * 128], AF.Exp, [pBn], ['pT%d' % e])
                    if use_tab:
                        tE4 = tabE.rearrange("p (s h) q -> p s h q", h=8)
                        na4 = min(nk, 4)
                        tt(pT[e][:, 0:na4 * 128].rearrange("p (s q) -> p s q", q=128), pT[e][:, 0:na4 * 128].rearrange("p (s q) -> p s q", q=128),
                           tE4[:, 0:na4, h, :], ALU.mult, ['pT%d' % e, 'tabE'], ['pT%d' % e])
                        if nk > 4:
                            tt(pT[e][:, 512:512 + (nk - 4) * 128].rearrange("p (s q) -> p s q", q=128),
                               pT[e][:, 512:512 + (nk - 4) * 128].rearrange("p (s q) -> p s q", q=128),
                               tE4[:, 4:nk, h, :], ALU.mult, ['pT%d' % e, 'tabE'], ['pT%d' % e])
                    po, pon = PSB(4 + h // 4)
                    col = (h % 4) * 65
                    lst = []
                    rd = ['pT%d' % e, 'vctx0', 'vctx1']
                    for j in range(nb):
                        if j < nk:
                            vb = v_buf[slots[j]]
                            rd.append('vbuf%d' % slots[j])
                        else:
                            vb = vctx[j - nk]
                        lst.append((po[:, col:col + 65], pT[e][:, j * 128:(j + 1) * 128], vb[:, h, :], j == 0, j == nb - 1))
                    mms(lst, rd, [pon])
                for b4 in range(2):
                    po, pon = PSB(4 + b4)
                    po3 = po[:, 0:260].rearrange("p (h e) -> p h e", e=65)
                    recip(rec[:, b4 * 4:(b4 + 1) * 4], po3[:, :, 64], [pon], ['rec'])
                    tt(yna.rearrange("p (h d) -> p h d", d=64)[:, b4 * 4:(b4 + 1) * 4, :], po3[:, :, 0:64],
                       rec[:, b4 * 4:(b4 + 1) * 4].unsqueeze(2).to_broadcast([128, 4, 64]), ALU.mult, [pon, 'rec'], ['yna'])
                p6, p6n = PSB(6)
                pt16 = p6.bitcast(BF16)
                trs([(pt16[:, c * 128:(c + 1) * 128], yna[:, c * 128:(c + 1) * 128], ident16) for c in range(4)], ['yna', 'c16'], [p6n])
                cp(ynast, pt16[:, 0:512].rearrange("p (c t) -> p c t", t=128), [p6n], ['ynast'], eng='act')
                dma(ynaT3[:, :, qtok:qtok + 128], ynast, ['ynast'], [], q='act')

            if not last:
                for i in range(2):
                    na_tile(i * 128, [], False)
            cur = -1
            for qt in range(NQ):
                c = cls[qt]
                if c != cur:
                    cur = c
                    b0 = l * 200 + c * 40
                    dma(tab, natab16[b0:b0 + 40].rearrange("s k q -> k s q"), ['natab16_%d' % l], ['tab'])
                    act(tabE.rearrange("p a b -> p (a b)"), tab.rearrange("p a b -> p (a b)"), AF.Exp, ['tab'], ['tabE'])
                nxt = [qt + 1 + dk for dk in rel[cls[qt + 1]]] if qt + 1 < NQ else []
                na_tile(CTX + qt * 128, [qt + dk for dk in rel[c]], True, nxt)
            S.barrier()
            if 'stopC' in dbg and l == 0:
                break
            AR.off = PERSIST
            ph = {}
            ph['sq'] = AR.alloc([8, 512], BF16)
            ph['rs'] = AR.alloc([512], F32)
            ph['ntmp'] = [AR.alloc([512], F32) for _ in range(2)]
            xt2 = [AR.alloc([8, 512], F32) for _ in range(2)]
            ys = AR.alloc([8, 512], BF16)
            yn_ = AR.alloc([4, 512], BF16)
            yg = AR.alloc([4, 512], BF16)
            gt = AR.alloc([24, 512], BF16)
            mg = AR.alloc([8, 512], BF16)
            h2 = AR.alloc([8, 512], BF16)
            actb = AR.alloc([22, 512], BF16)
            m1 = AR.alloc([512], F32)
            m2 = AR.alloc([512], F32)
            sa = AR.alloc([512], BF16)
            wbuf = [AR.alloc([4096], BF16) for _ in range(3)]
            wctr[0] = 0
            ygmT3 = ygmT_d.rearrange("(c p) t -> p c t", p=128)
            gateT3 = gateT_d.rearrange("(c p) t -> p c t", p=128)
            x1T3 = x1T.rearrange("(c p) t -> p c t", p=128)
            outT3 = outT.rearrange("(c p) t -> p c t", p=128)

            def load_w(blk):
                slot = wctr[0] % 3
                wctr[0] += 1
                dma(wbuf[slot], wblk16[wb0 + blk], ['w16_%d' % (wb0 + blk)], ['wb%d' % slot])
                return wbuf[slot], 'wb%d' % slot

            dtiles = (tiles[1:] if last else tiles)

            def d_loads(ti):
                (t0_, n_) = dtiles[ti]
                dma(xt2[ti % 2][:, :, :n_], xsrc3[:, :, t0_:t0_ + n_], [], ['Dxt%d' % (ti % 2)])
                dma(ys[:, :, :n_], yssdT3[:, :, t0_:t0_ + n_], [], ['ys'])
                dma(yn_[:, :, :n_], ynaT3[:, :, t0_:t0_ + n_], [], ['yn'])
                dma(yg[:, :, :n_], ygmT3[:, :, t0_:t0_ + n_], [], ['yg'])
                dma(gt[:, :, :n_], gateT3[:, :, t0_:t0_ + n_], [], ['gt'])
            d_loads(0)
            for ti, (t0, n) in enumerate(dtiles):
                isctx = t0 < CTX
                j = 1 if isctx else 0
                xt = xt2[ti % 2]
                xtn = 'Dxt%d' % (ti % 2)
                for jb in range(4):
                    w, wn = load_w(18 + jb)
                    for oo in range(2):
                        oc = 2 * jb + oo
                        base = oo * 2048
                        pa, pan = PSB(nextbank())
                        mms([(pa[:, :n], w[:, base + kc * 128:base + (kc + 1) * 128], ys[:, kc, :n], kc == 0, kc == 7) for kc in range(8)],
                            [wn, 'ys'], [pan])
                        pb_, pbn = PSB(nextbank())
                        mms([(pb_[:, :n], w[:, base + 1024 + kc * 128:base + 1024 + (kc + 1) * 128], yn_[:, kc, :n], kc == 0, kc == 3) for kc in range(4)],
                            [wn, 'yn'], [pbn])
                        pc_, pcn = PSB(nextbank())
                        mms([(pc_[:, :n], w[:, base + 1536 + kc * 128:base + 1536 + (kc + 1) * 128], yg[:, kc, :n], kc == 0, kc == 3) for kc in range(4)],
                            [wn, 'yg'], [pcn])
                        tt(m1[:, :n], pa[:, :n], gt[:, oc, :n], ALU.mult, [pan, 'gt'], ['m1'])
                        tt(m2[:, :n], pb_[:, :n], gt[:, 8 + oc, :n], ALU.mult, [pbn, 'gt'], ['m2'])
                        tt(m1[:, :n], m1[:, :n], m2[:, :n], ALU.add, ['m1', 'm2'], ['m1'])
                        tt(m2[:, :n], pc_[:, :n], gt[:, 16 + oc, :n], ALU.mult, [pcn, 'gt'], ['m2'])
                        tt(mg[:, oc, :n], m1[:, :n], m2[:, :n], ALU.add, ['m1', 'm2'], ['mg'])
                if ti + 1 < len(dtiles):
                    d_loads(ti + 1)
                for jb in range(2):
                    w, wn = load_w(22 + jb)
                    w3 = w.rearrange("p (k c) -> p k c", c=512)
                    for oc4 in range(4):
                        oc = jb * 4 + oc4
                        p_, pn = PSB(nextbank())
                        mms([(p_[:, :n], w3[:, kc, oc4 * 128:(oc4 + 1) * 128], mg[:, kc, :n], kc == 0, kc == 7) for kc in range(8)],
                            [wn, 'mg'], [pn])
                        stt(xt[:, oc, :n], p_[:, :n], modT[l][:, 16 + oc, j:j + 1], xt[:, oc, :n], ALU.mult, ALU.add,
                            [pn, 'modT%d' % l, xtn], [xtn])
                norm_mod(xt, n, A2[l][:, :, j], modT[l][:, 24:32, j], h2, xtn, 'DhT')
                for jb in range(11):
                    w, wn = load_w(24 + jb)
                    w3 = w.rearrange("p (k c) -> p k c", c=512)
                    for hcl in range(2):
                        hc = 2 * jb + hcl
                        pa, pan = PSB(nextbank())
                        mms([(pa[:, :n], w3[:, kc, (hcl * 2) * 128:(hcl * 2 + 1) * 128], h2[:, kc, :n], kc == 0, kc == 7) for kc in range(8)],
                            [wn, 'DhT'], [pan])
                        pb_, pbn = PSB(nextbank())
                        mms([(pb_[:, :n], w3[:, kc, (hcl * 2 + 1) * 128:(hcl * 2 + 2) * 128], h2[:, kc, :n], kc == 0, kc == 7) for kc in range(8)],
                            [wn, 'DhT'], [pbn])
                        act(sa[:, :n], pa[:, :n], AF.Silu, [pan], ['sa'])
                        tt(actb[:, hc, :n], sa[:, :n], pb_[:, :n], ALU.mult, ['sa', pbn], ['actb'])
                for oc in range(8):
                    w, wn = load_w(35 + oc)
                    wv = w[:, 0:2816].rearrange("p (k c) -> p k c", c=128)
                    p_, pn = PSB(nextbank())
                    mms([(p_[:, :n], wv[:, kc, :], actb[:, kc, :n], kc == 0, kc == 21) for kc in range(22)], [wn, 'actb'], [pn])
                    stt(xt[:, oc, :n], p_[:, :n], modT[l][:, 40 + oc, j:j + 1], xt[:, oc, :n], ALU.mult, ALU.add,
                        [pn, 'modT%d' % l, xtn], [xtn])
                if last:
                    dma(outT3[:, :, t0 - CTX:t0 - CTX + n], xt[:, :, :n], [xtn], [], q='act')
                else:
                    dma(x1T3[:, :, t0:t0 + n], xt[:, :, :n], [xtn], [], q='act')
            S.barrier()
        S.emit()
    return nc


def _slab(W):
    k = W.shape[0] // 128
    c = W.shape[1]
    out = np.zeros((128, 4096), np.float32)
    out[:, :k * c] = W.reshape(k, 128, c).transpose(1, 0, 2).reshape(128, k * c)
    return out


def pack_weights(inp, l):
    w_in = inp['w_in'][l]
    blks = []
    segs = [(0, 512), (512, 1024),
            (1024, 1536), (1536, 2048), (2048, 2560), (2560, 3072),
            (3104, 3616), (3616, 4128), (4128, 4640),
            (4640, 5152), (5152, 5664)]
    segs += [(5664 + 512 * i, 5664 + 512 * (i + 1)) for i in range(6)]
    for (a, b) in segs:
        blks.append(_slab(w_in[:, a:b]))
    blks.append(_slab(w_in[:, 3072:3104]))
    wa, wb_, wc = inp['w_branch_ssd'][l], inp['w_branch_na'][l], inp['w_branch_gm'][l]
    for j in range(4):
        blk = np.zeros((128, 4096), np.float32)
        for oo in range(2):
            oc = 2 * j + oo
            base = oo * 2048
            blk[:, base:base + 1024] = wa[:, oc * 128:(oc + 1) * 128].reshape(8, 128, 128).transpose(1, 0, 2).reshape(128, 1024)
            blk[:, base + 1024:base + 1536] = wb_[:, oc * 128:(oc + 1) * 128].reshape(4, 128, 128).transpose(1, 0, 2).reshape(128, 512)
            blk[:, base + 1536:base + 2048] = wc[:, oc * 128:(oc + 1) * 128].reshape(4, 128, 128).transpose(1, 0, 2).reshape(128, 512)
        blks.append(blk)
    wo = inp['w_out'][l]
    for j in range(2):
        blks.append(_slab(wo[:, j * 512:(j + 1) * 512]))
    wf = inp['w_ffn_in'][l]
    for j in range(11):
        slab = np.zeros((1024, 512), np.float32)
        for hcl in range(2):
            for ab in range(2):
                src = ab * 2816 + (2 * j + hcl) * 128
                slab[:, (hcl * 2 + ab) * 128:(hcl * 2 + ab + 1) * 128] = wf[:, src:src + 128]
        blks.append(_slab(slab))
    wfo = inp['w_ffn_out'][l]
    for oc in range(8):
        blks.append(_slab(wfo[:, oc * 128:(oc + 1) * 128]))
    assert len(blks) == NBLK
    return np.stack(blks)


def pcol(v):
    return np.ascontiguousarray(v.reshape(-1, 128).T)


def pack_shared(inp, L, depth):
    T = CTX + L
    R = L // 64
    f32 = np.float32
    wmod = np.stack([inp['w_mod'][l].reshape(8, 128, 12, 512).transpose(2, 1, 0, 3).reshape(12, 128, 4096)
                     for l in range(depth)]).reshape(depth * 12, 128, 4096)
    wblk = np.concatenate([pack_weights(inp, l) for l in range(depth)], axis=0)
    pvec = np.zeros((depth, 128, NPV), f32)
    brow = np.zeros((depth, NBR), f32)
    wsT = np.zeros((depth, 128, 1024), f32)
    for l in range(depth):
        pvec[l, :, 0:48] = pcol(inp['b_mod'][l])
        pvec[l, :, 48:56] = pcol(inp['norm1'][l])
        pvec[l, :, 56:64] = pcol(inp['norm2'][l])
        pvec[l, :, 64:88] = pcol(inp['b_gate'][l])
        pvec[l, :, 88:104] = pcol(inp['conv_b'][l])
        for k in range(5):
            pvec[l, :, 104 + k * 16:104 + (k + 1) * 16] = pcol(inp['conv_w'][l][k])
        pvec[l, :, 184:192] = pcol(inp['ssd_norm'][l])
        pvec[l, :, 192] = np.tile(inp['q_norm'][l], 2)
        pvec[l, :, 193] = np.tile(inp['k_norm'][l], 2)
        pvec[l, :, 194:202] = inp['b_spatial'][l].T
        brow[l, 0:32] = inp['dt_bias'][l].reshape(-1)
        brow[l, 32:64] = inp['a_log'][l].reshape(-1)
        brow[l, 64:80] = inp['d_skip'][l]
        brow[l, 80:592] = inp['gm_norm'][l]
        wsT[l] = inp['w_spatial'][l].transpose(2, 0, 1).reshape(128, 1024)
    freqs = (np.float32(10000.0) ** (-np.arange(32, dtype=f32) / np.float32(32))).astype(f32)
    pos = np.arange(L)
    ang_row = (pos // 64).astype(f32)[:, None] * freqs
    ang_col = (pos % 64).astype(f32)[:, None] * freqs
    ropeC = np.ones((128, T), f32)
    ropeS = np.zeros((128, T), f32)
    for half, ang in ((0, ang_row), (1, ang_col)):
        c = np.cos(ang).T.astype(f32)
        s = np.sin(ang).T.astype(f32)
        ropeC[half * 64:half * 64 + 32, CTX:] = c
        ropeC[half * 64 + 32:half * 64 + 64, CTX:] = c
        ropeS[half * 64:half * 64 + 32, CTX:] = s
        ropeS[half * 64 + 32:half * 64 + 64, CTX:] = s
    consts = np.zeros((128, NCONST), f32)
    i = np.arange(128)
    consts[:, 0:128] = np.eye(128, dtype=f32)
    consts[:, 128:256] = (i[:, None] <= i[None, :]).astype(f32)
    consts[:, 256:384] = (i[:, None] >= i[None, :]).astype(f32)
    consts[:, 384:512] = 1.0
    consts[:, 512:640] = ((i[:, None] // 64) == (i[None, :] // 64)).astype(f32)
    pm = np.zeros((128, 128), f32)
    for base in (0, 64):
        for n in range(32):
            pm[base + n + 32, base + n] = -1.0
            pm[base + n, base + n + 32] = 1.0
    consts[:, 640:768] = pm
    sel = np.zeros((16, 16, 128), f32)
    for h in range(16):
        sel[h, h, :] = 1.0
    consts[0:16, 768:768 + 2048] = sel.reshape(16, 2048)
    cls, rel, rep = na_classes(L)

    def rs_(r):
        return min(max(r - 4, 0), R - 8)
    natab = np.full((depth, 5, 5, 8, 128, 128), NEG, f32)
    kc = np.arange(64)
    qc = np.arange(64)
    cs = np.clip(qc - 8, 0, 48)
    colok = (kc[:, None] >= cs[None, :]) & (kc[:, None] < cs[None, :] + 16)
    ci = np.clip(kc[:, None] - qc[None, :] + 15, 0, 30)
    for c, qt in rep.items():
        for slot, dk in enumerate(rel[c]):
            kt = qt + dk
            for a in range(2):
                r = 2 * qt + a
                for b in range(2):
                    krow = 2 * kt + b
                    if not (rs_(r) <= krow < rs_(r) + 8):
                        continue
                    ri = krow - r + 7
                    for l in range(depth):
                        vals = inp['rpb'][l][:, ri, :][:, ci]
                        blk = np.where(colok[None], vals, np.float32(NEG))
                        natab[l, c, slot, :, b * 64:(b + 1) * 64, a * 64:(a + 1) * 64] = blk
    natab = natab.reshape(depth * 200, 128, 128)
    return dict(wmod=wmod, wblk=wblk, pvec=pvec, brow=brow, wsT=wsT, ropeC=ropeC, ropeS=ropeS,
                natab=natab, consts=consts)


def pack_core(inp, b, shared):
    xT = np.ascontiguousarray(np.concatenate([inp['ctx'][b].T, inp['x'][b].T], axis=1))
    cT = np.zeros((128, 8, 2), np.float32)
    cT[:, :, 0] = pcol(inp['c'][b])
    cT[:, :, 1] = pcol(inp['c_ctx'])
    m = dict(shared)
    m['xT'] = xT
    m['cT'] = cT.reshape(128, 16)
    return m


_NC_CACHE = {}


def kernel(**inputs):
    inp = {k: np.asarray(v, dtype=np.float32) for k, v in inputs.items()}
    B, L, _ = inp['x'].shape
    depth = inp['w_in'].shape[0]
    key = (L, depth)
    if key not in _NC_CACHE:
        _NC_CACHE[key] = build(L, depth)
    nc = _NC_CACHE[key]
    shared = pack_shared(inp, L, depth)
    in_maps = [pack_core(inp, b, shared) for b in range(B)]
    res = run_bass_kernel_spmd(nc, in_maps, core_ids=list(range(B)))
    out = np.stack([np.ascontiguousarray(res.results[b]['outT'].T) for b in range(B)])
    return out.astype(np.float32)
```

```python
import math
import numpy as np
import concourse.bass as bass
import concourse.mybir as mybir
from concourse.bass_utils import run_bass_kernel_spmd
from contextlib import ExitStack

F32 = mybir.dt.float32
BF16 = mybir.dt.bfloat16
U8 = mybir.dt.uint8
AF = mybir.ActivationFunctionType
ALU = mybir.AluOpType
AX = mybir.AxisListType

D = 1024
CTX = 256
EPS = 1e-6
NBLK = 43
NPV = 202
NBR = 592
NCONST = 768 + 2048
NEG = -30000.0

ENGS = ['pe', 'act', 'dve', 'pool', 'sp']
N_DSEM = 48
DSEM_RANGE = {'sp': (0, 22), 'act': (22, 32), 'pool': (32, 48)}


class Sched:
    def __init__(self, nc):
        self.nc = nc
        self.ops = {e: [] for e in ENGS}
        self.last_write = {}
        self.readers = {}
        self.known = {e: {} for e in ENGS}
        self.dsem_uses = [0] * N_DSEM
        self.dsem_next = {'sp': 0, 'pool': 0, 'act': 0}
        self.signal = {e: set() for e in ENGS}
        self.rec = None

    def record(self):
        self.rec = []

    def stop(self):
        r = self.rec
        self.rec = None
        return r

    def replay(self, lists):
        idx = [0] * len(lists)
        tot = [max(1, len(x)) for x in lists]
        while True:
            live = [k for k in range(len(lists)) if idx[k] < len(lists[k])]
            if not live:
                break
            k = min(live, key=lambda k: idx[k] / tot[k])
            self.op(*lists[k][idx[k]])
            idx[k] += 1

    def _need(self, eng, tok, waits):
        kind, key, val = tok
        if kind == 'eng' and key == eng and eng in ('pe', 'sp'):
            return
        k = (kind, key)
        if self.known[eng].get(k, 0) >= val:
            return
        self.known[eng][k] = val
        waits[k] = max(waits.get(k, 0), val)
        if kind == 'eng':
            self.signal[key].add(val)

    def op(self, eng, fn, reads=(), writes=(), dma=False):
        if self.rec is not None:
            self.rec.append((eng, fn, tuple(reads), tuple(writes), dma))
            return None
        waits = {}
        for b in reads:
            w = self.last_write.get(b)
            if w is not None:
                self._need(eng, w, waits)
        for b in writes:
            w = self.last_write.get(b)
            if w is not None:
                self._need(eng, w, waits)
            for r in self.readers.get(b, ()):
                self._need(eng, r, waits)
        idx = len(self.ops[eng]) + 1
        if dma:
            lo, hi = DSEM_RANGE[eng]
            j = lo + self.dsem_next[eng]
            self.dsem_next[eng] = (self.dsem_next[eng] + 1) % (hi - lo)
            n = self.dsem_uses[j]
            if n > 0:
                self._need(eng, ('dma', j, 16 * n), waits)
            self.dsem_uses[j] = n + 1
            tok = ('dma', j, 16 * (n + 1))
        else:
            tok = ('eng', eng, idx)
        for b in reads:
            self.readers.setdefault(b, []).append(tok)
        for b in writes:
            self.last_write[b] = tok
            self.readers[b] = []
        self.ops[eng].append((fn, waits, tok))
        return tok

    def barrier(self):
        toks = []
        for e in ENGS:
            for i in range(len(self.ops[e]), 0, -1):
                t = self.ops[e][i - 1][2]
                if t is not None and t[0] == 'eng':
                    toks.append(t)
                    break
        for j in range(N_DSEM):
            if self.dsem_uses[j] > 0:
                toks.append(('dma', j, 16 * self.dsem_uses[j]))
        for e in ENGS:
            waits = {}
            for t in toks:
                self._need(e, t, waits)
            self.ops[e].append((None, waits, None))
        self.last_write.clear()
        self.readers.clear()

    def emit(self):
        nc = self.nc
        with ExitStack() as st:
            esem = {e: st.enter_context(nc.semaphore('es_' + e)) for e in ENGS}
            dsem = [st.enter_context(nc.semaphore('ds_%d' % j)) for j in range(N_DSEM)]
            block = st.enter_context(nc.Block())
            sigmap = {}
            for e in ENGS:
                cnt = 0
                m = {}
                ss = self.signal[e]
                for i in range(1, len(self.ops[e]) + 1):
                    if i in ss:
                        cnt += 1
                        m[i] = cnt
                sigmap[e] = m

            def run(e, engobj):
                for i, (fn, waits, tok) in enumerate(self.ops[e], start=1):
                    for (kind, key), val in waits.items():
                        if kind == 'eng':
                            engobj.wait_ge(esem[key], sigmap[key][val])
                        else:
                            engobj.wait_ge(dsem[key], val)
                    if fn is None:
                        continue
                    ins = fn(engobj)
                    if tok[0] == 'dma':
                        ins.then_inc(dsem[tok[1]], 16)
                    elif i in sigmap[e]:
                        ins.then_inc(esem[e], 1)

            @block.tensor
            def _(eng):
                run('pe', eng)

            @block.scalar
            def _(eng):
                run('act', eng)

            @block.vector
            def _(eng):
                run('dve', eng)

            @block.gpsimd
            def _(eng):
                run('pool', eng)

            @block.sync
            def _(eng):
                run('sp', eng)


class Arena:
    def __init__(self, ap, size):
        self.ap = ap
        self.size = size
        self.off = 0

    def alloc(self, free_shape, dt):
        esz = 4 if dt == F32 else 2
        n = 1
        for s in free_shape:
            n *= s
        nb = (n * esz + 63) // 64 * 64
        assert self.off + nb <= self.size, "SBUF arena overflow %d + %d" % (self.off, nb)
        v = self.ap[:, self.off:self.off + n * esz].bitcast(dt)
        self.off += nb
        if len(free_shape) == 2:
            v = v.rearrange("p (a b) -> p a b", b=free_shape[1])
        elif len(free_shape) == 3:
            v = v.rearrange("p (a b c) -> p a b c", b=free_shape[1], c=free_shape[2])
        return v


def na_classes(L):
    R = L // 64
    NQ = R // 2

    def rs(r):
        return min(max(r - 4, 0), R - 8)
    info = []
    for qt in range(NQ):
        r0, r1 = 2 * qt, 2 * qt + 1
        lo = min(rs(r0), rs(r1)) // 2
        hi = (max(rs(r0), rs(r1)) + 7) // 2
        info.append(tuple(k - qt for k in range(lo, hi + 1)))
    cls = []
    for qt in range(NQ):
        if qt == 0:
            cls.append(0)
        elif qt == 1:
            cls.append(1)
        elif qt == NQ - 2:
            cls.append(3)
        elif qt == NQ - 1:
            cls.append(4)
        else:
            cls.append(2)
    rel = {}
    rep = {}
    for qt in range(NQ):
        c = cls[qt]
        if c in rel:
            assert rel[c] == info[qt], (c, rel[c], info[qt])
        else:
            rel[c] = info[qt]
            rep[c] = qt
    return cls, rel, rep


def build(L, depth=2, dbg=()):
    T = CTX + L
    NCH = T // 128
    TP = T + 8
    tiles = [(0, 256)] + [(CTX + 512 * i, 512) for i in range(L // 512)]
    nc = bass.Bass("TRN2", target_bir_lowering=False)

    def din(name, shape, dt=F32):
        return nc.dram_tensor(name, shape, dt, kind="ExternalInput").ap()

    def dscr(name, shape, dt):
        kind = "ExternalOutput" if name in dbg else "Internal"
        return nc.dram_tensor(name, shape, dt, kind=kind).ap()

    xT_in = din("xT", [D, T])
    cT_in = din("cT", [128, 16])
    wmod_in = din("wmod", [depth * 12, 128, 4096])
    wblk_in = din("wblk", [depth * NBLK, 128, 4096])
    pvec_in = din("pvec", [depth, 128, NPV])
    brow_in = din("brow", [depth, NBR])
    wsT_in = din("wsT", [depth, 128, 1024])
    ropeC_in = din("ropeC", [128, T])
    ropeS_in = din("ropeS", [128, T])
    natab_in = din("natab", [depth * 200, 128, 128])
    consts_in = din("consts", [128, NCONST])
    outT = nc.dram_tensor("outT", [D, L], F32, kind="ExternalOutput").ap()

    wblk16 = dscr("wblk16", [depth * NBLK, 128, 4096], BF16)
    natab16 = dscr("natab16", [depth * 200, 128, 128], BF16)
    x1T = dscr("x1T", [D, T], F32)
    zs_d = dscr("zs", [T, 1024], BF16)
    xbcT_d = dscr("xbcT", [2048, TP], BF16)
    dts_d = dscr("dts", [T, 32], F32)
    qT_d = dscr("qT", [512, T], BF16)
    kT_d = dscr("kT", [512, T], BF16)
    vv_d = dscr("vv", [T, 512], BF16)
    ygmT_d = dscr("ygmT", [512, T], BF16)
    gateT_d = dscr("gateT", [3072, T], BF16)
    xs_d = dscr("xs", [T, 1024], BF16)
    BT_d = dscr("BTs", [512, T], BF16)
    CT_d = dscr("CTs", [512, T], BF16)
    Bt_d = dscr("Btok", [T, 512], BF16)
    yp_d = dscr("ypart", [T, 1024], F32)
    yssdT_d = dscr("yssdT", [1024, T], BF16)
    ynaT_d = dscr("ynaT", [512, T], BF16)

    st = ExitStack()
    with st:
        ARENA = 186 * 1024
        arena_t = st.enter_context(nc.sbuf_tensor("arena", [128, ARENA], U8))
        ps = [st.enter_context(nc.psum_tensor("ps%d" % i, [128, 512], F32)) for i in range(8)]
        S = Sched(nc)
        AR = Arena(arena_t, ARENA)
        pb_ctr = [0]

        def PSB(i):
            return ps[i][:], 'P%d' % i

        def nextbank(lo=0, hi=8):
            i = lo + pb_ctr[0] % (hi - lo)
            pb_ctr[0] += 1
            return i

        def dma(out, in_, reads, writes, q='sp'):
            return S.op(q, lambda e: e.dma_start(out=out, in_=in_), reads, writes, dma=True)

        def act(out, in_, func, reads, writes, bias=None, scale=None, accum_out=None):
            kw = {}
            if bias is not None:
                kw['bias'] = bias
            if scale is not None:
                kw['scale'] = scale
            if accum_out is not None:
                kw['accum_out'] = accum_out
            return S.op('act', lambda e: e.activation(out=out, in_=in_, func=func, **kw), reads, writes)

        def tt(out, in0, in1, op, reads, writes, eng='dve'):
            return S.op(eng, lambda e: e.tensor_tensor(out=out, in0=in0, in1=in1, op=op), reads, writes)

        def ts(out, in0, s1, s2, op0, op1, reads, writes, eng='dve'):
            if s2 is None:
                return S.op(eng, lambda e: e.tensor_scalar(out=out, in0=in0, scalar1=s1, scalar2=None, op0=op0), reads, writes)
            return S.op(eng, lambda e: e.tensor_scalar(out=out, in0=in0, scalar1=s1, scalar2=s2, op0=op0, op1=op1), reads, writes)

        def stt(out, in0, scalar, in1, op0, op1, reads, writes):
            return S.op('dve', lambda e: e.scalar_tensor_tensor(out=out, in0=in0, scalar=scalar, in1=in1, op0=op0, op1=op1), reads, writes)

        def recip(out, in_, reads, writes):
            return S.op('dve', lambda e: e.reciprocal(out=out, in_=in_), reads, writes)

        def cp(out, in_, reads, writes, eng='dve'):
            if eng == 'act':
                return S.op('act', lambda e: e.copy(out=out, in_=in_), reads, writes)
            return S.op(eng, lambda e: e.tensor_copy(out=out, in_=in_), reads, writes)

        def mms(lst, reads, writes):
            def fn(e):
                ins = None
                for (o, l, r, s0, s1) in lst:
                    ins = e.matmul(o, lhsT=l, rhs=r, start=s0, stop=s1)
                return ins
            return S.op('pe', fn, reads, writes)

        def trs(lst, reads, writes):
            def fn(e):
                ins = None
                for (o, i_, idn) in lst:
                    ins = e.transpose(o, i_, idn)
                return ins
            return S.op('pe', fn, reads, writes)

        c32 = AR.alloc([NCONST], F32)
        c16 = AR.alloc([512], BF16)
        epsb = AR.alloc([1], F32)
        ident32 = c32[:, 0:128]
        Uf32 = c32[:, 128:256]
        Ub32 = c32[:, 256:384]
        sel32 = c32[0:16, 768:768 + 2048].rearrange("p (h s) -> p h s", s=128)
        ident16 = c16[:, 0:128]
        ones16 = c16[:, 128:256]
        bo16 = c16[:, 256:384]
        pm16 = c16[:, 384:512]
        pv = [AR.alloc([NPV], F32) for _ in range(depth)]
        br = [AR.alloc([NBR], F32) for _ in range(depth)]
        modT = [AR.alloc([48, 2], F32) for _ in range(depth)]
        A1 = [AR.alloc([8, 2], F32) for _ in range(depth)]
        A2 = [AR.alloc([8, 2], F32) for _ in range(depth)]
        scs = AR.alloc([8, 2], F32)
        mneg16 = [AR.alloc([128], BF16) for _ in range(2)]
        PERSIST = AR.off

        dma(c32, consts_in, [], ['c32'])
        S.op('dve', lambda e: e.memset(epsb, EPS), [], ['epsb'])
        cp(c16[:, 0:128], c32[:, 0:128], ['c32'], ['c16'])
        cp(c16[:, 128:512], c32[:, 384:768], ['c32'], ['c16'])
        for d_ in range(2):
            ts(mneg16[d_], (Uf32 if d_ == 0 else Ub32), -1.0, -NEG, ALU.add, ALU.mult, ['c32'], ['mneg16'])
        for l in range(depth):
            dma(pv[l], pvec_in[l], [], ['pv%d' % l])
            dma(br[l], brow_in[l:l + 1, :].partition_broadcast(128), [], ['br%d' % l])
        wsrc = wblk_in.rearrange("b p (h e) -> b p h e", e=2048)
        wdst = wblk16.rearrange("b p (h e) -> b p h e", e=2048)
        CB = 4

        def cast_weights(l_):
            for b0 in range(l_ * NBLK, (l_ + 1) * NBLK, CB):
                b1 = min((l_ + 1) * NBLK, b0 + CB)
                dma(wdst[b0:b1], wsrc[b0:b1], [], ['w16_%d' % b for b in range(b0, b1)], q='pool')

        def cast_natab(l_):
            for k in range(0, 200, 50):
                dma(natab16[l_ * 200 + k:l_ * 200 + k + 50], natab_in[l_ * 200 + k:l_ * 200 + k + 50], [], ['natab16_%d' % l_], q='pool')
        cast_weights(0)

        sc_raw = AR.alloc([16], F32)
        wm = [AR.alloc([8, 512], F32) for _ in range(2)]
        dma(sc_raw, cT_in, [], ['sc_raw'])
        act(scs.rearrange("p a b -> p (a b)"), sc_raw, AF.Silu, ['sc_raw'], ['scs'])
        for l in range(depth):
            psM, psMn = PSB(0)
            for blk in range(12):
                slot = blk % 2
                dma(wm[slot].rearrange("p a b -> p (a b)"), wmod_in[l * 12 + blk], [], ['wm%d' % slot])
                lst = []
                for oc4 in range(4):
                    oc = blk * 4 + oc4
                    for kc in range(8):
                        lst.append((psM[:, oc * 2:oc * 2 + 2], wm[slot][:, kc, oc4 * 128:(oc4 + 1) * 128], scs[:, kc, :], kc == 0, kc == 7))
                mms(lst, ['wm%d' % slot, 'scs'], [psMn])
            tt(modT[l], psM[:, 0:96].rearrange("p (a b) -> p a b", b=2), pv[l][:, 0:48].unsqueeze(2).to_broadcast([128, 48, 2]),
               ALU.add, [psMn, 'pv%d' % l], ['modT%d' % l])
            for (Ax, sci, nrm) in ((A1[l], 8, 48), (A2[l], 32, 56)):
                ts(Ax, modT[l][:, sci:sci + 8, :], 1.0, None, ALU.add, None, ['modT%d' % l], ['A%d' % l])
                tt(Ax, Ax, pv[l][:, nrm:nrm + 8].unsqueeze(2).to_broadcast([128, 8, 2]), ALU.mult, ['A%d' % l, 'pv%d' % l], ['A%d' % l])
        S.barrier()

        def norm_mod(xt, n, Acol, shcol, hT, xtn, hTn):
            sq = ph['sq']
            act(sq[:, :, :n], xt[:, :, :n], AF.Square, [xtn], ['sq'])
            bi = nextbank()
            pss, pn = PSB(bi)
            mms([(pss[:, :n], ones16, sq[:, c, :n], c == 0, c == 7) for c in range(8)], ['sq', 'c16'], [pn])
            rs = ph['rs']
            act(rs[:, :n], pss[:, :n], AF.Sqrt, [pn, 'epsb'], ['rs'], bias=epsb[:, 0:1], scale=1.0 / D)
            recip(rs[:, :n], rs[:, :n], ['rs'], ['rs'])
            for c in range(8):
                tmp = ph['ntmp'][c % 2]
                tt(tmp[:, :n], xt[:, c, :n], rs[:, :n], ALU.mult, [xtn, 'rs'], ['ntmp%d' % (c % 2)])
                act(hT[:, c, :n], tmp[:, :n], AF.Identity, ['ntmp%d' % (c % 2)], [hTn], bias=shcol[:, c:c + 1], scale=Acol[:, c:c + 1])

        def col_of(t):
            return 2 + t if t < CTX else 6 + t

        for l in range(depth):
            last = (l == depth - 1)
            xsrc = xT_in if l == 0 else x1T
            xsrc3 = xsrc.rearrange("(c p) t -> p c t", p=128)
            wb0 = l * NBLK
            pvl = pv[l]
            brl = br[l]
            AR.off = PERSIST
            ph = {}
            ph['sq'] = AR.alloc([8, 512], BF16)
            ph['rs'] = AR.alloc([512], F32)
            ph['ntmp'] = [AR.alloc([512], F32) for _ in range(2)]
            xt = AR.alloc([8, 512], F32)
            hT2 = [AR.alloc([8, 512], BF16) for _ in range(2)]
            wbuf = [AR.alloc([4096], BF16) for _ in range(3)]
            wsT32 = AR.alloc([1024], F32)
            wsT16 = AR.alloc([8, 128], BF16)
            zst = AR.alloc([4, 1024], BF16)
            xbcst = AR.alloc([16, 512], BF16)
            dst_ = AR.alloc([4, 32], F32)
            dtmp = AR.alloc([32], F32)
            qkst = AR.alloc([8, 512], BF16)
            sq16 = [AR.alloc([512], BF16) for _ in range(2)]
            rs2 = [AR.alloc([512], F32) for _ in range(2)]
            qtmp = [AR.alloc([512], F32) for _ in range(2)]
            qraw = [AR.alloc([512], F32) for _ in range(2)]
            vst = AR.alloc([4, 512], BF16)
            u16 = AR.alloc([4, 512], BF16)
            vg = AR.alloc([512], F32)
            vjunk = AR.alloc([512], BF16)
            ssv4 = AR.alloc([4, 4], F32)
            vn16 = [AR.alloc([512], BF16) for _ in range(4)]
            gtmp = AR.alloc([512], F32)
            ygm16 = [AR.alloc([512], BF16) for _ in range(4)]
            ygst = AR.alloc([4, 512], BF16)
            gst2 = [AR.alloc([4, 512], BF16) for _ in range(2)]
            dma(wsT32, wsT_in[l], [], ['wsT32'])
            cp(wsT16.rearrange("p a b -> p (a b)"), wsT32, ['wsT32'], ['wsT16'])
            wctr = [0]

            def load_w(blk):
                slot = wctr[0] % 3
                wctr[0] += 1
                dma(wbuf[slot], wblk16[wb0 + blk], ['w16_%d' % (wb0 + blk)], ['wb%d' % slot])
                return wbuf[slot], 'wb%d' % slot

            def a_norm(ti):
                (t0_, n_) = tiles[ti]
                j_ = 1 if t0_ < CTX else 0
                dma(xt[:, :, :n_], xsrc3[:, :, t0_:t0_ + n_], [], ['Axt'])
                norm_mod(xt, n_, A1[l][:, :, j_], modT[l][:, 0:8, j_], hT2[ti % 2], 'Axt', 'AhT%d' % (ti % 2))
            a_norm(0)
            for ti, (t0, n) in enumerate(tiles):
                isctx = t0 < CTX
                j = 1 if isctx else 0
                ntc = n // 128
                hT = hT2[ti % 2]
                hTn = 'AhT%d' % (ti % 2)
                if ti == len(tiles) // 2:
                    cast_natab(l)
                    if l + 1 < depth:
                        cast_weights(l + 1)

                def fm_group(w3, oc4, tagw):
                    bi = nextbank()
                    p_, pn = PSB(bi)
                    mms([(p_[:, :n], w3[:, kc, oc4 * 128:(oc4 + 1) * 128], hT[:, kc, :n], kc == 0, kc == 7) for kc in range(8)],
                        [tagw, hTn], [pn])
                    return p_, pn

                def tm_group(w3, tc, ncols, tagw):
                    bi = nextbank()
                    p_, pn = PSB(bi)
                    mms([(p_[:, :ncols], hT[:, kc, tc * 128:(tc + 1) * 128], w3[:, kc, :ncols], kc == 0, kc == 7) for kc in range(8)],
                        [tagw, hTn], [pn])
                    return p_, pn

                for blk in range(2):
                    w, wn = load_w(blk)
                    w3 = w.rearrange("p (k c) -> p k c", c=512)
                    for tc in range(ntc):
                        p_, pn = tm_group(w3, tc, 512, wn)
                        act(zst[:, tc, blk * 512:(blk + 1) * 512], p_, AF.Silu, [pn], ['zst'])
                dma(zs_d[t0:t0 + n, :].rearrange("(c p) f -> p c f", p=128), zst[:, :ntc, :], ['zst'], [], q='act')
                for blk in range(2, 6):
                    w, wn = load_w(blk)
                    w3 = w.rearrange("p (k c) -> p k c", c=512)
                    for oc4 in range(4):
                        p_, pn = fm_group(w3, oc4, wn)
                        ch = (blk - 2) * 4 + oc4
                        if ch % 2 == 0:
                            cp(xbcst[:, ch, :n], p_[:, :n], [pn], ['xbcst'], eng='act')
                        else:
                            cp(xbcst[:, ch, :n], p_[:, :n], [pn], ['xbcst'])
                c0 = col_of(t0)
                dma(xbcT_d.rearrange("(c p) t -> p c t", p=128)[:, :, c0:c0 + n], xbcst[:, :, :n], ['xbcst'], [], q='act')
                if ti + 1 < len(tiles):
                    a_norm(ti + 1)
                gi = 0
                for blk in range(6, 8):
                    w, wn = load_w(blk)
                    w3 = w.rearrange("p (k c) -> p k c", c=512)
                    for oc4 in range(4):
                        par = gi % 2
                        gi += 1
                        sfx = '%d' % par
                        p_, pn = fm_group(w3, oc4, wn)
                        act(sq16[par][:, :n], p_[:, :n], AF.Square, [pn], ['sq16' + sfx])
                        cp(qraw[par][:, :n], p_[:, :n], [pn], ['qraw' + sfx], eng='act')
                        b2 = nextbank()
                        p2, p2n = PSB(b2)
                        mms([(p2[:, :n], bo16, sq16[par][:, :n], True, True)], ['sq16' + sfx, 'c16'], [p2n])
                        act(rs2[par][:, :n], p2[:, :n], AF.Sqrt, [p2n, 'epsb'], ['rs2' + sfx], bias=epsb[:, 0:1], scale=1.0 / 64)
                        recip(rs2[par][:, :n], rs2[par][:, :n], ['rs2' + sfx], ['rs2' + sfx])
                        stt(qtmp[par][:, :n], qraw[par][:, :n], 0.125 if blk == 6 else 1.0, rs2[par][:, :n], ALU.mult, ALU.mult,
                            ['qraw' + sfx, 'rs2' + sfx], ['qtmp' + sfx])
                        wcol = 192 if blk == 6 else 193
                        act(qkst[:, (blk - 6) * 4 + oc4, :n], qtmp[par][:, :n], AF.Copy, ['qtmp' + sfx, 'pv%d' % l], ['qkst'],
                            scale=pvl[:, wcol:wcol + 1])
                dma(qT_d.rearrange("(c p) t -> p c t", p=128)[:, :, t0:t0 + n], qkst[:, 0:4, :n], ['qkst'], [], q='act')
                dma(kT_d.rearrange("(c p) t -> p c t", p=128)[:, :, t0:t0 + n], qkst[:, 4:8, :n], ['qkst'], [], q='act')
                w, wn = load_w(8)
                w3 = w.rearrange("p (k c) -> p k c", c=512)
                for tc in range(ntc):
                    p_, pn = tm_group(w3, tc, 512, wn)
                    cp(vst[:, tc, :], p_, [pn], ['vst'], eng='act')
                dma(vv_d[t0:t0 + n, :].rearrange("(c p) f -> p c f", p=128), vst[:, :ntc, :], ['vst'], [], q='act')
                w, wn = load_w(9)
                w3 = w.rearrange("p (k c) -> p k c", c=512)
                for tc in range(ntc):
                    p_, pn = tm_group(w3, tc, 512, wn)
                    act(u16[:, tc, :], p_, AF.Gelu_apprx_tanh, [pn], ['u16'])
                w, wn = load_w(10)
                w3 = w.rearrange("p (k c) -> p k c", c=512)
                for tc in range(ntc):
                    p_, pn = tm_group(w3, tc, 512, wn)
                    act(vg, p_, AF.Gelu_apprx_tanh, [pn], ['vg'])
                    act(vjunk, vg, AF.Square, ['vg'], ['vjunk', 'ssv%d' % tc], accum_out=ssv4[:, tc, 0:1])
                    act(ssv4[:, tc, 1:2], ssv4[:, tc, 0:1], AF.Sqrt, ['ssv%d' % tc, 'epsb'], ['ssv%d' % tc], bias=epsb[:, 0:1], scale=1.0 / 512)
                    recip(ssv4[:, tc, 2:3], ssv4[:, tc, 1:2], ['ssv%d' % tc], ['ssv%d' % tc])
                    stt(vn16[tc], vg, ssv4[:, tc, 2:3], brl[:, 80:592], ALU.mult, ALU.mult, ['vg', 'ssv%d' % tc, 'br%d' % l], ['vn16_%d' % tc])

                def gate_blocks(blks):
                    for blk in blks:
                        w, wn = load_w(blk)
                        w3 = w.rearrange("p (k c) -> p k c", c=512)
                        for oc4 in range(4):
                            p_, pn = fm_group(w3, oc4, wn)
                            gc = (blk - 11) * 4 + oc4
                            gsl = gst2[blk % 2]
                            act(gsl[:, oc4, :n], p_[:, :n], AF.Sigmoid, [pn, 'pv%d' % l], ['gst%d' % (blk % 2)], bias=pvl[:, 64 + gc:65 + gc])
                        r0 = (blk - 11) * 512
                        dma(gateT_d[r0:r0 + 512, :].rearrange("(c p) t -> p c t", p=128)[:, :, t0:t0 + n], gst2[blk % 2][:, :, :n],
                            ['gst%d' % (blk % 2)], [], q='act')
                gate_blocks([11, 12, 13])
                for tc in range(ntc):
                    bm = nextbank()
                    pm_, pmn = PSB(bm)
                    mms([(pm_[:, g * 64:(g + 1) * 64], wsT16[:, g, :], vn16[tc][:, g * 64:(g + 1) * 64], True, True) for g in range(8)],
                        ['wsT16', 'vn16_%d' % tc], [pmn])
                    tt(gtmp.rearrange("p (g d) -> p g d", d=64), pm_.rearrange("p (g d) -> p g d", d=64),
                       pvl[:, 194:202].unsqueeze(2).to_broadcast([128, 8, 64]), ALU.add, [pmn, 'pv%d' % l], ['gtmp'])
                    tt(ygm16[tc], gtmp, u16[:, tc, :], ALU.mult, ['gtmp', 'u16'], ['ygm16_%d' % tc])
                gate_blocks([14, 15])
                for tc in range(ntc):
                    bt_ = nextbank()
                    pt_, ptn = PSB(bt_)
                    pt16 = pt_.bitcast(BF16)
                    trs([(pt16[:, c * 128:(c + 1) * 128], ygm16[tc][:, c * 128:(c + 1) * 128], ident16) for c in range(4)],
                        ['ygm16_%d' % tc, 'c16'], [ptn])
                    cp(ygst[:, :, tc * 128:(tc + 1) * 128], pt16[:, 0:512].rearrange("p (c t) -> p c t", t=128), [ptn], ['ygst'], eng='act')
                dma(ygmT_d.rearrange("(c p) t -> p c t", p=128)[:, :, t0:t0 + n], ygst[:, :, :n], ['ygst'], [], q='act')
                gate_blocks([16])
                w, wn = load_w(17)
                w3 = w[:, 0:256].rearrange("p (k c) -> p k c", c=32)
                for tc in range(ntc):
                    p_, pn = tm_group(w3, tc, 32, wn)
                    tt(dtmp, p_[:, 0:32], brl[:, 0:32], ALU.add, [pn, 'br%d' % l], ['dtmp'])
                    act(dtmp, dtmp, AF.Exp, ['dtmp'], ['dtmp'])
                    act(dst_[:, tc, :], dtmp, AF.Ln, ['dtmp'], ['dst'], bias=1.0)
                dma(dts_d[t0:t0 + n, :].rearrange("(c p) f -> p c f", p=128), dst_[:, :ntc, :], ['dst'], [], q='act')
            S.barrier()
            if 'stopA' in dbg and l == 0:
                break
            AR.off = PERSIST
            cdiag = AR.alloc([80, 128], BF16)
            for j in range(80):
                ts(cdiag[:, j, :], ident32, pvl[:, 104 + j:105 + j], None, ALU.mult, None, ['c32', 'pv%d' % l], ['cdiag'])
            zt = AR.alloc([16, 4], BF16)
            S.op('dve', lambda e: e.memset(zt.rearrange("p a b -> p (a b)"), 0.0), [], ['zt'])
            xbcT3 = xbcT_d.rearrange("(c p) t -> p c t", p=128)
            dma(xbcT3[:, :, 0:2], zt[:, :, 0:2], ['zt'], ['xbcpad'], q='act')
            dma(xbcT3[:, :, 258:262], zt[:, :, 0:4], ['zt'], ['xbcpad'], q='act')
            dma(xbcT3[:, :, 262 + L:264 + L], zt[:, :, 0:2], ['zt'], ['xbcpad'], q='act')
            xpre = AR.alloc([16, 516], BF16)
            xact = AR.alloc([16, 512], BF16)
            cosT = AR.alloc([512], F32)
            sinT = AR.alloc([512], F32)
            t1 = AR.alloc([512], F32)
            t2 = AR.alloc([512], F32)
            bcrot = AR.alloc([8, 512], BF16)
            xsst = AR.alloc([4, 1024], BF16)
            btst = AR.alloc([4, 512], BF16)
            BT_d3 = BT_d.rearrange("(c p) t -> p c t", p=128)
            CT_d3 = CT_d.rearrange("(c p) t -> p c t", p=128)
            for (t0, n) in tiles:
                ntc = n // 128
                c0 = col_of(t0)
                dma(xpre[:, :, :n + 4], xbcT3[:, :, c0 - 2:c0 + n + 2], ['xbcpad'], ['xpre'])
                dma(cosT[:, :n], ropeC_in[:, t0:t0 + n], [], ['cosT'])
                dma(sinT[:, :n], ropeS_in[:, t0:t0 + n], [], ['sinT'])
                for ch in range(16):
                    bi = nextbank()
                    p_, pn = PSB(bi)
                    mms([(p_[:, :n], cdiag[:, k * 16 + ch, :], xpre[:, ch, k:k + n], k == 0, k == 4) for k in range(5)],
                        ['cdiag', 'xpre'], [pn])
                    act(xact[:, ch, :n], p_[:, :n], AF.Silu, [pn, 'pv%d' % l], ['xact'], bias=pvl[:, 88 + ch:89 + ch])
                for tc in range(ntc):
                    bi = nextbank()
                    p_, pn = PSB(bi)
                    pt16 = p_.bitcast(BF16)
                    trs([(pt16[:, ch * 128:(ch + 1) * 128], xact[:, ch, tc * 128:(tc + 1) * 128], ident16) for ch in range(8)],
                        ['xact', 'c16'], [pn])
                    cp(xsst[:, tc, :], pt16, [pn], ['xsst'], eng='act')
                dma(xs_d[t0:t0 + n, :].rearrange("(c p) f -> p c f", p=128), xsst[:, :ntc, :], ['xsst'], [], q='act')
                for i in range(8):
                    ch = 8 + i
                    bi = nextbank()
                    p_, pn = PSB(bi)
                    mms([(p_[:, :n], pm16, xact[:, ch, :n], True, True)], ['c16', 'xact'], [pn])
                    tt(t1[:, :n], xact[:, ch, :n], cosT[:, :n], ALU.mult, ['xact', 'cosT'], ['t1'])
                    tt(t2[:, :n], p_[:, :n], sinT[:, :n], ALU.mult, [pn, 'sinT'], ['t2'])
                    tt(bcrot[:, i, :n], t1[:, :n], t2[:, :n], ALU.add, ['t1', 't2'], ['bcrot'])
                dma(BT_d3[:, :, t0:t0 + n], bcrot[:, 0:4, :n], ['bcrot'], [], q='act')
                dma(CT_d3[:, :, t0:t0 + n], bcrot[:, 4:8, :n], ['bcrot'], [], q='act')
                for tc in range(ntc):
                    bi = nextbank()
                    p_, pn = PSB(bi)
                    pt16 = p_.bitcast(BF16)
                    trs([(pt16[:, g * 128:(g + 1) * 128], bcrot[:, g, tc * 128:(tc + 1) * 128], ident16) for g in range(4)],
                        ['bcrot', 'c16'], [pn])
                    cp(btst[:, tc, :], pt16[:, 0:512], [pn], ['btst'], eng='act')
                dma(Bt_d[t0:t0 + n, :].rearrange("(c p) f -> p c f", p=128), btst[:, :ntc, :], ['btst'], [], q='act')
            S.barrier()
            AR.off = PERSIST
            h32 = [AR.alloc([1024], F32) for _ in range(2)]
            h16 = [AR.alloc([1024], BF16) for _ in range(2)]
            Abc = AR.alloc([32], F32)
            for d in range(2):
                S.op('dve', lambda e, d=d: e.memset(h32[d], 0.0), [], ['h32_%d' % d])
                S.op('dve', lambda e, d=d: e.memset(h16[d], 0.0), [], ['h16_%d' % d])
            act(Abc, brl[:, 32:64], AF.Exp, ['br%d' % l], ['Abc'])
            ts(Abc, Abc, -1.0, None, ALU.mult, None, ['Abc'], ['Abc'])

            class _B:
                pass
            p1bufs = [[None, None], [None, None]]
            p2bufs = [None, None]
            for d in range(2):
                for par in range(2):
                    B_ = _B()
                    B_.xs = AR.alloc([1024], BF16)
                    B_.BT = AR.alloc([4, 128], BF16)
                    B_.CT = AR.alloc([4, 128], BF16)
                    B_.Bt = AR.alloc([512], BF16)
                    B_.dt = AR.alloc([32], F32)
                    B_.dtA = AR.alloc([16], F32)
                    B_.xdt = AR.alloc([1024], BF16)
                    if d == 0:
                        B_.xsD = AR.alloc([1024], BF16)
                    B_.nega = AR.alloc([16], F32)
                    B_.expa = AR.alloc([16], F32)
                    B_.acT = AR.alloc([128], F32)
                    B_.dec = AR.alloc([16, 128], BF16)
                    B_.cdB = AR.alloc([16], F32)
                    B_.cbm = AR.alloc([4, 128], BF16)
                    B_.MT = AR.alloc([16, 128], BF16)
                    B_.xdd = AR.alloc([1024], BF16)
                    p1bufs[d][par] = B_
                C_ = _B()
                C_.yi = AR.alloc([1024], F32)
                C_.ydir = AR.alloc([1024], F32)
                C_.yp = AR.alloc([1024], F32)
                C_.zs = AR.alloc([1024], BF16)
                C_.gg = AR.alloc([1024], F32)
                C_.gjunk = AR.alloc([1024], BF16)
                C_.ss = AR.alloc([4], F32)
                C_.gn = AR.alloc([1024], BF16)
                C_.yst = AR.alloc([8, 128], BF16)
                p2bufs[d] = C_
            yssdT3 = yssdT_d.rearrange("(c p) t -> p c t", p=128)

            def r3(v):
                return v.rearrange("p (h d) -> p h d", d=64)

            def scan_p1(c, d, want_y, par):
                sf = '_%d_%d' % (d, par)
                U32 = Uf32 if d == 0 else Ub32
                lend = 127 if d == 0 else 0
                tk = c * 128
                B_ = p1bufs[d][par]
                dma(B_.xs, xs_d[tk:tk + 128, :], [], ['xs' + sf])
                dma(B_.BT, BT_d3[:, :, tk:tk + 128], [], ['BT' + sf])
                dma(B_.CT, CT_d3[:, :, tk:tk + 128], [], ['CT' + sf])
                dma(B_.Bt, Bt_d[tk:tk + 128, :], [], ['Bt' + sf])
                dma(B_.dt, dts_d[tk:tk + 128, :], [], ['dt' + sf])
                dsl = B_.dt[:, d * 16:(d + 1) * 16]
                tt(B_.dtA, dsl, Abc[:, d * 16:(d + 1) * 16], ALU.mult, ['dt' + sf, 'Abc'], ['dtA' + sf])
                tt(r3(B_.xdt), r3(B_.xs), dsl.unsqueeze(2).to_broadcast([128, 16, 64]), ALU.mult, ['xs' + sf, 'dt' + sf], ['xdt' + sf], eng='pool')
                P0, P0n = PSB(0)
                mms([(P0[:, 0:16], U32, B_.dtA, True, True)], ['c32', 'dtA' + sf], [P0n])
                mms([(P0[0:16, 64:192], B_.dtA, U32, True, True)], ['c32', 'dtA' + sf], [P0n])
                act(B_.nega, P0[:, 0:16], AF.Copy, [P0n], ['nega' + sf], scale=-1.0)
                act(B_.expa, P0[:, 0:16], AF.Exp, [P0n], ['expa' + sf])
                act(B_.acT[0:16, :], P0[0:16, 64:192], AF.Copy, [P0n], ['acT' + sf])
                for q4 in range(4):
                    pb, pbn = PSB(1 if q4 % 2 == 0 else 0)
                    lst_ = []
                    for hh in range(4):
                        lst_.append((pb[:, hh * 128:(hh + 1) * 128], sel32[:, q4 * 4 + hh, :], B_.acT[0:16, :], True, False))
                        lst_.append((pb[:, hh * 128:(hh + 1) * 128], ident16, mneg16[d], False, True))
                    mms(lst_, ['c32', 'c16', 'mneg16', 'acT' + sf], [pbn])
                    for hh in range(4):
                        h = q4 * 4 + hh
                        act(B_.dec[:, h, :], pb[:, hh * 128:(hh + 1) * 128], AF.Exp, [pbn, 'nega' + sf], ['dec' + sf], bias=B_.nega[:, h:h + 1])
                    act(B_.cdB[:, q4 * 4:q4 * 4 + 4], pb.rearrange("p (h l) -> p h l", l=128)[:, :, lend], AF.Exp, [pbn], ['cdB' + sf])
                if want_y:
                    P3, P3n = PSB(3)
                    mms([(P3[:, g * 128:(g + 1) * 128], B_.BT[:, g, :], B_.CT[:, g, :], True, True) for g in range(4)],
                        ['BT' + sf, 'CT' + sf], [P3n])
                    P33 = P3.rearrange("p (g l) -> p g l", l=128)
                    for g in range(4):
                        tt(B_.MT[:, 4 * g:4 * g + 4, :], P33[:, g:g + 1, :].to_broadcast([128, 4, 128]), B_.dec[:, 4 * g:4 * g + 4, :], ALU.mult,
                           [P3n, 'dec' + sf], ['MT' + sf])
                    if d == 0:
                        tt(r3(B_.xsD), r3(B_.xs), brl[:, 64:80].unsqueeze(2).to_broadcast([128, 16, 64]), ALU.mult,
                           ['xs' + sf, 'br%d' % l], ['xsD' + sf], eng='pool')
                tt(r3(B_.xdd), r3(B_.xdt), B_.dec[:, :, lend:lend + 1].to_broadcast([128, 16, 64]), ALU.mult,
                   ['xdt' + sf, 'dec' + sf], ['xdd' + sf], eng='pool')

            def scan_p2(c, d, second, want_y, par):
                sf = '_%d_%d' % (d, par)
                sg = '_%d' % d
                tk = c * 128
                B_ = p1bufs[d][par]
                C_ = p2bufs[d]
                if want_y:
                    for half in range(2):
                        py, pyn = PSB(4 + half)
                        lst = []
                        if d == 0:
                            lst.append((py, ident16, B_.xsD[:, half * 512:(half + 1) * 512], True, False))
                        for hh in range(8):
                            h = half * 8 + hh
                            lst.append((py[:, hh * 64:(hh + 1) * 64], B_.MT[:, h, :], B_.xdt[:, h * 64:(h + 1) * 64],
                                        d != 0, (d != 0) or hh == 7))
                        mms(lst, ['c16', 'xsD' + sf, 'MT' + sf, 'xdt' + sf], [pyn])
                for half in range(2):
                    pS, pSn = PSB(6 + half)
                    mms([(pS[:, gg * 256:(gg + 1) * 256], B_.Bt[:, (half * 2 + gg) * 128:(half * 2 + gg + 1) * 128],
                          B_.xdd[:, (half * 2 + gg) * 256:(half * 2 + gg + 1) * 256], True, True) for gg in range(2)],
                        ['Bt' + sf, 'xdd' + sf], [pSn])
                if want_y:
                    for half in range(2):
                        pyi, pyin = PSB(2)
                        mms([(pyi[:, gg * 256:(gg + 1) * 256], B_.CT[:, half * 2 + gg, :],
                              h16[d][:, (half * 2 + gg) * 256:(half * 2 + gg + 1) * 256], True, True) for gg in range(2)],
                            ['CT' + sf, 'h16' + sg], [pyin])
                        cp(C_.yi[:, half * 512:(half + 1) * 512], pyi, [pyin], ['yi' + sg], eng='act')
                    tt(r3(C_.yi), r3(C_.yi), B_.expa.unsqueeze(2).to_broadcast([128, 16, 64]), ALU.mult, ['yi' + sg, 'expa' + sf], ['yi' + sg])
                    for half in range(2):
                        py, pyn = PSB(4 + half)
                        tt(C_.ydir[:, half * 512:(half + 1) * 512], C_.yi[:, half * 512:(half + 1) * 512], py, ALU.add,
                           ['yi' + sg, pyn], ['ydir' + sg])
                tt(r3(h32[d]), r3(h32[d]), B_.cdB.unsqueeze(2).to_broadcast([128, 16, 64]), ALU.mult, ['h32' + sg, 'cdB' + sf], ['h32' + sg], eng='pool')
                for half in range(2):
                    pS, pSn = PSB(6 + half)
                    tt(h32[d][:, half * 512:(half + 1) * 512], h32[d][:, half * 512:(half + 1) * 512], pS, ALU.add,
                       ['h32' + sg, pSn], ['h32' + sg])
                cp(h16[d], h32[d], ['h32' + sg], ['h16' + sg], eng='act')
                if not want_y:
                    return
                if not second:
                    dma(yp_d[tk:tk + 128, :], C_.ydir, ['ydir' + sg], ['ypd%d' % c], q='act')
                    return
                dma(C_.yp, yp_d[tk:tk + 128, :], ['ypd%d' % c], ['yp' + sg])
                dma(C_.zs, zs_d[tk:tk + 128, :], [], ['zs' + sg])
                tt(C_.ydir, C_.ydir, C_.yp, ALU.add, ['ydir' + sg, 'yp' + sg], ['ydir' + sg])
                tt(C_.gg, C_.ydir, C_.zs, ALU.mult, ['ydir' + sg, 'zs' + sg], ['gg' + sg])
                act(C_.gjunk, C_.gg, AF.Square, ['gg' + sg], ['gjunk' + sg, 'ss' + sg], accum_out=C_.ss[:, 0:1])
                act(C_.ss[:, 1:2], C_.ss[:, 0:1], AF.Sqrt, ['ss' + sg, 'epsb'], ['ss' + sg], bias=epsb[:, 0:1], scale=1.0 / 1024)
                recip(C_.ss[:, 2:3], C_.ss[:, 1:2], ['ss' + sg], ['ss' + sg])
                act(C_.gn, C_.gg, AF.Copy, ['gg' + sg, 'ss' + sg], ['gn' + sg], scale=C_.ss[:, 2:3])
                P2, P2n = PSB(2)
                pt16 = P2.bitcast(BF16)
                trs([(pt16[:, ch * 128:(ch + 1) * 128], C_.gn[:, ch * 128:(ch + 1) * 128], ident16) for ch in range(8)],
                    ['gn' + sg, 'c16'], [P2n])
                for ch in range(8):
                    act(C_.yst[:, ch, :], pt16[:, ch * 128:(ch + 1) * 128], AF.Copy, [P2n, 'pv%d' % l], ['yst' + sg],
                        scale=pvl[:, 184 + ch:185 + ch])
                dma(yssdT3[:, :, tk:tk + 128], C_.yst, ['yst' + sg], [], q='act')

            chain_f = list(range(NCH))
            chain_b = [1, 0] + list(range(NCH - 1, 1, -1))
            seen = set()

            def wy(c):
                return not (last and c < 2)

            def do_p2(c, d, par):
                scan_p2(c, d, c in seen, wy(c), par)
                seen.add(c)
            scan_p1(chain_f[0], 0, wy(chain_f[0]), 0)
            scan_p1(chain_b[0], 1, wy(chain_b[0]), 0)
            for i in range(NCH):
                la = []
                if i + 1 < NCH:
                    S.record()
                    scan_p1(chain_f[i + 1], 0, wy(chain_f[i + 1]), (i + 1) % 2)
                    la = S.stop()
                S.record()
                do_p2(chain_b[i], 1, i % 2)
                lb = S.stop()
                S.replay([la, lb])
                la = []
                if i + 1 < NCH:
                    S.record()
                    scan_p1(chain_b[i + 1], 1, wy(chain_b[i + 1]), (i + 1) % 2)
                    la = S.stop()
                S.record()
                do_p2(chain_f[i], 0, i % 2)
                lb = S.stop()
                S.replay([la, lb])
            S.barrier()
            if 'stopB' in dbg and l == 0:
                break
            AR.off = PERSIST
            cls, rel, rep = na_classes(L)
            NQ = L // 128
            tab = AR.alloc([40, 128], BF16)
            tabE = AR.alloc([40, 128], BF16)
            kctx = AR.alloc([2, 4, 128], BF16)
            vctx = [AR.alloc([8, 65], BF16) for _ in range(2)]
            NRING = 8
            kt_buf = [AR.alloc([4, 128], BF16) for _ in range(NRING)]
            v_buf = [AR.alloc([8, 65], BF16) for _ in range(NRING)]
            ring_has = {}
            qt_buf = [AR.alloc([4, 128], BF16) for _ in range(2)]
            pT = [AR.alloc([7 * 128], BF16) for _ in range(2)]
            rec = AR.alloc([8], F32)
            yna = AR.alloc([512], BF16)
            ynast = AR.alloc([4, 128], BF16)
            kT_d3 = kT_d.rearrange("(c p) t -> p c t", p=128)
            qT_d3 = qT_d.rearrange("(c p) t -> p c t", p=128)
            ynaT3 = ynaT_d.rearrange("(c p) t -> p c t", p=128)
            for i in range(2):
                S.op('dve', lambda e, i=i: e.memset(vctx[i].rearrange("p a b -> p (a b)"), 1.0), [], ['vctx%d' % i])
                dma(kctx[:, i, :, :], kT_d3[:, :, i * 128:(i + 1) * 128], [], ['kctx'])
                dma(vctx[i][:, :, 0:64], vv_d[i * 128:(i + 1) * 128, :].rearrange("t (h d) -> t h d", d=64), [], ['vctx%d' % i])
            for i in range(NRING):
                S.op('dve', lambda e, i=i: e.memset(v_buf[i].rearrange("p a b -> p (a b)"), 1.0), [], ['vbuf%d' % i])

            def ring_load(kt):
                sl = kt % NRING
                if ring_has.get(sl) == kt:
                    return
                ring_has[sl] = kt
                ktok = CTX + kt * 128
                dma(kt_buf[sl], kT_d3[:, :, ktok:ktok + 128], [], ['ktb%d' % sl])
                dma(v_buf[sl][:, :, 0:64], vv_d[ktok:ktok + 128, :].rearrange("t (h d) -> t h d", d=64), [], ['vbuf%d' % sl])

            def na_tile(qtok, kts, use_tab, nxt=()):
                qb = qt_buf[(qtok // 128) % 2]
                qbn = 'qt_buf%d' % ((qtok // 128) % 2)
                dma(qb, qT_d3[:, :, qtok:qtok + 128], [], [qbn])
                nk = len(kts)
                for kt in list(kts) + list(nxt):
                    ring_load(kt)
                slots = [kt % NRING for kt in kts]
                nb = nk + 2
                def st_qk(h):
                    cc, e = h // 2, h % 2
                    pA, pAn = PSB(2 * e)
                    pB, pBn = PSB(2 * e + 1)
                    lstA, lstB = [], []
                    rdA, rdB = set([qbn, 'kctx']), set([qbn, 'kctx'])
                    for j in range(nb):
                        bank, lst, rd = (pA, lstA, rdA) if j < 4 else (pB, lstB, rdB)
                        col = (j % 4) * 128
                        qop = qb[e * 64:(e + 1) * 64, cc, :]
                        if j < nk:
                            rd.add('ktb%d' % slots[j])
                            lst.append((bank[:, col:col + 128], kt_buf[slots[j]][e * 64:(e + 1) * 64, cc, :], qop, True, True))
                        else:
                            lst.append((bank[:, col:col + 128], kctx[e * 64:(e + 1) * 64, j - nk, cc, :], qop, True, True))
                    mms(lstA, sorted(rdA), [pAn])
                    if lstB:
                        mms(lstB, sorted(rdB), [pBn])

                def st_sm(h):
                    cc, e = h // 2, h % 2
                    pA, pAn = PSB(2 * e)
                    pB, pBn = PSB(2 * e + 1)
                    na_ = min(nb, 4)
                    act(pT[e][:, 0:na_ * 128], pA[:, 0:na_ * 128], AF.Exp, [pAn], ['pT%d' % e])
                    if nb > 4:
                        act(pT[e][:, 512:512 + (nb - 4) * 128], pB[:, 0:(nb - 4) * 128], AF.Exp, [pBn], ['pT%d' % e])
                    if use_tab:
                        tE4 = tabE.rearrange("p (s h) q -> p s h q", h=8)
                        na4 = min(nk, 4)
                        tt(pT[e][:, 0:na4 * 128].rearrange("p (s q) -> p s q", q=128), pT[e][:, 0:na4 * 128].rearrange("p (s q) -> p s q", q=128),
                           tE4[:, 0:na4, h, :], ALU.mult, ['pT%d' % e, 'tabE'], ['pT%d' % e])
                        if nk > 4:
                            tt(pT[e][:, 512:512 + (nk - 4) * 128].rearrange("p (s q) -> p s q", q=128),
                               pT[e][:, 512:512 + (nk - 4) * 128].rearrange("p (s q) -> p s q", q=128),
                               tE4[:, 4:nk, h, :], ALU.mult, ['pT%d' % e, 'tabE'], ['pT%d' % e])

                def st_pv(h):
                    cc, e = h // 2, h % 2
                    po, pon = PSB(4 + h // 4)
                    col = (h % 4) * 65
                    lst = []
                    rd = ['pT%d' % e, 'vctx0', 'vctx1']
                    for j in range(nb):
                        if j < nk:
                            vb = v_buf[slots[j]]
                            rd.append('vbuf%d' % slots[j])
                        else:
                            vb = vctx[j - nk]
                        lst.append((po[:, col:col + 65], pT[e][:, j * 128:(j + 1) * 128], vb[:, h, :], j == 0, j == nb - 1))
                    mms(lst, rd, [pon])

                st_qk(0)
                st_qk(1)
                for h in range(8):
                    st_sm(h)
                    st_pv(h)
                    if h + 2 < 8:
                        st_qk(h + 2)
                for b4 in range(2):
                    po, pon = PSB(4 + b4)
                    po3 = po[:, 0:260].rearrange("p (h e) -> p h e", e=65)
                    recip(rec[:, b4 * 4:(b4 + 1) * 4], po3[:, :, 64], [pon], ['rec'])
                    tt(yna.rearrange("p (h d) -> p h d", d=64)[:, b4 * 4:(b4 + 1) * 4, :], po3[:, :, 0:64],
                       rec[:, b4 * 4:(b4 + 1) * 4].unsqueeze(2).to_broadcast([128, 4, 64]), ALU.mult, [pon, 'rec'], ['yna'])
                p6, p6n = PSB(6)
                pt16 = p6.bitcast(BF16)
                trs([(pt16[:, c * 128:(c + 1) * 128], yna[:, c * 128:(c + 1) * 128], ident16) for c in range(4)], ['yna', 'c16'], [p6n])
                cp(ynast, pt16[:, 0:512].rearrange("p (c t) -> p c t", t=128), [p6n], ['ynast'], eng='act')
                dma(ynaT3[:, :, qtok:qtok + 128], ynast, ['ynast'], [], q='act')

            if not last:
                for i in range(2):
                    na_tile(i * 128, [], False)
            cur = -1
            for qt in range(NQ):
                c = cls[qt]
                if c != cur:
                    cur = c
                    b0 = l * 200 + c * 40
                    dma(tab, natab16[b0:b0 + 40].rearrange("s k q -> k s q"), ['natab16_%d' % l], ['tab'])
                    act(tabE.rearrange("p a b -> p (a b)"), tab.rearrange("p a b -> p (a b)"), AF.Exp, ['tab'], ['tabE'])
                nxt = [qt + 1 + dk for dk in rel[cls[qt + 1]]] if qt + 1 < NQ else []
                na_tile(CTX + qt * 128, [qt + dk for dk in rel[c]], True, nxt)
            S.barrier()
            if 'stopC' in dbg and l == 0:
                break
            AR.off = PERSIST
            ph = {}
            ph['sq'] = AR.alloc([8, 512], BF16)
            ph['rs'] = AR.alloc([512], F32)
            ph['ntmp'] = [AR.alloc([512], F32) for _ in range(2)]
            xt2 = [AR.alloc([8, 512], F32) for _ in range(2)]
            ys = AR.alloc([8, 512], BF16)
            yn_ = AR.alloc([4, 512], BF16)
            yg = AR.alloc([4, 512], BF16)
            gt = AR.alloc([24, 512], BF16)
            mg = AR.alloc([8, 512], BF16)
            h2 = AR.alloc([8, 512], BF16)
            actb = AR.alloc([22, 512], BF16)
            m1 = AR.alloc([512], F32)
            m2 = AR.alloc([512], F32)
            sa = AR.alloc([512], BF16)
            wbuf = [AR.alloc([4096], BF16) for _ in range(3)]
            wctr[0] = 0
            ygmT3 = ygmT_d.rearrange("(c p) t -> p c t", p=128)
            gateT3 = gateT_d.rearrange("(c p) t -> p c t", p=128)
            x1T3 = x1T.rearrange("(c p) t -> p c t", p=128)
            outT3 = outT.rearrange("(c p) t -> p c t", p=128)

            def load_w(blk):
                slot = wctr[0] % 3
                wctr[0] += 1
                dma(wbuf[slot], wblk16[wb0 + blk], ['w16_%d' % (wb0 + blk)], ['wb%d' % slot])
                return wbuf[slot], 'wb%d' % slot

            dtiles = (tiles[1:] if last else tiles)

            def d_loads(ti):
                (t0_, n_) = dtiles[ti]
                dma(xt2[ti % 2][:, :, :n_], xsrc3[:, :, t0_:t0_ + n_], [], ['Dxt%d' % (ti % 2)])
                dma(ys[:, :, :n_], yssdT3[:, :, t0_:t0_ + n_], [], ['ys'])
                dma(yn_[:, :, :n_], ynaT3[:, :, t0_:t0_ + n_], [], ['yn'])
                dma(yg[:, :, :n_], ygmT3[:, :, t0_:t0_ + n_], [], ['yg'])
                dma(gt[:, :, :n_], gateT3[:, :, t0_:t0_ + n_], [], ['gt'])
            d_loads(0)
            for ti, (t0, n) in enumerate(dtiles):
                isctx = t0 < CTX
                j = 1 if isctx else 0
                xt = xt2[ti % 2]
                xtn = 'Dxt%d' % (ti % 2)
                for jb in range(4):
                    w, wn = load_w(18 + jb)
                    for oo in range(2):
                        oc = 2 * jb + oo
                        base = oo * 2048
                        pa, pan = PSB(nextbank())
                        mms([(pa[:, :n], w[:, base + kc * 128:base + (kc + 1) * 128], ys[:, kc, :n], kc == 0, kc == 7) for kc in range(8)],
                            [wn, 'ys'], [pan])
                        pb_, pbn = PSB(nextbank())
                        mms([(pb_[:, :n], w[:, base + 1024 + kc * 128:base + 1024 + (kc + 1) * 128], yn_[:, kc, :n], kc == 0, kc == 3) for kc in range(4)],
                            [wn, 'yn'], [pbn])
                        pc_, pcn = PSB(nextbank())
                        mms([(pc_[:, :n], w[:, base + 1536 + kc * 128:base + 1536 + (kc + 1) * 128], yg[:, kc, :n], kc == 0, kc == 3) for kc in range(4)],
                            [wn, 'yg'], [pcn])
                        tt(m1[:, :n], pa[:, :n], gt[:, oc, :n], ALU.mult, [pan, 'gt'], ['m1'])
                        tt(m2[:, :n], pb_[:, :n], gt[:, 8 + oc, :n], ALU.mult, [pbn, 'gt'], ['m2'])
                        tt(m1[:, :n], m1[:, :n], m2[:, :n], ALU.add, ['m1', 'm2'], ['m1'])
                        tt(m2[:, :n], pc_[:, :n], gt[:, 16 + oc, :n], ALU.mult, [pcn, 'gt'], ['m2'])
                        tt(mg[:, oc, :n], m1[:, :n], m2[:, :n], ALU.add, ['m1', 'm2'], ['mg'])
                if ti + 1 < len(dtiles):
                    d_loads(ti + 1)
                for jb in range(2):
                    w, wn = load_w(22 + jb)
                    w3 = w.rearrange("p (k c) -> p k c", c=512)
                    for oc4 in range(4):
                        oc = jb * 4 + oc4
                        p_, pn = PSB(nextbank())
                        mms([(p_[:, :n], w3[:, kc, oc4 * 128:(oc4 + 1) * 128], mg[:, kc, :n], kc == 0, kc == 7) for kc in range(8)],
                            [wn, 'mg'], [pn])
                        stt(xt[:, oc, :n], p_[:, :n], modT[l][:, 16 + oc, j:j + 1], xt[:, oc, :n], ALU.mult, ALU.add,
                            [pn, 'modT%d' % l, xtn], [xtn])
                norm_mod(xt, n, A2[l][:, :, j], modT[l][:, 24:32, j], h2, xtn, 'DhT')
                for jb in range(11):
                    w, wn = load_w(24 + jb)
                    w3 = w.rearrange("p (k c) -> p k c", c=512)
                    for hcl in range(2):
                        hc = 2 * jb + hcl
                        pa, pan = PSB(nextbank())
                        mms([(pa[:, :n], w3[:, kc, (hcl * 2) * 128:(hcl * 2 + 1) * 128], h2[:, kc, :n], kc == 0, kc == 7) for kc in range(8)],
                            [wn, 'DhT'], [pan])
                        pb_, pbn = PSB(nextbank())
                        mms([(pb_[:, :n], w3[:, kc, (hcl * 2 + 1) * 128:(hcl * 2 + 2) * 128], h2[:, kc, :n], kc == 0, kc == 7) for kc in range(8)],
                            [wn, 'DhT'], [pbn])
                        act(sa[:, :n], pa[:, :n], AF.Silu, [pan], ['sa'])
                        tt(actb[:, hc, :n], sa[:, :n], pb_[:, :n], ALU.mult, ['sa', pbn], ['actb'])
                for oc in range(8):
                    w, wn = load_w(35 + oc)
                    wv = w[:, 0:2816].rearrange("p (k c) -> p k c", c=128)
                    p_, pn = PSB(nextbank())
                    mms([(p_[:, :n], wv[:, kc, :], actb[:, kc, :n], kc == 0, kc == 21) for kc in range(22)], [wn, 'actb'], [pn])
                    stt(xt[:, oc, :n], p_[:, :n], modT[l][:, 40 + oc, j:j + 1], xt[:, oc, :n], ALU.mult, ALU.add,
                        [pn, 'modT%d' % l, xtn], [xtn])
                if last:
                    dma(outT3[:, :, t0 - CTX:t0 - CTX + n], xt[:, :, :n], [xtn], [], q='act')
                else:
                    dma(x1T3[:, :, t0:t0 + n], xt[:, :, :n], [xtn], [], q='act')
            S.barrier()
        S.emit()
    return nc


def _slab(W):
    k = W.shape[0] // 128
    c = W.shape[1]
    out = np.zeros((128, 4096), np.float32)
    out[:, :k * c] = W.reshape(k, 128, c).transpose(1, 0, 2).reshape(128, k * c)
    return out


def pack_weights(inp, l):
    w_in = inp['w_in'][l]
    blks = []
    segs = [(0, 512), (512, 1024),
            (1024, 1536), (1536, 2048), (2048, 2560), (2560, 3072),
            (3104, 3616), (3616, 4128), (4128, 4640),
            (4640, 5152), (5152, 5664)]
    segs += [(5664 + 512 * i, 5664 + 512 * (i + 1)) for i in range(6)]
    for (a, b) in segs:
        blks.append(_slab(w_in[:, a:b]))
    blks.append(_slab(w_in[:, 3072:3104]))
    wa, wb_, wc = inp['w_branch_ssd'][l], inp['w_branch_na'][l], inp['w_branch_gm'][l]
    for j in range(4):
        blk = np.zeros((128, 4096), np.float32)
        for oo in range(2):
            oc = 2 * j + oo
            base = oo * 2048
            blk[:, base:base + 1024] = wa[:, oc * 128:(oc + 1) * 128].reshape(8, 128, 128).transpose(1, 0, 2).reshape(128, 1024)
            blk[:, base + 1024:base + 1536] = wb_[:, oc * 128:(oc + 1) * 128].reshape(4, 128, 128).transpose(1, 0, 2).reshape(128, 512)
            blk[:, base + 1536:base + 2048] = wc[:, oc * 128:(oc + 1) * 128].reshape(4, 128, 128).transpose(1, 0, 2).reshape(128, 512)
        blks.append(blk)
    wo = inp['w_out'][l]
    for j in range(2):
        blks.append(_slab(wo[:, j * 512:(j + 1) * 512]))
    wf = inp['w_ffn_in'][l]
    for j in range(11):
        slab = np.zeros((1024, 512), np.float32)
        for hcl in range(2):
            for ab in range(2):
                src = ab * 2816 + (2 * j + hcl) * 128
                slab[:, (hcl * 2 + ab) * 128:(hcl * 2 + ab + 1) * 128] = wf[:, src:src + 128]
        blks.append(_slab(slab))
    wfo = inp['w_ffn_out'][l]
    for oc in range(8):
        blks.append(_slab(wfo[:, oc * 128:(oc + 1) * 128]))
    assert len(blks) == NBLK
    return np.stack(blks)


def pcol(v):
    return np.ascontiguousarray(v.reshape(-1, 128).T)


def pack_shared(inp, L, depth):
    T = CTX + L
    R = L // 64
    f32 = np.float32
    wmod = np.stack([inp['w_mod'][l].reshape(8, 128, 12, 512).transpose(2, 1, 0, 3).reshape(12, 128, 4096)
                     for l in range(depth)]).reshape(depth * 12, 128, 4096)
    wblk = np.concatenate([pack_weights(inp, l) for l in range(depth)], axis=0)
    pvec = np.zeros((depth, 128, NPV), f32)
    brow = np.zeros((depth, NBR), f32)
    wsT = np.zeros((depth, 128, 1024), f32)
    for l in range(depth):
        pvec[l, :, 0:48] = pcol(inp['b_mod'][l])
        pvec[l, :, 48:56] = pcol(inp['norm1'][l])
        pvec[l, :, 56:64] = pcol(inp['norm2'][l])
        pvec[l, :, 64:88] = pcol(inp['b_gate'][l])
        pvec[l, :, 88:104] = pcol(inp['conv_b'][l])
        for k in range(5):
            pvec[l, :, 104 + k * 16:104 + (k + 1) * 16] = pcol(inp['conv_w'][l][k])
        pvec[l, :, 184:192] = pcol(inp['ssd_norm'][l])
        pvec[l, :, 192] = np.tile(inp['q_norm'][l], 2)
        pvec[l, :, 193] = np.tile(inp['k_norm'][l], 2)
        pvec[l, :, 194:202] = inp['b_spatial'][l].T
        brow[l, 0:32] = inp['dt_bias'][l].reshape(-1)
        brow[l, 32:64] = inp['a_log'][l].reshape(-1)
        brow[l, 64:80] = inp['d_skip'][l]
        brow[l, 80:592] = inp['gm_norm'][l]
        wsT[l] = inp['w_spatial'][l].transpose(2, 0, 1).reshape(128, 1024)
    freqs = (np.float32(10000.0) ** (-np.arange(32, dtype=f32) / np.float32(32))).astype(f32)
    pos = np.arange(L)
    ang_row = (pos // 64).astype(f32)[:, None] * freqs
    ang_col = (pos % 64).astype(f32)[:, None] * freqs
    ropeC = np.ones((128, T), f32)
    ropeS = np.zeros((128, T), f32)
    for half, ang in ((0, ang_row), (1, ang_col)):
        c = np.cos(ang).T.astype(f32)
        s = np.sin(ang).T.astype(f32)
        ropeC[half * 64:half * 64 + 32, CTX:] = c
        ropeC[half * 64 + 32:half * 64 + 64, CTX:] = c
        ropeS[half * 64:half * 64 + 32, CTX:] = s
        ropeS[half * 64 + 32:half * 64 + 64, CTX:] = s
    consts = np.zeros((128, NCONST), f32)
    i = np.arange(128)
    consts[:, 0:128] = np.eye(128, dtype=f32)
    consts[:, 128:256] = (i[:, None] <= i[None, :]).astype(f32)
    consts[:, 256:384] = (i[:, None] >= i[None, :]).astype(f32)
    consts[:, 384:512] = 1.0
    consts[:, 512:640] = ((i[:, None] // 64) == (i[None, :] // 64)).astype(f32)
    pm = np.zeros((128, 128), f32)
    for base in (0, 64):
        for n in range(32):
            pm[base + n + 32, base + n] = -1.0
            pm[base + n, base + n + 32] = 1.0
    consts[:, 640:768] = pm
    sel = np.zeros((16, 16, 128), f32)
    for h in range(16):
        sel[h, h, :] = 1.0
    consts[0:16, 768:768 + 2048] = sel.reshape(16, 2048)
    cls, rel, rep = na_classes(L)

    def rs_(r):
        return min(max(r - 4, 0), R - 8)
    natab = np.full((depth, 5, 5, 8, 128, 128), NEG, f32)
    kc = np.arange(64)
    qc = np.arange(64)
    cs = np.clip(qc - 8, 0, 48)
    colok = (kc[:, None] >= cs[None, :]) & (kc[:, None] < cs[None, :] + 16)
    ci = np.clip(kc[:, None] - qc[None, :] + 15, 0, 30)
    for c, qt in rep.items():
        for slot, dk in enumerate(rel[c]):
            kt = qt + dk
            for a in range(2):
                r = 2 * qt + a
                for b in range(2):
                    krow = 2 * kt + b
                    if not (rs_(r) <= krow < rs_(r) + 8):
                        continue
                    ri = krow - r + 7
                    for l in range(depth):
                        vals = inp['rpb'][l][:, ri, :][:, ci]
                        blk = np.where(colok[None], vals, np.float32(NEG))
                        natab[l, c, slot, :, b * 64:(b + 1) * 64, a * 64:(a + 1) * 64] = blk
    natab = natab.reshape(depth * 200, 128, 128)
    return dict(wmod=wmod, wblk=wblk, pvec=pvec, brow=brow, wsT=wsT, ropeC=ropeC, ropeS=ropeS,
                natab=natab, consts=consts)


def pack_core(inp, b, shared):
    xT = np.ascontiguousarray(np.concatenate([inp['ctx'][b].T, inp['x'][b].T], axis=1))
    cT = np.zeros((128, 8, 2), np.float32)
    cT[:, :, 0] = pcol(inp['c'][b])
    cT[:, :, 1] = pcol(inp['c_ctx'])
    m = dict(shared)
    m['xT'] = xT
    m['cT'] = cT.reshape(128, 16)
    return m


_NC_CACHE = {}


def kernel(**inputs):
    inp = {k: np.asarray(v, dtype=np.float32) for k, v in inputs.items()}
    B, L, _ = inp['x'].shape
    depth = inp['w_in'].shape[0]
    key = (L, depth)
    if key not in _NC_CACHE:
        _NC_CACHE[key] = build(L, depth)
    nc = _NC_CACHE[key]
    shared = pack_shared(inp, L, depth)
    in_maps = [pack_core(inp, b, shared) for b in range(B)]
    res = run_bass_kernel_spmd(nc, in_maps, core_ids=list(range(B)))
    out = np.stack([np.ascontiguousarray(res.results[b]['outT'].T) for b in range(B)])
    return out.astype(np.float32)
```

```python
import math
import numpy as np
import concourse.bass as bass
import concourse.mybir as mybir
from concourse.bass_utils import run_bass_kernel_spmd
from contextlib import ExitStack

F32 = mybir.dt.float32
BF16 = mybir.dt.bfloat16
U8 = mybir.dt.uint8
AF = mybir.ActivationFunctionType
ALU = mybir.AluOpType
AX = mybir.AxisListType

D = 1024
CTX = 256
EPS = 1e-6
NBLK = 43
NPV = 202
NBR = 592
NCONST = 768 + 2048
NEG = -30000.0

ENGS = ['pe', 'act', 'dve', 'pool', 'sp']
N_DSEM = 48
DSEM_RANGE = {'sp': (0, 22), 'act': (22, 32), 'pool': (32, 48)}


class Sched:
    def __init__(self, nc):
        self.nc = nc
        self.ops = {e: [] for e in ENGS}
        self.last_write = {}
        self.readers = {}
        self.known = {e: {} for e in ENGS}
        self.dsem_uses = [0] * N_DSEM
        self.dsem_next = {'sp': 0, 'pool': 0, 'act': 0}
        self.signal = {e: set() for e in ENGS}
        self.rec = None

    def record(self):
        self.rec = []

    def stop(self):
        r = self.rec
        self.rec = None
        return r

    def replay(self, lists):
        idx = [0] * len(lists)
        tot = [max(1, len(x)) for x in lists]
        while True:
            live = [k for k in range(len(lists)) if idx[k] < len(lists[k])]
            if not live:
                break
            k = min(live, key=lambda k: idx[k] / tot[k])
            self.op(*lists[k][idx[k]])
            idx[k] += 1

    def _need(self, eng, tok, waits):
        kind, key, val = tok
        if kind == 'eng' and key == eng and eng in ('pe', 'sp'):
            return
        k = (kind, key)
        if self.known[eng].get(k, 0) >= val:
            return
        self.known[eng][k] = val
        waits[k] = max(waits.get(k, 0), val)
        if kind == 'eng':
            self.signal[key].add(val)

    def op(self, eng, fn, reads=(), writes=(), dma=False):
        if self.rec is not None:
            self.rec.append((eng, fn, tuple(reads), tuple(writes), dma))
            return None
        waits = {}
        for b in reads:
            w = self.last_write.get(b)
            if w is not None:
                self._need(eng, w, waits)
        for b in writes:
            w = self.last_write.get(b)
            if w is not None:
                self._need(eng, w, waits)
            for r in self.readers.get(b, ()):
                self._need(eng, r, waits)
        idx = len(self.ops[eng]) + 1
        if dma:
            lo, hi = DSEM_RANGE[eng]
            j = lo + self.dsem_next[eng]
            self.dsem_next[eng] = (self.dsem_next[eng] + 1) % (hi - lo)
            n = self.dsem_uses[j]
            if n > 0:
                self._need(eng, ('dma', j, 16 * n), waits)
            self.dsem_uses[j] = n + 1
            tok = ('dma', j, 16 * (n + 1))
        else:
            tok = ('eng', eng, idx)
        for b in reads:
            self.readers.setdefault(b, []).append(tok)
        for b in writes:
            self.last_write[b] = tok
            self.readers[b] = []
        self.ops[eng].append((fn, waits, tok))
        return tok

    def barrier(self):
        toks = []
        for e in ENGS:
            for i in range(len(self.ops[e]), 0, -1):
                t = self.ops[e][i - 1][2]
                if t is not None and t[0] == 'eng':
                    toks.append(t)
                    break
        for j in range(N_DSEM):
            if self.dsem_uses[j] > 0:
                toks.append(('dma', j, 16 * self.dsem_uses[j]))
        for e in ENGS:
            waits = {}
            for t in toks:
                self._need(e, t, waits)
            self.ops[e].append((None, waits, None))
        self.last_write.clear()
        self.readers.clear()

    def emit(self):
        nc = self.nc
        with ExitStack() as st:
            esem = {e: st.enter_context(nc.semaphore('es_' + e)) for e in ENGS}
            dsem = [st.enter_context(nc.semaphore('ds_%d' % j)) for j in range(N_DSEM)]
            block = st.enter_context(nc.Block())
            sigmap = {}
            for e in ENGS:
                cnt = 0
                m = {}
                ss = self.signal[e]
                for i in range(1, len(self.ops[e]) + 1):
                    if i in ss:
                        cnt += 1
                        m[i] = cnt
                sigmap[e] = m

            def run(e, engobj):
                for i, (fn, waits, tok) in enumerate(self.ops[e], start=1):
                    for (kind, key), val in waits.items():
                        if kind == 'eng':
                            engobj.wait_ge(esem[key], sigmap[key][val])
                        else:
                            engobj.wait_ge(dsem[key], val)
                    if fn is None:
                        continue
                    ins = fn(engobj)
                    if tok[0] == 'dma':
                        ins.then_inc(dsem[tok[1]], 16)
                    elif i in sigmap[e]:
                        ins.then_inc(esem[e], 1)

            @block.tensor
            def _(eng):
                run('pe', eng)

            @block.scalar
            def _(eng):
                run('act', eng)

            @block.vector
            def _(eng):
                run('dve', eng)

            @block.gpsimd
            def _(eng):
                run('pool', eng)

            @block.sync
            def _(eng):
                run('sp', eng)


class Arena:
    def __init__(self, ap, size):
        self.ap = ap
        self.size = size
        self.off = 0

    def alloc(self, free_shape, dt):
        esz = 4 if dt == F32 else 2
        n = 1
        for s in free_shape:
            n *= s
        nb = (n * esz + 63) // 64 * 64
        assert self.off + nb <= self.size, "SBUF arena overflow %d + %d" % (self.off, nb)
        v = self.ap[:, self.off:self.off + n * esz].bitcast(dt)
        self.off += nb
        if len(free_shape) == 2:
            v = v.rearrange("p (a b) -> p a b", b=free_shape[1])
        elif len(free_shape) == 3:
            v = v.rearrange("p (a b c) -> p a b c", b=free_shape[1], c=free_shape[2])
        return v


def na_classes(L):
    R = L // 64
    NQ = R // 2

    def rs(r):
        return min(max(r - 4, 0), R - 8)
    info = []
    for qt in range(NQ):
        r0, r1 = 2 * qt, 2 * qt + 1
        lo = min(rs(r0), rs(r1)) // 2
        hi = (max(rs(r0), rs(r1)) + 7) // 2
        info.append(tuple(k - qt for k in range(lo, hi + 1)))
    cls = []
    for qt in range(NQ):
        if qt == 0:
            cls.append(0)
        elif qt == 1:
            cls.append(1)
        elif qt == NQ - 2:
            cls.append(3)
        elif qt == NQ - 1:
            cls.append(4)
        else:
            cls.append(2)
    rel = {}
    rep = {}
    for qt in range(NQ):
        c = cls[qt]
        if c in rel:
            assert rel[c] == info[qt], (c, rel[c], info[qt])
        else:
            rel[c] = info[qt]
            rep[c] = qt
    return cls, rel, rep


def build(L, depth=2, dbg=()):
    T = CTX + L
    NCH = T // 128
    TP = T + 8
    tiles = [(0, 256)] + [(CTX + 512 * i, 512) for i in range(L // 512)]
    nc = bass.Bass("TRN2", target_bir_lowering=False)

    def din(name, shape, dt=F32):
        return nc.dram_tensor(name, shape, dt, kind="ExternalInput").ap()

    def dscr(name, shape, dt):
        kind = "ExternalOutput" if name in dbg else "Internal"
        return nc.dram_tensor(name, shape, dt, kind=kind).ap()

    xT_in = din("xT", [D, T])
    cT_in = din("cT", [128, 16])
    wmod_in = din("wmod", [depth * 12, 128, 4096])
    wblk_in = din("wblk", [depth * NBLK, 128, 4096])
    pvec_in = din("pvec", [depth, 128, NPV])
    brow_in = din("brow", [depth, NBR])
    wsT_in = din("wsT", [depth, 128, 1024])
    ropeC_in = din("ropeC", [128, T])
    ropeS_in = din("ropeS", [128, T])
    natab_in = din("natab", [depth * 200, 128, 128])
    consts_in = din("consts", [128, NCONST])
    outT = nc.dram_tensor("outT", [D, L], F32, kind="ExternalOutput").ap()

    wblk16 = dscr("wblk16", [depth * NBLK, 128, 4096], BF16)
    natab16 = dscr("natab16", [depth * 200, 128, 128], BF16)
    x1T = dscr("x1T", [D, T], F32)
    zs_d = dscr("zs", [T, 1024], BF16)
    xbcT_d = dscr("xbcT", [2048, TP], BF16)
    dts_d = dscr("dts", [T, 32], F32)
    qT_d = dscr("qT", [512, T], BF16)
    kT_d = dscr("kT", [512, T], BF16)
    vv_d = dscr("vv", [T, 512], BF16)
    ygmT_d = dscr("ygmT", [512, T], BF16)
    gateT_d = dscr("gateT", [3072, T], BF16)
    xs_d = dscr("xs", [T, 1024], BF16)
    BT_d = dscr("BTs", [512, T], BF16)
    CT_d = dscr("CTs", [512, T], BF16)
    Bt_d = dscr("Btok", [T, 512], BF16)
    yp_d = dscr("ypart", [T, 1024], F32)
    yssdT_d = dscr("yssdT", [1024, T], BF16)
    ynaT_d = dscr("ynaT", [512, T], BF16)

    st = ExitStack()
    with st:
        ARENA = 186 * 1024
        arena_t = st.enter_context(nc.sbuf_tensor("arena", [128, ARENA], U8))
        ps = [st.enter_context(nc.psum_tensor("ps%d" % i, [128, 512], F32)) for i in range(8)]
        S = Sched(nc)
        AR = Arena(arena_t, ARENA)
        pb_ctr = [0]

        def PSB(i):
            return ps[i][:], 'P%d' % i

        def nextbank(lo=0, hi=8):
            i = lo + pb_ctr[0] % (hi - lo)
            pb_ctr[0] += 1
            return i

        def dma(out, in_, reads, writes, q='sp'):
            return S.op(q, lambda e: e.dma_start(out=out, in_=in_), reads, writes, dma=True)

        def act(out, in_, func, reads, writes, bias=None, scale=None, accum_out=None):
            kw = {}
            if bias is not None:
                kw['bias'] = bias
            if scale is not None:
                kw['scale'] = scale
            if accum_out is not None:
                kw['accum_out'] = accum_out
            return S.op('act', lambda e: e.activation(out=out, in_=in_, func=func, **kw), reads, writes)

        def tt(out, in0, in1, op, reads, writes, eng='dve'):
            return S.op(eng, lambda e: e.tensor_tensor(out=out, in0=in0, in1=in1, op=op), reads, writes)

        def ts(out, in0, s1, s2, op0, op1, reads, writes, eng='dve'):
            if s2 is None:
                return S.op(eng, lambda e: e.tensor_scalar(out=out, in0=in0, scalar1=s1, scalar2=None, op0=op0), reads, writes)
            return S.op(eng, lambda e: e.tensor_scalar(out=out, in0=in0, scalar1=s1, scalar2=s2, op0=op0, op1=op1), reads, writes)

        def stt(out, in0, scalar, in1, op0, op1, reads, writes):
            return S.op('dve', lambda e: e.scalar_tensor_tensor(out=out, in0=in0, scalar=scalar, in1=in1, op0=op0, op1=op1), reads, writes)

        def recip(out, in_, reads, writes):
            return S.op('dve', lambda e: e.reciprocal(out=out, in_=in_), reads, writes)

        def cp(out, in_, reads, writes, eng='dve'):
            if eng == 'act':
                return S.op('act', lambda e: e.copy(out=out, in_=in_), reads, writes)
            return S.op(eng, lambda e: e.tensor_copy(out=out, in_=in_), reads, writes)

        def mms(lst, reads, writes):
            def fn(e):
                ins = None
                for (o, l, r, s0, s1) in lst:
                    ins = e.matmul(o, lhsT=l, rhs=r, start=s0, stop=s1)
                return ins
            return S.op('pe', fn, reads, writes)

        def trs(lst, reads, writes):
            def fn(e):
                ins = None
                for (o, i_, idn) in lst:
                    ins = e.transpose(o, i_, idn)
                return ins
            return S.op('pe', fn, reads, writes)

        c32 = AR.alloc([NCONST], F32)
        c16 = AR.alloc([512], BF16)
        epsb = AR.alloc([1], F32)
        ident32 = c32[:, 0:128]
        Uf32 = c32[:, 128:256]
        Ub32 = c32[:, 256:384]
        sel32 = c32[0:16, 768:768 + 2048].rearrange("p (h s) -> p h s", s=128)
        ident16 = c16[:, 0:128]
        ones16 = c16[:, 128:256]
        bo16 = c16[:, 256:384]
        pm16 = c16[:, 384:512]
        pv = [AR.alloc([NPV], F32) for _ in range(depth)]
        br = [AR.alloc([NBR], F32) for _ in range(depth)]
        modT = [AR.alloc([48, 2], F32) for _ in range(depth)]
        A1 = [AR.alloc([8, 2], F32) for _ in range(depth)]
        A2 = [AR.alloc([8, 2], F32) for _ in range(depth)]
        scs = AR.alloc([8, 2], F32)
        mneg16 = [AR.alloc([128], BF16) for _ in range(2)]
        PERSIST = AR.off

        dma(c32, consts_in, [], ['c32'])
        S.op('dve', lambda e: e.memset(epsb, EPS), [], ['epsb'])
        cp(c16[:, 0:128], c32[:, 0:128], ['c32'], ['c16'])
        cp(c16[:, 128:512], c32[:, 384:768], ['c32'], ['c16'])
        for d_ in range(2):
            ts(mneg16[d_], (Uf32 if d_ == 0 else Ub32), -1.0, -NEG, ALU.add, ALU.mult, ['c32'], ['mneg16'])
        for l in range(depth):
            dma(pv[l], pvec_in[l], [], ['pv%d' % l])
            dma(br[l], brow_in[l:l + 1, :].partition_broadcast(128), [], ['br%d' % l])
        wsrc = wblk_in.rearrange("b p (h e) -> b p h e", e=2048)
        wdst = wblk16.rearrange("b p (h e) -> b p h e", e=2048)
        CB = 4

        def cast_weights(l_):
            for b0 in range(l_ * NBLK, (l_ + 1) * NBLK, CB):
                b1 = min((l_ + 1) * NBLK, b0 + CB)
                dma(wdst[b0:b1], wsrc[b0:b1], [], ['w16_%d' % b for b in range(b0, b1)], q='pool')

        def cast_natab(l_):
            for k in range(0, 200, 50):
                dma(natab16[l_ * 200 + k:l_ * 200 + k + 50], natab_in[l_ * 200 + k:l_ * 200 + k + 50], [], ['natab16_%d' % l_], q='pool')
        cast_weights(0)

        sc_raw = AR.alloc([16], F32)
        wm = [AR.alloc([8, 512], F32) for _ in range(2)]
        dma(sc_raw, cT_in, [], ['sc_raw'])
        act(scs.rearrange("p a b -> p (a b)"), sc_raw, AF.Silu, ['sc_raw'], ['scs'])
        for l in range(depth):
            psM, psMn = PSB(0)
            for blk in range(12):
                slot = blk % 2
                dma(wm[slot].rearrange("p a b -> p (a b)"), wmod_in[l * 12 + blk], [], ['wm%d' % slot])
                lst = []
                for oc4 in range(4):
                    oc = blk * 4 + oc4
                    for kc in range(8):
                        lst.append((psM[:, oc * 2:oc * 2 + 2], wm[slot][:, kc, oc4 * 128:(oc4 + 1) * 128], scs[:, kc, :], kc == 0, kc == 7))
                mms(lst, ['wm%d' % slot, 'scs'], [psMn])
            tt(modT[l], psM[:, 0:96].rearrange("p (a b) -> p a b", b=2), pv[l][:, 0:48].unsqueeze(2).to_broadcast([128, 48, 2]),
               ALU.add, [psMn, 'pv%d' % l], ['modT%d' % l])
            for (Ax, sci, nrm) in ((A1[l], 8, 48), (A2[l], 32, 56)):
                ts(Ax, modT[l][:, sci:sci + 8, :], 1.0, None, ALU.add, None, ['modT%d' % l], ['A%d' % l])
                tt(Ax, Ax, pv[l][:, nrm:nrm + 8].unsqueeze(2).to_broadcast([128, 8, 2]), ALU.mult, ['A%d' % l, 'pv%d' % l], ['A%d' % l])
        S.barrier()

        def norm_sq(xt, n, xtn):
            act(ph['sq'][:, :, :n], xt[:, :, :n], AF.Square, [xtn], ['sq'])

        def norm_rest(xt, n, Acol, shcol, hT, xtn, hTn):
            sq = ph['sq']
            bi = nextbank()
            pss, pn = PSB(bi)
            mms([(pss[:, :n], ones16, sq[:, c, :n], c == 0, c == 7) for c in range(8)], ['sq', 'c16'], [pn])
            rs = ph['rs']
            act(rs[:, :n], pss[:, :n], AF.Sqrt, [pn, 'epsb'], ['rs'], bias=epsb[:, 0:1], scale=1.0 / D)
            recip(rs[:, :n], rs[:, :n], ['rs'], ['rs'])
            for c in range(8):
                tmp = ph['ntmp'][c % 2]
                tt(tmp[:, :n], xt[:, c, :n], rs[:, :n], ALU.mult, [xtn, 'rs'], ['ntmp%d' % (c % 2)])
                act(hT[:, c, :n], tmp[:, :n], AF.Identity, ['ntmp%d' % (c % 2)], [hTn], bias=shcol[:, c:c + 1], scale=Acol[:, c:c + 1])

        def norm_mod(xt, n, Acol, shcol, hT, xtn, hTn):
            norm_sq(xt, n, xtn)
            norm_rest(xt, n, Acol, shcol, hT, xtn, hTn)

        def col_of(t):
            return 2 + t if t < CTX else 6 + t

        for l in range(depth):
            last = (l == depth - 1)
            xsrc = xT_in if l == 0 else x1T
            xsrc3 = xsrc.rearrange("(c p) t -> p c t", p=128)
            wb0 = l * NBLK
            pvl = pv[l]
            brl = br[l]
            AR.off = PERSIST
            ph = {}
            ph['sq'] = AR.alloc([8, 512], BF16)
            ph['rs'] = AR.alloc([512], F32)
            ph['ntmp'] = [AR.alloc([512], F32) for _ in range(2)]
            xt = AR.alloc([8, 512], F32)
            hT2 = [AR.alloc([8, 512], BF16) for _ in range(2)]
            wbuf = [AR.alloc([4096], BF16) for _ in range(3)]
            wsT32 = AR.alloc([1024], F32)
            wsT16 = AR.alloc([8, 128], BF16)
            zst = AR.alloc([4, 1024], BF16)
            xbcst = AR.alloc([16, 512], BF16)
            dst_ = AR.alloc([4, 32], F32)
            dtmp = AR.alloc([32], F32)
            qkst = AR.alloc([8, 512], BF16)
            sq16 = [AR.alloc([512], BF16) for _ in range(2)]
            rs2 = [AR.alloc([512], F32) for _ in range(2)]
            qtmp = [AR.alloc([512], F32) for _ in range(2)]
            qraw = [AR.alloc([512], F32) for _ in range(2)]
            vst = AR.alloc([4, 512], BF16)
            u16 = AR.alloc([4, 512], BF16)
            vg = AR.alloc([512], F32)
            vjunk = AR.alloc([512], BF16)
            ssv4 = AR.alloc([4, 4], F32)
            vn16 = [AR.alloc([512], BF16) for _ in range(4)]
            gtmp = AR.alloc([512], F32)
            ygm16 = [AR.alloc([512], BF16) for _ in range(4)]
            ygst = AR.alloc([4, 512], BF16)
            gst2 = [AR.alloc([4, 512], BF16) for _ in range(2)]
            dma(wsT32, wsT_in[l], [], ['wsT32'])
            cp(wsT16.rearrange("p a b -> p (a b)"), wsT32, ['wsT32'], ['wsT16'])
            wctr = [0]

            def load_w(blk):
                slot = wctr[0] % 3
                wctr[0] += 1
                dma(wbuf[slot], wblk16[wb0 + blk], ['w16_%d' % (wb0 + blk)], ['wb%d' % slot])
                return wbuf[slot], 'wb%d' % slot

            def a_norm1(ti):
                (t0_, n_) = tiles[ti]
                dma(xt[:, :, :n_], xsrc3[:, :, t0_:t0_ + n_], [], ['Axt'])
                norm_sq(xt, n_, 'Axt')

            def a_norm2(ti):
                (t0_, n_) = tiles[ti]
                j_ = 1 if t0_ < CTX else 0
                norm_rest(xt, n_, A1[l][:, :, j_], modT[l][:, 0:8, j_], hT2[ti % 2], 'Axt', 'AhT%d' % (ti % 2))
            a_norm1(0)
            a_norm2(0)
            for ti, (t0, n) in enumerate(tiles):
                isctx = t0 < CTX
                j = 1 if isctx else 0
                ntc = n // 128
                hT = hT2[ti % 2]
                hTn = 'AhT%d' % (ti % 2)
                if ti == len(tiles) // 2:
                    cast_natab(l)
                    if l + 1 < depth:
                        cast_weights(l + 1)

                def fm_group(w3, oc4, tagw):
                    bi = nextbank()
                    p_, pn = PSB(bi)
                    mms([(p_[:, :n], w3[:, kc, oc4 * 128:(oc4 + 1) * 128], hT[:, kc, :n], kc == 0, kc == 7) for kc in range(8)],
                        [tagw, hTn], [pn])
                    return p_, pn

                def tm_group(w3, tc, ncols, tagw):
                    bi = nextbank()
                    p_, pn = PSB(bi)
                    mms([(p_[:, :ncols], hT[:, kc, tc * 128:(tc + 1) * 128], w3[:, kc, :ncols], kc == 0, kc == 7) for kc in range(8)],
                        [tagw, hTn], [pn])
                    return p_, pn

                for blk in range(2):
                    w, wn = load_w(blk)
                    w3 = w.rearrange("p (k c) -> p k c", c=512)
                    for tc in range(ntc):
                        p_, pn = tm_group(w3, tc, 512, wn)
                        act(zst[:, tc, blk * 512:(blk + 1) * 512], p_, AF.Silu, [pn], ['zst'])
                dma(zs_d[t0:t0 + n, :].rearrange("(c p) f -> p c f", p=128), zst[:, :ntc, :], ['zst'], [], q='act')
                if ti + 1 < len(tiles):
                    a_norm1(ti + 1)
                for blk in range(2, 6):
                    w, wn = load_w(blk)
                    w3 = w.rearrange("p (k c) -> p k c", c=512)
                    for oc4 in range(4):
                        p_, pn = fm_group(w3, oc4, wn)
                        ch = (blk - 2) * 4 + oc4
                        if ch % 2 == 0:
                            cp(xbcst[:, ch, :n], p_[:, :n], [pn], ['xbcst'], eng='act')
                        else:
                            cp(xbcst[:, ch, :n], p_[:, :n], [pn], ['xbcst'])
                c0 = col_of(t0)
                dma(xbcT_d.rearrange("(c p) t -> p c t", p=128)[:, :, c0:c0 + n], xbcst[:, :, :n], ['xbcst'], [], q='act')
                if ti + 1 < len(tiles):
                    a_norm2(ti + 1)
                gi = 0
                pend = None

                def qk_tail(par, blk, oc4):
                    sfx = '%d' % par
                    b2 = nextbank()
                    p2, p2n = PSB(b2)
                    mms([(p2[:, :n], bo16, sq16[par][:, :n], True, True)], ['sq16' + sfx, 'c16'], [p2n])
                    act(rs2[par][:, :n], p2[:, :n], AF.Sqrt, [p2n, 'epsb'], ['rs2' + sfx], bias=epsb[:, 0:1], scale=1.0 / 64)
                    recip(rs2[par][:, :n], rs2[par][:, :n], ['rs2' + sfx], ['rs2' + sfx])
                    stt(qtmp[par][:, :n], qraw[par][:, :n], 0.125 if blk == 6 else 1.0, rs2[par][:, :n], ALU.mult, ALU.mult,
                        ['qraw' + sfx, 'rs2' + sfx], ['qtmp' + sfx])
                    wcol = 192 if blk == 6 else 193
                    act(qkst[:, (blk - 6) * 4 + oc4, :n], qtmp[par][:, :n], AF.Copy, ['qtmp' + sfx, 'pv%d' % l], ['qkst'],
                        scale=pvl[:, wcol:wcol + 1])
                for blk in range(6, 8):
                    w, wn = load_w(blk)
                    w3 = w.rearrange("p (k c) -> p k c", c=512)
                    for oc4 in range(4):
                        par = gi % 2
                        gi += 1
                        sfx = '%d' % par
                        p_, pn = fm_group(w3, oc4, wn)
                        act(sq16[par][:, :n], p_[:, :n], AF.Square, [pn], ['sq16' + sfx])
                        cp(qraw[par][:, :n], p_[:, :n], [pn], ['qraw' + sfx], eng='act')
                        if pend is not None:
                            qk_tail(*pend)
                        pend = (par, blk, oc4)
                qk_tail(*pend)
                dma(qT_d.rearrange("(c p) t -> p c t", p=128)[:, :, t0:t0 + n], qkst[:, 0:4, :n], ['qkst'], [], q='act')
                dma(kT_d.rearrange("(c p) t -> p c t", p=128)[:, :, t0:t0 + n], qkst[:, 4:8, :n], ['qkst'], [], q='act')
                w, wn = load_w(8)
                w3 = w.rearrange("p (k c) -> p k c", c=512)
                for tc in range(ntc):
                    p_, pn = tm_group(w3, tc, 512, wn)
                    cp(vst[:, tc, :], p_, [pn], ['vst'], eng='act')
                dma(vv_d[t0:t0 + n, :].rearrange("(c p) f -> p c f", p=128), vst[:, :ntc, :], ['vst'], [], q='act')
                w, wn = load_w(9)
                w3 = w.rearrange("p (k c) -> p k c", c=512)
                for tc in range(ntc):
                    p_, pn = tm_group(w3, tc, 512, wn)
                    act(u16[:, tc, :], p_, AF.Gelu_apprx_tanh, [pn], ['u16'])
                w, wn = load_w(10)
                w3 = w.rearrange("p (k c) -> p k c", c=512)
                for tc in range(ntc):
                    p_, pn = tm_group(w3, tc, 512, wn)
                    act(vg, p_, AF.Gelu_apprx_tanh, [pn], ['vg'])
                    act(vjunk, vg, AF.Square, ['vg'], ['vjunk', 'ssv%d' % tc], accum_out=ssv4[:, tc, 0:1])
                    act(ssv4[:, tc, 1:2], ssv4[:, tc, 0:1], AF.Sqrt, ['ssv%d' % tc, 'epsb'], ['ssv%d' % tc], bias=epsb[:, 0:1], scale=1.0 / 512)
                    recip(ssv4[:, tc, 2:3], ssv4[:, tc, 1:2], ['ssv%d' % tc], ['ssv%d' % tc])
                    stt(vn16[tc], vg, ssv4[:, tc, 2:3], brl[:, 80:592], ALU.mult, ALU.mult, ['vg', 'ssv%d' % tc, 'br%d' % l], ['vn16_%d' % tc])

                def gate_blocks(blks):
                    for blk in blks:
                        w, wn = load_w(blk)
                        w3 = w.rearrange("p (k c) -> p k c", c=512)
                        for oc4 in range(4):
                            p_, pn = fm_group(w3, oc4, wn)
                            gc = (blk - 11) * 4 + oc4
                            gsl = gst2[blk % 2]
                            act(gsl[:, oc4, :n], p_[:, :n], AF.Sigmoid, [pn, 'pv%d' % l], ['gst%d' % (blk % 2)], bias=pvl[:, 64 + gc:65 + gc])
                        r0 = (blk - 11) * 512
                        dma(gateT_d[r0:r0 + 512, :].rearrange("(c p) t -> p c t", p=128)[:, :, t0:t0 + n], gst2[blk % 2][:, :, :n],
                            ['gst%d' % (blk % 2)], [], q='act')
                gate_blocks([11, 12, 13])
                for tc in range(ntc):
                    bm = nextbank()
                    pm_, pmn = PSB(bm)
                    mms([(pm_[:, g * 64:(g + 1) * 64], wsT16[:, g, :], vn16[tc][:, g * 64:(g + 1) * 64], True, True) for g in range(8)],
                        ['wsT16', 'vn16_%d' % tc], [pmn])
                    tt(gtmp.rearrange("p (g d) -> p g d", d=64), pm_.rearrange("p (g d) -> p g d", d=64),
                       pvl[:, 194:202].unsqueeze(2).to_broadcast([128, 8, 64]), ALU.add, [pmn, 'pv%d' % l], ['gtmp'])
                    tt(ygm16[tc], gtmp, u16[:, tc, :], ALU.mult, ['gtmp', 'u16'], ['ygm16_%d' % tc])
                gate_blocks([14, 15])
                for tc in range(ntc):
                    bt_ = nextbank()
                    pt_, ptn = PSB(bt_)
                    pt16 = pt_.bitcast(BF16)
                    trs([(pt16[:, c * 128:(c + 1) * 128], ygm16[tc][:, c * 128:(c + 1) * 128], ident16) for c in range(4)],
                        ['ygm16_%d' % tc, 'c16'], [ptn])
                    cp(ygst[:, :, tc * 128:(tc + 1) * 128], pt16[:, 0:512].rearrange("p (c t) -> p c t", t=128), [ptn], ['ygst'], eng='act')
                dma(ygmT_d.rearrange("(c p) t -> p c t", p=128)[:, :, t0:t0 + n], ygst[:, :, :n], ['ygst'], [], q='act')
                gate_blocks([16])
                w, wn = load_w(17)
                w3 = w[:, 0:256].rearrange("p (k c) -> p k c", c=32)
                for tc in range(ntc):
                    p_, pn = tm_group(w3, tc, 32, wn)
                    tt(dtmp, p_[:, 0:32], brl[:, 0:32], ALU.add, [pn, 'br%d' % l], ['dtmp'])
                    act(dtmp, dtmp, AF.Exp, ['dtmp'], ['dtmp'])
                    act(dst_[:, tc, :], dtmp, AF.Ln, ['dtmp'], ['dst'], bias=1.0)
                dma(dts_d[t0:t0 + n, :].rearrange("(c p) f -> p c f", p=128), dst_[:, :ntc, :], ['dst'], [], q='act')
            S.barrier()
            if 'stopA' in dbg and l == 0:
                break
            AR.off = PERSIST
            cdiag = AR.alloc([80, 128], BF16)
            for j in range(80):
                ts(cdiag[:, j, :], ident32, pvl[:, 104 + j:105 + j], None, ALU.mult, None, ['c32', 'pv%d' % l], ['cdiag'])
            zt = AR.alloc([16, 4], BF16)
            S.op('dve', lambda e: e.memset(zt.rearrange("p a b -> p (a b)"), 0.0), [], ['zt'])
            xbcT3 = xbcT_d.rearrange("(c p) t -> p c t", p=128)
            dma(xbcT3[:, :, 0:2], zt[:, :, 0:2], ['zt'], ['xbcpad'], q='act')
            dma(xbcT3[:, :, 258:262], zt[:, :, 0:4], ['zt'], ['xbcpad'], q='act')
            dma(xbcT3[:, :, 262 + L:264 + L], zt[:, :, 0:2], ['zt'], ['xbcpad'], q='act')
            xpre = AR.alloc([16, 516], BF16)
            xact = AR.alloc([16, 512], BF16)
            cosT = AR.alloc([512], F32)
            sinT = AR.alloc([512], F32)
            t1 = AR.alloc([512], F32)
            t2 = AR.alloc([512], F32)
            bcrot = AR.alloc([8, 512], BF16)
            xsst = AR.alloc([4, 1024], BF16)
            btst = AR.alloc([4, 512], BF16)
            BT_d3 = BT_d.rearrange("(c p) t -> p c t", p=128)
            CT_d3 = CT_d.rearrange("(c p) t -> p c t", p=128)
            for (t0, n) in tiles:
                ntc = n // 128
                c0 = col_of(t0)
                dma(xpre[:, :, :n + 4], xbcT3[:, :, c0 - 2:c0 + n + 2], ['xbcpad'], ['xpre'])
                dma(cosT[:, :n], ropeC_in[:, t0:t0 + n], [], ['cosT'])
                dma(sinT[:, :n], ropeS_in[:, t0:t0 + n], [], ['sinT'])
                for ch in range(16):
                    bi = nextbank()
                    p_, pn = PSB(bi)
                    mms([(p_[:, :n], cdiag[:, k * 16 + ch, :], xpre[:, ch, k:k + n], k == 0, k == 4) for k in range(5)],
                        ['cdiag', 'xpre'], [pn])
                    act(xact[:, ch, :n], p_[:, :n], AF.Silu, [pn, 'pv%d' % l], ['xact'], bias=pvl[:, 88 + ch:89 + ch])
                for tc in range(ntc):
                    bi = nextbank()
                    p_, pn = PSB(bi)
                    pt16 = p_.bitcast(BF16)
                    trs([(pt16[:, ch * 128:(ch + 1) * 128], xact[:, ch, tc * 128:(tc + 1) * 128], ident16) for ch in range(8)],
                        ['xact', 'c16'], [pn])
                    cp(xsst[:, tc, :], pt16, [pn], ['xsst'], eng='act')
                dma(xs_d[t0:t0 + n, :].rearrange("(c p) f -> p c f", p=128), xsst[:, :ntc, :], ['xsst'], [], q='act')
                for i in range(8):
                    ch = 8 + i
                    bi = nextbank()
                    p_, pn = PSB(bi)
                    mms([(p_[:, :n], pm16, xact[:, ch, :n], True, True)], ['c16', 'xact'], [pn])
                    tt(t1[:, :n], xact[:, ch, :n], cosT[:, :n], ALU.mult, ['xact', 'cosT'], ['t1'])
                    tt(t2[:, :n], p_[:, :n], sinT[:, :n], ALU.mult, [pn, 'sinT'], ['t2'])
                    tt(bcrot[:, i, :n], t1[:, :n], t2[:, :n], ALU.add, ['t1', 't2'], ['bcrot'])
                dma(BT_d3[:, :, t0:t0 + n], bcrot[:, 0:4, :n], ['bcrot'], [], q='act')
                dma(CT_d3[:, :, t0:t0 + n], bcrot[:, 4:8, :n], ['bcrot'], [], q='act')
                for tc in range(ntc):
                    bi = nextbank()
                    p_, pn = PSB(bi)
                    pt16 = p_.bitcast(BF16)
                    trs([(pt16[:, g * 128:(g + 1) * 128], bcrot[:, g, tc * 128:(tc + 1) * 128], ident16) for g in range(4)],
                        ['bcrot', 'c16'], [pn])
                    cp(btst[:, tc, :], pt16[:, 0:512], [pn], ['btst'], eng='act')
                dma(Bt_d[t0:t0 + n, :].rearrange("(c p) f -> p c f", p=128), btst[:, :ntc, :], ['btst'], [], q='act')
            S.barrier()
            AR.off = PERSIST
            h32 = [AR.alloc([1024], F32) for _ in range(2)]
            h16 = [AR.alloc([1024], BF16) for _ in range(2)]
            Abc = AR.alloc([32], F32)
            for d in range(2):
                S.op('dve', lambda e, d=d: e.memset(h32[d], 0.0), [], ['h32_%d' % d])
                S.op('dve', lambda e, d=d: e.memset(h16[d], 0.0), [], ['h16_%d' % d])
            act(Abc, brl[:, 32:64], AF.Exp, ['br%d' % l], ['Abc'])
            ts(Abc, Abc, -1.0, None, ALU.mult, None, ['Abc'], ['Abc'])

            class _B:
                pass
            p1bufs = [[None, None], [None, None]]
            p2bufs = [None, None]
            for d in range(2):
                for par in range(2):
                    B_ = _B()
                    B_.xs = AR.alloc([1024], BF16)
                    B_.BT = AR.alloc([4, 128], BF16)
                    B_.CT = AR.alloc([4, 128], BF16)
                    B_.Bt = AR.alloc([512], BF16)
                    B_.dt = AR.alloc([32], F32)
                    B_.dtA = AR.alloc([16], F32)
                    B_.xdt = AR.alloc([1024], BF16)
                    if d == 0:
                        B_.xsD = AR.alloc([1024], BF16)
                    B_.nega = AR.alloc([16], F32)
                    B_.expa = AR.alloc([16], F32)
                    B_.acT = AR.alloc([128], F32)
                    B_.dec = AR.alloc([16, 128], BF16)
                    B_.cdB = AR.alloc([16], F32)
                    B_.cbm = AR.alloc([4, 128], BF16)
                    B_.MT = AR.alloc([16, 128], BF16)
                    B_.xdd = AR.alloc([1024], BF16)
                    p1bufs[d][par] = B_
                C_ = _B()
                C_.yi = AR.alloc([1024], F32)
                C_.ydir = AR.alloc([1024], F32)
                C_.yp = AR.alloc([1024], F32)
                C_.zs = AR.alloc([1024], BF16)
                C_.gg = AR.alloc([1024], F32)
                C_.gjunk = AR.alloc([1024], BF16)
                C_.ss = AR.alloc([4], F32)
                C_.gn = AR.alloc([1024], BF16)
                C_.yst = AR.alloc([8, 128], BF16)
                p2bufs[d] = C_
            yssdT3 = yssdT_d.rearrange("(c p) t -> p c t", p=128)

            def r3(v):
                return v.rearrange("p (h d) -> p h d", d=64)

            def scan_p1(c, d, want_y, par):
                sf = '_%d_%d' % (d, par)
                U32 = Uf32 if d == 0 else Ub32
                lend = 127 if d == 0 else 0
                tk = c * 128
                B_ = p1bufs[d][par]
                dma(B_.xs, xs_d[tk:tk + 128, :], [], ['xs' + sf])
                dma(B_.BT, BT_d3[:, :, tk:tk + 128], [], ['BT' + sf])
                dma(B_.CT, CT_d3[:, :, tk:tk + 128], [], ['CT' + sf])
                dma(B_.Bt, Bt_d[tk:tk + 128, :], [], ['Bt' + sf])
                dma(B_.dt, dts_d[tk:tk + 128, :], [], ['dt' + sf])
                dsl = B_.dt[:, d * 16:(d + 1) * 16]
                tt(B_.dtA, dsl, Abc[:, d * 16:(d + 1) * 16], ALU.mult, ['dt' + sf, 'Abc'], ['dtA' + sf])
                tt(r3(B_.xdt), r3(B_.xs), dsl.unsqueeze(2).to_broadcast([128, 16, 64]), ALU.mult, ['xs' + sf, 'dt' + sf], ['xdt' + sf], eng='pool')
                P0, P0n = PSB(0)
                mms([(P0[:, 0:16], U32, B_.dtA, True, True)], ['c32', 'dtA' + sf], [P0n])
                mms([(P0[0:16, 64:192], B_.dtA, U32, True, True)], ['c32', 'dtA' + sf], [P0n])
                act(B_.nega, P0[:, 0:16], AF.Copy, [P0n], ['nega' + sf], scale=-1.0)
                act(B_.expa, P0[:, 0:16], AF.Exp, [P0n], ['expa' + sf])
                act(B_.acT[0:16, :], P0[0:16, 64:192], AF.Copy, [P0n], ['acT' + sf])
                for q4 in range(4):
                    pb, pbn = PSB(1 if q4 % 2 == 0 else 0)
                    lst_ = []
                    for hh in range(4):
                        lst_.append((pb[:, hh * 128:(hh + 1) * 128], sel32[:, q4 * 4 + hh, :], B_.acT[0:16, :], True, False))
                        lst_.append((pb[:, hh * 128:(hh + 1) * 128], ident16, mneg16[d], False, True))
                    mms(lst_, ['c32', 'c16', 'mneg16', 'acT' + sf], [pbn])
                    for hh in range(4):
                        h = q4 * 4 + hh
                        act(B_.dec[:, h, :], pb[:, hh * 128:(hh + 1) * 128], AF.Exp, [pbn, 'nega' + sf], ['dec' + sf], bias=B_.nega[:, h:h + 1])
                    act(B_.cdB[:, q4 * 4:q4 * 4 + 4], pb.rearrange("p (h l) -> p h l", l=128)[:, :, lend], AF.Exp, [pbn], ['cdB' + sf])
                if want_y:
                    P3, P3n = PSB(3)
                    mms([(P3[:, g * 128:(g + 1) * 128], B_.BT[:, g, :], B_.CT[:, g, :], True, True) for g in range(4)],
                        ['BT' + sf, 'CT' + sf], [P3n])
                    P33 = P3.rearrange("p (g l) -> p g l", l=128)
                    for g in range(4):
                        tt(B_.MT[:, 4 * g:4 * g + 4, :], P33[:, g:g + 1, :].to_broadcast([128, 4, 128]), B_.dec[:, 4 * g:4 * g + 4, :], ALU.mult,
                           [P3n, 'dec' + sf], ['MT' + sf])
                    if d == 0:
                        tt(r3(B_.xsD), r3(B_.xs), brl[:, 64:80].unsqueeze(2).to_broadcast([128, 16, 64]), ALU.mult,
                           ['xs' + sf, 'br%d' % l], ['xsD' + sf], eng='pool')
                tt(r3(B_.xdd), r3(B_.xdt), B_.dec[:, :, lend:lend + 1].to_broadcast([128, 16, 64]), ALU.mult,
                   ['xdt' + sf, 'dec' + sf], ['xdd' + sf], eng='pool')

            def scan_p2(c, d, second, want_y, par):
                sf = '_%d_%d' % (d, par)
                sg = '_%d' % d
                tk = c * 128
                B_ = p1bufs[d][par]
                C_ = p2bufs[d]
                if want_y:
                    for half in range(2):
                        py, pyn = PSB(4 + half)
                        lst = []
                        if d == 0:
                            lst.append((py, ident16, B_.xsD[:, half * 512:(half + 1) * 512], True, False))
                        for hh in range(8):
                            h = half * 8 + hh
                            lst.append((py[:, hh * 64:(hh + 1) * 64], B_.MT[:, h, :], B_.xdt[:, h * 64:(h + 1) * 64],
                                        d != 0, (d != 0) or hh == 7))
                        mms(lst, ['c16', 'xsD' + sf, 'MT' + sf, 'xdt' + sf], [pyn])
                for half in range(2):
                    pS, pSn = PSB(6 + half)
                    mms([(pS[:, gg * 256:(gg + 1) * 256], B_.Bt[:, (half * 2 + gg) * 128:(half * 2 + gg + 1) * 128],
                          B_.xdd[:, (half * 2 + gg) * 256:(half * 2 + gg + 1) * 256], True, True) for gg in range(2)],
                        ['Bt' + sf, 'xdd' + sf], [pSn])
                if want_y:
                    for half in range(2):
                        pyi, pyin = PSB(2)
                        mms([(pyi[:, gg * 256:(gg + 1) * 256], B_.CT[:, half * 2 + gg, :],
                              h16[d][:, (half * 2 + gg) * 256:(half * 2 + gg + 1) * 256], True, True) for gg in range(2)],
                            ['CT' + sf, 'h16' + sg], [pyin])
                        cp(C_.yi[:, half * 512:(half + 1) * 512], pyi, [pyin], ['yi' + sg], eng='act')
                    tt(r3(C_.yi), r3(C_.yi), B_.expa.unsqueeze(2).to_broadcast([128, 16, 64]), ALU.mult, ['yi' + sg, 'expa' + sf], ['yi' + sg])
                    for half in range(2):
                        py, pyn = PSB(4 + half)
                        tt(C_.ydir[:, half * 512:(half + 1) * 512], C_.yi[:, half * 512:(half + 1) * 512], py, ALU.add,
                           ['yi' + sg, pyn], ['ydir' + sg])
                tt(r3(h32[d]), r3(h32[d]), B_.cdB.unsqueeze(2).to_broadcast([128, 16, 64]), ALU.mult, ['h32' + sg, 'cdB' + sf], ['h32' + sg], eng='pool')
                for half in range(2):
                    pS, pSn = PSB(6 + half)
                    tt(h32[d][:, half * 512:(half + 1) * 512], h32[d][:, half * 512:(half + 1) * 512], pS, ALU.add,
                       ['h32' + sg, pSn], ['h32' + sg])
                cp(h16[d], h32[d], ['h32' + sg], ['h16' + sg], eng='act')
                if not want_y:
                    return
                if not second:
                    dma(yp_d[tk:tk + 128, :], C_.ydir, ['ydir' + sg], ['ypd%d' % c], q='act')
                    return
                dma(C_.yp, yp_d[tk:tk + 128, :], ['ypd%d' % c], ['yp' + sg])
                dma(C_.zs, zs_d[tk:tk + 128, :], [], ['zs' + sg])
                tt(C_.ydir, C_.ydir, C_.yp, ALU.add, ['ydir' + sg, 'yp' + sg], ['ydir' + sg])
                tt(C_.gg, C_.ydir, C_.zs, ALU.mult, ['ydir' + sg, 'zs' + sg], ['gg' + sg])
                act(C_.gjunk, C_.gg, AF.Square, ['gg' + sg], ['gjunk' + sg, 'ss' + sg], accum_out=C_.ss[:, 0:1])
                act(C_.ss[:, 1:2], C_.ss[:, 0:1], AF.Sqrt, ['ss' + sg, 'epsb'], ['ss' + sg], bias=epsb[:, 0:1], scale=1.0 / 1024)
                recip(C_.ss[:, 2:3], C_.ss[:, 1:2], ['ss' + sg], ['ss' + sg])
                act(C_.gn, C_.gg, AF.Copy, ['gg' + sg, 'ss' + sg], ['gn' + sg], scale=C_.ss[:, 2:3])
                P2, P2n = PSB(2)
                pt16 = P2.bitcast(BF16)
                trs([(pt16[:, ch * 128:(ch + 1) * 128], C_.gn[:, ch * 128:(ch + 1) * 128], ident16) for ch in range(8)],
                    ['gn' + sg, 'c16'], [P2n])
                for ch in range(8):
                    act(C_.yst[:, ch, :], pt16[:, ch * 128:(ch + 1) * 128], AF.Copy, [P2n, 'pv%d' % l], ['yst' + sg],
                        scale=pvl[:, 184 + ch:185 + ch])
                dma(yssdT3[:, :, tk:tk + 128], C_.yst, ['yst' + sg], [], q='act')

            chain_f = list(range(NCH))
            chain_b = [1, 0] + list(range(NCH - 1, 1, -1))
            seen = set()

            def wy(c):
                return not (last and c < 2)

            def do_p2(c, d, par):
                scan_p2(c, d, c in seen, wy(c), par)
                seen.add(c)
            scan_p1(chain_f[0], 0, wy(chain_f[0]), 0)
            scan_p1(chain_b[0], 1, wy(chain_b[0]), 0)
            for i in range(NCH):
                la = []
                if i + 1 < NCH:
                    S.record()
                    scan_p1(chain_f[i + 1], 0, wy(chain_f[i + 1]), (i + 1) % 2)
                    la = S.stop()
                S.record()
                do_p2(chain_b[i], 1, i % 2)
                lb = S.stop()
                S.replay([la, lb])
                la = []
                if i + 1 < NCH:
                    S.record()
                    scan_p1(chain_b[i + 1], 1, wy(chain_b[i + 1]), (i + 1) % 2)
                    la = S.stop()
                S.record()
                do_p2(chain_f[i], 0, i % 2)
                lb = S.stop()
                S.replay([la, lb])
            S.barrier()
            if 'stopB' in dbg and l == 0:
                break
            AR.off = PERSIST
            cls, rel, rep = na_classes(L)
            NQ = L // 128
            tab = AR.alloc([40, 128], BF16)
            tabE = AR.alloc([40, 128], BF16)
            kctx = AR.alloc([2, 4, 128], BF16)
            vctx = [AR.alloc([8, 65], BF16) for _ in range(2)]
            NRING = 8
            kt_buf = [AR.alloc([4, 128], BF16) for _ in range(NRING)]
            v_buf = [AR.alloc([8, 65], BF16) for _ in range(NRING)]
            ring_has = {}
            qt_buf = [AR.alloc([4, 128], BF16) for _ in range(2)]
            pT = [AR.alloc([7 * 128], BF16) for _ in range(2)]
            rec = AR.alloc([8], F32)
            yna = AR.alloc([512], BF16)
            ynast = AR.alloc([4, 128], BF16)
            kT_d3 = kT_d.rearrange("(c p) t -> p c t", p=128)
            qT_d3 = qT_d.rearrange("(c p) t -> p c t", p=128)
            ynaT3 = ynaT_d.rearrange("(c p) t -> p c t", p=128)
            for i in range(2):
                S.op('dve', lambda e, i=i: e.memset(vctx[i].rearrange("p a b -> p (a b)"), 1.0), [], ['vctx%d' % i])
                dma(kctx[:, i, :, :], kT_d3[:, :, i * 128:(i + 1) * 128], [], ['kctx'])
                dma(vctx[i][:, :, 0:64], vv_d[i * 128:(i + 1) * 128, :].rearrange("t (h d) -> t h d", d=64), [], ['vctx%d' % i])
            for i in range(NRING):
                S.op('dve', lambda e, i=i: e.memset(v_buf[i].rearrange("p a b -> p (a b)"), 1.0), [], ['vbuf%d' % i])

            def ring_load(kt):
                sl = kt % NRING
                if ring_has.get(sl) == kt:
                    return
                ring_has[sl] = kt
                ktok = CTX + kt * 128
                dma(kt_buf[sl], kT_d3[:, :, ktok:ktok + 128], [], ['ktb%d' % sl])
                dma(v_buf[sl][:, :, 0:64], vv_d[ktok:ktok + 128, :].rearrange("t (h d) -> t h d", d=64), [], ['vbuf%d' % sl])

            def na_tile(qtok, kts, use_tab, nxt=()):
                qb = qt_buf[(qtok // 128) % 2]
                qbn = 'qt_buf%d' % ((qtok // 128) % 2)
                dma(qb, qT_d3[:, :, qtok:qtok + 128], [], [qbn])
                nk = len(kts)
                for kt in list(kts) + list(nxt):
                    ring_load(kt)
                slots = [kt % NRING for kt in kts]
                nb = nk + 2
                def st_qk(h):
                    cc, e = h // 2, h % 2
                    pA, pAn = PSB(2 * e)
                    pB, pBn = PSB(2 * e + 1)
                    lstA, lstB = [], []
                    rdA, rdB = set([qbn, 'kctx']), set([qbn, 'kctx'])
                    for j in range(nb):
                        bank, lst, rd = (pA, lstA, rdA) if j < 4 else (pB, lstB, rdB)
                        col = (j % 4) * 128
                        qop = qb[e * 64:(e + 1) * 64, cc, :]
                        if j < nk:
                            rd.add('ktb%d' % slots[j])
                            lst.append((bank[:, col:col + 128], kt_buf[slots[j]][e * 64:(e + 1) * 64, cc, :], qop, True, True))
                        else:
                            lst.append((bank[:, col:col + 128], kctx[e * 64:(e + 1) * 64, j - nk, cc, :], qop, True, True))
                    mms(lstA, sorted(rdA), [pAn])
                    if lstB:
                        mms(lstB, sorted(rdB), [pBn])

                def st_sm(h):
                    cc, e = h // 2, h % 2
                    pA, pAn = PSB(2 * e)
                    pB, pBn = PSB(2 * e + 1)
                    na_ = min(nb, 4)
                    act(pT[e][:, 0:na_ * 128], pA[:, 0:na_ * 128], AF.Exp, [pAn], ['pT%d' % e])
                    if nb > 4:
                        act(pT[e][:, 512:512 + (nb - 4) * 128], pB[:, 0:(nb - 4) * 128], AF.Exp, [pBn], ['pT%d' % e])
                    if use_tab:
                        tE4 = tabE.rearrange("p (s h) q -> p s h q", h=8)
                        na4 = min(nk, 4)
                        tt(pT[e][:, 0:na4 * 128].rearrange("p (s q) -> p s q", q=128), pT[e][:, 0:na4 * 128].rearrange("p (s q) -> p s q", q=128),
                           tE4[:, 0:na4, h, :], ALU.mult, ['pT%d' % e, 'tabE'], ['pT%d' % e])
                        if nk > 4:
                            tt(pT[e][:, 512:512 + (nk - 4) * 128].rearrange("p (s q) -> p s q", q=128),
                               pT[e][:, 512:512 + (nk - 4) * 128].rearrange("p (s q) -> p s q", q=128),
                               tE4[:, 4:nk, h, :], ALU.mult, ['pT%d' % e, 'tabE'], ['pT%d' % e])

                def st_pv(h):
                    cc, e = h // 2, h % 2
                    po, pon = PSB(4 + h // 4)
                    col = (h % 4) * 65
                    lst = []
                    rd = ['pT%d' % e, 'vctx0', 'vctx1']
                    for j in range(nb):
                        if j < nk:
                            vb = v_buf[slots[j]]
                            rd.append('vbuf%d' % slots[j])
                        else:
                            vb = vctx[j - nk]
                        lst.append((po[:, col:col + 65], pT[e][:, j * 128:(j + 1) * 128], vb[:, h, :], j == 0, j == nb - 1))
                    mms(lst, rd, [pon])

                st_qk(0)
                st_qk(1)
                for h in range(8):
                    st_sm(h)
                    st_pv(h)
                    if h + 2 < 8:
                        st_qk(h + 2)
                for b4 in range(2):
                    po, pon = PSB(4 + b4)
                    po3 = po[:, 0:260].rearrange("p (h e) -> p h e", e=65)
                    recip(rec[:, b4 * 4:(b4 + 1) * 4], po3[:, :, 64], [pon], ['rec'])
                    tt(yna.rearrange("p (h d) -> p h d", d=64)[:, b4 * 4:(b4 + 1) * 4, :], po3[:, :, 0:64],
                       rec[:, b4 * 4:(b4 + 1) * 4].unsqueeze(2).to_broadcast([128, 4, 64]), ALU.mult, [pon, 'rec'], ['yna'])
                p6, p6n = PSB(6)
                pt16 = p6.bitcast(BF16)
                trs([(pt16[:, c * 128:(c + 1) * 128], yna[:, c * 128:(c + 1) * 128], ident16) for c in range(4)], ['yna', 'c16'], [p6n])
                cp(ynast, pt16[:, 0:512].rearrange("p (c t) -> p c t", t=128), [p6n], ['ynast'], eng='act')
                dma(ynaT3[:, :, qtok:qtok + 128], ynast, ['ynast'], [], q='act')

            if not last:
                for i in range(2):
                    na_tile(i * 128, [], False)
            cur = -1
            for qt in range(NQ):
                c = cls[qt]
                if c != cur:
                    cur = c
                    b0 = l * 200 + c * 40
                    dma(tab, natab16[b0:b0 + 40].rearrange("s k q -> k s q"), ['natab16_%d' % l], ['tab'])
                    act(tabE.rearrange("p a b -> p (a b)"), tab.rearrange("p a b -> p (a b)"), AF.Exp, ['tab'], ['tabE'])
                nxt = [qt + 1 + dk for dk in rel[cls[qt + 1]]] if qt + 1 < NQ else []
                na_tile(CTX + qt * 128, [qt + dk for dk in rel[c]], True, nxt)
            S.barrier()
            if 'stopC' in dbg and l == 0:
                break
            AR.off = PERSIST
            ph = {}
            ph['sq'] = AR.alloc([8, 512], BF16)
            ph['rs'] = AR.alloc([512], F32)
            ph['ntmp'] = [AR.alloc([512], F32) for _ in range(2)]
            xt2 = [AR.alloc([8, 512], F32) for _ in range(2)]
            ys = AR.alloc([8, 512], BF16)
            yn_ = AR.alloc([4, 512], BF16)
            yg = AR.alloc([4, 512], BF16)
            gt = AR.alloc([24, 512], BF16)
            mg = AR.alloc([8, 512], BF16)
            h2 = AR.alloc([8, 512], BF16)
            actb = AR.alloc([22, 512], BF16)
            m1 = [AR.alloc([512], F32) for _ in range(2)]
            m2 = [AR.alloc([512], F32) for _ in range(2)]
            m3 = [AR.alloc([512], F32) for _ in range(2)]
            sa = AR.alloc([512], BF16)
            wbuf = [AR.alloc([4096], BF16) for _ in range(3)]
            wctr[0] = 0
            ygmT3 = ygmT_d.rearrange("(c p) t -> p c t", p=128)
            gateT3 = gateT_d.rearrange("(c p) t -> p c t", p=128)
            x1T3 = x1T.rearrange("(c p) t -> p c t", p=128)
            outT3 = outT.rearrange("(c p) t -> p c t", p=128)

            def load_w(blk):
                slot = wctr[0] % 3
                wctr[0] += 1
                dma(wbuf[slot], wblk16[wb0 + blk], ['w16_%d' % (wb0 + blk)], ['wb%d' % slot])
                return wbuf[slot], 'wb%d' % slot

            dtiles = (tiles[1:] if last else tiles)

            def d_loads(ti):
                (t0_, n_) = dtiles[ti]
                dma(xt2[ti % 2][:, :, :n_], xsrc3[:, :, t0_:t0_ + n_], [], ['Dxt%d' % (ti % 2)])
                dma(ys[:, :, :n_], yssdT3[:, :, t0_:t0_ + n_], [], ['ys'])
                dma(yn_[:, :, :n_], ynaT3[:, :, t0_:t0_ + n_], [], ['yn'])
                dma(yg[:, :, :n_], ygmT3[:, :, t0_:t0_ + n_], [], ['yg'])
                dma(gt[:, :, :n_], gateT3[:, :, t0_:t0_ + n_], [], ['gt'])
            d_loads(0)
            for ti, (t0, n) in enumerate(dtiles):
                isctx = t0 < CTX
                j = 1 if isctx else 0
                xt = xt2[ti % 2]
                xtn = 'Dxt%d' % (ti % 2)
                for jb in range(4):
                    w, wn = load_w(18 + jb)
                    for oo in range(2):
                        oc = 2 * jb + oo
                        base = oo * 2048
                        pa, pan = PSB(nextbank())
                        mms([(pa[:, :n], w[:, base + kc * 128:base + (kc + 1) * 128], ys[:, kc, :n], kc == 0, kc == 7) for kc in range(8)],
                            [wn, 'ys'], [pan])
                        pb_, pbn = PSB(nextbank())
                        mms([(pb_[:, :n], w[:, base + 1024 + kc * 128:base + 1024 + (kc + 1) * 128], yn_[:, kc, :n], kc == 0, kc == 3) for kc in range(4)],
                            [wn, 'yn'], [pbn])
                        pc_, pcn = PSB(nextbank())
                        mms([(pc_[:, :n], w[:, base + 1536 + kc * 128:base + 1536 + (kc + 1) * 128], yg[:, kc, :n], kc == 0, kc == 3) for kc in range(4)],
                            [wn, 'yg'], [pcn])
                        mp = oc % 2
                        ms_ = '%d' % mp
                        tt(m1[mp][:, :n], pa[:, :n], gt[:, oc, :n], ALU.mult, [pan, 'gt'], ['m1' + ms_])
                        tt(m2[mp][:, :n], pb_[:, :n], gt[:, 8 + oc, :n], ALU.mult, [pbn, 'gt'], ['m2' + ms_])
                        tt(m3[mp][:, :n], pc_[:, :n], gt[:, 16 + oc, :n], ALU.mult, [pcn, 'gt'], ['m3' + ms_])
                        tt(m1[mp][:, :n], m1[mp][:, :n], m2[mp][:, :n], ALU.add, ['m1' + ms_, 'm2' + ms_], ['m1' + ms_], eng='pool')
                        tt(mg[:, oc, :n], m1[mp][:, :n], m3[mp][:, :n], ALU.add, ['m1' + ms_, 'm3' + ms_], ['mg'], eng='pool')
                if ti + 1 < len(dtiles):
                    d_loads(ti + 1)
                for jb in range(2):
                    w, wn = load_w(22 + jb)
                    w3 = w.rearrange("p (k c) -> p k c", c=512)
                    for oc4 in range(4):
                        oc = jb * 4 + oc4
                        p_, pn = PSB(nextbank())
                        mms([(p_[:, :n], w3[:, kc, oc4 * 128:(oc4 + 1) * 128], mg[:, kc, :n], kc == 0, kc == 7) for kc in range(8)],
                            [wn, 'mg'], [pn])
                        stt(xt[:, oc, :n], p_[:, :n], modT[l][:, 16 + oc, j:j + 1], xt[:, oc, :n], ALU.mult, ALU.add,
                            [pn, 'modT%d' % l, xtn], [xtn])
                norm_mod(xt, n, A2[l][:, :, j], modT[l][:, 24:32, j], h2, xtn, 'DhT')
                for jb in range(11):
                    w, wn = load_w(24 + jb)
                    w3 = w.rearrange("p (k c) -> p k c", c=512)
                    for hcl in range(2):
                        hc = 2 * jb + hcl
                        pa, pan = PSB(nextbank())
                        mms([(pa[:, :n], w3[:, kc, (hcl * 2) * 128:(hcl * 2 + 1) * 128], h2[:, kc, :n], kc == 0, kc == 7) for kc in range(8)],
                            [wn, 'DhT'], [pan])
                        pb_, pbn = PSB(nextbank())
                        mms([(pb_[:, :n], w3[:, kc, (hcl * 2 + 1) * 128:(hcl * 2 + 2) * 128], h2[:, kc, :n], kc == 0, kc == 7) for kc in range(8)],
                            [wn, 'DhT'], [pbn])
                        act(sa[:, :n], pa[:, :n], AF.Silu, [pan], ['sa'])
                        tt(actb[:, hc, :n], sa[:, :n], pb_[:, :n], ALU.mult, ['sa', pbn], ['actb'])
                for oc in range(8):
                    w, wn = load_w(35 + oc)
                    wv = w[:, 0:2816].rearrange("p (k c) -> p k c", c=128)
                    p_, pn = PSB(nextbank())
                    mms([(p_[:, :n], wv[:, kc, :], actb[:, kc, :n], kc == 0, kc == 21) for kc in range(22)], [wn, 'actb'], [pn])
                    stt(xt[:, oc, :n], p_[:, :n], modT[l][:, 40 + oc, j:j + 1], xt[:, oc, :n], ALU.mult, ALU.add,
                        [pn, 'modT%d' % l, xtn], [xtn])
                if last:
                    dma(outT3[:, :, t0 - CTX:t0 - CTX + n], xt[:, :, :n], [xtn], [], q='act')
                else:
                    dma(x1T3[:, :, t0:t0 + n], xt[:, :, :n], [xtn], [], q='act')
            S.barrier()
        S.emit()
    return nc


def _slab(W):
    k = W.shape[0] // 128
    c = W.shape[1]
    out = np.zeros((128, 4096), np.float32)
    out[:, :k * c] = W.reshape(k, 128, c).transpose(1, 0, 2).reshape(128, k * c)
    return out


def pack_weights(inp, l):
    w_in = inp['w_in'][l]
    blks = []
    segs = [(0, 512), (512, 1024),
            (1024, 1536), (1536, 2048), (2048, 2560), (2560, 3072),
            (3104, 3616), (3616, 4128), (4128, 4640),
            (4640, 5152), (5152, 5664)]
    segs += [(5664 + 512 * i, 5664 + 512 * (i + 1)) for i in range(6)]
    for (a, b) in segs:
        blks.append(_slab(w_in[:, a:b]))
    blks.append(_slab(w_in[:, 3072:3104]))
    wa, wb_, wc = inp['w_branch_ssd'][l], inp['w_branch_na'][l], inp['w_branch_gm'][l]
    for j in range(4):
        blk = np.zeros((128, 4096), np.float32)
        for oo in range(2):
            oc = 2 * j + oo
            base = oo * 2048
            blk[:, base:base + 1024] = wa[:, oc * 128:(oc + 1) * 128].reshape(8, 128, 128).transpose(1, 0, 2).reshape(128, 1024)
            blk[:, base + 1024:base + 1536] = wb_[:, oc * 128:(oc + 1) * 128].reshape(4, 128, 128).transpose(1, 0, 2).reshape(128, 512)
            blk[:, base + 1536:base + 2048] = wc[:, oc * 128:(oc + 1) * 128].reshape(4, 128, 128).transpose(1, 0, 2).reshape(128, 512)
        blks.append(blk)
    wo = inp['w_out'][l]
    for j in range(2):
        blks.append(_slab(wo[:, j * 512:(j + 1) * 512]))
    wf = inp['w_ffn_in'][l]
    for j in range(11):
        slab = np.zeros((1024, 512), np.float32)
        for hcl in range(2):
            for ab in range(2):
                src = ab * 2816 + (2 * j + hcl) * 128
                slab[:, (hcl * 2 + ab) * 128:(hcl * 2 + ab + 1) * 128] = wf[:, src:src + 128]
        blks.append(_slab(slab))
    wfo = inp['w_ffn_out'][l]
    for oc in range(8):
        blks.append(_slab(wfo[:, oc * 128:(oc + 1) * 128]))
    assert len(blks) == NBLK
    return np.stack(blks)


def pcol(v):
    return np.ascontiguousarray(v.reshape(-1, 128).T)


def pack_shared(inp, L, depth):
    T = CTX + L
    R = L // 64
    f32 = np.float32
    wmod = np.stack([inp['w_mod'][l].reshape(8, 128, 12, 512).transpose(2, 1, 0, 3).reshape(12, 128, 4096)
                     for l in range(depth)]).reshape(depth * 12, 128, 4096)
    wblk = np.concatenate([pack_weights(inp, l) for l in range(depth)], axis=0)
    pvec = np.zeros((depth, 128, NPV), f32)
    brow = np.zeros((depth, NBR), f32)
    wsT = np.zeros((depth, 128, 1024), f32)
    for l in range(depth):
        pvec[l, :, 0:48] = pcol(inp['b_mod'][l])
        pvec[l, :, 48:56] = pcol(inp['norm1'][l])
        pvec[l, :, 56:64] = pcol(inp['norm2'][l])
        pvec[l, :, 64:88] = pcol(inp['b_gate'][l])
        pvec[l, :, 88:104] = pcol(inp['conv_b'][l])
        for k in range(5):
            pvec[l, :, 104 + k * 16:104 + (k + 1) * 16] = pcol(inp['conv_w'][l][k])
        pvec[l, :, 184:192] = pcol(inp['ssd_norm'][l])
        pvec[l, :, 192] = np.tile(inp['q_norm'][l], 2)
        pvec[l, :, 193] = np.tile(inp['k_norm'][l], 2)
        pvec[l, :, 194:202] = inp['b_spatial'][l].T
        brow[l, 0:32] = inp['dt_bias'][l].reshape(-1)
        brow[l, 32:64] = inp['a_log'][l].reshape(-1)
        brow[l, 64:80] = inp['d_skip'][l]
        brow[l, 80:592] = inp['gm_norm'][l]
        wsT[l] = inp['w_spatial'][l].transpose(2, 0, 1).reshape(128, 1024)
    freqs = (np.float32(10000.0) ** (-np.arange(32, dtype=f32) / np.float32(32))).astype(f32)
    pos = np.arange(L)
    ang_row = (pos // 64).astype(f32)[:, None] * freqs
    ang_col = (pos % 64).astype(f32)[:, None] * freqs
    ropeC = np.ones((128, T), f32)
    ropeS = np.zeros((128, T), f32)
    for half, ang in ((0, ang_row), (1, ang_col)):
        c = np.cos(ang).T.astype(f32)
        s = np.sin(ang).T.astype(f32)
        ropeC[half * 64:half * 64 + 32, CTX:] = c
        ropeC[half * 64 + 32:half * 64 + 64, CTX:] = c
        ropeS[half * 64:half * 64 + 32, CTX:] = s
        ropeS[half * 64 + 32:half * 64 + 64, CTX:] = s
    consts = np.zeros((128, NCONST), f32)
    i = np.arange(128)
    consts[:, 0:128] = np.eye(128, dtype=f32)
    consts[:, 128:256] = (i[:, None] <= i[None, :]).astype(f32)
    consts[:, 256:384] = (i[:, None] >= i[None, :]).astype(f32)
    consts[:, 384:512] = 1.0
    consts[:, 512:640] = ((i[:, None] // 64) == (i[None, :] // 64)).astype(f32)
    pm = np.zeros((128, 128), f32)
    for base in (0, 64):
        for n in range(32):
            pm[base + n + 32, base + n] = -1.0
            pm[base + n, base + n + 32] = 1.0
    consts[:, 640:768] = pm
    sel = np.zeros((16, 16, 128), f32)
    for h in range(16):
        sel[h, h, :] = 1.0
    consts[0:16, 768:768 + 2048] = sel.reshape(16, 2048)
    cls, rel, rep = na_classes(L)

    def rs_(r):
        return min(max(r - 4, 0), R - 8)
    natab = np.full((depth, 5, 5, 8, 128, 128), NEG, f32)
    kc = np.arange(64)
    qc = np.arange(64)
    cs = np.clip(qc - 8, 0, 48)
    colok = (kc[:, None] >= cs[None, :]) & (kc[:, None] < cs[None, :] + 16)
    ci = np.clip(kc[:, None] - qc[None, :] + 15, 0, 30)
    for c, qt in rep.items():
        for slot, dk in enumerate(rel[c]):
            kt = qt + dk
            for a in range(2):
                r = 2 * qt + a
                for b in range(2):
                    krow = 2 * kt + b
                    if not (rs_(r) <= krow < rs_(r) + 8):
                        continue
                    ri = krow - r + 7
                    for l in range(depth):
                        vals = inp['rpb'][l][:, ri, :][:, ci]
                        blk = np.where(colok[None], vals, np.float32(NEG))
                        natab[l, c, slot, :, b * 64:(b + 1) * 64, a * 64:(a + 1) * 64] = blk
    natab = natab.reshape(depth * 200, 128, 128)
    return dict(wmod=wmod, wblk=wblk, pvec=pvec, brow=brow, wsT=wsT, ropeC=ropeC, ropeS=ropeS,
                natab=natab, consts=consts)


def pack_core(inp, b, shared):
    xT = np.ascontiguousarray(np.concatenate([inp['ctx'][b].T, inp['x'][b].T], axis=1))
    cT = np.zeros((128, 8, 2), np.float32)
    cT[:, :, 0] = pcol(inp['c'][b])
    cT[:, :, 1] = pcol(inp['c_ctx'])
    m = dict(shared)
    m['xT'] = xT
    m['cT'] = cT.reshape(128, 16)
    return m


_NC_CACHE = {}


def kernel(**inputs):
    inp = {k: np.asarray(v, dtype=np.float32) for k, v in inputs.items()}
    B, L, _ = inp['x'].shape
    depth = inp['w_in'].shape[0]
    key = (L, depth)
    if key not in _NC_CACHE:
        _NC_CACHE[key] = build(L, depth)
    nc = _NC_CACHE[key]
    shared = pack_shared(inp, L, depth)
    in_maps = [pack_core(inp, b, shared) for b in range(B)]
    res = run_bass_kernel_spmd(nc, in_maps, core_ids=list(range(B)))
    out = np.stack([np.ascontiguousarray(res.results[b]['outT'].T) for b in range(B)])
    return out.astype(np.float32)
```

```python
import math
import numpy as np
import concourse.bass as bass
import concourse.mybir as mybir
from concourse.bass_utils import run_bass_kernel_spmd
from contextlib import ExitStack

F32 = mybir.dt.float32
BF16 = mybir.dt.bfloat16
U8 = mybir.dt.uint8
AF = mybir.ActivationFunctionType
ALU = mybir.AluOpType
AX = mybir.AxisListType

D = 1024
CTX = 256
EPS = 1e-6
NBLK = 43
NPV = 202
NBR = 592
NCONST = 768 + 2048
NEG = -30000.0

ENGS = ['pe', 'act', 'dve', 'pool', 'sp']
N_DSEM = 48
DSEM_RANGE = {'sp': (0, 22), 'act': (22, 32), 'pool': (32, 48)}


class Sched:
    def __init__(self, nc):
        self.nc = nc
        self.ops = {e: [] for e in ENGS}
        self.last_write = {}
        self.readers = {}
        self.known = {e: {} for e in ENGS}
        self.dsem_uses = [0] * N_DSEM
        self.dsem_next = {'sp': 0, 'pool': 0, 'act': 0}
        self.signal = {e: set() for e in ENGS}
        self.rec = None

    def record(self):
        self.rec = []

    def stop(self):
        r = self.rec
        self.rec = None
        return r

    def replay(self, lists):
        idx = [0] * len(lists)
        tot = [max(1, len(x)) for x in lists]
        while True:
            live = [k for k in range(len(lists)) if idx[k] < len(lists[k])]
            if not live:
                break
            k = min(live, key=lambda k: idx[k] / tot[k])
            self.op(*lists[k][idx[k]])
            idx[k] += 1

    def _need(self, eng, tok, waits):
        kind, key, val = tok
        if kind == 'eng' and key == eng and eng in ('pe', 'sp'):
            return
        k = (kind, key)
        if self.known[eng].get(k, 0) >= val:
            return
        self.known[eng][k] = val
        waits[k] = max(waits.get(k, 0), val)
        if kind == 'eng':
            self.signal[key].add(val)

    def op(self, eng, fn, reads=(), writes=(), dma=False):
        if self.rec is not None:
            self.rec.append((eng, fn, tuple(reads), tuple(writes), dma))
            return None
        waits = {}
        for b in reads:
            w = self.last_write.get(b)
            if w is not None:
                self._need(eng, w, waits)
        for b in writes:
            w = self.last_write.get(b)
            if w is not None:
                self._need(eng, w, waits)
            for r in self.readers.get(b, ()):
                self._need(eng, r, waits)
        idx = len(self.ops[eng]) + 1
        if dma:
            lo, hi = DSEM_RANGE[eng]
            j = lo + self.dsem_next[eng]
            self.dsem_next[eng] = (self.dsem_next[eng] + 1) % (hi - lo)
            n = self.dsem_uses[j]
            if n > 0:
                self._need(eng, ('dma', j, 16 * n), waits)
            self.dsem_uses[j] = n + 1
            tok = ('dma', j, 16 * (n + 1))
        else:
            tok = ('eng', eng, idx)
        for b in reads:
            self.readers.setdefault(b, []).append(tok)
        for b in writes:
            self.last_write[b] = tok
            self.readers[b] = []
        self.ops[eng].append((fn, waits, tok))
        return tok

    def barrier(self):
        toks = []
        for e in ENGS:
            for i in range(len(self.ops[e]), 0, -1):
                t = self.ops[e][i - 1][2]
                if t is not None and t[0] == 'eng':
                    toks.append(t)
                    break
        for j in range(N_DSEM):
            if self.dsem_uses[j] > 0:
                toks.append(('dma', j, 16 * self.dsem_uses[j]))
        for e in ENGS:
            waits = {}
            for t in toks:
                self._need(e, t, waits)
            self.ops[e].append((None, waits, None))
        self.last_write.clear()
        self.readers.clear()

    def emit(self):
        nc = self.nc
        with ExitStack() as st:
            esem = {e: st.enter_context(nc.semaphore('es_' + e)) for e in ENGS}
            dsem = [st.enter_context(nc.semaphore('ds_%d' % j)) for j in range(N_DSEM)]
            block = st.enter_context(nc.Block())
            sigmap = {}
            for e in ENGS:
                cnt = 0
                m = {}
                ss = self.signal[e]
                for i in range(1, len(self.ops[e]) + 1):
                    if i in ss:
                        cnt += 1
                        m[i] = cnt
                sigmap[e] = m

            def run(e, engobj):
                for i, (fn, waits, tok) in enumerate(self.ops[e], start=1):
                    for (kind, key), val in waits.items():
                        if kind == 'eng':
                            engobj.wait_ge(esem[key], sigmap[key][val])
                        else:
                            engobj.wait_ge(dsem[key], val)
                    if fn is None:
                        continue
                    ins = fn(engobj)
                    if tok[0] == 'dma':
                        ins.then_inc(dsem[tok[1]], 16)
                    elif i in sigmap[e]:
                        ins.then_inc(esem[e], 1)

            @block.tensor
            def _(eng):
                run('pe', eng)

            @block.scalar
            def _(eng):
                run('act', eng)

            @block.vector
            def _(eng):
                run('dve', eng)

            @block.gpsimd
            def _(eng):
                run('pool', eng)

            @block.sync
            def _(eng):
                run('sp', eng)


class Arena:
    def __init__(self, ap, size):
        self.ap = ap
        self.size = size
        self.off = 0

    def alloc(self, free_shape, dt):
        esz = 4 if dt == F32 else 2
        n = 1
        for s in free_shape:
            n *= s
        nb = (n * esz + 63) // 64 * 64
        assert self.off + nb <= self.size, "SBUF arena overflow %d + %d" % (self.off, nb)
        v = self.ap[:, self.off:self.off + n * esz].bitcast(dt)
        self.off += nb
        if len(free_shape) == 2:
            v = v.rearrange("p (a b) -> p a b", b=free_shape[1])
        elif len(free_shape) == 3:
            v = v.rearrange("p (a b c) -> p a b c", b=free_shape[1], c=free_shape[2])
        return v


def na_classes(L):
    R = L // 64
    NQ = R // 2

    def rs(r):
        return min(max(r - 4, 0), R - 8)
    info = []
    for qt in range(NQ):
        r0, r1 = 2 * qt, 2 * qt + 1
        lo = min(rs(r0), rs(r1)) // 2
        hi = (max(rs(r0), rs(r1)) + 7) // 2
        info.append(tuple(k - qt for k in range(lo, hi + 1)))
    cls = []
    for qt in range(NQ):
        if qt == 0:
            cls.append(0)
        elif qt == 1:
            cls.append(1)
        elif qt == NQ - 2:
            cls.append(3)
        elif qt == NQ - 1:
            cls.append(4)
        else:
            cls.append(2)
    rel = {}
    rep = {}
    for qt in range(NQ):
        c = cls[qt]
        if c in rel:
            assert rel[c] == info[qt], (c, rel[c], info[qt])
        else:
            rel[c] = info[qt]
            rep[c] = qt
    return cls, rel, rep


def build(L, depth=2, dbg=()):
    T = CTX + L
    NCH = T // 128
    TP = T + 8
    tiles = [(0, 256)] + [(CTX + 512 * i, 512) for i in range(L // 512)]
    nc = bass.Bass("TRN2", target_bir_lowering=False)

    def din(name, shape, dt=F32):
        return nc.dram_tensor(name, shape, dt, kind="ExternalInput").ap()

    def dscr(name, shape, dt):
        kind = "ExternalOutput" if name in dbg else "Internal"
        return nc.dram_tensor(name, shape, dt, kind=kind).ap()

    xT_in = din("xT", [D, T])
    cT_in = din("cT", [128, 16])
    wmod_in = din("wmod", [depth * 12, 128, 4096])
    wblk_in = din("wblk", [depth * NBLK, 128, 4096])
    pvec_in = din("pvec", [depth, 128, NPV])
    brow_in = din("brow", [depth, NBR])
    wsT_in = din("wsT", [depth, 128, 1024])
    ropeC_in = din("ropeC", [128, T])
    ropeS_in = din("ropeS", [128, T])
    natab_in = din("natab", [depth * 200, 128, 128])
    consts_in = din("consts", [128, NCONST])
    outT = nc.dram_tensor("outT", [D, L], F32, kind="ExternalOutput").ap()

    wblk16 = dscr("wblk16", [depth * NBLK, 128, 4096], BF16)
    natab16 = dscr("natab16", [depth * 200, 128, 128], BF16)
    x1T = dscr("x1T", [D, T], F32)
    zs_d = dscr("zs", [T, 1024], BF16)
    xbcT_d = dscr("xbcT", [2048, TP], BF16)
    dts_d = dscr("dts", [T, 32], F32)
    qT_d = dscr("qT", [512, T], BF16)
    kT_d = dscr("kT", [512, T], BF16)
    vv_d = dscr("vv", [T, 512], BF16)
    ygmT_d = dscr("ygmT", [512, T], BF16)
    gateT_d = dscr("gateT", [3072, T], BF16)
    xs_d = dscr("xs", [T, 1024], BF16)
    BT_d = dscr("BTs", [512, T], BF16)
    CT_d = dscr("CTs", [512, T], BF16)
    Bt_d = dscr("Btok", [T, 512], BF16)
    yp_d = dscr("ypart", [T, 1024], F32)
    yssdT_d = dscr("yssdT", [1024, T], BF16)
    ynaT_d = dscr("ynaT", [512, T], BF16)

    st = ExitStack()
    with st:
        ARENA = 186 * 1024
        arena_t = st.enter_context(nc.sbuf_tensor("arena", [128, ARENA], U8))
        ps = [st.enter_context(nc.psum_tensor("ps%d" % i, [128, 512], F32)) for i in range(8)]
        S = Sched(nc)
        AR = Arena(arena_t, ARENA)
        pb_ctr = [0]

        def PSB(i):
            return ps[i][:], 'P%d' % i

        def nextbank(lo=0, hi=8):
            i = lo + pb_ctr[0] % (hi - lo)
            pb_ctr[0] += 1
            return i

        def dma(out, in_, reads, writes, q='sp'):
            return S.op(q, lambda e: e.dma_start(out=out, in_=in_), reads, writes, dma=True)

        def act(out, in_, func, reads, writes, bias=None, scale=None, accum_out=None):
            kw = {}
            if bias is not None:
                kw['bias'] = bias
            if scale is not None:
                kw['scale'] = scale
            if accum_out is not None:
                kw['accum_out'] = accum_out
            return S.op('act', lambda e: e.activation(out=out, in_=in_, func=func, **kw), reads, writes)

        def tt(out, in0, in1, op, reads, writes, eng='dve'):
            return S.op(eng, lambda e: e.tensor_tensor(out=out, in0=in0, in1=in1, op=op), reads, writes)

        def ts(out, in0, s1, s2, op0, op1, reads, writes, eng='dve'):
            if s2 is None:
                return S.op(eng, lambda e: e.tensor_scalar(out=out, in0=in0, scalar1=s1, scalar2=None, op0=op0), reads, writes)
            return S.op(eng, lambda e: e.tensor_scalar(out=out, in0=in0, scalar1=s1, scalar2=s2, op0=op0, op1=op1), reads, writes)

        def stt(out, in0, scalar, in1, op0, op1, reads, writes):
            return S.op('dve', lambda e: e.scalar_tensor_tensor(out=out, in0=in0, scalar=scalar, in1=in1, op0=op0, op1=op1), reads, writes)

        def recip(out, in_, reads, writes):
            return S.op('dve', lambda e: e.reciprocal(out=out, in_=in_), reads, writes)

        def cp(out, in_, reads, writes, eng='dve'):
            if eng == 'act':
                return S.op('act', lambda e: e.copy(out=out, in_=in_), reads, writes)
            return S.op(eng, lambda e: e.tensor_copy(out=out, in_=in_), reads, writes)

        def mms(lst, reads, writes):
            def fn(e):
                ins = None
                for (o, l, r, s0, s1) in lst:
                    ins = e.matmul(o, lhsT=l, rhs=r, start=s0, stop=s1)
                return ins
            return S.op('pe', fn, reads, writes)

        def trs(lst, reads, writes):
            def fn(e):
                ins = None
                for (o, i_, idn) in lst:
                    ins = e.transpose(o, i_, idn)
                return ins
            return S.op('pe', fn, reads, writes)

        c32 = AR.alloc([NCONST], F32)
        c16 = AR.alloc([512], BF16)
        epsb = AR.alloc([1], F32)
        ident32 = c32[:, 0:128]
        Uf32 = c32[:, 128:256]
        Ub32 = c32[:, 256:384]
        sel32 = c32[0:16, 768:768 + 2048].rearrange("p (h s) -> p h s", s=128)
        ident16 = c16[:, 0:128]
        ones16 = c16[:, 128:256]
        bo16 = c16[:, 256:384]
        pm16 = c16[:, 384:512]
        pv = [AR.alloc([NPV], F32) for _ in range(depth)]
        br = [AR.alloc([NBR], F32) for _ in range(depth)]
        modT = [AR.alloc([48, 2], F32) for _ in range(depth)]
        A1 = [AR.alloc([8, 2], F32) for _ in range(depth)]
        A2 = [AR.alloc([8, 2], F32) for _ in range(depth)]
        scs = AR.alloc([8, 2], F32)
        mneg16 = [AR.alloc([128], BF16) for _ in range(2)]
        PERSIST = AR.off

        dma(c32, consts_in, [], ['c32'])
        S.op('dve', lambda e: e.memset(epsb, EPS), [], ['epsb'])
        cp(c16[:, 0:128], c32[:, 0:128], ['c32'], ['c16'])
        cp(c16[:, 128:512], c32[:, 384:768], ['c32'], ['c16'])
        for d_ in range(2):
            ts(mneg16[d_], (Uf32 if d_ == 0 else Ub32), -1.0, -NEG, ALU.add, ALU.mult, ['c32'], ['mneg16'])
        for l in range(depth):
            dma(pv[l], pvec_in[l], [], ['pv%d' % l])
            dma(br[l], brow_in[l:l + 1, :].partition_broadcast(128), [], ['br%d' % l])
        wsrc = wblk_in.rearrange("b p (h e) -> b p h e", e=2048)
        wdst = wblk16.rearrange("b p (h e) -> b p h e", e=2048)
        CB = 4

        def cast_weights(l_):
            for b0 in range(l_ * NBLK, (l_ + 1) * NBLK, CB):
                b1 = min((l_ + 1) * NBLK, b0 + CB)
                dma(wdst[b0:b1], wsrc[b0:b1], [], ['w16_%d' % b for b in range(b0, b1)], q='pool')

        def cast_natab(l_):
            for k in range(0, 200, 50):
                dma(natab16[l_ * 200 + k:l_ * 200 + k + 50], natab_in[l_ * 200 + k:l_ * 200 + k + 50], [], ['natab16_%d' % l_], q='pool')
        cast_weights(0)

        sc_raw = AR.alloc([16], F32)
        wm = [AR.alloc([8, 512], F32) for _ in range(2)]
        dma(sc_raw, cT_in, [], ['sc_raw'])
        act(scs.rearrange("p a b -> p (a b)"), sc_raw, AF.Silu, ['sc_raw'], ['scs'])
        for l in range(depth):
            psM, psMn = PSB(0)
            for blk in range(12):
                slot = blk % 2
                dma(wm[slot].rearrange("p a b -> p (a b)"), wmod_in[l * 12 + blk], [], ['wm%d' % slot])
                lst = []
                for oc4 in range(4):
                    oc = blk * 4 + oc4
                    for kc in range(8):
                        lst.append((psM[:, oc * 2:oc * 2 + 2], wm[slot][:, kc, oc4 * 128:(oc4 + 1) * 128], scs[:, kc, :], kc == 0, kc == 7))
                mms(lst, ['wm%d' % slot, 'scs'], [psMn])
            tt(modT[l], psM[:, 0:96].rearrange("p (a b) -> p a b", b=2), pv[l][:, 0:48].unsqueeze(2).to_broadcast([128, 48, 2]),
               ALU.add, [psMn, 'pv%d' % l], ['modT%d' % l])
            for (Ax, sci, nrm) in ((A1[l], 8, 48), (A2[l], 32, 56)):
                ts(Ax, modT[l][:, sci:sci + 8, :], 1.0, None, ALU.add, None, ['modT%d' % l], ['A%d' % l])
                tt(Ax, Ax, pv[l][:, nrm:nrm + 8].unsqueeze(2).to_broadcast([128, 8, 2]), ALU.mult, ['A%d' % l, 'pv%d' % l], ['A%d' % l])
        S.barrier()

        def norm_sq(xt, n, xtn):
            act(ph['sq'][:, :, :n], xt[:, :, :n], AF.Square, [xtn], ['sq'])

        def norm_rest(xt, n, Acol, shcol, hT, xtn, hTn):
            sq = ph['sq']
            bi = nextbank()
            pss, pn = PSB(bi)
            mms([(pss[:, :n], ones16, sq[:, c, :n], c == 0, c == 7) for c in range(8)], ['sq', 'c16'], [pn])
            rs = ph['rs']
            act(rs[:, :n], pss[:, :n], AF.Sqrt, [pn, 'epsb'], ['rs'], bias=epsb[:, 0:1], scale=1.0 / D)
            recip(rs[:, :n], rs[:, :n], ['rs'], ['rs'])
            for c in range(8):
                tmp = ph['ntmp'][c % 2]
                tt(tmp[:, :n], xt[:, c, :n], rs[:, :n], ALU.mult, [xtn, 'rs'], ['ntmp%d' % (c % 2)])
                act(hT[:, c, :n], tmp[:, :n], AF.Identity, ['ntmp%d' % (c % 2)], [hTn], bias=shcol[:, c:c + 1], scale=Acol[:, c:c + 1])

        def norm_mod(xt, n, Acol, shcol, hT, xtn, hTn):
            norm_sq(xt, n, xtn)
            norm_rest(xt, n, Acol, shcol, hT, xtn, hTn)

        def col_of(t):
            return 2 + t if t < CTX else 6 + t

        for l in range(depth):
            last = (l == depth - 1)
            xsrc = xT_in if l == 0 else x1T
            xsrc3 = xsrc.rearrange("(c p) t -> p c t", p=128)
            wb0 = l * NBLK
            pvl = pv[l]
            brl = br[l]
            AR.off = PERSIST
            ph = {}
            ph['sq'] = AR.alloc([8, 512], BF16)
            ph['rs'] = AR.alloc([512], F32)
            ph['ntmp'] = [AR.alloc([512], F32) for _ in range(2)]
            xt = AR.alloc([8, 512], F32)
            hT2 = [AR.alloc([8, 512], BF16) for _ in range(2)]
            wbuf = [AR.alloc([4096], BF16) for _ in range(3)]
            wsT32 = AR.alloc([1024], F32)
            wsT16 = AR.alloc([8, 128], BF16)
            zst = AR.alloc([4, 1024], BF16)
            xbcst = AR.alloc([16, 512], BF16)
            dst_ = AR.alloc([4, 32], F32)
            dtmp = AR.alloc([32], F32)
            qkst = AR.alloc([8, 512], BF16)
            sq16 = [AR.alloc([512], BF16) for _ in range(2)]
            rs2 = [AR.alloc([512], F32) for _ in range(2)]
            qtmp = [AR.alloc([512], F32) for _ in range(2)]
            qraw = [AR.alloc([512], F32) for _ in range(2)]
            vst = AR.alloc([4, 512], BF16)
            u16 = AR.alloc([4, 512], BF16)
            vg = AR.alloc([512], F32)
            vjunk = AR.alloc([512], BF16)
            ssv4 = AR.alloc([4, 4], F32)
            vn16 = [AR.alloc([512], BF16) for _ in range(4)]
            gtmp = AR.alloc([512], F32)
            ygm16 = [AR.alloc([512], BF16) for _ in range(4)]
            ygst = AR.alloc([4, 512], BF16)
            gst2 = [AR.alloc([4, 512], BF16) for _ in range(2)]
            dma(wsT32, wsT_in[l], [], ['wsT32'])
            cp(wsT16.rearrange("p a b -> p (a b)"), wsT32, ['wsT32'], ['wsT16'])
            wctr = [0]

            def load_w(blk):
                slot = wctr[0] % 3
                wctr[0] += 1
                dma(wbuf[slot], wblk16[wb0 + blk], ['w16_%d' % (wb0 + blk)], ['wb%d' % slot])
                return wbuf[slot], 'wb%d' % slot

            def a_norm1(ti):
                (t0_, n_) = tiles[ti]
                dma(xt[:, :, :n_], xsrc3[:, :, t0_:t0_ + n_], [], ['Axt'])
                norm_sq(xt, n_, 'Axt')

            def a_norm2(ti):
                (t0_, n_) = tiles[ti]
                j_ = 1 if t0_ < CTX else 0
                norm_rest(xt, n_, A1[l][:, :, j_], modT[l][:, 0:8, j_], hT2[ti % 2], 'Axt', 'AhT%d' % (ti % 2))
            a_norm1(0)
            a_norm2(0)
            for ti, (t0, n) in enumerate(tiles):
                isctx = t0 < CTX
                j = 1 if isctx else 0
                ntc = n // 128
                hT = hT2[ti % 2]
                hTn = 'AhT%d' % (ti % 2)
                if ti == len(tiles) // 2:
                    cast_natab(l)
                    if l + 1 < depth:
                        cast_weights(l + 1)

                def fm_group(w3, oc4, tagw):
                    bi = nextbank()
                    p_, pn = PSB(bi)
                    mms([(p_[:, :n], w3[:, kc, oc4 * 128:(oc4 + 1) * 128], hT[:, kc, :n], kc == 0, kc == 7) for kc in range(8)],
                        [tagw, hTn], [pn])
                    return p_, pn

                def tm_group(w3, tc, ncols, tagw):
                    bi = nextbank()
                    p_, pn = PSB(bi)
                    mms([(p_[:, :ncols], hT[:, kc, tc * 128:(tc + 1) * 128], w3[:, kc, :ncols], kc == 0, kc == 7) for kc in range(8)],
                        [tagw, hTn], [pn])
                    return p_, pn

                for blk in range(2):
                    w, wn = load_w(blk)
                    w3 = w.rearrange("p (k c) -> p k c", c=512)
                    for tc in range(ntc):
                        p_, pn = tm_group(w3, tc, 512, wn)
                        act(zst[:, tc, blk * 512:(blk + 1) * 512], p_, AF.Silu, [pn], ['zst'])
                dma(zs_d[t0:t0 + n, :].rearrange("(c p) f -> p c f", p=128), zst[:, :ntc, :], ['zst'], [], q='act')
                if ti + 1 < len(tiles):
                    a_norm1(ti + 1)
                for blk in range(2, 6):
                    w, wn = load_w(blk)
                    w3 = w.rearrange("p (k c) -> p k c", c=512)
                    for oc4 in range(4):
                        p_, pn = fm_group(w3, oc4, wn)
                        ch = (blk - 2) * 4 + oc4
                        if ch % 2 == 0:
                            cp(xbcst[:, ch, :n], p_[:, :n], [pn], ['xbcst'], eng='act')
                        else:
                            cp(xbcst[:, ch, :n], p_[:, :n], [pn], ['xbcst'])
                c0 = col_of(t0)
                dma(xbcT_d.rearrange("(c p) t -> p c t", p=128)[:, :, c0:c0 + n], xbcst[:, :, :n], ['xbcst'], [], q='act')
                if ti + 1 < len(tiles):
                    a_norm2(ti + 1)
                gi = 0
                pend = None

                def qk_tail(par, blk, oc4):
                    sfx = '%d' % par
                    b2 = nextbank()
                    p2, p2n = PSB(b2)
                    mms([(p2[:, :n], bo16, sq16[par][:, :n], True, True)], ['sq16' + sfx, 'c16'], [p2n])
                    act(rs2[par][:, :n], p2[:, :n], AF.Sqrt, [p2n, 'epsb'], ['rs2' + sfx], bias=epsb[:, 0:1], scale=1.0 / 64)
                    recip(rs2[par][:, :n], rs2[par][:, :n], ['rs2' + sfx], ['rs2' + sfx])
                    stt(qtmp[par][:, :n], qraw[par][:, :n], 0.125 if blk == 6 else 1.0, rs2[par][:, :n], ALU.mult, ALU.mult,
                        ['qraw' + sfx, 'rs2' + sfx], ['qtmp' + sfx])
                    wcol = 192 if blk == 6 else 193
                    act(qkst[:, (blk - 6) * 4 + oc4, :n], qtmp[par][:, :n], AF.Copy, ['qtmp' + sfx, 'pv%d' % l], ['qkst'],
                        scale=pvl[:, wcol:wcol + 1])
                for blk in range(6, 8):
                    w, wn = load_w(blk)
                    w3 = w.rearrange("p (k c) -> p k c", c=512)
                    for oc4 in range(4):
                        par = gi % 2
                        gi += 1
                        sfx = '%d' % par
                        p_, pn = fm_group(w3, oc4, wn)
                        act(sq16[par][:, :n], p_[:, :n], AF.Square, [pn], ['sq16' + sfx])
                        cp(qraw[par][:, :n], p_[:, :n], [pn], ['qraw' + sfx], eng='act')
                        if pend is not None:
                            qk_tail(*pend)
                        pend = (par, blk, oc4)
                qk_tail(*pend)
                dma(qT_d.rearrange("(c p) t -> p c t", p=128)[:, :, t0:t0 + n], qkst[:, 0:4, :n], ['qkst'], [], q='act')
                dma(kT_d.rearrange("(c p) t -> p c t", p=128)[:, :, t0:t0 + n], qkst[:, 4:8, :n], ['qkst'], [], q='act')
                w, wn = load_w(8)
                w3 = w.rearrange("p (k c) -> p k c", c=512)
                for tc in range(ntc):
                    p_, pn = tm_group(w3, tc, 512, wn)
                    cp(vst[:, tc, :], p_, [pn], ['vst'], eng='act')
                dma(vv_d[t0:t0 + n, :].rearrange("(c p) f -> p c f", p=128), vst[:, :ntc, :], ['vst'], [], q='act')
                w, wn = load_w(9)
                w3 = w.rearrange("p (k c) -> p k c", c=512)
                for tc in range(ntc):
                    p_, pn = tm_group(w3, tc, 512, wn)
                    act(u16[:, tc, :], p_, AF.Gelu_apprx_tanh, [pn], ['u16'])
                w, wn = load_w(10)
                w3 = w.rearrange("p (k c) -> p k c", c=512)
                for tc in range(ntc):
                    p_, pn = tm_group(w3, tc, 512, wn)
                    act(vg, p_, AF.Gelu_apprx_tanh, [pn], ['vg'])
                    act(vjunk, vg, AF.Square, ['vg'], ['vjunk', 'ssv%d' % tc], accum_out=ssv4[:, tc, 0:1])
                    act(ssv4[:, tc, 1:2], ssv4[:, tc, 0:1], AF.Sqrt, ['ssv%d' % tc, 'epsb'], ['ssv%d' % tc], bias=epsb[:, 0:1], scale=1.0 / 512)
                    recip(ssv4[:, tc, 2:3], ssv4[:, tc, 1:2], ['ssv%d' % tc], ['ssv%d' % tc])
                    stt(vn16[tc], vg, ssv4[:, tc, 2:3], brl[:, 80:592], ALU.mult, ALU.mult, ['vg', 'ssv%d' % tc, 'br%d' % l], ['vn16_%d' % tc])

                def gate_blocks(blks):
                    for blk in blks:
                        w, wn = load_w(blk)
                        w3 = w.rearrange("p (k c) -> p k c", c=512)
                        for oc4 in range(4):
                            p_, pn = fm_group(w3, oc4, wn)
                            gc = (blk - 11) * 4 + oc4
                            gsl = gst2[blk % 2]
                            act(gsl[:, oc4, :n], p_[:, :n], AF.Sigmoid, [pn, 'pv%d' % l], ['gst%d' % (blk % 2)], bias=pvl[:, 64 + gc:65 + gc])
                        r0 = (blk - 11) * 512
                        dma(gateT_d[r0:r0 + 512, :].rearrange("(c p) t -> p c t", p=128)[:, :, t0:t0 + n], gst2[blk % 2][:, :, :n],
                            ['gst%d' % (blk % 2)], [], q='act')
                gate_blocks([11, 12, 13])
                for tc in range(ntc):
                    bm = nextbank()
                    pm_, pmn = PSB(bm)
                    mms([(pm_[:, g * 64:(g + 1) * 64], wsT16[:, g, :], vn16[tc][:, g * 64:(g + 1) * 64], True, True) for g in range(8)],
                        ['wsT16', 'vn16_%d' % tc], [pmn])
                    tt(gtmp.rearrange("p (g d) -> p g d", d=64), pm_.rearrange("p (g d) -> p g d", d=64),
                       pvl[:, 194:202].unsqueeze(2).to_broadcast([128, 8, 64]), ALU.add, [pmn, 'pv%d' % l], ['gtmp'])
                    tt(ygm16[tc], gtmp, u16[:, tc, :], ALU.mult, ['gtmp', 'u16'], ['ygm16_%d' % tc])
                gate_blocks([14, 15])
                for tc in range(ntc):
                    bt_ = nextbank()
                    pt_, ptn = PSB(bt_)
                    pt16 = pt_.bitcast(BF16)
                    trs([(pt16[:, c * 128:(c + 1) * 128], ygm16[tc][:, c * 128:(c + 1) * 128], ident16) for c in range(4)],
                        ['ygm16_%d' % tc, 'c16'], [ptn])
                    cp(ygst[:, :, tc * 128:(tc + 1) * 128], pt16[:, 0:512].rearrange("p (c t) -> p c t", t=128), [ptn], ['ygst'], eng='act')
                dma(ygmT_d.rearrange("(c p) t -> p c t", p=128)[:, :, t0:t0 + n], ygst[:, :, :n], ['ygst'], [], q='act')
                gate_blocks([16])
                w, wn = load_w(17)
                w3 = w[:, 0:256].rearrange("p (k c) -> p k c", c=32)
                for tc in range(ntc):
                    p_, pn = tm_group(w3, tc, 32, wn)
                    tt(dtmp, p_[:, 0:32], brl[:, 0:32], ALU.add, [pn, 'br%d' % l], ['dtmp'])
                    act(dtmp, dtmp, AF.Exp, ['dtmp'], ['dtmp'])
                    act(dst_[:, tc, :], dtmp, AF.Ln, ['dtmp'], ['dst'], bias=1.0)
                dma(dts_d[t0:t0 + n, :].rearrange("(c p) f -> p c f", p=128), dst_[:, :ntc, :], ['dst'], [], q='act')
            S.barrier()
            if 'stopA' in dbg and l == 0:
                break
            AR.off = PERSIST
            cdiag = AR.alloc([80, 128], BF16)
            for j in range(80):
                ts(cdiag[:, j, :], ident32, pvl[:, 104 + j:105 + j], None, ALU.mult, None, ['c32', 'pv%d' % l], ['cdiag'])
            zt = AR.alloc([16, 4], BF16)
            S.op('dve', lambda e: e.memset(zt.rearrange("p a b -> p (a b)"), 0.0), [], ['zt'])
            xbcT3 = xbcT_d.rearrange("(c p) t -> p c t", p=128)
            dma(xbcT3[:, :, 0:2], zt[:, :, 0:2], ['zt'], ['xbcpad'], q='act')
            dma(xbcT3[:, :, 258:262], zt[:, :, 0:4], ['zt'], ['xbcpad'], q='act')
            dma(xbcT3[:, :, 262 + L:264 + L], zt[:, :, 0:2], ['zt'], ['xbcpad'], q='act')
            xpre = AR.alloc([16, 516], BF16)
            xact = AR.alloc([16, 512], BF16)
            cosT = AR.alloc([512], F32)
            sinT = AR.alloc([512], F32)
            t1 = AR.alloc([512], F32)
            t2 = AR.alloc([512], F32)
            bcrot = AR.alloc([8, 512], BF16)
            xsst = AR.alloc([4, 1024], BF16)
            btst = AR.alloc([4, 512], BF16)
            BT_d3 = BT_d.rearrange("(c p) t -> p c t", p=128)
            CT_d3 = CT_d.rearrange("(c p) t -> p c t", p=128)
            for (t0, n) in tiles:
                ntc = n // 128
                c0 = col_of(t0)
                dma(xpre[:, :, :n + 4], xbcT3[:, :, c0 - 2:c0 + n + 2], ['xbcpad'], ['xpre'])
                dma(cosT[:, :n], ropeC_in[:, t0:t0 + n], [], ['cosT'])
                dma(sinT[:, :n], ropeS_in[:, t0:t0 + n], [], ['sinT'])
                for ch in range(16):
                    bi = nextbank()
                    p_, pn = PSB(bi)
                    mms([(p_[:, :n], cdiag[:, k * 16 + ch, :], xpre[:, ch, k:k + n], k == 0, k == 4) for k in range(5)],
                        ['cdiag', 'xpre'], [pn])
                    act(xact[:, ch, :n], p_[:, :n], AF.Silu, [pn, 'pv%d' % l], ['xact'], bias=pvl[:, 88 + ch:89 + ch])
                for tc in range(ntc):
                    bi = nextbank()
                    p_, pn = PSB(bi)
                    pt16 = p_.bitcast(BF16)
                    trs([(pt16[:, ch * 128:(ch + 1) * 128], xact[:, ch, tc * 128:(tc + 1) * 128], ident16) for ch in range(8)],
                        ['xact', 'c16'], [pn])
                    cp(xsst[:, tc, :], pt16, [pn], ['xsst'], eng='act')
                dma(xs_d[t0:t0 + n, :].rearrange("(c p) f -> p c f", p=128), xsst[:, :ntc, :], ['xsst'], [], q='act')
                for i in range(8):
                    ch = 8 + i
                    bi = nextbank()
                    p_, pn = PSB(bi)
                    mms([(p_[:, :n], pm16, xact[:, ch, :n], True, True)], ['c16', 'xact'], [pn])
                    tt(t1[:, :n], xact[:, ch, :n], cosT[:, :n], ALU.mult, ['xact', 'cosT'], ['t1'])
                    tt(t2[:, :n], p_[:, :n], sinT[:, :n], ALU.mult, [pn, 'sinT'], ['t2'])
                    tt(bcrot[:, i, :n], t1[:, :n], t2[:, :n], ALU.add, ['t1', 't2'], ['bcrot'])
                dma(BT_d3[:, :, t0:t0 + n], bcrot[:, 0:4, :n], ['bcrot'], [], q='act')
                dma(CT_d3[:, :, t0:t0 + n], bcrot[:, 4:8, :n], ['bcrot'], [], q='act')
                for tc in range(ntc):
                    bi = nextbank()
                    p_, pn = PSB(bi)
                    pt16 = p_.bitcast(BF16)
                    trs([(pt16[:, g * 128:(g + 1) * 128], bcrot[:, g, tc * 128:(tc + 1) * 128], ident16) for g in range(4)],
                        ['bcrot', 'c16'], [pn])
                    cp(btst[:, tc, :], pt16[:, 0:512], [pn], ['btst'], eng='act')
                dma(Bt_d[t0:t0 + n, :].rearrange("(c p) f -> p c f", p=128), btst[:, :ntc, :], ['btst'], [], q='act')
            S.barrier()
            AR.off = PERSIST
            h32 = [AR.alloc([1024], F32) for _ in range(2)]
            h16 = [AR.alloc([1024], BF16) for _ in range(2)]
            Abc = AR.alloc([32], F32)
            for d in range(2):
                S.op('dve', lambda e, d=d: e.memset(h32[d], 0.0), [], ['h32_%d' % d])
                S.op('dve', lambda e, d=d: e.memset(h16[d], 0.0), [], ['h16_%d' % d])
            act(Abc, brl[:, 32:64], AF.Exp, ['br%d' % l], ['Abc'])
            ts(Abc, Abc, -1.0, None, ALU.mult, None, ['Abc'], ['Abc'])

            class _B:
                pass
            p1bufs = [[None, None], [None, None]]
            p2bufs = [None, None]
            for d in range(2):
                for par in range(2):
                    B_ = _B()
                    B_.xs = AR.alloc([1024], BF16)
                    B_.BT = AR.alloc([4, 128], BF16)
                    B_.CT = AR.alloc([4, 128], BF16)
                    B_.Bt = AR.alloc([512], BF16)
                    B_.dt = AR.alloc([32], F32)
                    B_.dtA = AR.alloc([16], F32)
                    B_.xdt = AR.alloc([1024], BF16)
                    if d == 0:
                        B_.xsD = AR.alloc([1024], BF16)
                    B_.nega = AR.alloc([16], F32)
                    B_.expa = AR.alloc([16], F32)
                    B_.acT = AR.alloc([128], F32)
                    B_.dec = AR.alloc([16, 128], BF16)
                    B_.cdB = AR.alloc([16], F32)
                    B_.cbm = AR.alloc([4, 128], BF16)
                    B_.MT = AR.alloc([16, 128], BF16)
                    B_.xdd = AR.alloc([1024], BF16)
                    p1bufs[d][par] = B_
                C_ = _B()
                C_.yi = AR.alloc([1024], F32)
                C_.ydir = AR.alloc([1024], F32)
                C_.yp = AR.alloc([1024], F32)
                C_.zs = AR.alloc([1024], BF16)
                C_.gg = AR.alloc([1024], F32)
                C_.gjunk = AR.alloc([1024], BF16)
                C_.ss = AR.alloc([4], F32)
                C_.gn = AR.alloc([1024], BF16)
                C_.yst = AR.alloc([8, 128], BF16)
                p2bufs[d] = C_
            yssdT3 = yssdT_d.rearrange("(c p) t -> p c t", p=128)

            def r3(v):
                return v.rearrange("p (h d) -> p h d", d=64)

            def scan_p1(c, d, want_y, par):
                sf = '_%d_%d' % (d, par)
                U32 = Uf32 if d == 0 else Ub32
                lend = 127 if d == 0 else 0
                tk = c * 128
                B_ = p1bufs[d][par]
                dma(B_.xs, xs_d[tk:tk + 128, :], [], ['xs' + sf])
                dma(B_.BT, BT_d3[:, :, tk:tk + 128], [], ['BT' + sf])
                dma(B_.CT, CT_d3[:, :, tk:tk + 128], [], ['CT' + sf])
                dma(B_.Bt, Bt_d[tk:tk + 128, :], [], ['Bt' + sf])
                dma(B_.dt, dts_d[tk:tk + 128, :], [], ['dt' + sf])
                dsl = B_.dt[:, d * 16:(d + 1) * 16]
                tt(B_.dtA, dsl, Abc[:, d * 16:(d + 1) * 16], ALU.mult, ['dt' + sf, 'Abc'], ['dtA' + sf])
                tt(r3(B_.xdt), r3(B_.xs), dsl.unsqueeze(2).to_broadcast([128, 16, 64]), ALU.mult, ['xs' + sf, 'dt' + sf], ['xdt' + sf], eng='pool')
                P0, P0n = PSB(0)
                mms([(P0[:, 0:16], U32, B_.dtA, True, True)], ['c32', 'dtA' + sf], [P0n])
                mms([(P0[0:16, 64:192], B_.dtA, U32, True, True)], ['c32', 'dtA' + sf], [P0n])
                act(B_.nega, P0[:, 0:16], AF.Copy, [P0n], ['nega' + sf], scale=-1.0)
                act(B_.expa, P0[:, 0:16], AF.Exp, [P0n], ['expa' + sf])
                act(B_.acT[0:16, :], P0[0:16, 64:192], AF.Copy, [P0n], ['acT' + sf])
                for q4 in range(4):
                    pb, pbn = PSB(1 if q4 % 2 == 0 else 0)
                    lst_ = []
                    for hh in range(4):
                        lst_.append((pb[:, hh * 128:(hh + 1) * 128], sel32[:, q4 * 4 + hh, :], B_.acT[0:16, :], True, False))
                        lst_.append((pb[:, hh * 128:(hh + 1) * 128], ident16, mneg16[d], False, True))
                    mms(lst_, ['c32', 'c16', 'mneg16', 'acT' + sf], [pbn])
                    for hh in range(4):
                        h = q4 * 4 + hh
                        act(B_.dec[:, h, :], pb[:, hh * 128:(hh + 1) * 128], AF.Exp, [pbn, 'nega' + sf], ['dec' + sf], bias=B_.nega[:, h:h + 1])
                    act(B_.cdB[:, q4 * 4:q4 * 4 + 4], pb.rearrange("p (h l) -> p h l", l=128)[:, :, lend], AF.Exp, [pbn], ['cdB' + sf])
                if want_y:
                    P3, P3n = PSB(3)
                    mms([(P3[:, g * 128:(g + 1) * 128], B_.BT[:, g, :], B_.CT[:, g, :], True, True) for g in range(4)],
                        ['BT' + sf, 'CT' + sf], [P3n])
                    P33 = P3.rearrange("p (g l) -> p g l", l=128)
                    for g in range(4):
                        tt(B_.MT[:, 4 * g:4 * g + 4, :], P33[:, g:g + 1, :].to_broadcast([128, 4, 128]), B_.dec[:, 4 * g:4 * g + 4, :], ALU.mult,
                           [P3n, 'dec' + sf], ['MT' + sf])
                    if d == 0:
                        tt(r3(B_.xsD), r3(B_.xs), brl[:, 64:80].unsqueeze(2).to_broadcast([128, 16, 64]), ALU.mult,
                           ['xs' + sf, 'br%d' % l], ['xsD' + sf], eng='pool')
                tt(r3(B_.xdd), r3(B_.xdt), B_.dec[:, :, lend:lend + 1].to_broadcast([128, 16, 64]), ALU.mult,
                   ['xdt' + sf, 'dec' + sf], ['xdd' + sf], eng='pool')

            def scan_p2(c, d, second, want_y, par):
                sf = '_%d_%d' % (d, par)
                sg = '_%d' % d
                tk = c * 128
                B_ = p1bufs[d][par]
                C_ = p2bufs[d]
                if want_y:
                    for half in range(2):
                        py, pyn = PSB(4 + half)
                        lst = []
                        if d == 0:
                            lst.append((py, ident16, B_.xsD[:, half * 512:(half + 1) * 512], True, False))
                        for hh in range(8):
                            h = half * 8 + hh
                            lst.append((py[:, hh * 64:(hh + 1) * 64], B_.MT[:, h, :], B_.xdt[:, h * 64:(h + 1) * 64],
                                        d != 0, (d != 0) or hh == 7))
                        mms(lst, ['c16', 'xsD' + sf, 'MT' + sf, 'xdt' + sf], [pyn])
                for half in range(2):
                    pS, pSn = PSB(6 + half)
                    mms([(pS[:, gg * 256:(gg + 1) * 256], B_.Bt[:, (half * 2 + gg) * 128:(half * 2 + gg + 1) * 128],
                          B_.xdd[:, (half * 2 + gg) * 256:(half * 2 + gg + 1) * 256], True, True) for gg in range(2)],
                        ['Bt' + sf, 'xdd' + sf], [pSn])
                if want_y:
                    for half in range(2):
                        pyi, pyin = PSB(2)
                        mms([(pyi[:, gg * 256:(gg + 1) * 256], B_.CT[:, half * 2 + gg, :],
                              h16[d][:, (half * 2 + gg) * 256:(half * 2 + gg + 1) * 256], True, True) for gg in range(2)],
                            ['CT' + sf, 'h16' + sg], [pyin])
                        cp(C_.yi[:, half * 512:(half + 1) * 512], pyi, [pyin], ['yi' + sg], eng='act')
                    tt(r3(C_.yi), r3(C_.yi), B_.expa.unsqueeze(2).to_broadcast([128, 16, 64]), ALU.mult, ['yi' + sg, 'expa' + sf], ['yi' + sg])
                    for half in range(2):
                        py, pyn = PSB(4 + half)
                        tt(C_.ydir[:, half * 512:(half + 1) * 512], C_.yi[:, half * 512:(half + 1) * 512], py, ALU.add,
                           ['yi' + sg, pyn], ['ydir' + sg])
                tt(r3(h32[d]), r3(h32[d]), B_.cdB.unsqueeze(2).to_broadcast([128, 16, 64]), ALU.mult, ['h32' + sg, 'cdB' + sf], ['h32' + sg], eng='pool')
                for half in range(2):
                    pS, pSn = PSB(6 + half)
                    tt(h32[d][:, half * 512:(half + 1) * 512], h32[d][:, half * 512:(half + 1) * 512], pS, ALU.add,
                       ['h32' + sg, pSn], ['h32' + sg])
                cp(h16[d], h32[d], ['h32' + sg], ['h16' + sg], eng='act')
                if not want_y:
                    return
                if not second:
                    dma(yp_d[tk:tk + 128, :], C_.ydir, ['ydir' + sg], ['ypd%d' % c], q='act')
                    return
                dma(C_.yp, yp_d[tk:tk + 128, :], ['ypd%d' % c], ['yp' + sg])
                dma(C_.zs, zs_d[tk:tk + 128, :], [], ['zs' + sg])
                tt(C_.ydir, C_.ydir, C_.yp, ALU.add, ['ydir' + sg, 'yp' + sg], ['ydir' + sg])
                tt(C_.gg, C_.ydir, C_.zs, ALU.mult, ['ydir' + sg, 'zs' + sg], ['gg' + sg])
                act(C_.gjunk, C_.gg, AF.Square, ['gg' + sg], ['gjunk' + sg, 'ss' + sg], accum_out=C_.ss[:, 0:1])
                act(C_.ss[:, 1:2], C_.ss[:, 0:1], AF.Sqrt, ['ss' + sg, 'epsb'], ['ss' + sg], bias=epsb[:, 0:1], scale=1.0 / 1024)
                recip(C_.ss[:, 2:3], C_.ss[:, 1:2], ['ss' + sg], ['ss' + sg])
                act(C_.gn, C_.gg, AF.Copy, ['gg' + sg, 'ss' + sg], ['gn' + sg], scale=C_.ss[:, 2:3])
                P2, P2n = PSB(2)
                pt16 = P2.bitcast(BF16)
                trs([(pt16[:, ch * 128:(ch + 1) * 128], C_.gn[:, ch * 128:(ch + 1) * 128], ident16) for ch in range(8)],
                    ['gn' + sg, 'c16'], [P2n])
                for ch in range(8):
                    act(C_.yst[:, ch, :], pt16[:, ch * 128:(ch + 1) * 128], AF.Copy, [P2n, 'pv%d' % l], ['yst' + sg],
                        scale=pvl[:, 184 + ch:185 + ch])
                dma(yssdT3[:, :, tk:tk + 128], C_.yst, ['yst' + sg], [], q='act')

            chain_f = list(range(NCH))
            chain_b = [1, 0] + list(range(NCH - 1, 1, -1))
            seen = set()

            def wy(c):
                return not (last and c < 2)

            def do_p2(c, d, par):
                scan_p2(c, d, c in seen, wy(c), par)
                seen.add(c)
            scan_p1(chain_f[0], 0, wy(chain_f[0]), 0)
            scan_p1(chain_b[0], 1, wy(chain_b[0]), 0)
            for i in range(NCH):
                la = []
                if i + 1 < NCH:
                    S.record()
                    scan_p1(chain_f[i + 1], 0, wy(chain_f[i + 1]), (i + 1) % 2)
                    la = S.stop()
                S.record()
                do_p2(chain_b[i], 1, i % 2)
                lb = S.stop()
                S.replay([la, lb])
                la = []
                if i + 1 < NCH:
                    S.record()
                    scan_p1(chain_b[i + 1], 1, wy(chain_b[i + 1]), (i + 1) % 2)
                    la = S.stop()
                S.record()
                do_p2(chain_f[i], 0, i % 2)
                lb = S.stop()
                S.replay([la, lb])
            S.barrier()
            if 'stopB' in dbg and l == 0:
                break
            AR.off = PERSIST
            cls, rel, rep = na_classes(L)
            NQ = L // 128
            tab = AR.alloc([40, 128], BF16)
            tabE = AR.alloc([40, 128], BF16)
            kctx = AR.alloc([2, 4, 128], BF16)
            vctx = [AR.alloc([8, 65], BF16) for _ in range(2)]
            NRING = 8
            kt_buf = [AR.alloc([4, 128], BF16) for _ in range(NRING)]
            v_buf = [AR.alloc([8, 65], BF16) for _ in range(NRING)]
            ring_has = {}
            qt_buf = [AR.alloc([4, 128], BF16) for _ in range(2)]
            pT = [AR.alloc([7 * 128], BF16) for _ in range(2)]
            rec = AR.alloc([8], F32)
            yna = AR.alloc([512], BF16)
            ynast = AR.alloc([4, 128], BF16)
            kT_d3 = kT_d.rearrange("(c p) t -> p c t", p=128)
            qT_d3 = qT_d.rearrange("(c p) t -> p c t", p=128)
            ynaT3 = ynaT_d.rearrange("(c p) t -> p c t", p=128)
            for i in range(2):
                S.op('dve', lambda e, i=i: e.memset(vctx[i].rearrange("p a b -> p (a b)"), 1.0), [], ['vctx%d' % i])
                dma(kctx[:, i, :, :], kT_d3[:, :, i * 128:(i + 1) * 128], [], ['kctx'])
                dma(vctx[i][:, :, 0:64], vv_d[i * 128:(i + 1) * 128, :].rearrange("t (h d) -> t h d", d=64), [], ['vctx%d' % i])
            for i in range(NRING):
                S.op('dve', lambda e, i=i: e.memset(v_buf[i].rearrange("p a b -> p (a b)"), 1.0), [], ['vbuf%d' % i])

            def ring_load(kt):
                sl = kt % NRING
                if ring_has.get(sl) == kt:
                    return
                ring_has[sl] = kt
                ktok = CTX + kt * 128
                dma(kt_buf[sl], kT_d3[:, :, ktok:ktok + 128], [], ['ktb%d' % sl])
                dma(v_buf[sl][:, :, 0:64], vv_d[ktok:ktok + 128, :].rearrange("t (h d) -> t h d", d=64), [], ['vbuf%d' % sl])

            def na_tile(qtok, kts, use_tab, nxt=()):
                qb = qt_buf[(qtok // 128) % 2]
                qbn = 'qt_buf%d' % ((qtok // 128) % 2)
                dma(qb, qT_d3[:, :, qtok:qtok + 128], [], [qbn])
                nk = len(kts)
                for kt in list(kts) + list(nxt):
                    ring_load(kt)
                slots = [kt % NRING for kt in kts]
                nb = nk + 2
                def st_qk(h):
                    cc, e = h // 2, h % 2
                    pA, pAn = PSB(2 * e)
                    pB, pBn = PSB(2 * e + 1)
                    lstA, lstB = [], []
                    rdA, rdB = set([qbn, 'kctx']), set([qbn, 'kctx'])
                    for j in range(nb):
                        bank, lst, rd = (pA, lstA, rdA) if j < 4 else (pB, lstB, rdB)
                        col = (j % 4) * 128
                        qop = qb[e * 64:(e + 1) * 64, cc, :]
                        if j < nk:
                            rd.add('ktb%d' % slots[j])
                            lst.append((bank[:, col:col + 128], kt_buf[slots[j]][e * 64:(e + 1) * 64, cc, :], qop, True, True))
                        else:
                            lst.append((bank[:, col:col + 128], kctx[e * 64:(e + 1) * 64, j - nk, cc, :], qop, True, True))
                    mms(lstA, sorted(rdA), [pAn])
                    if lstB:
                        mms(lstB, sorted(rdB), [pBn])

                def st_sm(h):
                    cc, e = h // 2, h % 2
                    pA, pAn = PSB(2 * e)
                    pB, pBn = PSB(2 * e + 1)
                    na_ = min(nb, 4)
                    act(pT[e][:, 0:na_ * 128], pA[:, 0:na_ * 128], AF.Exp, [pAn], ['pT%d' % e])
                    if nb > 4:
                        act(pT[e][:, 512:512 + (nb - 4) * 128], pB[:, 0:(nb - 4) * 128], AF.Exp, [pBn], ['pT%d' % e])
                    if use_tab:
                        tE4 = tabE.rearrange("p (s h) q -> p s h q", h=8)
                        na4 = min(nk, 4)
                        tt(pT[e][:, 0:na4 * 128].rearrange("p (s q) -> p s q", q=128), pT[e][:, 0:na4 * 128].rearrange("p (s q) -> p s q", q=128),
                           tE4[:, 0:na4, h, :], ALU.mult, ['pT%d' % e, 'tabE'], ['pT%d' % e])
                        if nk > 4:
                            tt(pT[e][:, 512:512 + (nk - 4) * 128].rearrange("p (s q) -> p s q", q=128),
                               pT[e][:, 512:512 + (nk - 4) * 128].rearrange("p (s q) -> p s q", q=128),
                               tE4[:, 4:nk, h, :], ALU.mult, ['pT%d' % e, 'tabE'], ['pT%d' % e])

                def st_pv(h):
                    cc, e = h // 2, h % 2
                    po, pon = PSB(4 + h // 4)
                    col = (h % 4) * 65
                    lst = []
                    rd = ['pT%d' % e, 'vctx0', 'vctx1']
                    for j in range(nb):
                        if j < nk:
                            vb = v_buf[slots[j]]
                            rd.append('vbuf%d' % slots[j])
                        else:
                            vb = vctx[j - nk]
                        lst.append((po[:, col:col + 65], pT[e][:, j * 128:(j + 1) * 128], vb[:, h, :], j == 0, j == nb - 1))
                    mms(lst, rd, [pon])

                st_qk(0)
                st_qk(1)
                for h in range(8):
                    st_sm(h)
                    st_pv(h)
                    if h + 2 < 8:
                        st_qk(h + 2)
                for b4 in range(2):
                    po, pon = PSB(4 + b4)
                    po3 = po[:, 0:260].rearrange("p (h e) -> p h e", e=65)
                    recip(rec[:, b4 * 4:(b4 + 1) * 4], po3[:, :, 64], [pon], ['rec'])
                    tt(yna.rearrange("p (h d) -> p h d", d=64)[:, b4 * 4:(b4 + 1) * 4, :], po3[:, :, 0:64],
                       rec[:, b4 * 4:(b4 + 1) * 4].unsqueeze(2).to_broadcast([128, 4, 64]), ALU.mult, [pon, 'rec'], ['yna'])
                p6, p6n = PSB(6)
                pt16 = p6.bitcast(BF16)
                trs([(pt16[:, c * 128:(c + 1) * 128], yna[:, c * 128:(c + 1) * 128], ident16) for c in range(4)], ['yna', 'c16'], [p6n])
                cp(ynast, pt16[:, 0:512].rearrange("p (c t) -> p c t", t=128), [p6n], ['ynast'], eng='act')
                dma(ynaT3[:, :, qtok:qtok + 128], ynast, ['ynast'], [], q='act')

            if not last:
                for i in range(2):
                    na_tile(i * 128, [], False)
            cur = -1
            for qt in range(NQ):
                c = cls[qt]
                if c != cur:
                    cur = c
                    b0 = l * 200 + c * 40
                    dma(tab, natab16[b0:b0 + 40].rearrange("s k q -> k s q"), ['natab16_%d' % l], ['tab'])
                    act(tabE.rearrange("p a b -> p (a b)"), tab.rearrange("p a b -> p (a b)"), AF.Exp, ['tab'], ['tabE'])
                nxt = [qt + 1 + dk for dk in rel[cls[qt + 1]]] if qt + 1 < NQ else []
                na_tile(CTX + qt * 128, [qt + dk for dk in rel[c]], True, nxt)
            S.barrier()
            if 'stopC' in dbg and l == 0:
                break
            AR.off = PERSIST
            ph = {}
            ph['sq'] = AR.alloc([8, 512], BF16)
            ph['rs'] = AR.alloc([512], F32)
            ph['ntmp'] = [AR.alloc([512], F32) for _ in range(2)]
            xt2 = [AR.alloc([8, 512], F32) for _ in range(2)]
            ys = AR.alloc([8, 512], BF16)
            yn_ = AR.alloc([4, 512], BF16)
            yg = AR.alloc([4, 512], BF16)
            gt = AR.alloc([24, 512], BF16)
            mg = AR.alloc([8, 512], BF16)
            h2 = AR.alloc([8, 512], BF16)
            actb = AR.alloc([22, 512], BF16)
            m1 = [AR.alloc([512], F32) for _ in range(2)]
            m2 = [AR.alloc([512], F32) for _ in range(2)]
            m3 = [AR.alloc([512], F32) for _ in range(2)]
            sa = AR.alloc([512], BF16)
            wbuf = [AR.alloc([4096], BF16) for _ in range(3)]
            wctr[0] = 0
            ygmT3 = ygmT_d.rearrange("(c p) t -> p c t", p=128)
            gateT3 = gateT_d.rearrange("(c p) t -> p c t", p=128)
            x1T3 = x1T.rearrange("(c p) t -> p c t", p=128)
            outT3 = outT.rearrange("(c p) t -> p c t", p=128)

            def load_w(blk):
                slot = wctr[0] % 3
                wctr[0] += 1
                dma(wbuf[slot], wblk16[wb0 + blk], ['w16_%d' % (wb0 + blk)], ['wb%d' % slot])
                return wbuf[slot], 'wb%d' % slot

            dtiles = (tiles[1:] if last else tiles)

            def d_loads(ti):
                (t0_, n_) = dtiles[ti]
                dma(xt2[ti % 2][:, :, :n_], xsrc3[:, :, t0_:t0_ + n_], [], ['Dxt%d' % (ti % 2)])
                dma(ys[:, :, :n_], yssdT3[:, :, t0_:t0_ + n_], [], ['ys'])
                dma(yn_[:, :, :n_], ynaT3[:, :, t0_:t0_ + n_], [], ['yn'])
                dma(yg[:, :, :n_], ygmT3[:, :, t0_:t0_ + n_], [], ['yg'])
                dma(gt[:, :, :n_], gateT3[:, :, t0_:t0_ + n_], [], ['gt'])
            d_loads(0)
            for ti, (t0, n) in enumerate(dtiles):
                isctx = t0 < CTX
                j = 1 if isctx else 0
                xt = xt2[ti % 2]
                xtn = 'Dxt%d' % (ti % 2)
                for jb in range(4):
                    w, wn = load_w(18 + jb)
                    for oo in range(2):
                        oc = 2 * jb + oo
                        base = oo * 2048
                        pa, pan = PSB(nextbank())
                        mms([(pa[:, :n], w[:, base + kc * 128:base + (kc + 1) * 128], ys[:, kc, :n], kc == 0, kc == 7) for kc in range(8)],
                            [wn, 'ys'], [pan])
                        pb_, pbn = PSB(nextbank())
                        mms([(pb_[:, :n], w[:, base + 1024 + kc * 128:base + 1024 + (kc + 1) * 128], yn_[:, kc, :n], kc == 0, kc == 3) for kc in range(4)],
                            [wn, 'yn'], [pbn])
                        pc_, pcn = PSB(nextbank())
                        mms([(pc_[:, :n], w[:, base + 1536 + kc * 128:base + 1536 + (kc + 1) * 128], yg[:, kc, :n], kc == 0, kc == 3) for kc in range(4)],
                            [wn, 'yg'], [pcn])
                        mp = oc % 2
                        ms_ = '%d' % mp
                        tt(m1[mp][:, :n], pa[:, :n], gt[:, oc, :n], ALU.mult, [pan, 'gt'], ['m1' + ms_])
                        tt(m2[mp][:, :n], pb_[:, :n], gt[:, 8 + oc, :n], ALU.mult, [pbn, 'gt'], ['m2' + ms_])
                        tt(m3[mp][:, :n], pc_[:, :n], gt[:, 16 + oc, :n], ALU.mult, [pcn, 'gt'], ['m3' + ms_])
                        tt(m1[mp][:, :n], m1[mp][:, :n], m2[mp][:, :n], ALU.add, ['m1' + ms_, 'm2' + ms_], ['m1' + ms_], eng='pool')
                        tt(mg[:, oc, :n], m1[mp][:, :n], m3[mp][:, :n], ALU.add, ['m1' + ms_, 'm3' + ms_], ['mg'], eng='pool')
                if ti + 1 < len(dtiles):
                    d_loads(ti + 1)
                for jb in range(2):
                    w, wn = load_w(22 + jb)
                    w3 = w.rearrange("p (k c) -> p k c", c=512)
                    for oc4 in range(4):
                        oc = jb * 4 + oc4
                        p_, pn = PSB(nextbank())
                        mms([(p_[:, :n], w3[:, kc, oc4 * 128:(oc4 + 1) * 128], mg[:, kc, :n], kc == 0, kc == 7) for kc in range(8)],
                            [wn, 'mg'], [pn])
                        stt(xt[:, oc, :n], p_[:, :n], modT[l][:, 16 + oc, j:j + 1], xt[:, oc, :n], ALU.mult, ALU.add,
                            [pn, 'modT%d' % l, xtn], [xtn])
                norm_mod(xt, n, A2[l][:, :, j], modT[l][:, 24:32, j], h2, xtn, 'DhT')
                for jb in range(11):
                    w, wn = load_w(24 + jb)
                    w3 = w.rearrange("p (k c) -> p k c", c=512)
                    for hcl in range(2):
                        hc = 2 * jb + hcl
                        pa, pan = PSB(nextbank())
                        mms([(pa[:, :n], w3[:, kc, (hcl * 2) * 128:(hcl * 2 + 1) * 128], h2[:, kc, :n], kc == 0, kc == 7) for kc in range(8)],
                            [wn, 'DhT'], [pan])
                        pb_, pbn = PSB(nextbank())
                        mms([(pb_[:, :n], w3[:, kc, (hcl * 2 + 1) * 128:(hcl * 2 + 2) * 128], h2[:, kc, :n], kc == 0, kc == 7) for kc in range(8)],
                            [wn, 'DhT'], [pbn])
                        act(sa[:, :n], pa[:, :n], AF.Silu, [pan], ['sa'])
                        tt(actb[:, hc, :n], sa[:, :n], pb_[:, :n], ALU.mult, ['sa', pbn], ['actb'])
                for oc in range(8):
                    w, wn = load_w(35 + oc)
                    wv = w[:, 0:2816].rearrange("p (k c) -> p k c", c=128)
                    p_, pn = PSB(nextbank())
                    mms([(p_[:, :n], wv[:, kc, :], actb[:, kc, :n], kc == 0, kc == 21) for kc in range(22)], [wn, 'actb'], [pn])
                    stt(xt[:, oc, :n], p_[:, :n], modT[l][:, 40 + oc, j:j + 1], xt[:, oc, :n], ALU.mult, ALU.add,
                        [pn, 'modT%d' % l, xtn], [xtn])
                if last:
                    dma(outT3[:, :, t0 - CTX:t0 - CTX + n], xt[:, :, :n], [xtn], [], q='act')
                else:
                    dma(x1T3[:, :, t0:t0 + n], xt[:, :, :n], [xtn], [], q='act')
            S.barrier()
        S.emit()
    return nc


def _slab(W):
    k = W.shape[0] // 128
    c = W.shape[1]
    out = np.zeros((128, 4096), np.float32)
    out[:, :k * c] = W.reshape(k, 128, c).transpose(1, 0, 2).reshape(128, k * c)
    return out


def pack_weights(inp, l):
    w_in = inp['w_in'][l]
    blks = []
    segs = [(0, 512), (512, 1024),
            (1024, 1536), (1536, 2048), (2048, 2560), (2560, 3072),
            (3104, 3616), (3616, 4128), (4128, 4640),
            (4640, 5152), (5152, 5664)]
    segs += [(5664 + 512 * i, 5664 + 512 * (i + 1)) for i in range(6)]
    for (a, b) in segs:
        blks.append(_slab(w_in[:, a:b]))
    blks.append(_slab(w_in[:, 3072:3104]))
    wa, wb_, wc = inp['w_branch_ssd'][l], inp['w_branch_na'][l], inp['w_branch_gm'][l]
    for j in range(4):
        blk = np.zeros((128, 4096), np.float32)
        for oo in range(2):
            oc = 2 * j + oo
            base = oo * 2048
            blk[:, base:base + 1024] = wa[:, oc * 128:(oc + 1) * 128].reshape(8, 128, 128).transpose(1, 0, 2).reshape(128, 1024)
            blk[:, base + 1024:base + 1536] = wb_[:, oc * 128:(oc + 1) * 128].reshape(4, 128, 128).transpose(1, 0, 2).reshape(128, 512)
            blk[:, base + 1536:base + 2048] = wc[:, oc * 128:(oc + 1) * 128].reshape(4, 128, 128).transpose(1, 0, 2).reshape(128, 512)
        blks.append(blk)
    wo = inp['w_out'][l]
    for j in range(2):
        blks.append(_slab(wo[:, j * 512:(j + 1) * 512]))
    wf = inp['w_ffn_in'][l]
    for j in range(11):
        slab = np.zeros((1024, 512), np.float32)
        for hcl in range(2):
            for ab in range(2):
                src = ab * 2816 + (2 * j + hcl) * 128
                slab[:, (hcl * 2 + ab) * 128:(hcl * 2 + ab + 1) * 128] = wf[:, src:src + 128]
        blks.append(_slab(slab))
    wfo = inp['w_ffn_out'][l]
    for oc in range(8):
        blks.append(_slab(wfo[:, oc * 128:(oc + 1) * 128]))
    assert len(blks) == NBLK
    return np.stack(blks)


def pcol(v):
    return np.ascontiguousarray(v.reshape(-1, 128).T)


def pack_shared(inp, L, depth):
    T = CTX + L
    R = L // 64
    f32 = np.float32
    wmod = np.stack([inp['w_mod'][l].reshape(8, 128, 12, 512).transpose(2, 1, 0, 3).reshape(12, 128, 4096)
                     for l in range(depth)]).reshape(depth * 12, 128, 4096)
    wblk = np.concatenate([pack_weights(inp, l) for l in range(depth)], axis=0)
    pvec = np.zeros((depth, 128, NPV), f32)
    brow = np.zeros((depth, NBR), f32)
    wsT = np.zeros((depth, 128, 1024), f32)
    for l in range(depth):
        pvec[l, :, 0:48] = pcol(inp['b_mod'][l])
        pvec[l, :, 48:56] = pcol(inp['norm1'][l])
        pvec[l, :, 56:64] = pcol(inp['norm2'][l])
        pvec[l, :, 64:88] = pcol(inp['b_gate'][l])
        pvec[l, :, 88:104] = pcol(inp['conv_b'][l])
        for k in range(5):
            pvec[l, :, 104 + k * 16:104 + (k + 1) * 16] = pcol(inp['conv_w'][l][k])
        pvec[l, :, 184:192] = pcol(inp['ssd_norm'][l])
        pvec[l, :, 192] = np.tile(inp['q_norm'][l], 2)
        pvec[l, :, 193] = np.tile(inp['k_norm'][l], 2)
        pvec[l, :, 194:202] = inp['b_spatial'][l].T
        brow[l, 0:32] = inp['dt_bias'][l].reshape(-1)
        brow[l, 32:64] = inp['a_log'][l].reshape(-1)
        brow[l, 64:80] = inp['d_skip'][l]
        brow[l, 80:592] = inp['gm_norm'][l]
        wsT[l] = inp['w_spatial'][l].transpose(2, 0, 1).reshape(128, 1024)
    freqs = (np.float32(10000.0) ** (-np.arange(32, dtype=f32) / np.float32(32))).astype(f32)
    pos = np.arange(L)
    ang_row = (pos // 64).astype(f32)[:, None] * freqs
    ang_col = (pos % 64).astype(f32)[:, None] * freqs
    ropeC = np.ones((128, T), f32)
    ropeS = np.zeros((128, T), f32)
    for half, ang in ((0, ang_row), (1, ang_col)):
        c = np.cos(ang).T.astype(f32)
        s = np.sin(ang).T.astype(f32)
        ropeC[half * 64:half * 64 + 32, CTX:] = c
        ropeC[half * 64 + 32:half * 64 + 64, CTX:] = c
        ropeS[half * 64:half * 64 + 32, CTX:] = s
        ropeS[half * 64 + 32:half * 64 + 64, CTX:] = s
    consts = np.zeros((128, NCONST), f32)
    i = np.arange(128)
    consts[:, 0:128] = np.eye(128, dtype=f32)
    consts[:, 128:256] = (i[:, None] <= i[None, :]).astype(f32)
    consts[:, 256:384] = (i[:, None] >= i[None, :]).astype(f32)
    consts[:, 384:512] = 1.0
    consts[:, 512:640] = ((i[:, None] // 64) == (i[None, :] // 64)).astype(f32)
    pm = np.zeros((128, 128), f32)
    for base in (0, 64):
        for n in range(32):
            pm[base + n + 32, base + n] = -1.0
            pm[base + n, base + n + 32] = 1.0
    consts[:, 640:768] = pm
    sel = np.zeros((16, 16, 128), f32)
    for h in range(16):
        sel[h, h, :] = 1.0
    consts[0:16, 768:768 + 2048] = sel.reshape(16, 2048)
    cls, rel, rep = na_classes(L)

    def rs_(r):
        return min(max(r - 4, 0), R - 8)
    natab = np.full((depth, 5, 5, 8, 128, 128), NEG, f32)
    kc = np.arange(64)
    qc = np.arange(64)
    cs = np.clip(qc - 8, 0, 48)
    colok = (kc[:, None] >= cs[None, :]) & (kc[:, None] < cs[None, :] + 16)
    ci = np.clip(kc[:, None] - qc[None, :] + 15, 0, 30)
    for c, qt in rep.items():
        for slot, dk in enumerate(rel[c]):
            kt = qt + dk
            for a in range(2):
                r = 2 * qt + a
                for b in range(2):
                    krow = 2 * kt + b
                    if not (rs_(r) <= krow < rs_(r) + 8):
                        continue
                    ri = krow - r + 7
                    for l in range(depth):
                        vals = inp['rpb'][l][:, ri, :][:, ci]
                        blk = np.where(colok[None], vals, np.float32(NEG))
                        natab[l, c, slot, :, b * 64:(b + 1) * 64, a * 64:(a + 1) * 64] = blk
    natab = natab.reshape(depth * 200, 128, 128)
    return dict(wmod=wmod, wblk=wblk, pvec=pvec, brow=brow, wsT=wsT, ropeC=ropeC, ropeS=ropeS,
                natab=natab, consts=consts)


def pack_core(inp, b, shared):
    xT = np.ascontiguousarray(np.concatenate([inp['ctx'][b].T, inp['x'][b].T], axis=1))
    cT = np.zeros((128, 8, 2), np.float32)
    cT[:, :, 0] = pcol(inp['c'][b])
    cT[:, :, 1] = pcol(inp['c_ctx'])
    m = dict(shared)
    m['xT'] = xT
    m['cT'] = cT.reshape(128, 16)
    return m


_NC_CACHE = {}


def kernel(**inputs):
    inp = {k: np.asarray(v, dtype=np.float32) for k, v in inputs.items()}
    B, L, _ = inp['x'].shape
    depth = inp['w_in'].shape[0]
    key = (L, depth)
    if key not in _NC_CACHE:
        _NC_CACHE[key] = build(L, depth)
    nc = _NC_CACHE[key]
    shared = pack_shared(inp, L, depth)
    slots = [0, 1, 4, 5][:B] if B <= 4 else list(range(B))
    n_cores = 8 if B <= 4 else B
    real = {slots[b]: pack_core(inp, b, shared) for b in range(B)}
    zero = None
    in_maps = []
    for i in range(n_cores):
        if i in real:
            in_maps.append(real[i])
        else:
            if zero is None:
                zero = {k: np.zeros_like(v) for k, v in real[slots[0]].items()}
            in_maps.append(zero)
    res = run_bass_kernel_spmd(nc, in_maps, core_ids=list(range(n_cores)))
    out = np.stack([np.ascontiguousarray(res.results[slots[b]]['outT'].T) for b in range(B)])
    return out.astype(np.float32)
```

```python
import math
import numpy as np
import concourse.bass as bass
import concourse.mybir as mybir
from concourse.bass_utils import run_bass_kernel_spmd
from contextlib import ExitStack

F32 = mybir.dt.float32
BF16 = mybir.dt.bfloat16
U8 = mybir.dt.uint8
AF = mybir.ActivationFunctionType
ALU = mybir.AluOpType
AX = mybir.AxisListType

D = 1024
CTX = 256
EPS = 1e-6
NBLK = 43
NPV = 202
NBR = 592
NCONST = 768 + 2048
NEG = -30000.0

ENGS = ['pe', 'act', 'dve', 'pool', 'sp']
N_DSEM = 48
DSEM_RANGE = {'sp': (0, 22), 'act': (22, 32), 'pool': (32, 48)}


class Sched:
    def __init__(self, nc):
        self.nc = nc
        self.ops = {e: [] for e in ENGS}
        self.last_write = {}
        self.readers = {}
        self.known = {e: {} for e in ENGS}
        self.dsem_uses = [0] * N_DSEM
        self.dsem_next = {'sp': 0, 'pool': 0, 'act': 0}
        self.signal = {e: set() for e in ENGS}
        self.rec = None

    def record(self):
        self.rec = []

    def stop(self):
        r = self.rec
        self.rec = None
        return r

    def replay(self, lists):
        idx = [0] * len(lists)
        tot = [max(1, len(x)) for x in lists]
        while True:
            live = [k for k in range(len(lists)) if idx[k] < len(lists[k])]
            if not live:
                break
            k = min(live, key=lambda k: idx[k] / tot[k])
            self.op(*lists[k][idx[k]])
            idx[k] += 1

    def _need(self, eng, tok, waits):
        kind, key, val = tok
        if kind == 'eng' and key == eng and eng in ('pe', 'sp'):
            return
        k = (kind, key)
        if self.known[eng].get(k, 0) >= val:
            return
        self.known[eng][k] = val
        waits[k] = max(waits.get(k, 0), val)
        if kind == 'eng':
            self.signal[key].add(val)

    def op(self, eng, fn, reads=(), writes=(), dma=False):
        if self.rec is not None:
            self.rec.append((eng, fn, tuple(reads), tuple(writes), dma))
            return None
        waits = {}
        for b in reads:
            w = self.last_write.get(b)
            if w is not None:
                self._need(eng, w, waits)
        for b in writes:
            w = self.last_write.get(b)
            if w is not None:
                self._need(eng, w, waits)
            for r in self.readers.get(b, ()):
                self._need(eng, r, waits)
        idx = len(self.ops[eng]) + 1
        if dma:
            lo, hi = DSEM_RANGE[eng]
            j = lo + self.dsem_next[eng]
            self.dsem_next[eng] = (self.dsem_next[eng] + 1) % (hi - lo)
            n = self.dsem_uses[j]
            if n > 0:
                self._need(eng, ('dma', j, 16 * n), waits)
            self.dsem_uses[j] = n + 1
            tok = ('dma', j, 16 * (n + 1))
        else:
            tok = ('eng', eng, idx)
        for b in reads:
            self.readers.setdefault(b, []).append(tok)
        for b in writes:
            self.last_write[b] = tok
            self.readers[b] = []
        self.ops[eng].append((fn, waits, tok))
        return tok

    def barrier(self):
        toks = []
        for e in ENGS:
            for i in range(len(self.ops[e]), 0, -1):
                t = self.ops[e][i - 1][2]
                if t is not None and t[0] == 'eng':
                    toks.append(t)
                    break
        for j in range(N_DSEM):
            if self.dsem_uses[j] > 0:
                toks.append(('dma', j, 16 * self.dsem_uses[j]))
        for e in ENGS:
            waits = {}
            for t in toks:
                self._need(e, t, waits)
            self.ops[e].append((None, waits, None))
        self.last_write.clear()
        self.readers.clear()

    def emit(self):
        nc = self.nc
        with ExitStack() as st:
            esem = {e: st.enter_context(nc.semaphore('es_' + e)) for e in ENGS}
            dsem = [st.enter_context(nc.semaphore('ds_%d' % j)) for j in range(N_DSEM)]
            block = st.enter_context(nc.Block())
            sigmap = {}
            for e in ENGS:
                cnt = 0
                m = {}
                ss = self.signal[e]
                for i in range(1, len(self.ops[e]) + 1):
                    if i in ss:
                        cnt += 1
                        m[i] = cnt
                sigmap[e] = m

            def run(e, engobj):
                for i, (fn, waits, tok) in enumerate(self.ops[e], start=1):
                    for (kind, key), val in waits.items():
                        if kind == 'eng':
                            engobj.wait_ge(esem[key], sigmap[key][val])
                        else:
                            engobj.wait_ge(dsem[key], val)
                    if fn is None:
                        continue
                    ins = fn(engobj)
                    if tok[0] == 'dma':
                        ins.then_inc(dsem[tok[1]], 16)
                    elif i in sigmap[e]:
                        ins.then_inc(esem[e], 1)

            @block.tensor
            def _(eng):
                run('pe', eng)

            @block.scalar
            def _(eng):
                run('act', eng)

            @block.vector
            def _(eng):
                run('dve', eng)

            @block.gpsimd
            def _(eng):
                run('pool', eng)

            @block.sync
            def _(eng):
                run('sp', eng)


class Arena:
    def __init__(self, ap, size):
        self.ap = ap
        self.size = size
        self.off = 0

    def alloc(self, free_shape, dt):
        esz = 4 if dt == F32 else 2
        n = 1
        for s in free_shape:
            n *= s
        nb = (n * esz + 63) // 64 * 64
        assert self.off + nb <= self.size, "SBUF arena overflow %d + %d" % (self.off, nb)
        v = self.ap[:, self.off:self.off + n * esz].bitcast(dt)
        self.off += nb
        if len(free_shape) == 2:
            v = v.rearrange("p (a b) -> p a b", b=free_shape[1])
        elif len(free_shape) == 3:
            v = v.rearrange("p (a b c) -> p a b c", b=free_shape[1], c=free_shape[2])
        return v


def na_classes(L):
    R = L // 64
    NQ = R // 2

    def rs(r):
        return min(max(r - 4, 0), R - 8)
    info = []
    for qt in range(NQ):
        r0, r1 = 2 * qt, 2 * qt + 1
        lo = min(rs(r0), rs(r1)) // 2
        hi = (max(rs(r0), rs(r1)) + 7) // 2
        info.append(tuple(k - qt for k in range(lo, hi + 1)))
    cls = []
    for qt in range(NQ):
        if qt == 0:
            cls.append(0)
        elif qt == 1:
            cls.append(1)
        elif qt == NQ - 2:
            cls.append(3)
        elif qt == NQ - 1:
            cls.append(4)
        else:
            cls.append(2)
    rel = {}
    rep = {}
    for qt in range(NQ):
        c = cls[qt]
        if c in rel:
            assert rel[c] == info[qt], (c, rel[c], info[qt])
        else:
            rel[c] = info[qt]
            rep[c] = qt
    return cls, rel, rep


def build(L, depth=2, dbg=()):
    T = CTX + L
    NCH = T // 128
    TP = T + 8
    tiles = [(0, 256)] + [(CTX + 512 * i, 512) for i in range(L // 512)]
    nc = bass.Bass("TRN2", target_bir_lowering=False)

    def din(name, shape, dt=F32):
        return nc.dram_tensor(name, shape, dt, kind="ExternalInput").ap()

    def dscr(name, shape, dt):
        kind = "ExternalOutput" if name in dbg else "Internal"
        return nc.dram_tensor(name, shape, dt, kind=kind).ap()

    xT_in = din("xT", [D, T])
    cT_in = din("cT", [128, 16])
    wmod_in = din("wmod", [depth * 12, 128, 4096])
    wblk_in = din("wblk", [depth * NBLK, 128, 4096])
    pvec_in = din("pvec", [depth, 128, NPV])
    brow_in = din("brow", [depth, NBR])
    wsT_in = din("wsT", [depth, 128, 1024])
    ropeC_in = din("ropeC", [128, T])
    ropeS_in = din("ropeS", [128, T])
    natab_in = din("natab", [depth * 200, 128, 128])
    consts_in = din("consts", [128, NCONST])
    outT = nc.dram_tensor("outT", [D, L], F32, kind="ExternalOutput").ap()

    wblk16 = dscr("wblk16", [depth * NBLK, 128, 4096], BF16)
    natab16 = dscr("natab16", [depth * 200, 128, 128], BF16)
    x1T = dscr("x1T", [D, T], F32)
    zs_d = dscr("zs", [T, 1024], BF16)
    xbcT_d = dscr("xbcT", [2048, TP], BF16)
    dts_d = dscr("dts", [T, 32], F32)
    qT_d = dscr("qT", [512, T], BF16)
    kT_d = dscr("kT", [512, T], BF16)
    vv_d = dscr("vv", [T, 512], BF16)
    ygmT_d = dscr("ygmT", [512, T], BF16)
    gateT_d = dscr("gateT", [3072, T], BF16)
    xs_d = dscr("xs", [T, 1024], BF16)
    BT_d = dscr("BTs", [512, T], BF16)
    CT_d = dscr("CTs", [512, T], BF16)
    Bt_d = dscr("Btok", [T, 512], BF16)
    yp_d = dscr("ypart", [T, 1024], F32)
    yssdT_d = dscr("yssdT", [1024, T], BF16)
    ynaT_d = dscr("ynaT", [512, T], BF16)

    st = ExitStack()
    with st:
        ARENA = 186 * 1024
        arena_t = st.enter_context(nc.sbuf_tensor("arena", [128, ARENA], U8))
        ps = [st.enter_context(nc.psum_tensor("ps%d" % i, [128, 512], F32)) for i in range(8)]
        S = Sched(nc)
        AR = Arena(arena_t, ARENA)
        pb_ctr = [0]

        def PSB(i):
            return ps[i][:], 'P%d' % i

        def nextbank(lo=0, hi=8):
            i = lo + pb_ctr[0] % (hi - lo)
            pb_ctr[0] += 1
            return i

        def dma(out, in_, reads, writes, q='sp'):
            return S.op(q, lambda e: e.dma_start(out=out, in_=in_), reads, writes, dma=True)

        def act(out, in_, func, reads, writes, bias=None, scale=None, accum_out=None):
            kw = {}
            if bias is not None:
                kw['bias'] = bias
            if scale is not None:
                kw['scale'] = scale
            if accum_out is not None:
                kw['accum_out'] = accum_out
            return S.op('act', lambda e: e.activation(out=out, in_=in_, func=func, **kw), reads, writes)

        def tt(out, in0, in1, op, reads, writes, eng='dve'):
            return S.op(eng, lambda e: e.tensor_tensor(out=out, in0=in0, in1=in1, op=op), reads, writes)

        def ts(out, in0, s1, s2, op0, op1, reads, writes, eng='dve'):
            if s2 is None:
                return S.op(eng, lambda e: e.tensor_scalar(out=out, in0=in0, scalar1=s1, scalar2=None, op0=op0), reads, writes)
            return S.op(eng, lambda e: e.tensor_scalar(out=out, in0=in0, scalar1=s1, scalar2=s2, op0=op0, op1=op1), reads, writes)

        def stt(out, in0, scalar, in1, op0, op1, reads, writes):
            return S.op('dve', lambda e: e.scalar_tensor_tensor(out=out, in0=in0, scalar=scalar, in1=in1, op0=op0, op1=op1), reads, writes)

        def recip(out, in_, reads, writes):
            return S.op('dve', lambda e: e.reciprocal(out=out, in_=in_), reads, writes)

        def cp(out, in_, reads, writes, eng='dve'):
            if eng == 'act':
                return S.op('act', lambda e: e.copy(out=out, in_=in_), reads, writes)
            return S.op(eng, lambda e: e.tensor_copy(out=out, in_=in_), reads, writes)

        def mms(lst, reads, writes):
            def fn(e):
                ins = None
                for (o, l, r, s0, s1) in lst:
                    ins = e.matmul(o, lhsT=l, rhs=r, start=s0, stop=s1)
                return ins
            return S.op('pe', fn, reads, writes)

        def trs(lst, reads, writes):
            def fn(e):
                ins = None
                for (o, i_, idn) in lst:
                    ins = e.transpose(o, i_, idn)
                return ins
            return S.op('pe', fn, reads, writes)

        c32 = AR.alloc([NCONST], F32)
        c16 = AR.alloc([512], BF16)
        epsb = AR.alloc([1], F32)
        ident32 = c32[:, 0:128]
        Uf32 = c32[:, 128:256]
        Ub32 = c32[:, 256:384]
        sel32 = c32[0:16, 768:768 + 2048].rearrange("p (h s) -> p h s", s=128)
        ident16 = c16[:, 0:128]
        ones16 = c16[:, 128:256]
        bo16 = c16[:, 256:384]
        pm16 = c16[:, 384:512]
        pv = [AR.alloc([NPV], F32) for _ in range(depth)]
        br = [AR.alloc([NBR], F32) for _ in range(depth)]
        modT = [AR.alloc([48, 2], F32) for _ in range(depth)]
        A1 = [AR.alloc([8, 2], F32) for _ in range(depth)]
        A2 = [AR.alloc([8, 2], F32) for _ in range(depth)]
        scs = AR.alloc([8, 2], F32)
        mneg16 = [AR.alloc([128], BF16) for _ in range(2)]
        sel16f = AR.alloc([2048], BF16)
        sel16 = sel16f[0:16, :].rearrange("p (h s) -> p h s", s=128)
        PERSIST = AR.off

        dma(c32, consts_in, [], ['c32'])
        S.op('dve', lambda e: e.memset(epsb, EPS), [], ['epsb'])
        cp(c16[:, 0:128], c32[:, 0:128], ['c32'], ['c16'])
        cp(c16[:, 128:512], c32[:, 384:768], ['c32'], ['c16'])
        cp(sel16f[0:16, :], c32[0:16, 768:768 + 2048], ['c32'], ['sel16'])
        for d_ in range(2):
            ts(mneg16[d_], (Uf32 if d_ == 0 else Ub32), -1.0, -NEG, ALU.add, ALU.mult, ['c32'], ['mneg16'])
        for l in range(depth):
            dma(pv[l], pvec_in[l], [], ['pv%d' % l])
            dma(br[l], brow_in[l:l + 1, :].partition_broadcast(128), [], ['br%d' % l])
        wsrc = wblk_in.rearrange("b p (h e) -> b p h e", e=2048)
        wdst = wblk16.rearrange("b p (h e) -> b p h e", e=2048)
        CB = 4

        def cast_weights(l_):
            for b0 in range(l_ * NBLK, (l_ + 1) * NBLK, CB):
                b1 = min((l_ + 1) * NBLK, b0 + CB)
                dma(wdst[b0:b1], wsrc[b0:b1], [], ['w16_%d' % b for b in range(b0, b1)], q='pool')

        def cast_natab(l_):
            for k in range(0, 200, 50):
                dma(natab16[l_ * 200 + k:l_ * 200 + k + 50], natab_in[l_ * 200 + k:l_ * 200 + k + 50], [], ['natab16_%d' % l_], q='pool')
        cast_weights(0)

        sc_raw = AR.alloc([16], F32)
        wm = [AR.alloc([8, 512], F32) for _ in range(2)]
        dma(sc_raw, cT_in, [], ['sc_raw'])
        act(scs.rearrange("p a b -> p (a b)"), sc_raw, AF.Silu, ['sc_raw'], ['scs'])
        for l in range(depth):
            psM, psMn = PSB(0)
            for blk in range(12):
                slot = blk % 2
                dma(wm[slot].rearrange("p a b -> p (a b)"), wmod_in[l * 12 + blk], [], ['wm%d' % slot])
                lst = []
                for oc4 in range(4):
                    oc = blk * 4 + oc4
                    for kc in range(8):
                        lst.append((psM[:, oc * 2:oc * 2 + 2], wm[slot][:, kc, oc4 * 128:(oc4 + 1) * 128], scs[:, kc, :], kc == 0, kc == 7))
                mms(lst, ['wm%d' % slot, 'scs'], [psMn])
            tt(modT[l], psM[:, 0:96].rearrange("p (a b) -> p a b", b=2), pv[l][:, 0:48].unsqueeze(2).to_broadcast([128, 48, 2]),
               ALU.add, [psMn, 'pv%d' % l], ['modT%d' % l])
            for (Ax, sci, nrm) in ((A1[l], 8, 48), (A2[l], 32, 56)):
                ts(Ax, modT[l][:, sci:sci + 8, :], 1.0, None, ALU.add, None, ['modT%d' % l], ['A%d' % l])
                tt(Ax, Ax, pv[l][:, nrm:nrm + 8].unsqueeze(2).to_broadcast([128, 8, 2]), ALU.mult, ['A%d' % l, 'pv%d' % l], ['A%d' % l])
        S.barrier()

        def norm_sq(xt, n, xtn):
            act(ph['sq'][:, :, :n], xt[:, :, :n], AF.Square, [xtn], ['sq'])

        def norm_rest(xt, n, Acol, shcol, hT, xtn, hTn):
            sq = ph['sq']
            bi = nextbank()
            pss, pn = PSB(bi)
            mms([(pss[:, :n], ones16, sq[:, c, :n], c == 0, c == 7) for c in range(8)], ['sq', 'c16'], [pn])
            rs = ph['rs']
            act(rs[:, :n], pss[:, :n], AF.Sqrt, [pn, 'epsb'], ['rs'], bias=epsb[:, 0:1], scale=1.0 / D)
            recip(rs[:, :n], rs[:, :n], ['rs'], ['rs'])
            for c in range(8):
                tmp = ph['ntmp'][c % 2]
                tt(tmp[:, :n], xt[:, c, :n], rs[:, :n], ALU.mult, [xtn, 'rs'], ['ntmp%d' % (c % 2)])
                act(hT[:, c, :n], tmp[:, :n], AF.Identity, ['ntmp%d' % (c % 2)], [hTn], bias=shcol[:, c:c + 1], scale=Acol[:, c:c + 1])

        def norm_mod(xt, n, Acol, shcol, hT, xtn, hTn):
            norm_sq(xt, n, xtn)
            norm_rest(xt, n, Acol, shcol, hT, xtn, hTn)

        def col_of(t):
            return 2 + t if t < CTX else 6 + t

        for l in range(depth):
            last = (l == depth - 1)
            xsrc = xT_in if l == 0 else x1T
            xsrc3 = xsrc.rearrange("(c p) t -> p c t", p=128)
            wb0 = l * NBLK
            pvl = pv[l]
            brl = br[l]
            AR.off = PERSIST
            ph = {}
            ph['sq'] = AR.alloc([8, 512], BF16)
            ph['rs'] = AR.alloc([512], F32)
            ph['ntmp'] = [AR.alloc([512], F32) for _ in range(2)]
            xt = AR.alloc([8, 512], F32)
            hT2 = [AR.alloc([8, 512], BF16) for _ in range(2)]
            wbuf = [AR.alloc([4096], BF16) for _ in range(3)]
            wsT32 = AR.alloc([1024], F32)
            wsT16 = AR.alloc([8, 128], BF16)
            zst = AR.alloc([4, 1024], BF16)
            xbcst = AR.alloc([16, 512], BF16)
            dst_ = AR.alloc([4, 32], F32)
            dtmp = AR.alloc([32], F32)
            qkst = AR.alloc([8, 512], BF16)
            sq16 = [AR.alloc([512], BF16) for _ in range(2)]
            rs2 = [AR.alloc([512], F32) for _ in range(2)]
            qtmp = [AR.alloc([512], F32) for _ in range(2)]
            qraw = [AR.alloc([512], F32) for _ in range(2)]
            vst = AR.alloc([4, 512], BF16)
            u16 = AR.alloc([4, 512], BF16)
            vg = AR.alloc([512], F32)
            vjunk = AR.alloc([512], BF16)
            ssv4 = AR.alloc([4, 4], F32)
            vn16 = [AR.alloc([512], BF16) for _ in range(4)]
            gtmp = AR.alloc([512], F32)
            ygm16 = [AR.alloc([512], BF16) for _ in range(4)]
            ygst = AR.alloc([4, 512], BF16)
            gst2 = [AR.alloc([4, 512], BF16) for _ in range(2)]
            dma(wsT32, wsT_in[l], [], ['wsT32'])
            cp(wsT16.rearrange("p a b -> p (a b)"), wsT32, ['wsT32'], ['wsT16'])
            wctr = [0]

            def load_w(blk):
                slot = wctr[0] % 3
                wctr[0] += 1
                dma(wbuf[slot], wblk16[wb0 + blk], ['w16_%d' % (wb0 + blk)], ['wb%d' % slot])
                return wbuf[slot], 'wb%d' % slot

            def a_norm1(ti):
                (t0_, n_) = tiles[ti]
                dma(xt[:, :, :n_], xsrc3[:, :, t0_:t0_ + n_], [], ['Axt'])
                norm_sq(xt, n_, 'Axt')

            def a_norm2(ti):
                (t0_, n_) = tiles[ti]
                j_ = 1 if t0_ < CTX else 0
                norm_rest(xt, n_, A1[l][:, :, j_], modT[l][:, 0:8, j_], hT2[ti % 2], 'Axt', 'AhT%d' % (ti % 2))
            a_norm1(0)
            a_norm2(0)
            for ti, (t0, n) in enumerate(tiles):
                isctx = t0 < CTX
                j = 1 if isctx else 0
                ntc = n // 128
                hT = hT2[ti % 2]
                hTn = 'AhT%d' % (ti % 2)
                if ti == len(tiles) // 2:
                    cast_natab(l)
                    if l + 1 < depth:
                        cast_weights(l + 1)

                def fm_group(w3, oc4, tagw):
                    bi = nextbank()
                    p_, pn = PSB(bi)
                    mms([(p_[:, :n], w3[:, kc, oc4 * 128:(oc4 + 1) * 128], hT[:, kc, :n], kc == 0, kc == 7) for kc in range(8)],
                        [tagw, hTn], [pn])
                    return p_, pn

                def tm_group(w3, tc, ncols, tagw):
                    bi = nextbank()
                    p_, pn = PSB(bi)
                    mms([(p_[:, :ncols], hT[:, kc, tc * 128:(tc + 1) * 128], w3[:, kc, :ncols], kc == 0, kc == 7) for kc in range(8)],
                        [tagw, hTn], [pn])
                    return p_, pn

                for blk in range(2):
                    w, wn = load_w(blk)
                    w3 = w.rearrange("p (k c) -> p k c", c=512)
                    for tc in range(ntc):
                        p_, pn = tm_group(w3, tc, 512, wn)
                        act(zst[:, tc, blk * 512:(blk + 1) * 512], p_, AF.Silu, [pn], ['zst'])
                dma(zs_d[t0:t0 + n, :].rearrange("(c p) f -> p c f", p=128), zst[:, :ntc, :], ['zst'], [], q='act')
                if ti + 1 < len(tiles):
                    a_norm1(ti + 1)
                for blk in range(2, 6):
                    w, wn = load_w(blk)
                    w3 = w.rearrange("p (k c) -> p k c", c=512)
                    for oc4 in range(4):
                        p_, pn = fm_group(w3, oc4, wn)
                        ch = (blk - 2) * 4 + oc4
                        if ch % 2 == 0:
                            cp(xbcst[:, ch, :n], p_[:, :n], [pn], ['xbcst'], eng='act')
                        else:
                            cp(xbcst[:, ch, :n], p_[:, :n], [pn], ['xbcst'])
                c0 = col_of(t0)
                dma(xbcT_d.rearrange("(c p) t -> p c t", p=128)[:, :, c0:c0 + n], xbcst[:, :, :n], ['xbcst'], [], q='act')
                if ti + 1 < len(tiles):
                    a_norm2(ti + 1)
                gi = 0
                pend = None

                def qk_tail(par, blk, oc4):
                    sfx = '%d' % par
                    b2 = nextbank()
                    p2, p2n = PSB(b2)
                    mms([(p2[:, :n], bo16, sq16[par][:, :n], True, True)], ['sq16' + sfx, 'c16'], [p2n])
                    act(rs2[par][:, :n], p2[:, :n], AF.Sqrt, [p2n, 'epsb'], ['rs2' + sfx], bias=epsb[:, 0:1], scale=1.0 / 64)
                    recip(rs2[par][:, :n], rs2[par][:, :n], ['rs2' + sfx], ['rs2' + sfx])
                    stt(qtmp[par][:, :n], qraw[par][:, :n], 0.125 if blk == 6 else 1.0, rs2[par][:, :n], ALU.mult, ALU.mult,
                        ['qraw' + sfx, 'rs2' + sfx], ['qtmp' + sfx])
                    wcol = 192 if blk == 6 else 193
                    act(qkst[:, (blk - 6) * 4 + oc4, :n], qtmp[par][:, :n], AF.Copy, ['qtmp' + sfx, 'pv%d' % l], ['qkst'],
                        scale=pvl[:, wcol:wcol + 1])
                for blk in range(6, 8):
                    w, wn = load_w(blk)
                    w3 = w.rearrange("p (k c) -> p k c", c=512)
                    for oc4 in range(4):
                        par = gi % 2
                        gi += 1
                        sfx = '%d' % par
                        p_, pn = fm_group(w3, oc4, wn)
                        act(sq16[par][:, :n], p_[:, :n], AF.Square, [pn], ['sq16' + sfx])
                        cp(qraw[par][:, :n], p_[:, :n], [pn], ['qraw' + sfx], eng='act')
                        if pend is not None:
                            qk_tail(*pend)
                        pend = (par, blk, oc4)
                qk_tail(*pend)
                dma(qT_d.rearrange("(c p) t -> p c t", p=128)[:, :, t0:t0 + n], qkst[:, 0:4, :n], ['qkst'], [], q='act')
                dma(kT_d.rearrange("(c p) t -> p c t", p=128)[:, :, t0:t0 + n], qkst[:, 4:8, :n], ['qkst'], [], q='act')
                w, wn = load_w(8)
                w3 = w.rearrange("p (k c) -> p k c", c=512)
                for tc in range(ntc):
                    p_, pn = tm_group(w3, tc, 512, wn)
                    cp(vst[:, tc, :], p_, [pn], ['vst'], eng='act')
                dma(vv_d[t0:t0 + n, :].rearrange("(c p) f -> p c f", p=128), vst[:, :ntc, :], ['vst'], [], q='act')
                w, wn = load_w(9)
                w3 = w.rearrange("p (k c) -> p k c", c=512)
                for tc in range(ntc):
                    p_, pn = tm_group(w3, tc, 512, wn)
                    act(u16[:, tc, :], p_, AF.Gelu_apprx_tanh, [pn], ['u16'])
                w, wn = load_w(10)
                w3 = w.rearrange("p (k c) -> p k c", c=512)
                for tc in range(ntc):
                    p_, pn = tm_group(w3, tc, 512, wn)
                    act(vg, p_, AF.Gelu_apprx_tanh, [pn], ['vg'])
                    act(vjunk, vg, AF.Square, ['vg'], ['vjunk', 'ssv%d' % tc], accum_out=ssv4[:, tc, 0:1])
                    act(ssv4[:, tc, 1:2], ssv4[:, tc, 0:1], AF.Sqrt, ['ssv%d' % tc, 'epsb'], ['ssv%d' % tc], bias=epsb[:, 0:1], scale=1.0 / 512)
                    recip(ssv4[:, tc, 2:3], ssv4[:, tc, 1:2], ['ssv%d' % tc], ['ssv%d' % tc])
                    stt(vn16[tc], vg, ssv4[:, tc, 2:3], brl[:, 80:592], ALU.mult, ALU.mult, ['vg', 'ssv%d' % tc, 'br%d' % l], ['vn16_%d' % tc])

                def gate_blocks(blks):
                    for blk in blks:
                        w, wn = load_w(blk)
                        w3 = w.rearrange("p (k c) -> p k c", c=512)
                        for oc4 in range(4):
                            p_, pn = fm_group(w3, oc4, wn)
                            gc = (blk - 11) * 4 + oc4
                            gsl = gst2[blk % 2]
                            act(gsl[:, oc4, :n], p_[:, :n], AF.Sigmoid, [pn, 'pv%d' % l], ['gst%d' % (blk % 2)], bias=pvl[:, 64 + gc:65 + gc])
                        r0 = (blk - 11) * 512
                        dma(gateT_d[r0:r0 + 512, :].rearrange("(c p) t -> p c t", p=128)[:, :, t0:t0 + n], gst2[blk % 2][:, :, :n],
                            ['gst%d' % (blk % 2)], [], q='act')
                gate_blocks([11, 12, 13])
                for tc in range(ntc):
                    bm = nextbank()
                    pm_, pmn = PSB(bm)
                    mms([(pm_[:, g * 64:(g + 1) * 64], wsT16[:, g, :], vn16[tc][:, g * 64:(g + 1) * 64], True, True) for g in range(8)],
                        ['wsT16', 'vn16_%d' % tc], [pmn])
                    tt(gtmp.rearrange("p (g d) -> p g d", d=64), pm_.rearrange("p (g d) -> p g d", d=64),
                       pvl[:, 194:202].unsqueeze(2).to_broadcast([128, 8, 64]), ALU.add, [pmn, 'pv%d' % l], ['gtmp'])
                    tt(ygm16[tc], gtmp, u16[:, tc, :], ALU.mult, ['gtmp', 'u16'], ['ygm16_%d' % tc])
                gate_blocks([14, 15])
                for tc in range(ntc):
                    bt_ = nextbank()
                    pt_, ptn = PSB(bt_)
                    pt16 = pt_.bitcast(BF16)
                    trs([(pt16[:, c * 128:(c + 1) * 128], ygm16[tc][:, c * 128:(c + 1) * 128], ident16) for c in range(4)],
                        ['ygm16_%d' % tc, 'c16'], [ptn])
                    cp(ygst[:, :, tc * 128:(tc + 1) * 128], pt16[:, 0:512].rearrange("p (c t) -> p c t", t=128), [ptn], ['ygst'], eng='act')
                dma(ygmT_d.rearrange("(c p) t -> p c t", p=128)[:, :, t0:t0 + n], ygst[:, :, :n], ['ygst'], [], q='act')
                gate_blocks([16])
                w, wn = load_w(17)
                w3 = w[:, 0:256].rearrange("p (k c) -> p k c", c=32)
                for tc in range(ntc):
                    p_, pn = tm_group(w3, tc, 32, wn)
                    tt(dtmp, p_[:, 0:32], brl[:, 0:32], ALU.add, [pn, 'br%d' % l], ['dtmp'])
                    act(dtmp, dtmp, AF.Exp, ['dtmp'], ['dtmp'])
                    act(dst_[:, tc, :], dtmp, AF.Ln, ['dtmp'], ['dst'], bias=1.0)
                dma(dts_d[t0:t0 + n, :].rearrange("(c p) f -> p c f", p=128), dst_[:, :ntc, :], ['dst'], [], q='act')
            S.barrier()
            if 'stopA' in dbg and l == 0:
                break
            AR.off = PERSIST
            cdiag = AR.alloc([80, 128], BF16)
            for j in range(80):
                ts(cdiag[:, j, :], ident32, pvl[:, 104 + j:105 + j], None, ALU.mult, None, ['c32', 'pv%d' % l], ['cdiag'])
            zt = AR.alloc([16, 4], BF16)
            S.op('dve', lambda e: e.memset(zt.rearrange("p a b -> p (a b)"), 0.0), [], ['zt'])
            xbcT3 = xbcT_d.rearrange("(c p) t -> p c t", p=128)
            dma(xbcT3[:, :, 0:2], zt[:, :, 0:2], ['zt'], ['xbcpad'], q='act')
            dma(xbcT3[:, :, 258:262], zt[:, :, 0:4], ['zt'], ['xbcpad'], q='act')
            dma(xbcT3[:, :, 262 + L:264 + L], zt[:, :, 0:2], ['zt'], ['xbcpad'], q='act')
            xpre = AR.alloc([16, 516], BF16)
            xact = AR.alloc([16, 512], BF16)
            cosT = AR.alloc([512], F32)
            sinT = AR.alloc([512], F32)
            t1 = AR.alloc([512], F32)
            t2 = AR.alloc([512], F32)
            bcrot = AR.alloc([8, 512], BF16)
            xsst = AR.alloc([4, 1024], BF16)
            btst = AR.alloc([4, 512], BF16)
            BT_d3 = BT_d.rearrange("(c p) t -> p c t", p=128)
            CT_d3 = CT_d.rearrange("(c p) t -> p c t", p=128)
            for (t0, n) in tiles:
                ntc = n // 128
                c0 = col_of(t0)
                dma(xpre[:, :, :n + 4], xbcT3[:, :, c0 - 2:c0 + n + 2], ['xbcpad'], ['xpre'])
                dma(cosT[:, :n], ropeC_in[:, t0:t0 + n], [], ['cosT'])
                dma(sinT[:, :n], ropeS_in[:, t0:t0 + n], [], ['sinT'])
                for ch in range(16):
                    bi = nextbank()
                    p_, pn = PSB(bi)
                    mms([(p_[:, :n], cdiag[:, k * 16 + ch, :], xpre[:, ch, k:k + n], k == 0, k == 4) for k in range(5)],
                        ['cdiag', 'xpre'], [pn])
                    act(xact[:, ch, :n], p_[:, :n], AF.Silu, [pn, 'pv%d' % l], ['xact'], bias=pvl[:, 88 + ch:89 + ch])
                for tc in range(ntc):
                    bi = nextbank()
                    p_, pn = PSB(bi)
                    pt16 = p_.bitcast(BF16)
                    trs([(pt16[:, ch * 128:(ch + 1) * 128], xact[:, ch, tc * 128:(tc + 1) * 128], ident16) for ch in range(8)],
                        ['xact', 'c16'], [pn])
                    cp(xsst[:, tc, :], pt16, [pn], ['xsst'], eng='act')
                dma(xs_d[t0:t0 + n, :].rearrange("(c p) f -> p c f", p=128), xsst[:, :ntc, :], ['xsst'], [], q='act')
                for i in range(8):
                    ch = 8 + i
                    bi = nextbank()
                    p_, pn = PSB(bi)
                    mms([(p_[:, :n], pm16, xact[:, ch, :n], True, True)], ['c16', 'xact'], [pn])
                    tt(t1[:, :n], xact[:, ch, :n], cosT[:, :n], ALU.mult, ['xact', 'cosT'], ['t1'])
                    tt(t2[:, :n], p_[:, :n], sinT[:, :n], ALU.mult, [pn, 'sinT'], ['t2'])
                    tt(bcrot[:, i, :n], t1[:, :n], t2[:, :n], ALU.add, ['t1', 't2'], ['bcrot'])
                dma(BT_d3[:, :, t0:t0 + n], bcrot[:, 0:4, :n], ['bcrot'], [], q='act')
                dma(CT_d3[:, :, t0:t0 + n], bcrot[:, 4:8, :n], ['bcrot'], [], q='act')
                for tc in range(ntc):
                    bi = nextbank()
                    p_, pn = PSB(bi)
                    pt16 = p_.bitcast(BF16)
                    trs([(pt16[:, g * 128:(g + 1) * 128], bcrot[:, g, tc * 128:(tc + 1) * 128], ident16) for g in range(4)],
                        ['bcrot', 'c16'], [pn])
                    cp(btst[:, tc, :], pt16[:, 0:512], [pn], ['btst'], eng='act')
                dma(Bt_d[t0:t0 + n, :].rearrange("(c p) f -> p c f", p=128), btst[:, :ntc, :], ['btst'], [], q='act')
            S.barrier()
            AR.off = PERSIST
            h32 = [AR.alloc([1024], F32) for _ in range(2)]
            h16 = [AR.alloc([1024], BF16) for _ in range(2)]
            Abc = AR.alloc([32], F32)
            for d in range(2):
                S.op('dve', lambda e, d=d: e.memset(h32[d], 0.0), [], ['h32_%d' % d])
                S.op('dve', lambda e, d=d: e.memset(h16[d], 0.0), [], ['h16_%d' % d])
            act(Abc, brl[:, 32:64], AF.Exp, ['br%d' % l], ['Abc'])
            ts(Abc, Abc, -1.0, None, ALU.mult, None, ['Abc'], ['Abc'])

            class _B:
                pass
            p1bufs = [[None, None], [None, None]]
            p2bufs = [None, None]
            for d in range(2):
                for par in range(2):
                    B_ = _B()
                    B_.xs = AR.alloc([1024], BF16)
                    B_.BT = AR.alloc([4, 128], BF16)
                    B_.CT = AR.alloc([4, 128], BF16)
                    B_.Bt = AR.alloc([512], BF16)
                    B_.dt = AR.alloc([32], F32)
                    B_.dtA = AR.alloc([16], F32)
                    B_.xdt = AR.alloc([1024], BF16)
                    if d == 0:
                        B_.xsD = AR.alloc([1024], BF16)
                    B_.nega = AR.alloc([16], F32)
                    B_.expa = AR.alloc([16], F32)
                    B_.acT = AR.alloc([128], F32)
                    B_.ahl = AR.alloc([2, 128], BF16)
                    B_.dec = AR.alloc([16, 128], BF16)
                    B_.cdB = AR.alloc([16], F32)
                    B_.cbm = AR.alloc([4, 128], BF16)
                    B_.MT = AR.alloc([16, 128], BF16)
                    B_.xdd = AR.alloc([1024], BF16)
                    p1bufs[d][par] = B_
                C_ = _B()
                C_.yi = AR.alloc([1024], F32)
                C_.ydir = AR.alloc([1024], F32)
                C_.yp = AR.alloc([1024], F32)
                C_.zs = AR.alloc([1024], BF16)
                C_.gg = AR.alloc([1024], F32)
                C_.gjunk = AR.alloc([1024], BF16)
                C_.ss = AR.alloc([4], F32)
                C_.gn = AR.alloc([1024], BF16)
                C_.yst = AR.alloc([8, 128], BF16)
                p2bufs[d] = C_
            yssdT3 = yssdT_d.rearrange("(c p) t -> p c t", p=128)

            def r3(v):
                return v.rearrange("p (h d) -> p h d", d=64)

            def scan_p1(c, d, want_y, par):
                sf = '_%d_%d' % (d, par)
                U32 = Uf32 if d == 0 else Ub32
                lend = 127 if d == 0 else 0
                tk = c * 128
                B_ = p1bufs[d][par]
                dma(B_.xs, xs_d[tk:tk + 128, :], [], ['xs' + sf])
                dma(B_.BT, BT_d3[:, :, tk:tk + 128], [], ['BT' + sf])
                dma(B_.CT, CT_d3[:, :, tk:tk + 128], [], ['CT' + sf])
                dma(B_.Bt, Bt_d[tk:tk + 128, :], [], ['Bt' + sf])
                dma(B_.dt, dts_d[tk:tk + 128, :], [], ['dt' + sf])
                dsl = B_.dt[:, d * 16:(d + 1) * 16]
                tt(B_.dtA, dsl, Abc[:, d * 16:(d + 1) * 16], ALU.mult, ['dt' + sf, 'Abc'], ['dtA' + sf])
                tt(r3(B_.xdt), r3(B_.xs), dsl.unsqueeze(2).to_broadcast([128, 16, 64]), ALU.mult, ['xs' + sf, 'dt' + sf], ['xdt' + sf], eng='pool')
                P0, P0n = PSB(0)
                mms([(P0[:, 0:16], U32, B_.dtA, True, True)], ['c32', 'dtA' + sf], [P0n])
                mms([(P0[0:16, 64:192], B_.dtA, U32, True, True)], ['c32', 'dtA' + sf], [P0n])
                act(B_.nega, P0[:, 0:16], AF.Copy, [P0n], ['nega' + sf], scale=-1.0)
                act(B_.expa, P0[:, 0:16], AF.Exp, [P0n], ['expa' + sf])
                act(B_.acT[0:16, :], P0[0:16, 64:192], AF.Copy, [P0n], ['acT' + sf])
                cp(B_.ahl[0:16, 0, :], B_.acT[0:16, :], ['acT' + sf], ['ahl' + sf])
                tt(B_.ahl[0:16, 1, :], B_.acT[0:16, :], B_.ahl[0:16, 0, :], ALU.subtract, ['acT' + sf, 'ahl' + sf], ['ahl' + sf])
                for q4 in range(4):
                    pb, pbn = PSB(1 if q4 % 2 == 0 else 0)
                    lst_ = []
                    for hh in range(4):
                        lst_.append((pb[:, hh * 128:(hh + 1) * 128], sel16[:, q4 * 4 + hh, :], B_.ahl[0:16, 0, :], True, False))
                        lst_.append((pb[:, hh * 128:(hh + 1) * 128], sel16[:, q4 * 4 + hh, :], B_.ahl[0:16, 1, :], False, False))
                        lst_.append((pb[:, hh * 128:(hh + 1) * 128], ident16, mneg16[d], False, True))
                    mms(lst_, ['sel16', 'c16', 'mneg16', 'ahl' + sf], [pbn])
                    for hh in range(4):
                        h = q4 * 4 + hh
                        act(B_.dec[:, h, :], pb[:, hh * 128:(hh + 1) * 128], AF.Exp, [pbn, 'nega' + sf], ['dec' + sf], bias=B_.nega[:, h:h + 1])
                    act(B_.cdB[:, q4 * 4:q4 * 4 + 4], pb.rearrange("p (h l) -> p h l", l=128)[:, :, lend], AF.Exp, [pbn], ['cdB' + sf])
                if want_y:
                    P3, P3n = PSB(3)
                    mms([(P3[:, g * 128:(g + 1) * 128], B_.BT[:, g, :], B_.CT[:, g, :], True, True) for g in range(4)],
                        ['BT' + sf, 'CT' + sf], [P3n])
                    P33 = P3.rearrange("p (g l) -> p g l", l=128)
                    for g in range(4):
                        tt(B_.MT[:, 4 * g:4 * g + 4, :], P33[:, g:g + 1, :].to_broadcast([128, 4, 128]), B_.dec[:, 4 * g:4 * g + 4, :], ALU.mult,
                           [P3n, 'dec' + sf], ['MT' + sf])
                    if d == 0:
                        tt(r3(B_.xsD), r3(B_.xs), brl[:, 64:80].unsqueeze(2).to_broadcast([128, 16, 64]), ALU.mult,
                           ['xs' + sf, 'br%d' % l], ['xsD' + sf], eng='pool')
                tt(r3(B_.xdd), r3(B_.xdt), B_.dec[:, :, lend:lend + 1].to_broadcast([128, 16, 64]), ALU.mult,
                   ['xdt' + sf, 'dec' + sf], ['xdd' + sf], eng='pool')

            def scan_p2(c, d, second, want_y, par):
                sf = '_%d_%d' % (d, par)
                sg = '_%d' % d
                tk = c * 128
                B_ = p1bufs[d][par]
                C_ = p2bufs[d]
                if want_y:
                    for half in range(2):
                        py, pyn = PSB(4 + half)
                        lst = []
                        if d == 0:
                            lst.append((py, ident16, B_.xsD[:, half * 512:(half + 1) * 512], True, False))
                        for hh in range(8):
                            h = half * 8 + hh
                            lst.append((py[:, hh * 64:(hh + 1) * 64], B_.MT[:, h, :], B_.xdt[:, h * 64:(h + 1) * 64],
                                        d != 0, (d != 0) or hh == 7))
                        mms(lst, ['c16', 'xsD' + sf, 'MT' + sf, 'xdt' + sf], [pyn])
                for half in range(2):
                    pS, pSn = PSB(6 + half)
                    mms([(pS[:, gg * 256:(gg + 1) * 256], B_.Bt[:, (half * 2 + gg) * 128:(half * 2 + gg + 1) * 128],
                          B_.xdd[:, (half * 2 + gg) * 256:(half * 2 + gg + 1) * 256], True, True) for gg in range(2)],
                        ['Bt' + sf, 'xdd' + sf], [pSn])
                if want_y:
                    for half in range(2):
                        pyi, pyin = PSB(2)
                        mms([(pyi[:, gg * 256:(gg + 1) * 256], B_.CT[:, half * 2 + gg, :],
                              h16[d][:, (half * 2 + gg) * 256:(half * 2 + gg + 1) * 256], True, True) for gg in range(2)],
                            ['CT' + sf, 'h16' + sg], [pyin])
                        cp(C_.yi[:, half * 512:(half + 1) * 512], pyi, [pyin], ['yi' + sg], eng='act')
                    tt(r3(C_.yi), r3(C_.yi), B_.expa.unsqueeze(2).to_broadcast([128, 16, 64]), ALU.mult, ['yi' + sg, 'expa' + sf], ['yi' + sg])
                    for half in range(2):
                        py, pyn = PSB(4 + half)
                        tt(C_.ydir[:, half * 512:(half + 1) * 512], C_.yi[:, half * 512:(half + 1) * 512], py, ALU.add,
                           ['yi' + sg, pyn], ['ydir' + sg])
                tt(r3(h32[d]), r3(h32[d]), B_.cdB.unsqueeze(2).to_broadcast([128, 16, 64]), ALU.mult, ['h32' + sg, 'cdB' + sf], ['h32' + sg], eng='pool')
                for half in range(2):
                    pS, pSn = PSB(6 + half)
                    tt(h32[d][:, half * 512:(half + 1) * 512], h32[d][:, half * 512:(half + 1) * 512], pS, ALU.add,
                       ['h32' + sg, pSn], ['h32' + sg])
                cp(h16[d], h32[d], ['h32' + sg], ['h16' + sg], eng='act')
                if not want_y:
                    return
                if not second:
                    dma(yp_d[tk:tk + 128, :], C_.ydir, ['ydir' + sg], ['ypd%d' % c], q='act')
                    return
                dma(C_.yp, yp_d[tk:tk + 128, :], ['ypd%d' % c], ['yp' + sg])
                dma(C_.zs, zs_d[tk:tk + 128, :], [], ['zs' + sg])
                tt(C_.ydir, C_.ydir, C_.yp, ALU.add, ['ydir' + sg, 'yp' + sg], ['ydir' + sg])
                tt(C_.gg, C_.ydir, C_.zs, ALU.mult, ['ydir' + sg, 'zs' + sg], ['gg' + sg])
                act(C_.gjunk, C_.gg, AF.Square, ['gg' + sg], ['gjunk' + sg, 'ss' + sg], accum_out=C_.ss[:, 0:1])
                act(C_.ss[:, 1:2], C_.ss[:, 0:1], AF.Sqrt, ['ss' + sg, 'epsb'], ['ss' + sg], bias=epsb[:, 0:1], scale=1.0 / 1024)
                recip(C_.ss[:, 2:3], C_.ss[:, 1:2], ['ss' + sg], ['ss' + sg])
                act(C_.gn, C_.gg, AF.Copy, ['gg' + sg, 'ss' + sg], ['gn' + sg], scale=C_.ss[:, 2:3])
                P2, P2n = PSB(2)
                pt16 = P2.bitcast(BF16)
                trs([(pt16[:, ch * 128:(ch + 1) * 128], C_.gn[:, ch * 128:(ch + 1) * 128], ident16) for ch in range(8)],
                    ['gn' + sg, 'c16'], [P2n])
                for ch in range(8):
                    act(C_.yst[:, ch, :], pt16[:, ch * 128:(ch + 1) * 128], AF.Copy, [P2n, 'pv%d' % l], ['yst' + sg],
                        scale=pvl[:, 184 + ch:185 + ch])
                dma(yssdT3[:, :, tk:tk + 128], C_.yst, ['yst' + sg], [], q='act')

            chain_f = list(range(NCH))
            chain_b = [1, 0] + list(range(NCH - 1, 1, -1))
            seen = set()

            def wy(c):
                return not (last and c < 2)

            def do_p2(c, d, par):
                scan_p2(c, d, c in seen, wy(c), par)
                seen.add(c)
            scan_p1(chain_f[0], 0, wy(chain_f[0]), 0)
            scan_p1(chain_b[0], 1, wy(chain_b[0]), 0)
            for i in range(NCH):
                la = []
                if i + 1 < NCH:
                    S.record()
                    scan_p1(chain_f[i + 1], 0, wy(chain_f[i + 1]), (i + 1) % 2)
                    la = S.stop()
                S.record()
                do_p2(chain_b[i], 1, i % 2)
                lb = S.stop()
                S.replay([la, lb])
                la = []
                if i + 1 < NCH:
                    S.record()
                    scan_p1(chain_b[i + 1], 1, wy(chain_b[i + 1]), (i + 1) % 2)
                    la = S.stop()
                S.record()
                do_p2(chain_f[i], 0, i % 2)
                lb = S.stop()
                S.replay([la, lb])
            S.barrier()
            if 'stopB' in dbg and l == 0:
                break
            AR.off = PERSIST
            cls, rel, rep = na_classes(L)
            NQ = L // 128
            tab = AR.alloc([40, 128], BF16)
            tabE = AR.alloc([40, 128], BF16)
            kctx = AR.alloc([2, 4, 128], BF16)
            vctx = [AR.alloc([8, 65], BF16) for _ in range(2)]
            NRING = 8
            kt_buf = [AR.alloc([4, 128], BF16) for _ in range(NRING)]
            v_buf = [AR.alloc([8, 65], BF16) for _ in range(NRING)]
            ring_has = {}
            qt_buf = [AR.alloc([4, 128], BF16) for _ in range(2)]
            pT = [AR.alloc([7 * 128], BF16) for _ in range(2)]
            rec = AR.alloc([8], F32)
            yna = AR.alloc([512], BF16)
            ynast = AR.alloc([4, 128], BF16)
            kT_d3 = kT_d.rearrange("(c p) t -> p c t", p=128)
            qT_d3 = qT_d.rearrange("(c p) t -> p c t", p=128)
            ynaT3 = ynaT_d.rearrange("(c p) t -> p c t", p=128)
            for i in range(2):
                S.op('dve', lambda e, i=i: e.memset(vctx[i].rearrange("p a b -> p (a b)"), 1.0), [], ['vctx%d' % i])
                dma(kctx[:, i, :, :], kT_d3[:, :, i * 128:(i + 1) * 128], [], ['kctx'])
                dma(vctx[i][:, :, 0:64], vv_d[i * 128:(i + 1) * 128, :].rearrange("t (h d) -> t h d", d=64), [], ['vctx%d' % i])
            for i in range(NRING):
                S.op('dve', lambda e, i=i: e.memset(v_buf[i].rearrange("p a b -> p (a b)"), 1.0), [], ['vbuf%d' % i])

            def ring_load(kt):
                sl = kt % NRING
                if ring_has.get(sl) == kt:
                    return
                ring_has[sl] = kt
                ktok = CTX + kt * 128
                dma(kt_buf[sl], kT_d3[:, :, ktok:ktok + 128], [], ['ktb%d' % sl])
                dma(v_buf[sl][:, :, 0:64], vv_d[ktok:ktok + 128, :].rearrange("t (h d) -> t h d", d=64), [], ['vbuf%d' % sl])

            def na_tile(qtok, kts, use_tab, nxt=()):
                qb = qt_buf[(qtok // 128) % 2]
                qbn = 'qt_buf%d' % ((qtok // 128) % 2)
                dma(qb, qT_d3[:, :, qtok:qtok + 128], [], [qbn])
                nk = len(kts)
                for kt in list(kts) + list(nxt):
                    ring_load(kt)
                slots = [kt % NRING for kt in kts]
                nb = nk + 2
                def st_qk(h):
                    cc, e = h // 2, h % 2
                    pA, pAn = PSB(2 * e)
                    pB, pBn = PSB(2 * e + 1)
                    lstA, lstB = [], []
                    rdA, rdB = set([qbn, 'kctx']), set([qbn, 'kctx'])
                    for j in range(nb):
                        bank, lst, rd = (pA, lstA, rdA) if j < 4 else (pB, lstB, rdB)
                        col = (j % 4) * 128
                        qop = qb[e * 64:(e + 1) * 64, cc, :]
                        if j < nk:
                            rd.add('ktb%d' % slots[j])
                            lst.append((bank[:, col:col + 128], kt_buf[slots[j]][e * 64:(e + 1) * 64, cc, :], qop, True, True))
                        else:
                            lst.append((bank[:, col:col + 128], kctx[e * 64:(e + 1) * 64, j - nk, cc, :], qop, True, True))
                    mms(lstA, sorted(rdA), [pAn])
                    if lstB:
                        mms(lstB, sorted(rdB), [pBn])

                def st_sm(h):
                    cc, e = h // 2, h % 2
                    pA, pAn = PSB(2 * e)
                    pB, pBn = PSB(2 * e + 1)
                    na_ = min(nb, 4)
                    act(pT[e][:, 0:na_ * 128], pA[:, 0:na_ * 128], AF.Exp, [pAn], ['pT%d' % e])
                    if nb > 4:
                        act(pT[e][:, 512:512 + (nb - 4) * 128], pB[:, 0:(nb - 4) * 128], AF.Exp, [pBn], ['pT%d' % e])
                    if use_tab:
                        tE4 = tabE.rearrange("p (s h) q -> p s h q", h=8)
                        na4 = min(nk, 4)
                        tt(pT[e][:, 0:na4 * 128].rearrange("p (s q) -> p s q", q=128), pT[e][:, 0:na4 * 128].rearrange("p (s q) -> p s q", q=128),
                           tE4[:, 0:na4, h, :], ALU.mult, ['pT%d' % e, 'tabE'], ['pT%d' % e])
                        if nk > 4:
                            tt(pT[e][:, 512:512 + (nk - 4) * 128].rearrange("p (s q) -> p s q", q=128),
                               pT[e][:, 512:512 + (nk - 4) * 128].rearrange("p (s q) -> p s q", q=128),
                               tE4[:, 4:nk, h, :], ALU.mult, ['pT%d' % e, 'tabE'], ['pT%d' % e])

                def st_pv(h):
                    cc, e = h // 2, h % 2
                    po, pon = PSB(4 + h // 4)
                    col = (h % 4) * 65
                    lst = []
                    rd = ['pT%d' % e, 'vctx0', 'vctx1']
                    for j in range(nb):
                        if j < nk:
                            vb = v_buf[slots[j]]
                            rd.append('vbuf%d' % slots[j])
                        else:
                            vb = vctx[j - nk]
                        lst.append((po[:, col:col + 65], pT[e][:, j * 128:(j + 1) * 128], vb[:, h, :], j == 0, j == nb - 1))
                    mms(lst, rd, [pon])

                st_qk(0)
                st_qk(1)
                for h in range(8):
                    st_sm(h)
                    st_pv(h)
                    if h + 2 < 8:
                        st_qk(h + 2)
                for b4 in range(2):
                    po, pon = PSB(4 + b4)
                    po3 = po[:, 0:260].rearrange("p (h e) -> p h e", e=65)
                    recip(rec[:, b4 * 4:(b4 + 1) * 4], po3[:, :, 64], [pon], ['rec'])
                    tt(yna.rearrange("p (h d) -> p h d", d=64)[:, b4 * 4:(b4 + 1) * 4, :], po3[:, :, 0:64],
                       rec[:, b4 * 4:(b4 + 1) * 4].unsqueeze(2).to_broadcast([128, 4, 64]), ALU.mult, [pon, 'rec'], ['yna'])
                p6, p6n = PSB(6)
                pt16 = p6.bitcast(BF16)
                trs([(pt16[:, c * 128:(c + 1) * 128], yna[:, c * 128:(c + 1) * 128], ident16) for c in range(4)], ['yna', 'c16'], [p6n])
                cp(ynast, pt16[:, 0:512].rearrange("p (c t) -> p c t", t=128), [p6n], ['ynast'], eng='act')
                dma(ynaT3[:, :, qtok:qtok + 128], ynast, ['ynast'], [], q='act')

            if not last:
                for i in range(2):
                    na_tile(i * 128, [], False)
            cur = -1
            for qt in range(NQ):
                c = cls[qt]
                if c != cur:
                    cur = c
                    b0 = l * 200 + c * 40
                    dma(tab, natab16[b0:b0 + 40].rearrange("s k q -> k s q"), ['natab16_%d' % l], ['tab'])
                    act(tabE.rearrange("p a b -> p (a b)"), tab.rearrange("p a b -> p (a b)"), AF.Exp, ['tab'], ['tabE'])
                nxt = [qt + 1 + dk for dk in rel[cls[qt + 1]]] if qt + 1 < NQ else []
                na_tile(CTX + qt * 128, [qt + dk for dk in rel[c]], True, nxt)
            S.barrier()
            if 'stopC' in dbg and l == 0:
                break
            AR.off = PERSIST
            ph = {}
            ph['sq'] = AR.alloc([8, 512], BF16)
            ph['rs'] = AR.alloc([512], F32)
            ph['ntmp'] = [AR.alloc([512], F32) for _ in range(2)]
            xt2 = [AR.alloc([8, 512], F32) for _ in range(2)]
            ys = AR.alloc([8, 512], BF16)
            yn_ = AR.alloc([4, 512], BF16)
            yg = AR.alloc([4, 512], BF16)
            gt = AR.alloc([24, 512], BF16)
            mg = AR.alloc([8, 512], BF16)
            h2 = AR.alloc([8, 512], BF16)
            actb = AR.alloc([22, 512], BF16)
            m1 = [AR.alloc([512], F32) for _ in range(2)]
            m2 = [AR.alloc([512], F32) for _ in range(2)]
            m3 = [AR.alloc([512], F32) for _ in range(2)]
            sa = AR.alloc([512], BF16)
            wbuf = [AR.alloc([4096], BF16) for _ in range(3)]
            wctr[0] = 0
            ygmT3 = ygmT_d.rearrange("(c p) t -> p c t", p=128)
            gateT3 = gateT_d.rearrange("(c p) t -> p c t", p=128)
            x1T3 = x1T.rearrange("(c p) t -> p c t", p=128)
            outT3 = outT.rearrange("(c p) t -> p c t", p=128)

            def load_w(blk):
                slot = wctr[0] % 3
                wctr[0] += 1
                dma(wbuf[slot], wblk16[wb0 + blk], ['w16_%d' % (wb0 + blk)], ['wb%d' % slot])
                return wbuf[slot], 'wb%d' % slot

            dtiles = (tiles[1:] if last else tiles)

            def d_loads(ti):
                (t0_, n_) = dtiles[ti]
                dma(xt2[ti % 2][:, :, :n_], xsrc3[:, :, t0_:t0_ + n_], [], ['Dxt%d' % (ti % 2)])
                dma(ys[:, :, :n_], yssdT3[:, :, t0_:t0_ + n_], [], ['ys'])
                dma(yn_[:, :, :n_], ynaT3[:, :, t0_:t0_ + n_], [], ['yn'])
                dma(yg[:, :, :n_], ygmT3[:, :, t0_:t0_ + n_], [], ['yg'])
                dma(gt[:, :, :n_], gateT3[:, :, t0_:t0_ + n_], [], ['gt'])
            d_loads(0)
            for ti, (t0, n) in enumerate(dtiles):
                isctx = t0 < CTX
                j = 1 if isctx else 0
                xt = xt2[ti % 2]
                xtn = 'Dxt%d' % (ti % 2)
                for jb in range(4):
                    w, wn = load_w(18 + jb)
                    for oo in range(2):
                        oc = 2 * jb + oo
                        base = oo * 2048
                        pa, pan = PSB(nextbank())
                        mms([(pa[:, :n], w[:, base + kc * 128:base + (kc + 1) * 128], ys[:, kc, :n], kc == 0, kc == 7) for kc in range(8)],
                            [wn, 'ys'], [pan])
                        pb_, pbn = PSB(nextbank())
                        mms([(pb_[:, :n], w[:, base + 1024 + kc * 128:base + 1024 + (kc + 1) * 128], yn_[:, kc, :n], kc == 0, kc == 3) for kc in range(4)],
                            [wn, 'yn'], [pbn])
                        pc_, pcn = PSB(nextbank())
                        mms([(pc_[:, :n], w[:, base + 1536 + kc * 128:base + 1536 + (kc + 1) * 128], yg[:, kc, :n], kc == 0, kc == 3) for kc in range(4)],
                            [wn, 'yg'], [pcn])
                        mp = oc % 2
                        ms_ = '%d' % mp
                        tt(m1[mp][:, :n], pa[:, :n], gt[:, oc, :n], ALU.mult, [pan, 'gt'], ['m1' + ms_])
                        tt(m2[mp][:, :n], pb_[:, :n], gt[:, 8 + oc, :n], ALU.mult, [pbn, 'gt'], ['m2' + ms_])
                        tt(m3[mp][:, :n], pc_[:, :n], gt[:, 16 + oc, :n], ALU.mult, [pcn, 'gt'], ['m3' + ms_])
                        tt(m1[mp][:, :n], m1[mp][:, :n], m2[mp][:, :n], ALU.add, ['m1' + ms_, 'm2' + ms_], ['m1' + ms_], eng='pool')
                        tt(mg[:, oc, :n], m1[mp][:, :n], m3[mp][:, :n], ALU.add, ['m1' + ms_, 'm3' + ms_], ['mg'], eng='pool')
                if ti + 1 < len(dtiles):
                    d_loads(ti + 1)
                for jb in range(2):
                    w, wn = load_w(22 + jb)
                    w3 = w.rearrange("p (k c) -> p k c", c=512)
                    for oc4 in range(4):
                        oc = jb * 4 + oc4
                        p_, pn = PSB(nextbank())
                        mms([(p_[:, :n], w3[:, kc, oc4 * 128:(oc4 + 1) * 128], mg[:, kc, :n], kc == 0, kc == 7) for kc in range(8)],
                            [wn, 'mg'], [pn])
                        stt(xt[:, oc, :n], p_[:, :n], modT[l][:, 16 + oc, j:j + 1], xt[:, oc, :n], ALU.mult, ALU.add,
                            [pn, 'modT%d' % l, xtn], [xtn])
                norm_mod(xt, n, A2[l][:, :, j], modT[l][:, 24:32, j], h2, xtn, 'DhT')
                for jb in range(11):
                    w, wn = load_w(24 + jb)
                    w3 = w.rearrange("p (k c) -> p k c", c=512)
                    for hcl in range(2):
                        hc = 2 * jb + hcl
                        pa, pan = PSB(nextbank())
                        mms([(pa[:, :n], w3[:, kc, (hcl * 2) * 128:(hcl * 2 + 1) * 128], h2[:, kc, :n], kc == 0, kc == 7) for kc in range(8)],
                            [wn, 'DhT'], [pan])
                        pb_, pbn = PSB(nextbank())
                        mms([(pb_[:, :n], w3[:, kc, (hcl * 2 + 1) * 128:(hcl * 2 + 2) * 128], h2[:, kc, :n], kc == 0, kc == 7) for kc in range(8)],
                            [wn, 'DhT'], [pbn])
                        act(sa[:, :n], pa[:, :n], AF.Silu, [pan], ['sa'])
                        tt(actb[:, hc, :n], sa[:, :n], pb_[:, :n], ALU.mult, ['sa', pbn], ['actb'])
                for oc in range(8):
                    w, wn = load_w(35 + oc)
                    wv = w[:, 0:2816].rearrange("p (k c) -> p k c", c=128)
                    p_, pn = PSB(nextbank())
                    mms([(p_[:, :n], wv[:, kc, :], actb[:, kc, :n], kc == 0, kc == 21) for kc in range(22)], [wn, 'actb'], [pn])
                    stt(xt[:, oc, :n], p_[:, :n], modT[l][:, 40 + oc, j:j + 1], xt[:, oc, :n], ALU.mult, ALU.add,
                        [pn, 'modT%d' % l, xtn], [xtn])
                if last:
                    dma(outT3[:, :, t0 - CTX:t0 - CTX + n], xt[:, :, :n], [xtn], [], q='act')
                else:
                    dma(x1T3[:, :, t0:t0 + n], xt[:, :, :n], [xtn], [], q='act')
            S.barrier()
        S.emit()
    return nc


def _slab(W):
    k = W.shape[0] // 128
    c = W.shape[1]
    out = np.zeros((128, 4096), np.float32)
    out[:, :k * c] = W.reshape(k, 128, c).transpose(1, 0, 2).reshape(128, k * c)
    return out


def pack_weights(inp, l):
    w_in = inp['w_in'][l]
    blks = []
    segs = [(0, 512), (512, 1024),
            (1024, 1536), (1536, 2048), (2048, 2560), (2560, 3072),
            (3104, 3616), (3616, 4128), (4128, 4640),
            (4640, 5152), (5152, 5664)]
    segs += [(5664 + 512 * i, 5664 + 512 * (i + 1)) for i in range(6)]
    for (a, b) in segs:
        blks.append(_slab(w_in[:, a:b]))
    blks.append(_slab(w_in[:, 3072:3104]))
    wa, wb_, wc = inp['w_branch_ssd'][l], inp['w_branch_na'][l], inp['w_branch_gm'][l]
    for j in range(4):
        blk = np.zeros((128, 4096), np.float32)
        for oo in range(2):
            oc = 2 * j + oo
            base = oo * 2048
            blk[:, base:base + 1024] = wa[:, oc * 128:(oc + 1) * 128].reshape(8, 128, 128).transpose(1, 0, 2).reshape(128, 1024)
            blk[:, base + 1024:base + 1536] = wb_[:, oc * 128:(oc + 1) * 128].reshape(4, 128, 128).transpose(1, 0, 2).reshape(128, 512)
            blk[:, base + 1536:base + 2048] = wc[:, oc * 128:(oc + 1) * 128].reshape(4, 128, 128).transpose(1, 0, 2).reshape(128, 512)
        blks.append(blk)
    wo = inp['w_out'][l]
    for j in range(2):
        blks.append(_slab(wo[:, j * 512:(j + 1) * 512]))
    wf = inp['w_ffn_in'][l]
    for j in range(11):
        slab = np.zeros((1024, 512), np.float32)
        for hcl in range(2):
            for ab in range(2):
                src = ab * 2816 + (2 * j + hcl) * 128
                slab[:, (hcl * 2 + ab) * 128:(hcl * 2 + ab + 1) * 128] = wf[:, src:src + 128]
        blks.append(_slab(slab))
    wfo = inp['w_ffn_out'][l]
    for oc in range(8):
        blks.append(_slab(wfo[:, oc * 128:(oc + 1) * 128]))
    assert len(blks) == NBLK
    return np.stack(blks)


def pcol(v):
    return np.ascontiguousarray(v.reshape(-1, 128).T)


def pack_shared(inp, L, depth):
    T = CTX + L
    R = L // 64
    f32 = np.float32
    wmod = np.stack([inp['w_mod'][l].reshape(8, 128, 12, 512).transpose(2, 1, 0, 3).reshape(12, 128, 4096)
                     for l in range(depth)]).reshape(depth * 12, 128, 4096)
    wblk = np.concatenate([pack_weights(inp, l) for l in range(depth)], axis=0)
    pvec = np.zeros((depth, 128, NPV), f32)
    brow = np.zeros((depth, NBR), f32)
    wsT = np.zeros((depth, 128, 1024), f32)
    for l in range(depth):
        pvec[l, :, 0:48] = pcol(inp['b_mod'][l])
        pvec[l, :, 48:56] = pcol(inp['norm1'][l])
        pvec[l, :, 56:64] = pcol(inp['norm2'][l])
        pvec[l, :, 64:88] = pcol(inp['b_gate'][l])
        pvec[l, :, 88:104] = pcol(inp['conv_b'][l])
        for k in range(5):
            pvec[l, :, 104 + k * 16:104 + (k + 1) * 16] = pcol(inp['conv_w'][l][k])
        pvec[l, :, 184:192] = pcol(inp['ssd_norm'][l])
        pvec[l, :, 192] = np.tile(inp['q_norm'][l], 2)
        pvec[l, :, 193] = np.tile(inp['k_norm'][l], 2)
        pvec[l, :, 194:202] = inp['b_spatial'][l].T
        brow[l, 0:32] = inp['dt_bias'][l].reshape(-1)
        brow[l, 32:64] = inp['a_log'][l].reshape(-1)
        brow[l, 64:80] = inp['d_skip'][l]
        brow[l, 80:592] = inp['gm_norm'][l]
        wsT[l] = inp['w_spatial'][l].transpose(2, 0, 1).reshape(128, 1024)
    freqs = (np.float32(10000.0) ** (-np.arange(32, dtype=f32) / np.float32(32))).astype(f32)
    pos = np.arange(L)
    ang_row = (pos // 64).astype(f32)[:, None] * freqs
    ang_col = (pos % 64).astype(f32)[:, None] * freqs
    ropeC = np.ones((128, T), f32)
    ropeS = np.zeros((128, T), f32)
    for half, ang in ((0, ang_row), (1, ang_col)):
        c = np.cos(ang).T.astype(f32)
        s = np.sin(ang).T.astype(f32)
        ropeC[half * 64:half * 64 + 32, CTX:] = c
        ropeC[half * 64 + 32:half * 64 + 64, CTX:] = c
        ropeS[half * 64:half * 64 + 32, CTX:] = s
        ropeS[half * 64 + 32:half * 64 + 64, CTX:] = s
    consts = np.zeros((128, NCONST), f32)
    i = np.arange(128)
    consts[:, 0:128] = np.eye(128, dtype=f32)
    consts[:, 128:256] = (i[:, None] <= i[None, :]).astype(f32)
    consts[:, 256:384] = (i[:, None] >= i[None, :]).astype(f32)
    consts[:, 384:512] = 1.0
    consts[:, 512:640] = ((i[:, None] // 64) == (i[None, :] // 64)).astype(f32)
    pm = np.zeros((128, 128), f32)
    for base in (0, 64):
        for n in range(32):
            pm[base + n + 32, base + n] = -1.0
            pm[base + n, base + n + 32] = 1.0
    consts[:, 640:768] = pm
    sel = np.zeros((16, 16, 128), f32)
    for h in range(16):
        sel[h, h, :] = 1.0
    consts[0:16, 768:768 + 2048] = sel.reshape(16, 2048)
    cls, rel, rep = na_classes(L)

    def rs_(r):
        return min(max(r - 4, 0), R - 8)
    natab = np.full((depth, 5, 5, 8, 128, 128), NEG, f32)
    kc = np.arange(64)
    qc = np.arange(64)
    cs = np.clip(qc - 8, 0, 48)
    colok = (kc[:, None] >= cs[None, :]) & (kc[:, None] < cs[None, :] + 16)
    ci = np.clip(kc[:, None] - qc[None, :] + 15, 0, 30)
    for c, qt in rep.items():
        for slot, dk in enumerate(rel[c]):
            kt = qt + dk
            for a in range(2):
                r = 2 * qt + a
                for b in range(2):
                    krow = 2 * kt + b
                    if not (rs_(r) <= krow < rs_(r) + 8):
                        continue
                    ri = krow - r + 7
                    for l in range(depth):
                        vals = inp['rpb'][l][:, ri, :][:, ci]
                        blk = np.where(colok[None], vals, np.float32(NEG))
                        natab[l, c, slot, :, b * 64:(b + 1) * 64, a * 64:(a + 1) * 64] = blk
    natab = natab.reshape(depth * 200, 128, 128)
    return dict(wmod=wmod, wblk=wblk, pvec=pvec, brow=brow, wsT=wsT, ropeC=ropeC, ropeS=ropeS,
                natab=natab, consts=consts)


def pack_core(inp, b, shared):
    xT = np.ascontiguousarray(np.concatenate([inp['ctx'][b].T, inp['x'][b].T], axis=1))
    cT = np.zeros((128, 8, 2), np.float32)
    cT[:, :, 0] = pcol(inp['c'][b])
    cT[:, :, 1] = pcol(inp['c_ctx'])
    m = dict(shared)
    m['xT'] = xT
    m['cT'] = cT.reshape(128, 16)
    return m


_NC_CACHE = {}


def kernel(**inputs):
    inp = {k: np.asarray(v, dtype=np.float32) for k, v in inputs.items()}
    B, L, _ = inp['x'].shape
    depth = inp['w_in'].shape[0]
    key = (L, depth)
    if key not in _NC_CACHE:
        _NC_CACHE[key] = build(L, depth)
    nc = _NC_CACHE[key]
    shared = pack_shared(inp, L, depth)
    slots = [0, 1, 4, 5][:B] if B <= 4 else list(range(B))
    n_cores = 8 if B <= 4 else B
    real = {slots[b]: pack_core(inp, b, shared) for b in range(B)}
    zero = None
    in_maps = []
    for i in range(n_cores):
        if i in real:
            in_maps.append(real[i])
        else:
            if zero is None:
                zero = {k: np.zeros_like(v) for k, v in real[slots[0]].items()}
            in_maps.append(zero)
    res = run_bass_kernel_spmd(nc, in_maps, core_ids=list(range(n_cores)))
    out = np.stack([np.ascontiguousarray(res.results[slots[b]]['outT'].T) for b in range(B)])
    return out.astype(np.float32)
```

```python
import math
import numpy as np
import concourse.bass as bass
import concourse.mybir as mybir
from concourse.bass_utils import run_bass_kernel_spmd
from contextlib import ExitStack

F32 = mybir.dt.float32
BF16 = mybir.dt.bfloat16
U8 = mybir.dt.uint8
AF = mybir.ActivationFunctionType
ALU = mybir.AluOpType
AX = mybir.AxisListType

D = 1024
CTX = 256
EPS = 1e-6
NBLK = 43
NPV = 202
NBR = 592
NCONST = 768 + 2048
NEG = -30000.0

ENGS = ['pe', 'act', 'dve', 'pool', 'sp']
N_DSEM = 48
DSEM_RANGE = {'sp': (0, 22), 'act': (22, 32), 'pool': (32, 48)}


class Sched:
    def __init__(self, nc):
        self.nc = nc
        self.ops = {e: [] for e in ENGS}
        self.last_write = {}
        self.readers = {}
        self.known = {e: {} for e in ENGS}
        self.dsem_uses = [0] * N_DSEM
        self.dsem_next = {'sp': 0, 'pool': 0, 'act': 0}
        self.signal = {e: set() for e in ENGS}
        self.rec = None

    def record(self):
        self.rec = []

    def stop(self):
        r = self.rec
        self.rec = None
        return r

    def replay(self, lists):
        idx = [0] * len(lists)
        tot = [max(1, len(x)) for x in lists]
        while True:
            live = [k for k in range(len(lists)) if idx[k] < len(lists[k])]
            if not live:
                break
            k = min(live, key=lambda k: idx[k] / tot[k])
            self.op(*lists[k][idx[k]])
            idx[k] += 1

    def _need(self, eng, tok, waits):
        kind, key, val = tok
        if kind == 'eng' and key == eng and eng in ('pe', 'sp'):
            return
        k = (kind, key)
        if self.known[eng].get(k, 0) >= val:
            return
        self.known[eng][k] = val
        waits[k] = max(waits.get(k, 0), val)
        if kind == 'eng':
            self.signal[key].add(val)

    def op(self, eng, fn, reads=(), writes=(), dma=False):
        if self.rec is not None:
            self.rec.append((eng, fn, tuple(reads), tuple(writes), dma))
            return None
        waits = {}
        for b in reads:
            w = self.last_write.get(b)
            if w is not None:
                self._need(eng, w, waits)
        for b in writes:
            w = self.last_write.get(b)
            if w is not None:
                self._need(eng, w, waits)
            for r in self.readers.get(b, ()):
                self._need(eng, r, waits)
        idx = len(self.ops[eng]) + 1
        if dma:
            lo, hi = DSEM_RANGE[eng]
            j = lo + self.dsem_next[eng]
            self.dsem_next[eng] = (self.dsem_next[eng] + 1) % (hi - lo)
            n = self.dsem_uses[j]
            if n > 0:
                self._need(eng, ('dma', j, 16 * n), waits)
            self.dsem_uses[j] = n + 1
            tok = ('dma', j, 16 * (n + 1))
        else:
            tok = ('eng', eng, idx)
        for b in reads:
            self.readers.setdefault(b, []).append(tok)
        for b in writes:
            self.last_write[b] = tok
            self.readers[b] = []
        self.ops[eng].append((fn, waits, tok))
        return tok

    def barrier(self):
        toks = []
        for e in ENGS:
            for i in range(len(self.ops[e]), 0, -1):
                t = self.ops[e][i - 1][2]
                if t is not None and t[0] == 'eng':
                    toks.append(t)
                    break
        for j in range(N_DSEM):
            if self.dsem_uses[j] > 0:
                toks.append(('dma', j, 16 * self.dsem_uses[j]))
        for e in ENGS:
            waits = {}
            for t in toks:
                self._need(e, t, waits)
            self.ops[e].append((None, waits, None))
        self.last_write.clear()
        self.readers.clear()

    def emit(self):
        nc = self.nc
        with ExitStack() as st:
            esem = {e: st.enter_context(nc.semaphore('es_' + e)) for e in ENGS}
            dsem = [st.enter_context(nc.semaphore('ds_%d' % j)) for j in range(N_DSEM)]
            block = st.enter_context(nc.Block())
            sigmap = {}
            for e in ENGS:
                cnt = 0
                m = {}
                ss = self.signal[e]
                for i in range(1, len(self.ops[e]) + 1):
                    if i in ss:
                        cnt += 1
                        m[i] = cnt
                sigmap[e] = m

            def run(e, engobj):
                for i, (fn, waits, tok) in enumerate(self.ops[e], start=1):
                    for (kind, key), val in waits.items():
                        if kind == 'eng':
                            engobj.wait_ge(esem[key], sigmap[key][val])
                        else:
                            engobj.wait_ge(dsem[key], val)
                    if fn is None:
                        continue
                    ins = fn(engobj)
                    if tok[0] == 'dma':
                        ins.then_inc(dsem[tok[1]], 16)
                    elif i in sigmap[e]:
                        ins.then_inc(esem[e], 1)

            @block.tensor
            def _(eng):
                run('pe', eng)

            @block.scalar
            def _(eng):
                run('act', eng)

            @block.vector
            def _(eng):
                run('dve', eng)

            @block.gpsimd
            def _(eng):
                run('pool', eng)

            @block.sync
            def _(eng):
                run('sp', eng)


class Arena:
    def __init__(self, ap, size):
        self.ap = ap
        self.size = size
        self.off = 0

    def alloc(self, free_shape, dt):
        esz = 4 if dt == F32 else 2
        n = 1
        for s in free_shape:
            n *= s
        nb = (n * esz + 63) // 64 * 64
        assert self.off + nb <= self.size, "SBUF arena overflow %d + %d" % (self.off, nb)
        v = self.ap[:, self.off:self.off + n * esz].bitcast(dt)
        self.off += nb
        if len(free_shape) == 2:
            v = v.rearrange("p (a b) -> p a b", b=free_shape[1])
        elif len(free_shape) == 3:
            v = v.rearrange("p (a b c) -> p a b c", b=free_shape[1], c=free_shape[2])
        return v


def na_classes(L):
    R = L // 64
    NQ = R // 2

    def rs(r):
        return min(max(r - 4, 0), R - 8)
    info = []
    for qt in range(NQ):
        r0, r1 = 2 * qt, 2 * qt + 1
        lo = min(rs(r0), rs(r1)) // 2
        hi = (max(rs(r0), rs(r1)) + 7) // 2
        info.append(tuple(k - qt for k in range(lo, hi + 1)))
    cls = []
    for qt in range(NQ):
        if qt == 0:
            cls.append(0)
        elif qt == 1:
            cls.append(1)
        elif qt == NQ - 2:
            cls.append(3)
        elif qt == NQ - 1:
            cls.append(4)
        else:
            cls.append(2)
    rel = {}
    rep = {}
    for qt in range(NQ):
        c = cls[qt]
        if c in rel:
            assert rel[c] == info[qt], (c, rel[c], info[qt])
        else:
            rel[c] = info[qt]
            rep[c] = qt
    return cls, rel, rep


def build(L, depth=2, dbg=()):
    T = CTX + L
    NCH = T // 128
    TP = T + 8
    tiles = [(0, 256)] + [(CTX + 512 * i, 512) for i in range(L // 512)]
    nc = bass.Bass("TRN2", target_bir_lowering=False)

    def din(name, shape, dt=F32):
        return nc.dram_tensor(name, shape, dt, kind="ExternalInput").ap()

    def dscr(name, shape, dt):
        kind = "ExternalOutput" if name in dbg else "Internal"
        return nc.dram_tensor(name, shape, dt, kind=kind).ap()

    xT_in = din("xT", [D, T])
    cT_in = din("cT", [128, 16])
    wmod_in = din("wmod", [depth * 12, 128, 4096])
    wblk_in = din("wblk", [depth * NBLK, 128, 4096])
    pvec_in = din("pvec", [depth, 128, NPV])
    brow_in = din("brow", [depth, NBR])
    wsT_in = din("wsT", [depth, 128, 1024])
    ropeC_in = din("ropeC", [128, T])
    ropeS_in = din("ropeS", [128, T])
    natab_in = din("natab", [depth * 200, 128, 128])
    consts_in = din("consts", [128, NCONST])
    outT = nc.dram_tensor("outT", [D, L], F32, kind="ExternalOutput").ap()

    wblk16 = dscr("wblk16", [depth * NBLK, 128, 4096], BF16)
    natab16 = dscr("natab16", [depth * 200, 128, 128], BF16)
    x1T = dscr("x1T", [D, T], F32)
    zs_d = dscr("zs", [T, 1024], BF16)
    xbcT_d = dscr("xbcT", [2048, TP], BF16)
    dts_d = dscr("dts", [T, 32], F32)
    qT_d = dscr("qT", [512, T], BF16)
    kT_d = dscr("kT", [512, T], BF16)
    vv_d = dscr("vv", [T, 512], BF16)
    ygmT_d = dscr("ygmT", [512, T], BF16)
    gateT_d = dscr("gateT", [3072, T], BF16)
    xs_d = dscr("xs", [T, 1024], BF16)
    BT_d = dscr("BTs", [512, T], BF16)
    CT_d = dscr("CTs", [512, T], BF16)
    Bt_d = dscr("Btok", [T, 512], BF16)
    yp_d = dscr("ypart", [T, 1024], F32)
    yssdT_d = dscr("yssdT", [1024, T], BF16)
    ynaT_d = dscr("ynaT", [512, T], BF16)

    st = ExitStack()
    with st:
        ARENA = 189 * 1024
        arena_t = st.enter_context(nc.sbuf_tensor("arena", [128, ARENA], U8))
        ps = [st.enter_context(nc.psum_tensor("ps%d" % i, [128, 512], F32)) for i in range(8)]
        S = Sched(nc)
        AR = Arena(arena_t, ARENA)
        pb_ctr = [0]

        def PSB(i):
            return ps[i][:], 'P%d' % i

        def nextbank(lo=0, hi=8):
            i = lo + pb_ctr[0] % (hi - lo)
            pb_ctr[0] += 1
            return i

        def dma(out, in_, reads, writes, q='sp'):
            return S.op(q, lambda e: e.dma_start(out=out, in_=in_), reads, writes, dma=True)

        def act(out, in_, func, reads, writes, bias=None, scale=None, accum_out=None):
            kw = {}
            if bias is not None:
                kw['bias'] = bias
            if scale is not None:
                kw['scale'] = scale
            if accum_out is not None:
                kw['accum_out'] = accum_out
            return S.op('act', lambda e: e.activation(out=out, in_=in_, func=func, **kw), reads, writes)

        def tt(out, in0, in1, op, reads, writes, eng='dve'):
            return S.op(eng, lambda e: e.tensor_tensor(out=out, in0=in0, in1=in1, op=op), reads, writes)

        def ts(out, in0, s1, s2, op0, op1, reads, writes, eng='dve'):
            if s2 is None:
                return S.op(eng, lambda e: e.tensor_scalar(out=out, in0=in0, scalar1=s1, scalar2=None, op0=op0), reads, writes)
            return S.op(eng, lambda e: e.tensor_scalar(out=out, in0=in0, scalar1=s1, scalar2=s2, op0=op0, op1=op1), reads, writes)

        def stt(out, in0, scalar, in1, op0, op1, reads, writes):
            return S.op('dve', lambda e: e.scalar_tensor_tensor(out=out, in0=in0, scalar=scalar, in1=in1, op0=op0, op1=op1), reads, writes)

        def recip(out, in_, reads, writes):
            return S.op('dve', lambda e: e.reciprocal(out=out, in_=in_), reads, writes)

        def cp(out, in_, reads, writes, eng='dve'):
            if eng == 'act':
                return S.op('act', lambda e: e.copy(out=out, in_=in_), reads, writes)
            return S.op(eng, lambda e: e.tensor_copy(out=out, in_=in_), reads, writes)

        def mms(lst, reads, writes):
            def fn(e):
                ins = None
                for (o, l, r, s0, s1) in lst:
                    ins = e.matmul(o, lhsT=l, rhs=r, start=s0, stop=s1)
                return ins
            return S.op('pe', fn, reads, writes)

        def trs(lst, reads, writes):
            def fn(e):
                ins = None
                for (o, i_, idn) in lst:
                    ins = e.transpose(o, i_, idn)
                return ins
            return S.op('pe', fn, reads, writes)

        c32 = AR.alloc([NCONST], F32)
        c16 = AR.alloc([512], BF16)
        epsb = AR.alloc([1], F32)
        ident32 = c32[:, 0:128]
        Uf32 = c32[:, 128:256]
        Ub32 = c32[:, 256:384]
        sel32 = c32[0:16, 768:768 + 2048].rearrange("p (h s) -> p h s", s=128)
        ident16 = c16[:, 0:128]
        ones16 = c16[:, 128:256]
        bo16 = c16[:, 256:384]
        pm16 = c16[:, 384:512]
        pv = [AR.alloc([NPV], F32) for _ in range(depth)]
        br = [AR.alloc([NBR], F32) for _ in range(depth)]
        modT = [AR.alloc([48, 2], F32) for _ in range(depth)]
        A1 = [AR.alloc([8, 2], F32) for _ in range(depth)]
        A2 = [AR.alloc([8, 2], F32) for _ in range(depth)]
        scs = AR.alloc([8, 2], F32)
        mneg16 = [AR.alloc([512], BF16) for _ in range(2)]
        sel16f = AR.alloc([2048], BF16)
        sel16 = sel16f[0:16, :].rearrange("p (h s) -> p h s", s=128)
        PERSIST = AR.off

        dma(c32, consts_in, [], ['c32'])
        S.op('dve', lambda e: e.memset(epsb, EPS), [], ['epsb'])
        cp(c16[:, 0:128], c32[:, 0:128], ['c32'], ['c16'])
        cp(c16[:, 128:512], c32[:, 384:768], ['c32'], ['c16'])
        cp(sel16f[0:16, :], c32[0:16, 768:768 + 2048], ['c32'], ['sel16'])
        for d_ in range(2):
            for r_ in range(4):
                ts(mneg16[d_][:, r_ * 128:(r_ + 1) * 128], (Uf32 if d_ == 0 else Ub32), -1.0, -NEG, ALU.add, ALU.mult, ['c32'], ['mneg16'])
        for l in range(depth):
            dma(pv[l], pvec_in[l], [], ['pv%d' % l])
            dma(br[l], brow_in[l:l + 1, :].partition_broadcast(128), [], ['br%d' % l])
        wsrc = wblk_in.rearrange("b p (h e) -> b p h e", e=2048)
        wdst = wblk16.rearrange("b p (h e) -> b p h e", e=2048)
        CB = 4

        def cast_weights(l_):
            for b0 in range(l_ * NBLK, (l_ + 1) * NBLK, CB):
                b1 = min((l_ + 1) * NBLK, b0 + CB)
                dma(wdst[b0:b1], wsrc[b0:b1], [], ['w16_%d' % b for b in range(b0, b1)], q='pool')

        def cast_natab(l_):
            for k in range(0, 200, 50):
                dma(natab16[l_ * 200 + k:l_ * 200 + k + 50], natab_in[l_ * 200 + k:l_ * 200 + k + 50], [], ['natab16_%d' % l_], q='pool')
        cast_weights(0)

        sc_raw = AR.alloc([16], F32)
        wm = [AR.alloc([8, 512], F32) for _ in range(2)]
        dma(sc_raw, cT_in, [], ['sc_raw'])
        act(scs.rearrange("p a b -> p (a b)"), sc_raw, AF.Silu, ['sc_raw'], ['scs'])
        for l in range(depth):
            psM, psMn = PSB(0)
            for blk in range(12):
                slot = blk % 2
                dma(wm[slot].rearrange("p a b -> p (a b)"), wmod_in[l * 12 + blk], [], ['wm%d' % slot])
                lst = []
                for oc4 in range(4):
                    oc = blk * 4 + oc4
                    for kc in range(8):
                        lst.append((psM[:, oc * 2:oc * 2 + 2], wm[slot][:, kc, oc4 * 128:(oc4 + 1) * 128], scs[:, kc, :], kc == 0, kc == 7))
                mms(lst, ['wm%d' % slot, 'scs'], [psMn])
            tt(modT[l], psM[:, 0:96].rearrange("p (a b) -> p a b", b=2), pv[l][:, 0:48].unsqueeze(2).to_broadcast([128, 48, 2]),
               ALU.add, [psMn, 'pv%d' % l], ['modT%d' % l])
            for (Ax, sci, nrm) in ((A1[l], 8, 48), (A2[l], 32, 56)):
                ts(Ax, modT[l][:, sci:sci + 8, :], 1.0, None, ALU.add, None, ['modT%d' % l], ['A%d' % l])
                tt(Ax, Ax, pv[l][:, nrm:nrm + 8].unsqueeze(2).to_broadcast([128, 8, 2]), ALU.mult, ['A%d' % l, 'pv%d' % l], ['A%d' % l])
        S.barrier()

        def norm_sq(xt, n, xtn):
            act(ph['sq'][:, :, :n], xt[:, :, :n], AF.Square, [xtn], ['sq'])

        def norm_rest(xt, n, Acol, shcol, hT, xtn, hTn):
            sq = ph['sq']
            bi = nextbank()
            pss, pn = PSB(bi)
            mms([(pss[:, :n], ones16, sq[:, c, :n], c == 0, c == 7) for c in range(8)], ['sq', 'c16'], [pn])
            rs = ph['rs']
            act(rs[:, :n], pss[:, :n], AF.Sqrt, [pn, 'epsb'], ['rs'], bias=epsb[:, 0:1], scale=1.0 / D)
            recip(rs[:, :n], rs[:, :n], ['rs'], ['rs'])
            for c in range(8):
                tmp = ph['ntmp'][c % 2]
                tt(tmp[:, :n], xt[:, c, :n], rs[:, :n], ALU.mult, [xtn, 'rs'], ['ntmp%d' % (c % 2)])
                act(hT[:, c, :n], tmp[:, :n], AF.Identity, ['ntmp%d' % (c % 2)], [hTn], bias=shcol[:, c:c + 1], scale=Acol[:, c:c + 1])

        def norm_mod(xt, n, Acol, shcol, hT, xtn, hTn):
            norm_sq(xt, n, xtn)
            norm_rest(xt, n, Acol, shcol, hT, xtn, hTn)

        def col_of(t):
            return 2 + t if t < CTX else 6 + t

        for l in range(depth):
            last = (l == depth - 1)
            xsrc = xT_in if l == 0 else x1T
            xsrc3 = xsrc.rearrange("(c p) t -> p c t", p=128)
            wb0 = l * NBLK
            pvl = pv[l]
            brl = br[l]
            AR.off = PERSIST
            ph = {}
            ph['sq'] = AR.alloc([8, 512], BF16)
            ph['rs'] = AR.alloc([512], F32)
            ph['ntmp'] = [AR.alloc([512], F32) for _ in range(2)]
            xt = AR.alloc([8, 512], F32)
            hT2 = [AR.alloc([8, 512], BF16) for _ in range(2)]
            wbuf = [AR.alloc([4096], BF16) for _ in range(3)]
            wsT32 = AR.alloc([1024], F32)
            wsT16 = AR.alloc([8, 128], BF16)
            zst = AR.alloc([4, 1024], BF16)
            xbcst = AR.alloc([16, 512], BF16)
            dst_ = AR.alloc([4, 32], F32)
            dtmp = AR.alloc([32], F32)
            qkst = AR.alloc([8, 512], BF16)
            sq16 = [AR.alloc([512], BF16) for _ in range(2)]
            rs2 = [AR.alloc([512], F32) for _ in range(2)]
            qtmp = [AR.alloc([512], F32) for _ in range(2)]
            qraw = [AR.alloc([512], F32) for _ in range(2)]
            vst = AR.alloc([4, 512], BF16)
            u16 = AR.alloc([4, 512], BF16)
            vg = AR.alloc([512], F32)
            vjunk = AR.alloc([512], BF16)
            ssv4 = AR.alloc([4, 4], F32)
            vn16 = [AR.alloc([512], BF16) for _ in range(4)]
            gtmp = AR.alloc([512], F32)
            ygm16 = [AR.alloc([512], BF16) for _ in range(4)]
            ygst = AR.alloc([4, 512], BF16)
            gst2 = [AR.alloc([4, 512], BF16) for _ in range(2)]
            dma(wsT32, wsT_in[l], [], ['wsT32'])
            cp(wsT16.rearrange("p a b -> p (a b)"), wsT32, ['wsT32'], ['wsT16'])
            wctr = [0]

            def load_w(blk):
                slot = wctr[0] % 3
                wctr[0] += 1
                dma(wbuf[slot], wblk16[wb0 + blk], ['w16_%d' % (wb0 + blk)], ['wb%d' % slot])
                return wbuf[slot], 'wb%d' % slot

            def a_norm1(ti):
                (t0_, n_) = tiles[ti]
                dma(xt[:, :, :n_], xsrc3[:, :, t0_:t0_ + n_], [], ['Axt'])
                norm_sq(xt, n_, 'Axt')

            def a_norm2(ti):
                (t0_, n_) = tiles[ti]
                j_ = 1 if t0_ < CTX else 0
                norm_rest(xt, n_, A1[l][:, :, j_], modT[l][:, 0:8, j_], hT2[ti % 2], 'Axt', 'AhT%d' % (ti % 2))
            a_norm1(0)
            a_norm2(0)
            for ti, (t0, n) in enumerate(tiles):
                isctx = t0 < CTX
                j = 1 if isctx else 0
                ntc = n // 128
                hT = hT2[ti % 2]
                hTn = 'AhT%d' % (ti % 2)
                if ti == len(tiles) // 2:
                    cast_natab(l)
                    if l + 1 < depth:
                        cast_weights(l + 1)

                def fm_group(w3, oc4, tagw):
                    bi = nextbank()
                    p_, pn = PSB(bi)
                    mms([(p_[:, :n], w3[:, kc, oc4 * 128:(oc4 + 1) * 128], hT[:, kc, :n], kc == 0, kc == 7) for kc in range(8)],
                        [tagw, hTn], [pn])
                    return p_, pn

                def tm_group(w3, tc, ncols, tagw):
                    bi = nextbank()
                    p_, pn = PSB(bi)
                    mms([(p_[:, :ncols], hT[:, kc, tc * 128:(tc + 1) * 128], w3[:, kc, :ncols], kc == 0, kc == 7) for kc in range(8)],
                        [tagw, hTn], [pn])
                    return p_, pn

                for blk in range(2):
                    w, wn = load_w(blk)
                    w3 = w.rearrange("p (k c) -> p k c", c=512)
                    for tc in range(ntc):
                        p_, pn = tm_group(w3, tc, 512, wn)
                        act(zst[:, tc, blk * 512:(blk + 1) * 512], p_, AF.Silu, [pn], ['zst'])
                dma(zs_d[t0:t0 + n, :].rearrange("(c p) f -> p c f", p=128), zst[:, :ntc, :], ['zst'], [], q='act')
                if ti + 1 < len(tiles):
                    a_norm1(ti + 1)
                for blk in range(2, 6):
                    w, wn = load_w(blk)
                    w3 = w.rearrange("p (k c) -> p k c", c=512)
                    for oc4 in range(4):
                        p_, pn = fm_group(w3, oc4, wn)
                        ch = (blk - 2) * 4 + oc4
                        if ch % 2 == 0:
                            cp(xbcst[:, ch, :n], p_[:, :n], [pn], ['xbcst'], eng='act')
                        else:
                            cp(xbcst[:, ch, :n], p_[:, :n], [pn], ['xbcst'])
                c0 = col_of(t0)
                dma(xbcT_d.rearrange("(c p) t -> p c t", p=128)[:, :, c0:c0 + n], xbcst[:, :, :n], ['xbcst'], [], q='act')
                if ti + 1 < len(tiles):
                    a_norm2(ti + 1)
                gi = 0
                pend = None

                def qk_tail(par, blk, oc4):
                    sfx = '%d' % par
                    b2 = nextbank()
                    p2, p2n = PSB(b2)
                    mms([(p2[:, :n], bo16, sq16[par][:, :n], True, True)], ['sq16' + sfx, 'c16'], [p2n])
                    act(rs2[par][:, :n], p2[:, :n], AF.Sqrt, [p2n, 'epsb'], ['rs2' + sfx], bias=epsb[:, 0:1], scale=1.0 / 64)
                    recip(rs2[par][:, :n], rs2[par][:, :n], ['rs2' + sfx], ['rs2' + sfx])
                    stt(qtmp[par][:, :n], qraw[par][:, :n], 0.125 if blk == 6 else 1.0, rs2[par][:, :n], ALU.mult, ALU.mult,
                        ['qraw' + sfx, 'rs2' + sfx], ['qtmp' + sfx])
                    wcol = 192 if blk == 6 else 193
                    act(qkst[:, (blk - 6) * 4 + oc4, :n], qtmp[par][:, :n], AF.Copy, ['qtmp' + sfx, 'pv%d' % l], ['qkst'],
                        scale=pvl[:, wcol:wcol + 1])
                for blk in range(6, 8):
                    w, wn = load_w(blk)
                    w3 = w.rearrange("p (k c) -> p k c", c=512)
                    for oc4 in range(4):
                        par = gi % 2
                        gi += 1
                        sfx = '%d' % par
                        p_, pn = fm_group(w3, oc4, wn)
                        act(sq16[par][:, :n], p_[:, :n], AF.Square, [pn], ['sq16' + sfx])
                        cp(qraw[par][:, :n], p_[:, :n], [pn], ['qraw' + sfx], eng='act')
                        if pend is not None:
                            qk_tail(*pend)
                        pend = (par, blk, oc4)
                qk_tail(*pend)
                dma(qT_d.rearrange("(c p) t -> p c t", p=128)[:, :, t0:t0 + n], qkst[:, 0:4, :n], ['qkst'], [], q='act')
                dma(kT_d.rearrange("(c p) t -> p c t", p=128)[:, :, t0:t0 + n], qkst[:, 4:8, :n], ['qkst'], [], q='act')
                w, wn = load_w(8)
                w3 = w.rearrange("p (k c) -> p k c", c=512)
                for tc in range(ntc):
                    p_, pn = tm_group(w3, tc, 512, wn)
                    cp(vst[:, tc, :], p_, [pn], ['vst'], eng='act')
                dma(vv_d[t0:t0 + n, :].rearrange("(c p) f -> p c f", p=128), vst[:, :ntc, :], ['vst'], [], q='act')
                w, wn = load_w(9)
                w3 = w.rearrange("p (k c) -> p k c", c=512)
                for tc in range(ntc):
                    p_, pn = tm_group(w3, tc, 512, wn)
                    act(u16[:, tc, :], p_, AF.Gelu_apprx_tanh, [pn], ['u16'])
                w, wn = load_w(10)
                w3 = w.rearrange("p (k c) -> p k c", c=512)
                for tc in range(ntc):
                    p_, pn = tm_group(w3, tc, 512, wn)
                    act(vg, p_, AF.Gelu_apprx_tanh, [pn], ['vg'])
                    act(vjunk, vg, AF.Square, ['vg'], ['vjunk', 'ssv%d' % tc], accum_out=ssv4[:, tc, 0:1])
                    act(ssv4[:, tc, 1:2], ssv4[:, tc, 0:1], AF.Sqrt, ['ssv%d' % tc, 'epsb'], ['ssv%d' % tc], bias=epsb[:, 0:1], scale=1.0 / 512)
                    recip(ssv4[:, tc, 2:3], ssv4[:, tc, 1:2], ['ssv%d' % tc], ['ssv%d' % tc])
                    stt(vn16[tc], vg, ssv4[:, tc, 2:3], brl[:, 80:592], ALU.mult, ALU.mult, ['vg', 'ssv%d' % tc, 'br%d' % l], ['vn16_%d' % tc])

                def gate_blocks(blks):
                    for blk in blks:
                        w, wn = load_w(blk)
                        w3 = w.rearrange("p (k c) -> p k c", c=512)
                        for oc4 in range(4):
                            p_, pn = fm_group(w3, oc4, wn)
                            gc = (blk - 11) * 4 + oc4
                            gsl = gst2[blk % 2]
                            act(gsl[:, oc4, :n], p_[:, :n], AF.Sigmoid, [pn, 'pv%d' % l], ['gst%d' % (blk % 2)], bias=pvl[:, 64 + gc:65 + gc])
                        r0 = (blk - 11) * 512
                        dma(gateT_d[r0:r0 + 512, :].rearrange("(c p) t -> p c t", p=128)[:, :, t0:t0 + n], gst2[blk % 2][:, :, :n],
                            ['gst%d' % (blk % 2)], [], q='act')
                gate_blocks([11, 12, 13])
                for tc in range(ntc):
                    bm = nextbank()
                    pm_, pmn = PSB(bm)
                    mms([(pm_[:, g * 64:(g + 1) * 64], wsT16[:, g, :], vn16[tc][:, g * 64:(g + 1) * 64], True, True) for g in range(8)],
                        ['wsT16', 'vn16_%d' % tc], [pmn])
                    tt(gtmp.rearrange("p (g d) -> p g d", d=64), pm_.rearrange("p (g d) -> p g d", d=64),
                       pvl[:, 194:202].unsqueeze(2).to_broadcast([128, 8, 64]), ALU.add, [pmn, 'pv%d' % l], ['gtmp'])
                    tt(ygm16[tc], gtmp, u16[:, tc, :], ALU.mult, ['gtmp', 'u16'], ['ygm16_%d' % tc])
                gate_blocks([14, 15])
                for tc in range(ntc):
                    bt_ = nextbank()
                    pt_, ptn = PSB(bt_)
                    pt16 = pt_.bitcast(BF16)
                    trs([(pt16[:, c * 128:(c + 1) * 128], ygm16[tc][:, c * 128:(c + 1) * 128], ident16) for c in range(4)],
                        ['ygm16_%d' % tc, 'c16'], [ptn])
                    cp(ygst[:, :, tc * 128:(tc + 1) * 128], pt16[:, 0:512].rearrange("p (c t) -> p c t", t=128), [ptn], ['ygst'], eng='act')
                dma(ygmT_d.rearrange("(c p) t -> p c t", p=128)[:, :, t0:t0 + n], ygst[:, :, :n], ['ygst'], [], q='act')
                gate_blocks([16])
                w, wn = load_w(17)
                w3 = w[:, 0:256].rearrange("p (k c) -> p k c", c=32)
                for tc in range(ntc):
                    p_, pn = tm_group(w3, tc, 32, wn)
                    tt(dtmp, p_[:, 0:32], brl[:, 0:32], ALU.add, [pn, 'br%d' % l], ['dtmp'])
                    act(dtmp, dtmp, AF.Exp, ['dtmp'], ['dtmp'])
                    act(dst_[:, tc, :], dtmp, AF.Ln, ['dtmp'], ['dst'], bias=1.0)
                dma(dts_d[t0:t0 + n, :].rearrange("(c p) f -> p c f", p=128), dst_[:, :ntc, :], ['dst'], [], q='act')
            S.barrier()
            if 'stopA' in dbg and l == 0:
                break
            AR.off = PERSIST
            cdiag = AR.alloc([80, 128], BF16)
            for j in range(80):
                ts(cdiag[:, j, :], ident32, pvl[:, 104 + j:105 + j], None, ALU.mult, None, ['c32', 'pv%d' % l], ['cdiag'])
            zt = AR.alloc([16, 4], BF16)
            S.op('dve', lambda e: e.memset(zt.rearrange("p a b -> p (a b)"), 0.0), [], ['zt'])
            xbcT3 = xbcT_d.rearrange("(c p) t -> p c t", p=128)
            dma(xbcT3[:, :, 0:2], zt[:, :, 0:2], ['zt'], ['xbcpad'], q='act')
            dma(xbcT3[:, :, 258:262], zt[:, :, 0:4], ['zt'], ['xbcpad'], q='act')
            dma(xbcT3[:, :, 262 + L:264 + L], zt[:, :, 0:2], ['zt'], ['xbcpad'], q='act')
            xpre = AR.alloc([16, 516], BF16)
            xact = AR.alloc([16, 512], BF16)
            cosT = AR.alloc([512], F32)
            sinT = AR.alloc([512], F32)
            t1 = AR.alloc([512], F32)
            t2 = AR.alloc([512], F32)
            bcrot = AR.alloc([8, 512], BF16)
            xsst = AR.alloc([4, 1024], BF16)
            btst = AR.alloc([4, 512], BF16)
            BT_d3 = BT_d.rearrange("(c p) t -> p c t", p=128)
            CT_d3 = CT_d.rearrange("(c p) t -> p c t", p=128)
            for (t0, n) in tiles:
                ntc = n // 128
                c0 = col_of(t0)
                dma(xpre[:, :, :n + 4], xbcT3[:, :, c0 - 2:c0 + n + 2], ['xbcpad'], ['xpre'])
                dma(cosT[:, :n], ropeC_in[:, t0:t0 + n], [], ['cosT'])
                dma(sinT[:, :n], ropeS_in[:, t0:t0 + n], [], ['sinT'])
                for ch in range(16):
                    bi = nextbank()
                    p_, pn = PSB(bi)
                    mms([(p_[:, :n], cdiag[:, k * 16 + ch, :], xpre[:, ch, k:k + n], k == 0, k == 4) for k in range(5)],
                        ['cdiag', 'xpre'], [pn])
                    act(xact[:, ch, :n], p_[:, :n], AF.Silu, [pn, 'pv%d' % l], ['xact'], bias=pvl[:, 88 + ch:89 + ch])
                for tc in range(ntc):
                    bi = nextbank()
                    p_, pn = PSB(bi)
                    pt16 = p_.bitcast(BF16)
                    trs([(pt16[:, ch * 128:(ch + 1) * 128], xact[:, ch, tc * 128:(tc + 1) * 128], ident16) for ch in range(8)],
                        ['xact', 'c16'], [pn])
                    cp(xsst[:, tc, :], pt16, [pn], ['xsst'], eng='act')
                dma(xs_d[t0:t0 + n, :].rearrange("(c p) f -> p c f", p=128), xsst[:, :ntc, :], ['xsst'], [], q='act')
                for i in range(8):
                    ch = 8 + i
                    bi = nextbank()
                    p_, pn = PSB(bi)
                    mms([(p_[:, :n], pm16, xact[:, ch, :n], True, True)], ['c16', 'xact'], [pn])
                    tt(t1[:, :n], xact[:, ch, :n], cosT[:, :n], ALU.mult, ['xact', 'cosT'], ['t1'])
                    tt(t2[:, :n], p_[:, :n], sinT[:, :n], ALU.mult, [pn, 'sinT'], ['t2'])
                    tt(bcrot[:, i, :n], t1[:, :n], t2[:, :n], ALU.add, ['t1', 't2'], ['bcrot'])
                dma(BT_d3[:, :, t0:t0 + n], bcrot[:, 0:4, :n], ['bcrot'], [], q='act')
                dma(CT_d3[:, :, t0:t0 + n], bcrot[:, 4:8, :n], ['bcrot'], [], q='act')
                for tc in range(ntc):
                    bi = nextbank()
                    p_, pn = PSB(bi)
                    pt16 = p_.bitcast(BF16)
                    trs([(pt16[:, g * 128:(g + 1) * 128], bcrot[:, g, tc * 128:(tc + 1) * 128], ident16) for g in range(4)],
                        ['bcrot', 'c16'], [pn])
                    cp(btst[:, tc, :], pt16[:, 0:512], [pn], ['btst'], eng='act')
                dma(Bt_d[t0:t0 + n, :].rearrange("(c p) f -> p c f", p=128), btst[:, :ntc, :], ['btst'], [], q='act')
            S.barrier()
            AR.off = PERSIST
            h32 = [AR.alloc([1024], F32) for _ in range(2)]
            h16 = [AR.alloc([1024], BF16) for _ in range(2)]
            Abc = AR.alloc([32], F32)
            for d in range(2):
                S.op('dve', lambda e, d=d: e.memset(h32[d], 0.0), [], ['h32_%d' % d])
                S.op('dve', lambda e, d=d: e.memset(h16[d], 0.0), [], ['h16_%d' % d])
            act(Abc, brl[:, 32:64], AF.Exp, ['br%d' % l], ['Abc'])
            ts(Abc, Abc, -1.0, None, ALU.mult, None, ['Abc'], ['Abc'])

            class _B:
                pass
            p1bufs = [[None, None], [None, None]]
            p2bufs = [None, None]
            for d in range(2):
                for par in range(2):
                    B_ = _B()
                    B_.xs = AR.alloc([1024], BF16)
                    B_.BT = AR.alloc([4, 128], BF16)
                    B_.CT = AR.alloc([4, 128], BF16)
                    B_.Bt = AR.alloc([512], BF16)
                    B_.dt = AR.alloc([32], F32)
                    B_.dtA = AR.alloc([16], F32)
                    B_.xdt = AR.alloc([1024], BF16)
                    if d == 0:
                        B_.xsD = AR.alloc([1024], BF16)
                    B_.nega = AR.alloc([16], F32)
                    B_.expa = AR.alloc([16], F32)
                    B_.acT = AR.alloc([128], F32)
                    B_.ahl = AR.alloc([2, 128], BF16)
                    B_.dec = AR.alloc([16, 128], BF16)
                    B_.cdB = AR.alloc([16], F32)
                    B_.cbm = AR.alloc([4, 128], BF16)
                    B_.MT = AR.alloc([16, 128], BF16)
                    B_.xdd = AR.alloc([1024], BF16)
                    p1bufs[d][par] = B_
                C_ = _B()
                C_.yi = AR.alloc([1024], F32)
                C_.ydir = AR.alloc([1024], F32)
                C_.yp = AR.alloc([1024], F32)
                C_.zs = AR.alloc([1024], BF16)
                C_.gg = AR.alloc([1024], F32)
                C_.gjunk = AR.alloc([1024], BF16)
                C_.ss = AR.alloc([4], F32)
                C_.gn = AR.alloc([1024], BF16)
                C_.yst = AR.alloc([8, 128], BF16)
                p2bufs[d] = C_
            yssdT3 = yssdT_d.rearrange("(c p) t -> p c t", p=128)

            def r3(v):
                return v.rearrange("p (h d) -> p h d", d=64)

            def scan_p1(c, d, want_y, par):
                sf = '_%d_%d' % (d, par)
                U32 = Uf32 if d == 0 else Ub32
                lend = 127 if d == 0 else 0
                tk = c * 128
                B_ = p1bufs[d][par]
                dma(B_.xs, xs_d[tk:tk + 128, :], [], ['xs' + sf])
                dma(B_.BT, BT_d3[:, :, tk:tk + 128], [], ['BT' + sf])
                dma(B_.CT, CT_d3[:, :, tk:tk + 128], [], ['CT' + sf])
                dma(B_.Bt, Bt_d[tk:tk + 128, :], [], ['Bt' + sf])
                dma(B_.dt, dts_d[tk:tk + 128, :], [], ['dt' + sf])
                dsl = B_.dt[:, d * 16:(d + 1) * 16]
                tt(B_.dtA, dsl, Abc[:, d * 16:(d + 1) * 16], ALU.mult, ['dt' + sf, 'Abc'], ['dtA' + sf])
                tt(r3(B_.xdt), r3(B_.xs), dsl.unsqueeze(2).to_broadcast([128, 16, 64]), ALU.mult, ['xs' + sf, 'dt' + sf], ['xdt' + sf], eng='pool')
                P0, P0n = PSB(0)
                mms([(P0[:, 0:16], U32, B_.dtA, True, True)], ['c32', 'dtA' + sf], [P0n])
                mms([(P0[0:16, 64:192], B_.dtA, U32, True, True)], ['c32', 'dtA' + sf], [P0n])
                act(B_.nega, P0[:, 0:16], AF.Copy, [P0n], ['nega' + sf], scale=-1.0)
                act(B_.expa, P0[:, 0:16], AF.Exp, [P0n], ['expa' + sf])
                act(B_.acT[0:16, :], P0[0:16, 64:192], AF.Copy, [P0n], ['acT' + sf])
                cp(B_.ahl[0:16, 0, :], B_.acT[0:16, :], ['acT' + sf], ['ahl' + sf])
                tt(B_.ahl[0:16, 1, :], B_.acT[0:16, :], B_.ahl[0:16, 0, :], ALU.subtract, ['acT' + sf, 'ahl' + sf], ['ahl' + sf])
                for q4 in range(4):
                    pb, pbn = PSB(1 if q4 % 2 == 0 else 0)
                    lst_ = [(pb, ident16, mneg16[d], True, False)]
                    for hh in range(4):
                        lst_.append((pb[:, hh * 128:(hh + 1) * 128], sel16[:, q4 * 4 + hh, :], B_.ahl[0:16, 0, :], False, False))
                        lst_.append((pb[:, hh * 128:(hh + 1) * 128], sel16[:, q4 * 4 + hh, :], B_.ahl[0:16, 1, :], False, hh == 3))
                    mms(lst_, ['sel16', 'c16', 'mneg16', 'ahl' + sf], [pbn])
                    for hh in range(4):
                        h = q4 * 4 + hh
                        act(B_.dec[:, h, :], pb[:, hh * 128:(hh + 1) * 128], AF.Exp, [pbn, 'nega' + sf], ['dec' + sf], bias=B_.nega[:, h:h + 1])
                    act(B_.cdB[:, q4 * 4:q4 * 4 + 4], pb.rearrange("p (h l) -> p h l", l=128)[:, :, lend], AF.Exp, [pbn], ['cdB' + sf])
                if want_y:
                    P3, P3n = PSB(3)
                    mms([(P3[:, g * 128:(g + 1) * 128], B_.BT[:, g, :], B_.CT[:, g, :], True, True) for g in range(4)],
                        ['BT' + sf, 'CT' + sf], [P3n])
                    P33 = P3.rearrange("p (g l) -> p g l", l=128)
                    for g in range(4):
                        tt(B_.MT[:, 4 * g:4 * g + 4, :], P33[:, g:g + 1, :].to_broadcast([128, 4, 128]), B_.dec[:, 4 * g:4 * g + 4, :], ALU.mult,
                           [P3n, 'dec' + sf], ['MT' + sf])
                    if d == 0:
                        tt(r3(B_.xsD), r3(B_.xs), brl[:, 64:80].unsqueeze(2).to_broadcast([128, 16, 64]), ALU.mult,
                           ['xs' + sf, 'br%d' % l], ['xsD' + sf], eng='pool')
                tt(r3(B_.xdd), r3(B_.xdt), B_.dec[:, :, lend:lend + 1].to_broadcast([128, 16, 64]), ALU.mult,
                   ['xdt' + sf, 'dec' + sf], ['xdd' + sf], eng='pool')

            def scan_p2(c, d, second, want_y, par):
                sf = '_%d_%d' % (d, par)
                sg = '_%d' % d
                tk = c * 128
                B_ = p1bufs[d][par]
                C_ = p2bufs[d]
                if want_y:
                    for half in range(2):
                        py, pyn = PSB(4 + half)
                        lst = []
                        if d == 0:
                            lst.append((py, ident16, B_.xsD[:, half * 512:(half + 1) * 512], True, False))
                        for hh in range(8):
                            h = half * 8 + hh
                            lst.append((py[:, hh * 64:(hh + 1) * 64], B_.MT[:, h, :], B_.xdt[:, h * 64:(h + 1) * 64],
                                        d != 0, (d != 0) or hh == 7))
                        mms(lst, ['c16', 'xsD' + sf, 'MT' + sf, 'xdt' + sf], [pyn])
                for half in range(2):
                    pS, pSn = PSB(6 + half)
                    mms([(pS[:, gg * 256:(gg + 1) * 256], B_.Bt[:, (half * 2 + gg) * 128:(half * 2 + gg + 1) * 128],
                          B_.xdd[:, (half * 2 + gg) * 256:(half * 2 + gg + 1) * 256], True, True) for gg in range(2)],
                        ['Bt' + sf, 'xdd' + sf], [pSn])
                if want_y:
                    for half in range(2):
                        pyi, pyin = PSB(2)
                        mms([(pyi[:, gg * 256:(gg + 1) * 256], B_.CT[:, half * 2 + gg, :],
                              h16[d][:, (half * 2 + gg) * 256:(half * 2 + gg + 1) * 256], True, True) for gg in range(2)],
                            ['CT' + sf, 'h16' + sg], [pyin])
                        cp(C_.yi[:, half * 512:(half + 1) * 512], pyi, [pyin], ['yi' + sg], eng='act')
                    tt(r3(C_.yi), r3(C_.yi), B_.expa.unsqueeze(2).to_broadcast([128, 16, 64]), ALU.mult, ['yi' + sg, 'expa' + sf], ['yi' + sg])
                    for half in range(2):
                        py, pyn = PSB(4 + half)
                        tt(C_.ydir[:, half * 512:(half + 1) * 512], C_.yi[:, half * 512:(half + 1) * 512], py, ALU.add,
                           ['yi' + sg, pyn], ['ydir' + sg])
                tt(r3(h32[d]), r3(h32[d]), B_.cdB.unsqueeze(2).to_broadcast([128, 16, 64]), ALU.mult, ['h32' + sg, 'cdB' + sf], ['h32' + sg], eng='pool')
                for half in range(2):
                    pS, pSn = PSB(6 + half)
                    tt(h32[d][:, half * 512:(half + 1) * 512], h32[d][:, half * 512:(half + 1) * 512], pS, ALU.add,
                       ['h32' + sg, pSn], ['h32' + sg])
                cp(h16[d], h32[d], ['h32' + sg], ['h16' + sg], eng='act')
                if not want_y:
                    return
                if not second:
                    dma(yp_d[tk:tk + 128, :], C_.ydir, ['ydir' + sg], ['ypd%d' % c], q='act')
                    return
                dma(C_.yp, yp_d[tk:tk + 128, :], ['ypd%d' % c], ['yp' + sg])
                dma(C_.zs, zs_d[tk:tk + 128, :], [], ['zs' + sg])
                tt(C_.ydir, C_.ydir, C_.yp, ALU.add, ['ydir' + sg, 'yp' + sg], ['ydir' + sg])
                tt(C_.gg, C_.ydir, C_.zs, ALU.mult, ['ydir' + sg, 'zs' + sg], ['gg' + sg])
                act(C_.gjunk, C_.gg, AF.Square, ['gg' + sg], ['gjunk' + sg, 'ss' + sg], accum_out=C_.ss[:, 0:1])
                act(C_.ss[:, 1:2], C_.ss[:, 0:1], AF.Sqrt, ['ss' + sg, 'epsb'], ['ss' + sg], bias=epsb[:, 0:1], scale=1.0 / 1024)
                recip(C_.ss[:, 2:3], C_.ss[:, 1:2], ['ss' + sg], ['ss' + sg])
                act(C_.gn, C_.gg, AF.Copy, ['gg' + sg, 'ss' + sg], ['gn' + sg], scale=C_.ss[:, 2:3])
                P2, P2n = PSB(2)
                pt16 = P2.bitcast(BF16)
                trs([(pt16[:, ch * 128:(ch + 1) * 128], C_.gn[:, ch * 128:(ch + 1) * 128], ident16) for ch in range(8)],
                    ['gn' + sg, 'c16'], [P2n])
                for ch in range(8):
                    act(C_.yst[:, ch, :], pt16[:, ch * 128:(ch + 1) * 128], AF.Copy, [P2n, 'pv%d' % l], ['yst' + sg],
                        scale=pvl[:, 184 + ch:185 + ch])
                dma(yssdT3[:, :, tk:tk + 128], C_.yst, ['yst' + sg], [], q='act')

            chain_f = list(range(NCH))
            chain_b = [1, 0] + list(range(NCH - 1, 1, -1))
            seen = set()

            def wy(c):
                return not (last and c < 2)

            def do_p2(c, d, par):
                scan_p2(c, d, c in seen, wy(c), par)
                seen.add(c)
            scan_p1(chain_f[0], 0, wy(chain_f[0]), 0)
            scan_p1(chain_b[0], 1, wy(chain_b[0]), 0)
            for i in range(NCH):
                la = []
                if i + 1 < NCH:
                    S.record()
                    scan_p1(chain_f[i + 1], 0, wy(chain_f[i + 1]), (i + 1) % 2)
                    la = S.stop()
                S.record()
                do_p2(chain_b[i], 1, i % 2)
                lb = S.stop()
                S.replay([la, lb])
                la = []
                if i + 1 < NCH:
                    S.record()
                    scan_p1(chain_b[i + 1], 1, wy(chain_b[i + 1]), (i + 1) % 2)
                    la = S.stop()
                S.record()
                do_p2(chain_f[i], 0, i % 2)
                lb = S.stop()
                S.replay([la, lb])
            S.barrier()
            if 'stopB' in dbg and l == 0:
                break
            AR.off = PERSIST
            cls, rel, rep = na_classes(L)
            NQ = L // 128
            tab = AR.alloc([40, 128], BF16)
            tabE = AR.alloc([40, 128], BF16)
            kctx = AR.alloc([2, 4, 128], BF16)
            vctx = [AR.alloc([8, 65], BF16) for _ in range(2)]
            NRING = 8
            kt_buf = [AR.alloc([4, 128], BF16) for _ in range(NRING)]
            v_buf = [AR.alloc([8, 65], BF16) for _ in range(NRING)]
            ring_has = {}
            qt_buf = [AR.alloc([4, 128], BF16) for _ in range(2)]
            pT = [AR.alloc([7 * 128], BF16) for _ in range(2)]
            rec = AR.alloc([8], F32)
            yna = AR.alloc([512], BF16)
            ynast = AR.alloc([4, 128], BF16)
            kT_d3 = kT_d.rearrange("(c p) t -> p c t", p=128)
            qT_d3 = qT_d.rearrange("(c p) t -> p c t", p=128)
            ynaT3 = ynaT_d.rearrange("(c p) t -> p c t", p=128)
            for i in range(2):
                S.op('dve', lambda e, i=i: e.memset(vctx[i].rearrange("p a b -> p (a b)"), 1.0), [], ['vctx%d' % i])
                dma(kctx[:, i, :, :], kT_d3[:, :, i * 128:(i + 1) * 128], [], ['kctx'])
                dma(vctx[i][:, :, 0:64], vv_d[i * 128:(i + 1) * 128, :].rearrange("t (h d) -> t h d", d=64), [], ['vctx%d' % i])
            for i in range(NRING):
                S.op('dve', lambda e, i=i: e.memset(v_buf[i].rearrange("p a b -> p (a b)"), 1.0), [], ['vbuf%d' % i])

            def ring_load(kt):
                sl = kt % NRING
                if ring_has.get(sl) == kt:
                    return
                ring_has[sl] = kt
                ktok = CTX + kt * 128
                dma(kt_buf[sl], kT_d3[:, :, ktok:ktok + 128], [], ['ktb%d' % sl])
                dma(v_buf[sl][:, :, 0:64], vv_d[ktok:ktok + 128, :].rearrange("t (h d) -> t h d", d=64), [], ['vbuf%d' % sl])

            def na_tile(qtok, kts, use_tab, nxt=()):
                qb = qt_buf[(qtok // 128) % 2]
                qbn = 'qt_buf%d' % ((qtok // 128) % 2)
                dma(qb, qT_d3[:, :, qtok:qtok + 128], [], [qbn])
                nk = len(kts)
                for kt in list(kts) + list(nxt):
                    ring_load(kt)
                slots = [kt % NRING for kt in kts]
                nb = nk + 2
                def st_qk(h):
                    cc, e = h // 2, h % 2
                    pA, pAn = PSB(2 * e)
                    pB, pBn = PSB(2 * e + 1)
                    lstA, lstB = [], []
                    rdA, rdB = set([qbn, 'kctx']), set([qbn, 'kctx'])
                    for j in range(nb):
                        bank, lst, rd = (pA, lstA, rdA) if j < 4 else (pB, lstB, rdB)
                        col = (j % 4) * 128
                        qop = qb[e * 64:(e + 1) * 64, cc, :]
                        if j < nk:
                            rd.add('ktb%d' % slots[j])
                            lst.append((bank[:, col:col + 128], kt_buf[slots[j]][e * 64:(e + 1) * 64, cc, :], qop, True, True))
                        else:
                            lst.append((bank[:, col:col + 128], kctx[e * 64:(e + 1) * 64, j - nk, cc, :], qop, True, True))
                    mms(lstA, sorted(rdA), [pAn])
                    if lstB:
                        mms(lstB, sorted(rdB), [pBn])

                def st_sm(h):
                    cc, e = h // 2, h % 2
                    pA, pAn = PSB(2 * e)
                    pB, pBn = PSB(2 * e + 1)
                    na_ = min(nb, 4)
                    act(pT[e][:, 0:na_ * 128], pA[:, 0:na_ * 128], AF.Exp, [pAn], ['pT%d' % e])
                    if nb > 4:
                        act(pT[e][:, 512:512 + (nb - 4) * 128], pB[:, 0:(nb - 4) * 128], AF.Exp, [pBn], ['pT%d' % e])
                    if use_tab:
                        tE4 = tabE.rearrange("p (s h) q -> p s h q", h=8)
                        na4 = min(nk, 4)
                        tt(pT[e][:, 0:na4 * 128].rearrange("p (s q) -> p s q", q=128), pT[e][:, 0:na4 * 128].rearrange("p (s q) -> p s q", q=128),
                           tE4[:, 0:na4, h, :], ALU.mult, ['pT%d' % e, 'tabE'], ['pT%d' % e])
                        if nk > 4:
                            tt(pT[e][:, 512:512 + (nk - 4) * 128].rearrange("p (s q) -> p s q", q=128),
                               pT[e][:, 512:512 + (nk - 4) * 128].rearrange("p (s q) -> p s q", q=128),
                               tE4[:, 4:nk, h, :], ALU.mult, ['pT%d' % e, 'tabE'], ['pT%d' % e])

                def st_pv(h):
                    cc, e = h // 2, h % 2
                    po, pon = PSB(4 + h // 4)
                    col = (h % 4) * 65
                    lst = []
                    rd = ['pT%d' % e, 'vctx0', 'vctx1']
                    for j in range(nb):
                        if j < nk:
                            vb = v_buf[slots[j]]
                            rd.append('vbuf%d' % slots[j])
                        else:
                            vb = vctx[j - nk]
                        lst.append((po[:, col:col + 65], pT[e][:, j * 128:(j + 1) * 128], vb[:, h, :], j == 0, j == nb - 1))
                    mms(lst, rd, [pon])

                st_qk(0)
                st_qk(1)
                for h in range(8):
                    st_sm(h)
                    st_pv(h)
                    if h + 2 < 8:
                        st_qk(h + 2)
                for b4 in range(2):
                    po, pon = PSB(4 + b4)
                    po3 = po[:, 0:260].rearrange("p (h e) -> p h e", e=65)
                    recip(rec[:, b4 * 4:(b4 + 1) * 4], po3[:, :, 64], [pon], ['rec'])
                    tt(yna.rearrange("p (h d) -> p h d", d=64)[:, b4 * 4:(b4 + 1) * 4, :], po3[:, :, 0:64],
                       rec[:, b4 * 4:(b4 + 1) * 4].unsqueeze(2).to_broadcast([128, 4, 64]), ALU.mult, [pon, 'rec'], ['yna'])
                p6, p6n = PSB(6)
                pt16 = p6.bitcast(BF16)
                trs([(pt16[:, c * 128:(c + 1) * 128], yna[:, c * 128:(c + 1) * 128], ident16) for c in range(4)], ['yna', 'c16'], [p6n])
                cp(ynast, pt16[:, 0:512].rearrange("p (c t) -> p c t", t=128), [p6n], ['ynast'], eng='act')
                dma(ynaT3[:, :, qtok:qtok + 128], ynast, ['ynast'], [], q='act')

            if not last:
                for i in range(2):
                    na_tile(i * 128, [], False)
            cur = -1
            for qt in range(NQ):
                c = cls[qt]
                if c != cur:
                    cur = c
                    b0 = l * 200 + c * 40
                    dma(tab, natab16[b0:b0 + 40].rearrange("s k q -> k s q"), ['natab16_%d' % l], ['tab'])
                    act(tabE.rearrange("p a b -> p (a b)"), tab.rearrange("p a b -> p (a b)"), AF.Exp, ['tab'], ['tabE'])
                nxt = [qt + 1 + dk for dk in rel[cls[qt + 1]]] if qt + 1 < NQ else []
                na_tile(CTX + qt * 128, [qt + dk for dk in rel[c]], True, nxt)
            S.barrier()
            if 'stopC' in dbg and l == 0:
                break
            AR.off = PERSIST
            ph = {}
            ph['sq'] = AR.alloc([8, 512], BF16)
            ph['rs'] = AR.alloc([512], F32)
            ph['ntmp'] = [AR.alloc([512], F32) for _ in range(2)]
            xt2 = [AR.alloc([8, 512], F32) for _ in range(2)]
            ys = AR.alloc([8, 512], BF16)
            yn_ = AR.alloc([4, 512], BF16)
            yg = AR.alloc([4, 512], BF16)
            gt = AR.alloc([24, 512], BF16)
            mg = AR.alloc([8, 512], BF16)
            h2 = AR.alloc([8, 512], BF16)
            actb = AR.alloc([22, 512], BF16)
            m1 = [AR.alloc([512], F32) for _ in range(2)]
            m2 = [AR.alloc([512], F32) for _ in range(2)]
            m3 = [AR.alloc([512], F32) for _ in range(2)]
            sa = AR.alloc([512], BF16)
            wbuf = [AR.alloc([4096], BF16) for _ in range(3)]
            wctr[0] = 0
            ygmT3 = ygmT_d.rearrange("(c p) t -> p c t", p=128)
            gateT3 = gateT_d.rearrange("(c p) t -> p c t", p=128)
            x1T3 = x1T.rearrange("(c p) t -> p c t", p=128)
            outT3 = outT.rearrange("(c p) t -> p c t", p=128)

            def load_w(blk):
                slot = wctr[0] % 3
                wctr[0] += 1
                dma(wbuf[slot], wblk16[wb0 + blk], ['w16_%d' % (wb0 + blk)], ['wb%d' % slot])
                return wbuf[slot], 'wb%d' % slot

            dtiles = (tiles[1:] if last else tiles)

            def d_loads(ti):
                (t0_, n_) = dtiles[ti]
                dma(xt2[ti % 2][:, :, :n_], xsrc3[:, :, t0_:t0_ + n_], [], ['Dxt%d' % (ti % 2)])
                dma(ys[:, :, :n_], yssdT3[:, :, t0_:t0_ + n_], [], ['ys'])
                dma(yn_[:, :, :n_], ynaT3[:, :, t0_:t0_ + n_], [], ['yn'])
                dma(yg[:, :, :n_], ygmT3[:, :, t0_:t0_ + n_], [], ['yg'])
                dma(gt[:, :, :n_], gateT3[:, :, t0_:t0_ + n_], [], ['gt'])
            d_loads(0)
            for ti, (t0, n) in enumerate(dtiles):
                isctx = t0 < CTX
                j = 1 if isctx else 0
                xt = xt2[ti % 2]
                xtn = 'Dxt%d' % (ti % 2)
                for jb in range(4):
                    w, wn = load_w(18 + jb)
                    for oo in range(2):
                        oc = 2 * jb + oo
                        base = oo * 2048
                        pa, pan = PSB(nextbank())
                        mms([(pa[:, :n], w[:, base + kc * 128:base + (kc + 1) * 128], ys[:, kc, :n], kc == 0, kc == 7) for kc in range(8)],
                            [wn, 'ys'], [pan])
                        pb_, pbn = PSB(nextbank())
                        mms([(pb_[:, :n], w[:, base + 1024 + kc * 128:base + 1024 + (kc + 1) * 128], yn_[:, kc, :n], kc == 0, kc == 3) for kc in range(4)],
                            [wn, 'yn'], [pbn])
                        pc_, pcn = PSB(nextbank())
                        mms([(pc_[:, :n], w[:, base + 1536 + kc * 128:base + 1536 + (kc + 1) * 128], yg[:, kc, :n], kc == 0, kc == 3) for kc in range(4)],
                            [wn, 'yg'], [pcn])
                        mp = oc % 2
                        ms_ = '%d' % mp
                        tt(m1[mp][:, :n], pa[:, :n], gt[:, oc, :n], ALU.mult, [pan, 'gt'], ['m1' + ms_])
                        tt(m2[mp][:, :n], pb_[:, :n], gt[:, 8 + oc, :n], ALU.mult, [pbn, 'gt'], ['m2' + ms_])
                        tt(m3[mp][:, :n], pc_[:, :n], gt[:, 16 + oc, :n], ALU.mult, [pcn, 'gt'], ['m3' + ms_])
                        tt(m1[mp][:, :n], m1[mp][:, :n], m2[mp][:, :n], ALU.add, ['m1' + ms_, 'm2' + ms_], ['m1' + ms_], eng='pool')
                        tt(mg[:, oc, :n], m1[mp][:, :n], m3[mp][:, :n], ALU.add, ['m1' + ms_, 'm3' + ms_], ['mg'], eng='pool')
                if ti + 1 < len(dtiles):
                    d_loads(ti + 1)
                for jb in range(2):
                    w, wn = load_w(22 + jb)
                    w3 = w.rearrange("p (k c) -> p k c", c=512)
                    for oc4 in range(4):
                        oc = jb * 4 + oc4
                        p_, pn = PSB(nextbank())
                        mms([(p_[:, :n], w3[:, kc, oc4 * 128:(oc4 + 1) * 128], mg[:, kc, :n], kc == 0, kc == 7) for kc in range(8)],
                            [wn, 'mg'], [pn])
                        stt(xt[:, oc, :n], p_[:, :n], modT[l][:, 16 + oc, j:j + 1], xt[:, oc, :n], ALU.mult, ALU.add,
                            [pn, 'modT%d' % l, xtn], [xtn])
                norm_mod(xt, n, A2[l][:, :, j], modT[l][:, 24:32, j], h2, xtn, 'DhT')
                for jb in range(11):
                    w, wn = load_w(24 + jb)
                    w3 = w.rearrange("p (k c) -> p k c", c=512)
                    for hcl in range(2):
                        hc = 2 * jb + hcl
                        pa, pan = PSB(nextbank())
                        mms([(pa[:, :n], w3[:, kc, (hcl * 2) * 128:(hcl * 2 + 1) * 128], h2[:, kc, :n], kc == 0, kc == 7) for kc in range(8)],
                            [wn, 'DhT'], [pan])
                        pb_, pbn = PSB(nextbank())
                        mms([(pb_[:, :n], w3[:, kc, (hcl * 2 + 1) * 128:(hcl * 2 + 2) * 128], h2[:, kc, :n], kc == 0, kc == 7) for kc in range(8)],
                            [wn, 'DhT'], [pbn])
                        act(sa[:, :n], pa[:, :n], AF.Silu, [pan], ['sa'])
                        tt(actb[:, hc, :n], sa[:, :n], pb_[:, :n], ALU.mult, ['sa', pbn], ['actb'])
                for oc in range(8):
                    w, wn = load_w(35 + oc)
                    wv = w[:, 0:2816].rearrange("p (k c) -> p k c", c=128)
                    p_, pn = PSB(nextbank())
                    mms([(p_[:, :n], wv[:, kc, :], actb[:, kc, :n], kc == 0, kc == 21) for kc in range(22)], [wn, 'actb'], [pn])
                    stt(xt[:, oc, :n], p_[:, :n], modT[l][:, 40 + oc, j:j + 1], xt[:, oc, :n], ALU.mult, ALU.add,
                        [pn, 'modT%d' % l, xtn], [xtn])
                if last:
                    dma(outT3[:, :, t0 - CTX:t0 - CTX + n], xt[:, :, :n], [xtn], [], q='act')
                else:
                    dma(x1T3[:, :, t0:t0 + n], xt[:, :, :n], [xtn], [], q='act')
            S.barrier()
        S.emit()
    return nc


def _slab(W):
    k = W.shape[0] // 128
    c = W.shape[1]
    out = np.zeros((128, 4096), np.float32)
    out[:, :k * c] = W.reshape(k, 128, c).transpose(1, 0, 2).reshape(128, k * c)
    return out


def pack_weights(inp, l):
    w_in = inp['w_in'][l]
    blks = []
    segs = [(0, 512), (512, 1024),
            (1024, 1536), (1536, 2048), (2048, 2560), (2560, 3072),
            (3104, 3616), (3616, 4128), (4128, 4640),
            (4640, 5152), (5152, 5664)]
    segs += [(5664 + 512 * i, 5664 + 512 * (i + 1)) for i in range(6)]
    for (a, b) in segs:
        blks.append(_slab(w_in[:, a:b]))
    blks.append(_slab(w_in[:, 3072:3104]))
    wa, wb_, wc = inp['w_branch_ssd'][l], inp['w_branch_na'][l], inp['w_branch_gm'][l]
    for j in range(4):
        blk = np.zeros((128, 4096), np.float32)
        for oo in range(2):
            oc = 2 * j + oo
            base = oo * 2048
            blk[:, base:base + 1024] = wa[:, oc * 128:(oc + 1) * 128].reshape(8, 128, 128).transpose(1, 0, 2).reshape(128, 1024)
            blk[:, base + 1024:base + 1536] = wb_[:, oc * 128:(oc + 1) * 128].reshape(4, 128, 128).transpose(1, 0, 2).reshape(128, 512)
            blk[:, base + 1536:base + 2048] = wc[:, oc * 128:(oc + 1) * 128].reshape(4, 128, 128).transpose(1, 0, 2).reshape(128, 512)
        blks.append(blk)
    wo = inp['w_out'][l]
    for j in range(2):
        blks.append(_slab(wo[:, j * 512:(j + 1) * 512]))
    wf = inp['w_ffn_in'][l]
    for j in range(11):
        slab = np.zeros((1024, 512), np.float32)
        for hcl in range(2):
            for ab in range(2):
                src = ab * 2816 + (2 * j + hcl) * 128
                slab[:, (hcl * 2 + ab) * 128:(hcl * 2 + ab + 1) * 128] = wf[:, src:src + 128]
        blks.append(_slab(slab))
    wfo = inp['w_ffn_out'][l]
    for oc in range(8):
        blks.append(_slab(wfo[:, oc * 128:(oc + 1) * 128]))
    assert len(blks) == NBLK
    return np.stack(blks)


def pcol(v):
    return np.ascontiguousarray(v.reshape(-1, 128).T)


def pack_shared(inp, L, depth):
    T = CTX + L
    R = L // 64
    f32 = np.float32
    wmod = np.stack([inp['w_mod'][l].reshape(8, 128, 12, 512).transpose(2, 1, 0, 3).reshape(12, 128, 4096)
                     for l in range(depth)]).reshape(depth * 12, 128, 4096)
    wblk = np.concatenate([pack_weights(inp, l) for l in range(depth)], axis=0)
    pvec = np.zeros((depth, 128, NPV), f32)
    brow = np.zeros((depth, NBR), f32)
    wsT = np.zeros((depth, 128, 1024), f32)
    for l in range(depth):
        pvec[l, :, 0:48] = pcol(inp['b_mod'][l])
        pvec[l, :, 48:56] = pcol(inp['norm1'][l])
        pvec[l, :, 56:64] = pcol(inp['norm2'][l])
        pvec[l, :, 64:88] = pcol(inp['b_gate'][l])
        pvec[l, :, 88:104] = pcol(inp['conv_b'][l])
        for k in range(5):
            pvec[l, :, 104 + k * 16:104 + (k + 1) * 16] = pcol(inp['conv_w'][l][k])
        pvec[l, :, 184:192] = pcol(inp['ssd_norm'][l])
        pvec[l, :, 192] = np.tile(inp['q_norm'][l], 2)
        pvec[l, :, 193] = np.tile(inp['k_norm'][l], 2)
        pvec[l, :, 194:202] = inp['b_spatial'][l].T
        brow[l, 0:32] = inp['dt_bias'][l].reshape(-1)
        brow[l, 32:64] = inp['a_log'][l].reshape(-1)
        brow[l, 64:80] = inp['d_skip'][l]
        brow[l, 80:592] = inp['gm_norm'][l]
        wsT[l] = inp['w_spatial'][l].transpose(2, 0, 1).reshape(128, 1024)
    freqs = (np.float32(10000.0) ** (-np.arange(32, dtype=f32) / np.float32(32))).astype(f32)
    pos = np.arange(L)
    ang_row = (pos // 64).astype(f32)[:, None] * freqs
    ang_col = (pos % 64).astype(f32)[:, None] * freqs
    ropeC = np.ones((128, T), f32)
    ropeS = np.zeros((128, T), f32)
    for half, ang in ((0, ang_row), (1, ang_col)):
        c = np.cos(ang).T.astype(f32)
        s = np.sin(ang).T.astype(f32)
        ropeC[half * 64:half * 64 + 32, CTX:] = c
        ropeC[half * 64 + 32:half * 64 + 64, CTX:] = c
        ropeS[half * 64:half * 64 + 32, CTX:] = s
        ropeS[half * 64 + 32:half * 64 + 64, CTX:] = s
    consts = np.zeros((128, NCONST), f32)
    i = np.arange(128)
    consts[:, 0:128] = np.eye(128, dtype=f32)
    consts[:, 128:256] = (i[:, None] <= i[None, :]).astype(f32)
    consts[:, 256:384] = (i[:, None] >= i[None, :]).astype(f32)
    consts[:, 384:512] = 1.0
    consts[:, 512:640] = ((i[:, None] // 64) == (i[None, :] // 64)).astype(f32)
    pm = np.zeros((128, 128), f32)
    for base in (0, 64):
        for n in range(32):
            pm[base + n + 32, base + n] = -1.0
            pm[base + n, base + n + 32] = 1.0
    consts[:, 640:768] = pm
    sel = np.zeros((16, 16, 128), f32)
    for h in range(16):
        sel[h, h, :] = 1.0
    consts[0:16, 768:768 + 2048] = sel.reshape(16, 2048)
    cls, rel, rep = na_classes(L)

    def rs_(r):
        return min(max(r - 4, 0), R - 8)
    natab = np.full((depth, 5, 5, 8, 128, 128), NEG, f32)
    kc = np.arange(64)
    qc = np.arange(64)
    cs = np.clip(qc - 8, 0, 48)
    colok = (kc[:, None] >= cs[None, :]) & (kc[:, None] < cs[None, :] + 16)
    ci = np.clip(kc[:, None] - qc[None, :] + 15, 0, 30)
    for c, qt in rep.items():
        for slot, dk in enumerate(rel[c]):
            kt = qt + dk
            for a in range(2):
                r = 2 * qt + a
                for b in range(2):
                    krow = 2 * kt + b
                    if not (rs_(r) <= krow < rs_(r) + 8):
                        continue
                    ri = krow - r + 7
                    for l in range(depth):
                        vals = inp['rpb'][l][:, ri, :][:, ci]
                        blk = np.where(colok[None], vals, np.float32(NEG))
                        natab[l, c, slot, :, b * 64:(b + 1) * 64, a * 64:(a + 1) * 64] = blk
    natab = natab.reshape(depth * 200, 128, 128)
    return dict(wmod=wmod, wblk=wblk, pvec=pvec, brow=brow, wsT=wsT, ropeC=ropeC, ropeS=ropeS,
                natab=natab, consts=consts)


def pack_core(inp, b, shared):
    xT = np.ascontiguousarray(np.concatenate([inp['ctx'][b].T, inp['x'][b].T], axis=1))
    cT = np.zeros((128, 8, 2), np.float32)
    cT[:, :, 0] = pcol(inp['c'][b])
    cT[:, :, 1] = pcol(inp['c_ctx'])
    m = dict(shared)
    m['xT'] = xT
    m['cT'] = cT.reshape(128, 16)
    return m


_NC_CACHE = {}


def kernel(**inputs):
    inp = {k: np.asarray(v, dtype=np.float32) for k, v in inputs.items()}
    B, L, _ = inp['x'].shape
    depth = inp['w_in'].shape[0]
    key = (L, depth)
    if key not in _NC_CACHE:
        _NC_CACHE[key] = build(L, depth)
    nc = _NC_CACHE[key]
    shared = pack_shared(inp, L, depth)
    slots = [0, 1, 4, 5][:B] if B <= 4 else list(range(B))
    n_cores = 8 if B <= 4 else B
    real = {slots[b]: pack_core(inp, b, shared) for b in range(B)}
    zero = None
    in_maps = []
    for i in range(n_cores):
        if i in real:
            in_maps.append(real[i])
        else:
            if zero is None:
                zero = {k: np.zeros_like(v) for k, v in real[slots[0]].items()}
            in_maps.append(zero)
    res = run_bass_kernel_spmd(nc, in_maps, core_ids=list(range(n_cores)))
    out = np.stack([np.ascontiguousarray(res.results[slots[b]]['outT'].T) for b in range(B)])
    return out.astype(np.float32)
```
